# Optimizing a Trainium2 kernel written in Bass

```python
import jax, jax.numpy as jnp
from jax import lax
import numpy as np

D_MODEL = 1024
BATCH = 8
SEQ = 2048
DEPTH = 1
DEC_BATCH = 128
DEC_SEQ = 4
PAST_LEN = 16384
PAGE_SIZE = 128

N_META = 16
C_CONV = D_MODEL
K_CONV = 31
N_HEADS = 8
D_HEAD = 128
D_QK = N_HEADS * D_HEAD
D_QKV = 3 * D_QK
K_SHORT = 4
CHUNK = 64
D_FF = 4 * D_MODEL
EPS = 1e-6
D_IN = 2 * C_CONV + D_QKV + D_QK + 2 * N_HEADS + 2 * D_MODEL

kernel_name = "conformer_gdn_parallel_hybrid_step"


def rms_norm(x, g):
    xf = x.astype(jnp.float32)
    y = xf * lax.rsqrt(jnp.mean(xf * xf, axis=-1, keepdims=True) + EPS)
    return (y * g.astype(jnp.float32)).astype(x.dtype)


def layer_norm(x, g, b):
    xf = x.astype(jnp.float32)
    mu = jnp.mean(xf, axis=-1, keepdims=True)
    var = jnp.mean(jnp.square(xf - mu), axis=-1, keepdims=True)
    y = (xf - mu) * lax.rsqrt(var + EPS)
    return (y * g.astype(jnp.float32) + b.astype(jnp.float32)).astype(x.dtype)


def l2_normalize(x):
    return x * lax.rsqrt(jnp.sum(x * x, axis=-1, keepdims=True) + 1e-6)


def causal_depthwise_conv(x, buf, w):
    ext = jnp.concatenate([buf.astype(x.dtype), x], axis=1)
    y = lax.conv_general_dilated(ext, w[:, None, :].astype(x.dtype), window_strides=(1,),
                                 padding='VALID', dimension_numbers=('NWC', 'WIO', 'NWC'),
                                 feature_group_count=x.shape[-1])
    return y, ext[:, -(w.shape[0] - 1):]


def gdn_intra(q, k, v, beta, g):
    L = q.shape[-2]
    G = jnp.cumsum(g, axis=-1)
    pos = jnp.arange(L)
    causal = pos[:, None] >= pos[None, :]
    decay = jnp.exp(jnp.where(causal, G[..., :, None] - G[..., None, :], -jnp.inf))
    M = beta[..., :, None] * jnp.einsum('...id,...jd->...ij', k, k) * decay
    A = jnp.tril(M, -1) + jnp.eye(L, dtype=M.dtype)
    rhs = jnp.concatenate([v * beta[..., None], k * (beta * jnp.exp(G))[..., None]], axis=-1)
    sol = lax.linalg.triangular_solve(A, rhs, left_side=True, lower=True, unit_diagonal=True)
    value, kcum = sol[..., :v.shape[-1]], sol[..., v.shape[-1]:]
    attn = jnp.einsum('...id,...jd->...ij', q, k) * decay
    q_dec = q * jnp.exp(G)[..., None]
    g_last = G[..., -1]
    k_dec = k * jnp.exp(g_last[..., None] - G)[..., None]
    return value, kcum, attn, q_dec, k_dec, jnp.exp(g_last)


def gdn_inter(S, value, kcum, attn, q_dec, k_dec, last):
    u = value - jnp.einsum('...ld,...de->...le', kcum, S)
    o = jnp.einsum('...ld,...de->...le', q_dec, S) + jnp.einsum('...ij,...je->...ie', attn, u)
    S = S * last[..., None, None] + jnp.einsum('...ld,...le->...de', k_dec, u)
    return S, o


def gated_delta(q, k, v, beta, g, S0, n_lead, chunk):
    B, H, T, _ = q.shape
    S = S0
    outs = []
    if n_lead > 0:
        S, o_lead = gdn_inter(S, *gdn_intra(q[:, :, :n_lead], k[:, :, :n_lead], v[:, :, :n_lead],
                                             beta[:, :, :n_lead], g[:, :, :n_lead]))
        outs.append(o_lead)
    nc = (T - n_lead) // chunk

    def blocks(a):
        a = a[:, :, n_lead:]
        a = a.reshape((B, H, nc, chunk) + a.shape[3:])
        return jnp.moveaxis(a, 2, 0)

    intra = gdn_intra(blocks(q), blocks(k), blocks(v), blocks(beta), blocks(g))
    S, o = lax.scan(lambda s, xs: gdn_inter(s, *xs), S, intra)
    o = jnp.moveaxis(o, 0, 2).reshape(B, H, nc * chunk, v.shape[-1])
    outs.append(o)
    return jnp.concatenate(outs, axis=2), S


def layer(x, conv_buf, qkv_buf, S0, n_lead, chunk, g_mix, w_in, w_dw, b_dw, ln_g, ln_b, w_cout,
          w_short, a_log, dt_bias, g_head, w_dout, w_o, g_mlp, w_up, w_down):
    Bx, T, _ = x.shape
    f32 = jnp.float32
    h = rms_norm(x, g_mix)
    proj = h @ w_in
    cuts = np.cumsum([C_CONV, C_CONV, D_QKV, D_QK, N_HEADS, N_HEADS, D_MODEL]).tolist()
    glu_a, glu_b, qkv, z, b_raw, a_raw, gate_c, gate_d = jnp.split(proj, cuts, axis=-1)

    glu = glu_a * jax.nn.sigmoid(glu_b)
    c, new_conv = causal_depthwise_conv(glu, conv_buf, w_dw)
    c = jax.nn.silu(layer_norm(c + b_dw, ln_g, ln_b))
    conv_out = c @ w_cout

    qkv_c, new_qkv = causal_depthwise_conv(qkv, qkv_buf, w_short)
    qkv_c = jax.nn.silu(qkv_c).astype(f32)
    q, k, v = jnp.split(qkv_c, 3, axis=-1)
    heads = lambda a: a.reshape(Bx, T, N_HEADS, D_HEAD).transpose(0, 2, 1, 3)
    q = l2_normalize(heads(q)) * (D_HEAD ** -0.5)
    k = l2_normalize(heads(k))
    v = heads(v)
    beta = jax.nn.sigmoid(b_raw.astype(f32)).transpose(0, 2, 1)
    g = (-jnp.exp(a_log.astype(f32)) * jax.nn.softplus(a_raw.astype(f32) + dt_bias.astype(f32))).transpose(0, 2, 1)
    o, S_new = gated_delta(q, k, v, beta, g, S0.astype(f32), n_lead, chunk)
    o = o.transpose(0, 2, 1, 3)
    o = rms_norm(o, g_head) * jax.nn.silu(z.astype(f32).reshape(Bx, T, N_HEADS, D_HEAD))
    dn_out = o.reshape(Bx, T, D_QK).astype(x.dtype) @ w_dout

    mix = jax.nn.sigmoid(gate_c) * conv_out + jax.nn.sigmoid(gate_d) * dn_out
    x = x + mix @ w_o
    hm = rms_norm(x, g_mlp) @ w_up
    x = x + jnp.square(jax.nn.relu(hm)) @ w_down
    return x, new_conv, new_qkv, S_new.astype(S0.dtype)


def setup_inputs(seed: int = 0) -> dict:
    key = jax.random.key(seed)
    ks = jax.random.split(key, 24)
    f32 = jnp.float32
    nrm = lambda k, shape, s: jax.random.normal(k, shape, f32) * s
    return {
        "x_prompt": nrm(ks[0], (BATCH, SEQ, D_MODEL), 1.0),
        "x_sample": nrm(ks[1], (DEC_BATCH, DEC_SEQ, D_MODEL), 1.0),
        "state_conv": nrm(ks[2], (DEPTH, DEC_BATCH, K_CONV - 1, C_CONV), 0.5),
        "state_qkv_conv": nrm(ks[3], (DEPTH, DEC_BATCH, K_SHORT - 1, D_QKV), 1.0),
        "state_delta": nrm(ks[4], (DEPTH, DEC_BATCH, N_HEADS, D_HEAD, D_HEAD), 0.5),
        "meta_tokens": nrm(ks[5], (N_META, D_MODEL), 1.0),
        "g_mix": 1.0 + nrm(ks[6], (DEPTH, D_MODEL), 0.01),
        "w_in": nrm(ks[7], (DEPTH, D_MODEL, D_IN), D_MODEL ** -0.5),
        "w_dw": nrm(ks[8], (DEPTH, K_CONV, C_CONV), K_CONV ** -0.5),
        "b_dw": nrm(ks[9], (DEPTH, C_CONV), 0.01),
        "ln_g": 1.0 + nrm(ks[10], (DEPTH, C_CONV), 0.01),
        "ln_b": nrm(ks[11], (DEPTH, C_CONV), 0.01),
        "w_cout": nrm(ks[12], (DEPTH, C_CONV, D_MODEL), C_CONV ** -0.5),
        "w_short": nrm(ks[13], (DEPTH, K_SHORT, D_QKV), K_SHORT ** -0.5),
        "a_log": jnp.log(jax.random.uniform(ks[14], (DEPTH, N_HEADS), f32, 1.0, 16.0)),
        "dt_bias": nrm(ks[15], (DEPTH, N_HEADS), 0.5),
        "g_head": 1.0 + nrm(ks[16], (DEPTH, D_HEAD), 0.01),
        "w_dout": nrm(ks[17], (DEPTH, D_QK, D_MODEL), D_QK ** -0.5),
        "w_o": nrm(ks[18], (DEPTH, D_MODEL, D_MODEL), D_MODEL ** -0.5),
        "g_mlp": 1.0 + nrm(ks[19], (DEPTH, D_MODEL), 0.01),
        "w_up": nrm(ks[20], (DEPTH, D_MODEL, D_FF), D_MODEL ** -0.5),
        "w_down": nrm(ks[21], (DEPTH, D_FF, D_MODEL), D_FF ** -0.5),
        "g_final": 1.0 + nrm(ks[22], (D_MODEL,), 0.01),
    }


def reference(x_prompt, x_sample, state_conv, state_qkv_conv, state_delta, meta_tokens, g_mix, w_in,
              w_dw, b_dw, ln_g, ln_b, w_cout, w_short, a_log, dt_bias, g_head, w_dout, w_o, g_mlp,
              w_up, w_down, g_final):
    Bp = x_prompt.shape[0]
    meta = jnp.broadcast_to(meta_tokens.astype(x_prompt.dtype)[None], (Bp, N_META, D_MODEL))
    xp = jnp.concatenate([meta, x_prompt], axis=1)
    xs = x_sample
    pc_l, pq_l, pS_l, sc_l, sq_l, sS_l = [], [], [], [], [], []
    for l in range(DEPTH):
        params = (g_mix[l], w_in[l], w_dw[l], b_dw[l], ln_g[l], ln_b[l], w_cout[l], w_short[l],
                  a_log[l], dt_bias[l], g_head[l], w_dout[l], w_o[l], g_mlp[l], w_up[l], w_down[l])
        zc = jnp.zeros((Bp, K_CONV - 1, C_CONV), xp.dtype)
        zq = jnp.zeros((Bp, K_SHORT - 1, D_QKV), xp.dtype)
        zS = jnp.zeros((Bp, N_HEADS, D_HEAD, D_HEAD), state_delta.dtype)
        xp, pc, pq, pS = layer(xp, zc, zq, zS, N_META, CHUNK, *params)
        xs, sc, sq, sS = layer(xs, state_conv[l], state_qkv_conv[l], state_delta[l], 0, xs.shape[1], *params)
        pc_l.append(pc); pq_l.append(pq); pS_l.append(pS)
        sc_l.append(sc); sq_l.append(sq); sS_l.append(sS)
    y_prompt = rms_norm(xp[:, N_META:], g_final)
    y_sample = rms_norm(xs, g_final)
    prompt_conv = jnp.stack(pc_l)
    prompt_qkv_conv = jnp.stack(pq_l)
    prompt_delta = jnp.stack(pS_l)
    sample_conv = jnp.stack(sc_l)
    sample_qkv_conv = jnp.stack(sq_l)
    sample_delta = jnp.stack(sS_l)
    return (y_prompt, y_sample, prompt_conv, prompt_qkv_conv, prompt_delta, sample_conv, sample_qkv_conv, sample_delta)
```

```python
import numpy as np
from contextlib import ExitStack
import concourse.bass as bass
import concourse.mybir as mybir
from concourse.bass_utils import run_bass_kernel_spmd

F32 = mybir.dt.float32
BF16 = mybir.dt.bfloat16
AF = mybir.ActivationFunctionType
ALU = mybir.AluOpType

ENGS = ("pe", "act", "dve", "pool", "sp")
SAME_ENGINE_SYNC = {"pe": False, "act": True, "dve": True, "pool": True, "sp": False}

D = 1024
DIN = 8208
DFF = 4096
NPT = 2064
EPS = 1e-6
NEG = -30000.0


class Buf:
    __slots__ = ("name", "lw", "rd", "excl")

    def __init__(self, name, excl=False):
        self.name = name
        self.lw = None
        self.rd = {}
        self.excl = excl


class Prog:
    def __init__(self, nc):
        self.nc = nc
        self.ops = {e: [] for e in ENGS}
        self.dma_counts = {}
        self.dma_keys = []
        self.enabled = True
        self.stop_at = None

    def ck(self, name):
        if self.stop_at is not None and name == self.stop_at:
            self.enabled = False

    def _collect(self, eng, reads, writes):
        deps = {}

        def add(k, v):
            if k[0] == "eng" and k[1] == eng and not SAME_ENGINE_SYNC[eng]:
                return
            if k not in deps or deps[k] < v:
                deps[k] = v

        for b in reads:
            if b.lw is not None:
                add((b.lw[0], b.lw[1]), b.lw[2])
            if b.excl:
                for k, v in b.rd.items():
                    if k != ("eng", eng):
                        add(k, v)
        for b in writes:
            if b.lw is not None:
                add((b.lw[0], b.lw[1]), b.lw[2])
            for k, v in b.rd.items():
                add(k, v)
        return deps

    def _commit(self, tok, reads, writes):
        k = (tok[0], tok[1])
        for b in reads:
            if b.rd.get(k, -1) < tok[2]:
                b.rd[k] = tok[2]
        for b in writes:
            b.lw = tok
            b.rd = {}

    def op(self, eng, fn, reads=(), writes=()):
        if not self.enabled:
            return
        deps = self._collect(eng, reads, writes)
        idx = len(self.ops[eng])
        self.ops[eng].append({"fn": fn, "deps": deps, "dma": None})
        self._commit(("eng", eng, idx), reads, writes)

    def dma(self, eng, fn, semkey, reads=(), writes=()):
        if not self.enabled:
            return
        deps = self._collect(eng, reads, writes)
        if semkey not in self.dma_counts:
            self.dma_counts[semkey] = 0
            self.dma_keys.append(semkey)
        self.dma_counts[semkey] += 16
        cnt = self.dma_counts[semkey]
        self.ops[eng].append({"fn": fn, "deps": deps, "dma": semkey})
        self._commit(("dma", semkey, cnt), reads, writes)

    def wait_all(self, eng, bufs):
        deps = self._collect(eng, bufs, ())
        self.ops[eng].append({"fn": None, "deps": deps, "dma": None})

    def handover(self, old, new):
        merged = {}
        for b in old:
            if b.lw is not None:
                k = (b.lw[0], b.lw[1])
                merged[k] = max(merged.get(k, -1), b.lw[2])
            for k, v in b.rd.items():
                merged[k] = max(merged.get(k, -1), v)
        for b in new:
            b.lw = None
            b.rd = dict(merged)

    def emit(self):
        nc = self.nc
        sig = {e: set() for e in ENGS}
        for e in ENGS:
            for o in self.ops[e]:
                for (kind, key), v in o["deps"].items():
                    if kind == "eng":
                        sig[key].add(v)
        cum = {}
        for e in ENGS:
            c = 0
            m = {}
            for i in range(len(self.ops[e])):
                if i in sig[e]:
                    c += 1
                    m[i] = c
            cum[e] = m
        with ExitStack() as st:
            esem = {e: st.enter_context(nc.semaphore("s_" + e)) for e in ENGS}
            dsem = {k: st.enter_context(nc.semaphore("d_%d" % i)) for i, k in enumerate(self.dma_keys)}
            block = st.enter_context(nc.Block())
            engobj = {"pe": block.tensor, "act": block.scalar, "dve": block.vector,
                      "pool": block.gpsimd, "sp": block.sync}

            def make(ename):
                ops = self.ops[ename]

                def body(eng):
                    seen = {}
                    for i, o in enumerate(ops):
                        for (kind, key), v in o["deps"].items():
                            if kind == "eng":
                                val = cum[key][v]
                                s = esem[key]
                            else:
                                val = v
                                s = dsem[key]
                            if seen.get((kind, key), 0) >= val:
                                continue
                            seen[(kind, key)] = val
                            eng.wait_ge(s, val)
                        if o["fn"] is None:
                            continue
                        inst = o["fn"](eng)
                        if o["dma"] is not None:
                            inst.then_inc(dsem[o["dma"]], 16)
                        elif i in cum[ename]:
                            inst.then_inc(esem[ename], 1)
                return body

            for e in ENGS:
                if self.ops[e]:
                    engobj[e](make(e))


class TileD:
    def __init__(self, kind, n, t0, col, xi):
        self.kind, self.n, self.t0, self.col, self.xi = kind, n, t0, col, xi
        self.nlev = {128: 6, 16: 3, 64: 1}[n] if kind == "P" else 1


def make_groups(with_sample=True):
    groups = []
    ptiles = [(i * 128, min(128, NPT - i * 128)) for i in range(17)]
    split = [ptiles[0:4], ptiles[4:8], ptiles[8:12], ptiles[12:17]]
    for gi, pts in enumerate(split):
        tiles = []
        col = 0
        for xi, (t0, n) in enumerate(pts):
            tiles.append(TileD("P", n, t0, col, xi))
            col += n
        blocks = []
        c = 0
        while c < col:
            blocks.append((c, min(col, c + 512)))
            c += 512
        groups.append(dict(tiles=tiles, pc0=0, npc=col, ncol=col, blocks=blocks, last=(gi == 3), first=False))
    if with_sample:
        groups.append(dict(tiles=[TileD("S", 64, 0, 0, 0)], pc0=64, npc=0, ncol=64, blocks=[(0, 64)], last=False, first=True))
    return groups


def wsched():
    L = []
    for j in range(6):
        L.append(("in", 2048 + 512 * j, 512))
    L += [("in", 5120, 512), ("in", 5632, 512), ("in", 6144, 16)]
    for q in range(2):
        L.append(("in", q * 512, 512))
        L.append(("in", 1024 + q * 512, 512))
    for q in range(2):
        L += [("cout", q * 512, 512), ("in", 6160 + q * 512, 512)]
    for q in range(2):
        L += [("dout", q * 512, 512), ("in", 7184 + q * 512, 512)]
    L += [("o", 0, 512), ("o", 512, 512)]
    for j in range(8):
        L.append(("up", j * 512, 512))
    for nb in range(2):
        for kb in range(4):
            L.append(("down", kb, nb))
    return L


def build(with_sample=True, NW=3, debug=False, max_groups=None, stop_at=None):
    nc = bass.Bass("TRN2", target_bir_lowering=False)
    din = lambda name, shape: nc.dram_tensor(name, shape, F32, kind="ExternalInput").ap()
    dout = lambda name, shape: nc.dram_tensor(name, shape, F32, kind="ExternalOutput").ap()
    x_p = din("x_p", [2048, D]); x_s = din("x_s", [64, D])
    st_conv = din("st_conv", [16, 30, D]); st_qkv = din("st_qkv", [16, 3, 3072]); st_delta = din("st_delta", [16, 8, 128, 128])
    meta = din("meta", [16, D])
    g_mix = din("g_mix", [D]); w_in = din("w_in", [D, DIN]); w_dw = din("w_dw", [31, D]); b_dw = din("b_dw", [D])
    ln_g = din("ln_g", [D]); ln_b = din("ln_b", [D]); w_cout = din("w_cout", [D, D]); w_short = din("w_short", [4, 3072])
    a_log = din("a_log", [8]); dt_bias = din("dt_bias", [8]); g_head = din("g_head", [128]); w_dout = din("w_dout", [D, D])
    w_o = din("w_o", [D, D]); g_mlp = din("g_mlp", [D]); w_up = din("w_up", [D, DFF]); w_down = din("w_down", [DFF, D])
    g_final = din("g_final", [D])
    y_p = dout("y_p", [2048, D]); y_s = dout("y_s", [64, D])
    o_pconv = dout("p_conv", [30, D]); o_pqkv = dout("p_qkv", [3, 3072]); o_pdelta = dout("p_delta", [8, 128, 128])
    o_sconv = dout("s_conv", [16, 30, D]); o_sqkv = dout("s_qkv", [16, 3, 3072]); o_sdelta = dout("s_delta", [16, 8, 128, 128])
    wmap = {"in": w_in, "cout": w_cout, "dout": w_dout, "o": w_o, "up": w_up}

    P = Prog(nc)
    P.stop_at = stop_at
    groups = make_groups(with_sample)
    if max_groups:
        groups = groups[:max_groups]
    MAXC = 528
    MAXPC = max(g["npc"] for g in groups)

    with ExitStack() as st:
        st.enter_context(nc.allow_non_contiguous_dma(reason="small params"))
        cnt = [0]

        def sb(shape, dt, name=None):
            cnt[0] += 1
            return st.enter_context(nc.sbuf_tensor(name or ("t%d" % cnt[0]), shape, dt))

        psT = [st.enter_context(nc.psum_tensor("ps%d" % i, [128, 512], F32)) for i in range(8)]
        psB = [Buf("ps%d" % i, excl=True) for i in range(8)]
        ps_reserved = set()
        ps_ctr = [0]

        pool_ctr = {"d": 0, "g": 0}

        def psum(pool=None):
            if pool == "d":
                i = pool_ctr["d"] % 3
                pool_ctr["d"] += 1
                return psT[i], psB[i], i
            if pool == "g":
                i = 3 + pool_ctr["g"] % 5
                pool_ctr["g"] += 1
                return psT[i], psB[i], i
            while True:
                i = ps_ctr[0] % 8
                ps_ctr[0] += 1
                if i not in ps_reserved:
                    return psT[i], psB[i], i

        def bf(ap):
            return ap[:].bitcast(BF16)

        def mm(out, lhsT, rhs, start, stop, reads, writes, skip=False):
            if skip:
                P.op("pe", lambda e: e.matmul(out, lhsT=lhsT, rhs=rhs, start=start, stop=stop, skip_group_check=True), reads, writes)
            else:
                P.op("pe", lambda e: e.matmul(out, lhsT=lhsT, rhs=rhs, start=start, stop=stop), reads, writes)

        def tr(out, in_, idn, reads, writes):
            P.op("pe", lambda e: e.transpose(out=out, in_=in_, identity=idn), reads, writes)

        def act(out, in_, func, reads, writes, **kw):
            P.op("act", lambda e: e.activation(out=out, in_=in_, func=func, **kw), reads, writes)

        def ts(eng, out, in0, s1, s2, op0, op1, reads, writes):
            if op1 is None:
                P.op(eng, lambda e: e.tensor_scalar(out=out, in0=in0, scalar1=s1, scalar2=None, op0=op0), reads, writes)
            else:
                P.op(eng, lambda e: e.tensor_scalar(out=out, in0=in0, scalar1=s1, scalar2=s2, op0=op0, op1=op1), reads, writes)

        def tt(eng, out, in0, in1, op, reads, writes):
            P.op(eng, lambda e: e.tensor_tensor(out=out, in0=in0, in1=in1, op=op), reads, writes)

        def stt(out, in0, scalar, in1, op0, op1, reads, writes):
            P.op("dve", lambda e: e.scalar_tensor_tensor(out=out, in0=in0, scalar=scalar, in1=in1, op0=op0, op1=op1), reads, writes)

        def cp(eng, out, in_, reads, writes):
            P.op(eng, lambda e: e.tensor_copy(out=out, in_=in_), reads, writes)

        Bc = Buf("consts")
        identf = sb([128, 128], F32); ident = sb([128, 128], BF16)
        onesf = sb([128, 128], F32); onesb = sb([128, 128], BF16)
        triU = sb([128, 128], F32); mmin = sb([128, 128], F32); offd = sb([128, 128], F32)
        blk = sb([64, 64], F32); triUs = sb([64, 64], F32); mmins = sb([64, 64], F32); offds = sb([64, 64], F32)
        bmask = sb([64, 16], F32)
        tmpc = sb([64, 64], F32)

        def pool(fn):
            P.op("pool", fn, reads=[Bc], writes=[Bc])

        pool(lambda e: e.memset(identf[:], 0.0))
        pool(lambda e: e.affine_select(out=identf[:], in_=identf[:], pattern=[[-1, 128]], compare_op=ALU.not_equal,
                                       fill=1.0, base=0, channel_multiplier=1))
        pool(lambda e: e.tensor_copy(out=ident[:], in_=identf[:]))
        pool(lambda e: e.memset(onesf[:], 1.0))
        pool(lambda e: e.memset(onesb[:], 1.0))
        pool(lambda e: e.affine_select(out=triU[:], in_=onesf[:], pattern=[[1, 128]], compare_op=ALU.is_ge,
                                       fill=0.0, base=0, channel_multiplier=-1))
        pool(lambda e: e.memset(mmin[:], 0.0))
        pool(lambda e: e.affine_select(out=mmin[:], in_=mmin[:], pattern=[[1, 128]], compare_op=ALU.is_ge,
                                       fill=NEG, base=0, channel_multiplier=-1))
        pool(lambda e: e.affine_select(out=offd[:], in_=onesf[:], pattern=[[1, 128]], compare_op=ALU.not_equal,
                                       fill=0.0, base=0, channel_multiplier=-1))
        pool(lambda e: e.affine_select(out=blk[:], in_=onesf[:64, :64], pattern=[[-4, 16], [0, 4]], compare_op=ALU.is_ge,
                                       fill=0.0, base=0, channel_multiplier=1))
        pool(lambda e: e.affine_select(out=blk[:], in_=blk[:], pattern=[[4, 16], [0, 4]], compare_op=ALU.is_ge,
                                       fill=0.0, base=3, channel_multiplier=-1))
        pool(lambda e: e.tensor_tensor(out=triUs[:], in0=triU[:64, :64], in1=blk[:], op=ALU.mult))
        pool(lambda e: e.tensor_tensor(out=offds[:], in0=offd[:64, :64], in1=blk[:], op=ALU.mult))
        pool(lambda e: e.tensor_scalar(out=tmpc[:], in0=blk[:], scalar1=-NEG, scalar2=NEG, op0=ALU.mult, op1=ALU.add))
        pool(lambda e: e.tensor_tensor(out=mmins[:], in0=mmin[:64, :64], in1=blk[:], op=ALU.mult))
        pool(lambda e: e.tensor_tensor(out=mmins[:], in0=mmins[:], in1=tmpc[:], op=ALU.add))
        pool(lambda e: e.affine_select(out=bmask[:], in_=onesf[:64, :16], pattern=[[-4, 16]], compare_op=ALU.is_ge,
                                       fill=0.0, base=0, channel_multiplier=1))
        pool(lambda e: e.affine_select(out=bmask[:], in_=bmask[:], pattern=[[4, 16]], compare_op=ALU.is_ge,
                                       fill=0.0, base=3, channel_multiplier=-1))

        NXT = 5
        xt = [sb([128, D], F32, "xt%d" % i) for i in range(NXT)]; Bxt = [Buf("xt%d" % i) for i in range(NXT)]
        Bp = Buf("params")
        pstage = sb([40, 128], F32)
        pv = sb([128, 40], F32)
        wdw = sb([128, 8, 31], F32)
        wsh = sb([128, 24, 4], F32)
        ghead = sb([128, 1], F32)
        dtb = sb([128, 8], F32); nexpA = sb([128, 8], F32)
        gfin = sb([128, D], F32)
        for i, v in enumerate([g_mix, g_mlp, b_dw, ln_g, ln_b]):
            P.dma("sp", lambda e, i=i, v=v: e.dma_start(out=pstage[8 * i:8 * i + 8, :], in_=v.rearrange("(k p) -> k p", p=128)), "par", writes=[Bp])
        P.dma("sp", lambda e: e.dma_start(out=xt[0][0:31, :], in_=w_dw[:, :]), ("x", 0), writes=[Bxt[0]])
        for k in range(3):
            P.dma("sp", lambda e, k=k: e.dma_start(out=xt[1 + k][0:4, :], in_=w_short[:, k * 1024:(k + 1) * 1024]), ("x", 1 + k), writes=[Bxt[1 + k]])
        P.dma("sp", lambda e: e.dma_start(out=ghead[:], in_=g_head.rearrange("(p o) -> p o", o=1)), "par", writes=[Bp])
        P.dma("sp", lambda e: e.dma_start(out=dtb[:], in_=dt_bias.partition_broadcast(128)), "par", writes=[Bp])
        P.dma("sp", lambda e: e.dma_start(out=nexpA[:], in_=a_log.partition_broadcast(128)), "par", writes=[Bp])
        P.dma("sp", lambda e: e.dma_start(out=gfin[:], in_=g_final.partition_broadcast(128)), "par", writes=[Bp])
        act(nexpA[:], nexpA[:], AF.Exp, [Bp], [Bp])
        P.op("act", lambda e: e.mul(nexpA[:], nexpA[:], -1.0), [Bp], [Bp])
        pt, pb, _ = psum()
        tr(pt[:, 0:40], pstage[:, :], identf[:40, :40], [Bp, Bc], [pb])
        cp("dve", pv[:], pt[:, 0:40], [pb], [Bp])
        pt, pb, _ = psum()
        for m in range(8):
            tr(pt[:, m * 31:(m + 1) * 31], xt[0][0:31, m * 128:(m + 1) * 128], identf[:31, :31], [Bxt[0], Bc], [pb])
        cp("dve", wdw[:].rearrange("p m i -> p (m i)"), pt[:, 0:248], [pb], [Bp])
        pt, pb, _ = psum()
        for m in range(24):
            tr(pt[:, m * 4:(m + 1) * 4], xt[1 + m // 8][0:4, (m % 8) * 128:(m % 8 + 1) * 128], identf[:4, :4], [Bxt[1 + m // 8], Bc], [pb])
        cp("dve", wsh[:].rearrange("p m i -> p (m i)"), pt[:, 0:96], [pb], [Bp])
        gmix = pv[:, 0:8]; gmlp = pv[:, 8:16]; bdw = pv[:, 16:24]; lng = pv[:, 24:32]; lnb = pv[:, 32:40]

        P.ck("params")
        scr4 = sb([128, 2 * D], BF16, "scr4"); Bscr = Buf("scr4")
        hb = scr4[:, 0:D]; Bhb = Bscr
        junk = scr4[:, D:2 * D]; Bjunk = Bscr
        junk2 = sb([128, 128], BF16); Bjunk2 = Buf("junk2")
        ssx = sb([128, 4], F32); Bssx = Buf("ssx")
        hT = sb([128, 8, MAXC], BF16, "hT"); BhT = Buf("hT")
        NG = 2
        gext = [sb([128, 30 + MAXC], BF16, "gext%d" % i) for i in range(NG)]; Bgext = [Buf("gext%d" % i) for i in range(NG)]
        ghalo = sb([128, 8, 30], BF16); Bghalo = Buf("ghalo")
        qext = [sb([128, 3 + MAXC], BF16) for _ in range(NG)]; Bqext = [Buf("qext%d" % i) for i in range(NG)]
        qhalo = sb([128, 24, 3], BF16); Bqhalo = Buf("qhalo")
        glut = sb([128, 8, 30], F32, "glut"); Bglut = Buf("glut")
        qt = sb([128, 24, 3], F32); Bqt = Buf("qt")
        arena = sb([128, 32 * MAXC], BF16, "arena")

        def arena_views(mc):
            return (arena[:, 0:24 * mc].rearrange("p (m c) -> p m c", c=mc),
                    arena[:, 24 * mc:32 * mc].rearrange("p (m c) -> p m c", c=mc),
                    arena[:, 0:32 * mc].rearrange("p (m c) -> p m c", c=mc))
        BqkvT = Buf("qkvT"); Bcpre = Buf("cpre"); BmixT = Buf("mixT"); BaT = Buf("aT")
        if with_sample:
            o0 = 2048
            gexS = arena[:, o0:o0 + 4352].rearrange("p (m s t) -> p m s t", m=8, s=16); BgexS = Buf("gexS")
            o0 += 4352
            qexS = arena[:, o0:o0 + 2688].rearrange("p (m s t) -> p m s t", m=24, s=16); BqexS = Buf("qexS")
            o0 += 2688
            glutS = arena[:, o0:o0 + 1024].bitcast(F32).rearrange("p (m c) -> p m c", m=8); BglutS = Buf("glutS")
            o0 += 1024
            qtS = arena[:, o0:o0 + 3072].bitcast(F32).rearrange("p (m c) -> p m c", m=24); BqtS = Buf("qtS")
            o0 += 3072
            assert o0 <= 32 * MAXC
        f4 = [sb([128, 512], F32) for _ in range(2)]; Bf4 = [Buf("f4_%d" % i) for i in range(2)]
        f4i = [0]

        def ftmp():
            i = f4i[0] % 2
            f4i[0] += 1
            return f4[i], Bf4[i]

        cs2 = sb([128, 2, 512], BF16); Bcs2 = Buf("cs2")
        lnst = [sb([128, MAXC], F32) for _ in range(2)]; Blnst = Buf("lnst")
        decT = sb([128, 8, 128], F32); BdecT = Buf("decT")
        decTs = sb([128, 8, 128], F32); BdecTs = Buf("decTs")
        gm = decTs; Bgm = BdecTs
        EB = sb([128, 8, 128], F32); BEB = Buf("EB")
        cT = sb([128, 8, MAXC], BF16, "cT"); BcT = Buf("cT")
        zs = [sb([128, D], BF16) for _ in range(NXT)]; Bzs = [Buf("zs%d" % i) for i in range(NXT)]
        ba = sb([128, NXT, 16], F32); Bba = Buf("ba")
        onT = sb([128, 8, MAXC], BF16, "onT"); BonT = Buf("onT")
        wbuf = [sb([128, 8, 512], BF16) for _ in range(NW)]; Bw = [Buf("w%d" % i) for i in range(NW)]
        dgr = sb([128, 31 * 128], BF16, "dgr")
        dg31 = dgr[:, :].rearrange("p (i c) -> p i c", i=31); Bdg31q = [Buf("dg31q%d" % i) for i in range(4)]
        sqT = scr4[:, :].rearrange("p (j c) -> p j c", j=16); BsqT = Bscr
        dg4 = [sb([128, 4, 128], BF16) for _ in range(2)]; Bdg4 = [Buf("dg4a"), Buf("dg4b")]
        sc = sb([128, 64], F32, "sc"); sc2 = sb([128, 64], F32, "sc2"); Bsc = Buf("sc")
        Wp = [sb([128, 8, 128], BF16) for _ in range(2)]; BWp = [Buf("Wp0"), Buf("Wp1")]
        Qp = [sb([128, 8, 128], BF16) for _ in range(2)]; BQp = [Buf("Qp0"), Buf("Qp1")]
        Zp = [sb([128, 8, 128], BF16) for _ in range(2)]; BZp = [Buf("Zp0"), Buf("Zp1")]
        attnT = sb([128, 8, 128], BF16); BattnT = Buf("attnT")
        rhsk = sb([128, 8, 128], BF16); rhsv = sb([128, 8, 128], BF16); kdec = sb([128, 8, 128], BF16)
        Brhsk = Buf("rhsk"); Brhsv = Buf("rhsv"); Bkdec = Buf("kdec")
        nkcT = sb([128, 8, 128], BF16); BnkcT = Buf("nkcT")
        u = sb([128, 8, 128], BF16); Bu = Buf("u")
        qd = sb([128, 8, 128], BF16); Bqd = Buf("qd")
        on = sb([128, D], BF16, "on"); Bon = Buf("on")
        S = sb([128, 8, 128], F32); BS = Buf("S")
        Sbf = sb([128, 8, 128], BF16); BSbf = Buf("Sbf")
        ms = sb([128, 16], F32, "ms"); Bms = Buf("ms")
        yt = [sb([128, D], F32)]; Byt = [Buf("yt0")]
        stgT = yt[0]; BstgT = Byt[0]
        if with_sample:
            v3 = lambda a: a[:, :].rearrange("p (h c) -> p h c", h=8)
            S0 = [v3(xt[1]), v3(xt[2]), v3(yt[0])]; BS0 = [Bxt[1], Bxt[2], Byt[0]]
            S0key = [("x", 1), ("x", 2), ("yt", 0)]
            So = [v3(xt[3]), v3(xt[4])]; BSo = [Bxt[3], Bxt[4]]
            S0b = [v3(zs[1]), v3(zs[2])]; BS0b = [Bzs[1], Bzs[2]]
            um = [zs[3], zs[4]]; Bum = [Bzs[3], Bzs[4]]
            oTs = sb([128, 8, 64], F32, "oTs"); BoTs = Buf("oTs")
            uTs = sb([128, 8, 64], BF16, "uTs"); BuTs = Buf("uTs")
            stg = yt[0]; Bstg = Byt[0]
        Bout = {k: Buf("o_" + k) for k in ["y_p", "y_s", "p_conv", "p_qkv", "p_delta", "s_conv", "s_qkv", "s_delta"]}

        P.op("pool", lambda e: e.memset(ghalo[:], 0.0), [], [Bghalo])
        P.op("pool", lambda e: e.memset(qhalo[:], 0.0), [], [Bqhalo])
        P.op("pool", lambda e: e.memset(S[:], 0.0), [], [BS])
        P.op("pool", lambda e: e.memset(Sbf[:], 0.0), [], [BSbf])

        sched = wsched()
        allreq = sched * len(groups)
        wstate = {"next_load": 0, "next_use": 0}

        NT = len(sched)
        wsc = nc.dram_tensor("wsc", [NT, 128, 4096], BF16, kind="Internal").ap()
        Bwsc = [Buf("wsc%d" % j) for j in range(NT)]
        Bring = [Buf("ring%d" % j) for j in range(8)]
        pstate = {"next": 0}
        PLOOK = 6

        def wdims(j):
            kind, a, b = sched[j]
            if kind == "down":
                return w_down[a * 1024:(a + 1) * 1024, b * 512:(b + 1) * 512], 512
            return wmap[kind][:, a:a + b], b

        def pro_issue(upto):
            while pstate["next"] < min(NT, upto):
                j = pstate["next"]
                src, n = wdims(j)
                P.dma("pool", lambda e, j=j, src=src, n=n: e.dma_start(
                    out=wsc[j][:, 0:8 * n].rearrange("p (kc n) -> p kc n", n=n),
                    in_=src.rearrange("(kc p) n -> p kc n", p=128)), ("pro", j % 8), writes=[Bwsc[j], Bring[j % 8]])
                pstate["next"] += 1

        def w_issue(i):
            j = i % NT
            slot = i % NW
            pro_issue(j + PLOOK)
            _, n = wdims(j)
            P.dma("sp", lambda e, j=j, n=n, slot=slot: e.dma_start(
                out=wbuf[slot][:, :, 0:n], in_=wsc[j][:, 0:8 * n].rearrange("p (kc n) -> p kc n", n=n)),
                ("w", slot), reads=[Bwsc[j]], writes=[Bw[slot]])

        def wnext(tag):
            i = wstate["next_use"]
            assert allreq[i] == tag, (allreq[i], tag)
            while wstate["next_load"] < min(len(allreq), i + NW - 1):
                w_issue(wstate["next_load"])
                wstate["next_load"] += 1
            wstate["next_use"] += 1
            return wbuf[i % NW], Bw[i % NW]

        def wprefetch():
            i = wstate["next_use"]
            while wstate["next_load"] < min(len(allreq), i + NW - 1):
                w_issue(wstate["next_load"])
                wstate["next_load"] += 1

        def fm(wt, Bwt, mloc, src, Bsrc, blkc, KC=8, kc0=0, pool=None):
            c0, c1 = blkc
            pt, pb, _ = psum(pool)
            for kc in range(KC):
                mm(pt[:, 0:c1 - c0], wt[:, kc, mloc * 128:(mloc + 1) * 128], src[:, kc0 + kc, c0:c1], kc == 0, kc == KC - 1,
                   [Bwt, Bsrc], [pb])
            return pt, pb

        def norm_to_T(G, gvec):
            for t in G["tiles"]:
                n = t.n
                x = xt[t.xi]; Bx = Bxt[t.xi]
                act(junk[:n, :], x[:n, :], AF.Square, [Bx], [Bjunk, Bssx], accum_out=ssx[:n, 0:1])
                act(ssx[:n, 1:2], ssx[:n, 0:1], AF.Sqrt, [Bssx], [Bssx], scale=1.0 / D, bias=EPS)
                P.op("dve", lambda e, n=n: e.reciprocal(out=ssx[:n, 2:3], in_=ssx[:n, 1:2]), [Bssx], [Bssx])
                ts("dve", hb[:n, :], x[:n, :], ssx[:n, 2:3], None, ALU.mult, None, [Bx, Bssx], [Bhb])
                pt, pb, _ = psum()
                pv_ = bf(pt).rearrange("p (k c) -> p k c", k=8)
                for kc in range(8):
                    tr(pv_[:, kc, 0:n], hb[:n, kc * 128:(kc + 1) * 128], ident[:n, :n], [Bhb, Bc], [pb])
                tt("dve", hT[:, :, t.col:t.col + n], pv_[:, :, 0:n], gvec.unsqueeze(2).broadcast_to([128, 8, n]), ALU.mult,
                   [pb, Bp], [BhT])

        def tail_out(src, Bsrc, nm, ncols, nrows, dst_fn, key):
            for m0 in range(0, nm, 4):
                pt, pb, _ = psum()
                for mm_ in range(4):
                    tr(pt[:ncols, mm_ * 128:(mm_ + 1) * 128], src[:, m0 + mm_, :], identf[:, :], [Bsrc, Bc], [pb])
                cp("dve", stgT[:ncols, (m0 % 8) * 128:(m0 % 8 + 4) * 128], pt[:ncols, :], [pb], [BstgT])
                if (m0 + 4) % 8 == 0:
                    dst_fn(m0 // 8)

        def emit_sample_tails():
            def sconv_out(k):
                for s_ in range(16):
                    P.dma("sp", lambda e, s_=s_: e.dma_start(out=o_sconv[s_, 26:30, :], in_=stgT[4 * s_:4 * s_ + 4, :]), ("yt", 0),
                          reads=[BstgT])
            tail_out(glutS, BglutS, 8, 64, 64, sconv_out, None)

            def sqkv_out(k):
                for s_ in range(16):
                    P.dma("sp", lambda e, s_=s_, k=k: e.dma_start(out=o_sqkv[s_, :, k * 1024:(k + 1) * 1024], in_=stgT[4 * s_ + 1:4 * s_ + 4, :]), ("yt", 0),
                          reads=[BstgT])
            tail_out(qtS, BqtS, 24, 64, 64, sqkv_out, None)

        xloaded = set()

        def load_x(gi_, t):
            if (gi_, t.xi) in xloaded:
                return
            xloaded.add((gi_, t.xi))
            x = xt[t.xi]; Bx = Bxt[t.xi]
            if t.kind == "S":
                P.dma("sp", lambda e, x=x: e.dma_start(out=x[0:64, :], in_=x_s[:, :]), ("x", t.xi), writes=[Bx])
            elif t.t0 == 0:
                P.dma("sp", lambda e, x=x: e.dma_start(out=x[0:16, :], in_=meta[:, :]), ("x", t.xi), writes=[Bx])
                P.dma("sp", lambda e, x=x: e.dma_start(out=x[16:128, :], in_=x_p[0:112, :]), ("x", t.xi), writes=[Bx])
            else:
                P.dma("sp", lambda e, x=x, t=t: e.dma_start(out=x[0:t.n, :], in_=x_p[t.t0 - 16:t.t0 - 16 + t.n, :]), ("x", t.xi), writes=[Bx])

        for gi, G in enumerate(groups):
            tiles = G["tiles"]; blocks = G["blocks"]; pc0 = G["pc0"]; npc = G["npc"]
            isSG = (tiles[0].kind == "S")
            sblock = (0, 64) if isSG else None
            qkvT, cpre, aT = arena_views(64 if isSG else MAXC)
            mixT = cpre
            for t in tiles:
                load_x(gi, t)
            if isSG:
                P.handover([BaT, BqkvT, BmixT, Bcpre], [BgexS, BqexS, BglutS, BqtS, BaT])
                for sg in range(4):
                    P.dma("sp", lambda e, sg=sg: e.dma_start(out=stg[0:120, :], in_=st_conv[4 * sg:4 * sg + 4].rearrange("s t c -> (s t) c")),
                          ("yt", 0), writes=[Bstg])
                    pts = []
                    for half in range(2):
                        pt, pb, _ = psum()
                        for mm_ in range(4):
                            m = half * 4 + mm_
                            tr(pt[:, mm_ * 120:(mm_ + 1) * 120], stg[0:120, m * 128:(m + 1) * 128], identf[:120, :120], [Bstg, Bc], [pb])
                        act(gexS[:, half * 4:half * 4 + 4, 4 * sg:4 * sg + 4, 0:30],
                            pt[:, 0:480].rearrange("p (m s t) -> p m s t", m=4, s=4), AF.Copy, [pb], [BgexS])
                for hf in range(3):
                    P.dma("sp", lambda e, hf=hf: e.dma_start(out=stg[0:48, :], in_=st_qkv.rearrange("s t c -> (s t) c")[:, hf * 1024:(hf + 1) * 1024]),
                          ("yt", 0), writes=[Bstg])
                    for half in range(2):
                        pt, pb, _ = psum()
                        for mm_ in range(4):
                            m = half * 4 + mm_
                            tr(pt[:, mm_ * 48:(mm_ + 1) * 48], stg[0:48, m * 128:(m + 1) * 128], identf[:48, :48], [Bstg, Bc], [pb])
                        act(qexS[:, hf * 8 + half * 4:hf * 8 + half * 4 + 4, :, 0:3],
                            pt[:, 0:192].rearrange("p (m s t) -> p m s t", m=4, s=16), AF.Copy, [pb], [BqexS])
                P.dma("sp", lambda e: e.dma_start(out=o_sconv[:, 0:26, :], in_=st_conv[:, 4:30, :]), "o_s_conv")
            P.ck("st")
            wprefetch()
            norm_to_T(G, gmix)
            P.ck("p0")

            P.handover([BaT], [BqkvT, Bcpre])
            QT = [(0, 8), (8, 16), (16, 24), (24, 31)]
            wts = {}

            def build31(m):
                for qi, (i0, i1) in enumerate(QT):
                    tt("pool", dg31[:, i0:i1, :], ident[:].unsqueeze(1).broadcast_to([128, i1 - i0, 128]),
                       wdw[:, m, i0:i1].unsqueeze(2).broadcast_to([128, i1 - i0, 128]), ALU.mult, [Bc, Bp], [Bdg31q[qi]])

            def stageA(m):
                q, ml = m // 4, m % 4
                if ml == 0:
                    wts[q] = (wnext(("in", q * 512, 512)), wnext(("in", 1024 + q * 512, 512)))
                (wa, Bwa), (wb_, Bwb) = wts[q]
                gx = gext[m % NG]; Bgx = Bgext[m % NG]
                if npc:
                    cp("pool", gx[:, 0:30], ghalo[:, m, :], [Bghalo], [Bgx])
                for bk in blocks:
                    c0, c1 = bk; n = c1 - c0
                    pa, pba = fm(wa, Bwa, ml, hT, BhT, bk, pool="d")
                    yield
                    pb2, pbb = fm(wb_, Bwb, ml, hT, BhT, bk, pool="d")
                    sg_, Bsg = ftmp()
                    act(sg_[:, 0:n], pb2[:, 0:n], AF.Sigmoid, [pbb], [Bsg])
                    if bk == sblock:
                        tt("dve", gexS[:, m, :, 30:34], pa[:, 0:64].rearrange("p (s t) -> p s t", t=4),
                           sg_[:, 0:64].rearrange("p (s t) -> p s t", t=4), ALU.mult, [pba, Bsg], [BgexS])
                        tt("dve", glutS[:, m, :], pa[:, 0:64], sg_[:, 0:64], ALU.mult, [pba, Bsg], [BglutS])
                    else:
                        tt("dve", gx[:, 30 + c0 - pc0:30 + c1 - pc0], pa[:, 0:n], sg_[:, 0:n], ALU.mult, [pba, Bsg], [Bgx])
                        if G["last"] and c1 == G["ncol"]:
                            lo = max(c0, G["ncol"] - 30)
                            tt("dve", glut[:, m, 30 - (c1 - lo):30], pa[:, lo - c0:n], sg_[:, lo - c0:n], ALU.mult, [pba, Bsg], [Bglut])
                        elif G["last"] and c1 > G["ncol"] - 30:
                            lo = max(c0, G["ncol"] - 30)
                            off = lo - (G["ncol"] - 30)
                            tt("dve", glut[:, m, off:off + c1 - lo], pa[:, lo - c0:n], sg_[:, lo - c0:n], ALU.mult, [pba, Bsg], [Bglut])
                    yield
                if npc:
                    cp("pool", ghalo[:, m, :], gx[:, npc:npc + 30], [Bgx], [Bghalo])

            def stageB(m):
                gx = gext[m % NG]; Bgx = Bgext[m % NG]
                for bi, bk in enumerate(blocks):
                    c0, c1 = bk; n = c1 - c0
                    pt, pb, _ = psum("d")
                    for i in range(31):
                        Bq = Bdg31q[[qi for qi, (i0, i1) in enumerate(QT) if i0 <= i < i1][0]]
                        if bk == sblock:
                            mm(pt[:, 0:64].rearrange("p (s t) -> p s t", t=4), dg31[:, i, :], gexS[:, m, :, i:i + 4], i == 0, i == 30,
                               [Bq, BgexS], [pb])
                        else:
                            mm(pt[:, 0:n], dg31[:, i, :], gx[:, c0 - pc0 + i:c1 - pc0 + i], i == 0, i == 30, [Bq, Bgx], [pb])
                        if i in (7, 15, 23):
                            yield
                    act(cpre[:, m, c0:c1], pt[:, 0:n], AF.Identity, [pb], [Bcpre], bias=bdw[:, m:m + 1])
                    yield

            def conv_gen():
                yield from stageA(0)
                for m in range(8):
                    build31(m)
                    if m + 1 < 8:
                        yield from stageA(m + 1)
                    yield from stageB(m)
                for bk in blocks:
                    c0, c1 = bk; n = c1 - c0
                    if n > 64:
                        sA = psum("d"); sB = psum("d")
                    else:
                        sA = psum("d"); sB = None
                    for m in range(8):
                        act(cs2[:, 1, 0:n], cpre[:, m, c0:c1], AF.Square, [Bcpre], [Bcs2])
                        if sB is not None:
                            mm(sA[0][:, 0:n], onesb[:, :], cpre[:, m, c0:c1], m == 0, m == 7, [Bc, Bcpre], [sA[1]])
                            mm(sB[0][:, 0:n], onesb[:, :], cs2[:, 1, 0:n], m == 0, m == 7, [Bc, Bcs2], [sB[1]])
                        else:
                            cp("pool", cs2[:, 0, 0:n], cpre[:, m, c0:c1], [Bcpre], [Bcs2])
                            mm(sA[0][:, 0:2 * n].rearrange("p (a n) -> p a n", a=2), onesb[:, :], cs2[:, :, 0:n], m == 0, m == 7,
                               [Bc, Bcs2], [sA[1]])
                        yield
                    if sB is not None:
                        psum_s, Bs1 = sA[0][:, 0:n], sA[1]
                        psum_q, Bs2 = sB[0][:, 0:n], sB[1]
                    else:
                        psum_s, Bs1 = sA[0][:, 0:n], sA[1]
                        psum_q, Bs2 = sA[0][:, n:2 * n], sA[1]
                    mean, Bmean = ftmp(); var, Bvar = ftmp()
                    msq = lnst[1][:, c0:c1]
                    act(mean[:, 0:n], psum_s, AF.Copy, [Bs1], [Bmean], scale=1.0 / D)
                    tt("pool", msq, mean[:, 0:n], mean[:, 0:n], ALU.mult, [Bmean], [Blnst])
                    stt(var[:, 0:n], psum_q, 1.0 / D, msq, ALU.mult, ALU.subtract, [Bs2, Blnst], [Bvar])
                    act(var[:, 0:n], var[:, 0:n], AF.Sqrt, [Bvar], [Bvar], bias=EPS)
                    P.op("dve", lambda e, n=n, c0=c0, c1=c1, var=var: e.reciprocal(out=lnst[0][:, c0:c1], in_=var[:, 0:n]), [Bvar], [Blnst])
                    stt(lnst[1][:, c0:c1], mean[:, 0:n], -1.0, lnst[0][:, c0:c1], ALU.mult, ALU.mult, [Bmean, Blnst], [Blnst])
                    yield
                for m in range(8):
                    for bk in blocks:
                        c0, c1 = bk; n = c1 - c0
                        t1, Bt1 = ftmp(); t2, Bt2 = ftmp()
                        tt("dve", t1[:, 0:n], cpre[:, m, c0:c1], lnst[0][:, c0:c1], ALU.mult, [Bcpre, Blnst], [Bt1])
                        tt("pool", t2[:, 0:n], t1[:, 0:n], lnst[1][:, c0:c1], ALU.add, [Bt1, Blnst], [Bt2])
                        act(cT[:, m, c0:c1], t2[:, 0:n], AF.Silu, [Bt2, Bp], [BcT], scale=lng[:, m:m + 1], bias=lnb[:, m:m + 1])
                        yield
                P.handover([Bcpre], [BmixT])
                for q in range(2):
                    wc, Bwc = wnext(("cout", q * 512, 512))
                    wg, Bwg = wnext(("in", 6160 + q * 512, 512))
                    for ml in range(4):
                        m = q * 4 + ml
                        for bk in blocks:
                            c0, c1 = bk; n = c1 - c0
                            pc_, pbc = fm(wc, Bwc, ml, cT, BcT, bk, pool="d")
                            yield
                            pg, pbg = fm(wg, Bwg, ml, hT, BhT, bk, pool="d")
                            sg_, Bsg = ftmp()
                            act(sg_[:, 0:n], pg[:, 0:n], AF.Sigmoid, [pbg], [Bsg])
                            tt("dve", mixT[:, m, c0:c1], pc_[:, 0:n], sg_[:, 0:n], ALU.mult, [pbc, Bsg], [BmixT])
                            yield

            P.ck("p3")
            wq4 = {}

            def stage4A(m):
                j, ml = m // 4, m % 4
                if ml == 0:
                    wq4[j] = wnext(("in", 2048 + 512 * j, 512))
                wq, Bwq = wq4[j]
                qx = qext[m % NG]; Bqx = Bqext[m % NG]
                d4 = dg4[m % 2]; Bd4 = Bdg4[m % 2]
                tt("pool", d4[:], ident[:].unsqueeze(1).broadcast_to([128, 4, 128]),
                   wsh[:, m, :].unsqueeze(2).broadcast_to([128, 4, 128]), ALU.mult, [Bc, Bp], [Bd4])
                if npc:
                    cp("pool", qx[:, 0:3], qhalo[:, m, :], [Bqhalo], [Bqx])
                for bk in blocks:
                    c0, c1 = bk; n = c1 - c0
                    pq, pbq = fm(wq, Bwq, ml, hT, BhT, bk)
                    if bk == sblock:
                        act(qexS[:, m, :, 3:7], pq[:, 0:64].rearrange("p (s t) -> p s t", t=4), AF.Copy, [pbq], [BqexS])
                        act(qtS[:, m, :], pq[:, 0:64], AF.Copy, [pbq], [BqtS])
                    else:
                        act(qx[:, 3 + c0 - pc0:3 + c1 - pc0], pq[:, 0:n], AF.Copy, [pbq], [Bqx])
                        if G["last"] and c1 == G["ncol"]:
                            assert n >= 3
                            act(qt[:, m, :], pq[:, n - 3:n], AF.Copy, [pbq], [Bqt])
                if npc:
                    cp("pool", qhalo[:, m, :], qx[:, npc:npc + 3], [Bqx], [Bqhalo])

            def stage4B(m):
                qx = qext[m % NG]; Bqx = Bqext[m % NG]
                d4 = dg4[m % 2]; Bd4 = Bdg4[m % 2]
                for bk in blocks:
                    c0, c1 = bk; n = c1 - c0
                    pt, pb, _ = psum()
                    for i in range(4):
                        if bk == sblock:
                            mm(pt[:, 0:64].rearrange("p (s t) -> p s t", t=4), d4[:, i, :], qexS[:, m, :, i:i + 4], i == 0, i == 3,
                               [Bd4, BqexS], [pb])
                        else:
                            mm(pt[:, 0:n], d4[:, i, :], qx[:, c0 - pc0 + i:c1 - pc0 + i], i == 0, i == 3, [Bd4, Bqx], [pb])
                    act(qkvT[:, m, c0:c1], pt[:, 0:n], AF.Silu, [pb], [BqkvT])

            stage4A(0)
            for m in range(24):
                if m + 1 < 24:
                    stage4A(m + 1)
                stage4B(m)
            P.ck("p4")
            for nb in range(2):
                wz, Bwz = wnext(("in", 5120 + nb * 512, 512))
                for t in tiles:
                    pt, pb, _ = psum()
                    for kc in range(8):
                        mm(pt[:t.n, :], hT[:, kc, t.col:t.col + t.n], wz[:, kc, :], kc == 0, kc == 7, [BhT, Bwz], [pb])
                    act(zs[t.xi][:t.n, nb * 512:(nb + 1) * 512], pt[:t.n, :], AF.Silu, [pb], [Bzs[t.xi]])
            wba, Bwba = wnext(("in", 6144, 16))
            for t in tiles:
                pt, pb, _ = psum()
                for kc in range(8):
                    mm(pt[:t.n, 0:16], hT[:, kc, t.col:t.col + t.n], wba[:, kc, 0:16], kc == 0, kc == 7, [BhT, Bwba], [pb])
                act(ba[:t.n, t.xi, :], pt[:t.n, 0:16], AF.Copy, [pb], [Bba])
            wprefetch()

            P.ck("zba")
            cg = conv_gen()

            def fill(k=1):
                for _ in range(k):
                    next(cg, None)
            for t in tiles:
                n = t.n; col = t.col; isS = (t.kind == "S")
                tU = triUs if isS else triU
                tM = mmins if isS else mmin
                tO = offds if isS else offd
                tB = blk if isS else onesf
                qTt = lambda h: qkvT[:, h, col:col + n]
                kTt = lambda h: qkvT[:, 8 + h, col:col + n]
                vTt = lambda h: qkvT[:, 16 + h, col:col + n]
                tt("dve", sqT[:, :, 0:n], qkvT[:, 0:16, col:col + n], qkvT[:, 0:16, col:col + n], ALU.mult, [BqkvT], [BsqT])
                pt, pb, _ = psum("g")
                for j in range(16):
                    mm(pt[:n, j:j + 1], sqT[:, j, 0:n], onesb[:, 0:1], True, True, [BsqT, Bc], [pb])
                ts("dve", sc[:n, 0:16], pt[:n, 0:16], 1e-6, None, ALU.add, None, [pb], [Bsc])
                fill()
                act(sc[:n, 24:32], sc[:n, 8:16], AF.Ln, [Bsc], [Bsc])
                act(sc[:n, 16:24], sc[:n, 24:32], AF.Exp, [Bsc], [Bsc], scale=0.5)
                act(sc[:n, 24:32], sc[:n, 24:32], AF.Exp, [Bsc], [Bsc], scale=-0.5)
                act(sc[:n, 32:40], ba[:n, t.xi, 0:8], AF.Exp, [Bba], [Bsc], scale=-1.0)
                ts("dve", sc[:n, 32:40], sc[:n, 32:40], 1.0, None, ALU.add, None, [Bsc], [Bsc])
                P.op("dve", lambda e, n=n: e.reciprocal(out=sc[:n, 32:40], in_=sc[:n, 32:40]), [Bsc], [Bsc])
                tt("dve", sc[:n, 40:48], ba[:n, t.xi, 8:16], dtb[:n, :], ALU.add, [Bba, Bp], [Bsc])
                stt(sc[:n, 48:56], sc[:n, 40:48], -1.0, sc[:n, 40:48], ALU.mult, ALU.max, [Bsc], [Bsc])
                act(sc[:n, 48:56], sc[:n, 48:56], AF.Exp, [Bsc], [Bsc], scale=-1.0)
                act(sc[:n, 48:56], sc[:n, 48:56], AF.Ln, [Bsc], [Bsc], bias=1.0)
                stt(sc[:n, 40:48], sc[:n, 40:48], 0.0, sc[:n, 48:56], ALU.max, ALU.add, [Bsc], [Bsc])
                tt("dve", sc[:n, 56:64], sc[:n, 40:48], nexpA[:n, :], ALU.mult, [Bsc, Bp], [Bsc])
                tt("dve", sc2[:n, 0:8], sc[:n, 32:40], sc[:n, 24:32], ALU.mult, [Bsc], [Bsc])
                stt(sc2[:n, 8:16], sc2[:n, 0:8], -1.0, sc[:n, 24:32], ALU.mult, ALU.mult, [Bsc], [Bsc])
                pt, pb, _ = psum("g")
                mm(pt[:n, 0:8], tU[:n, :n], sc[:n, 56:64], True, True, [Bc, Bsc], [pb])
                mm(pt[:n, 8:16], tB[:n, :n], sc[:n, 56:64], True, True, [Bc, Bsc], [pb])
                cp("dve", sc2[:n, 16:24], pt[:n, 0:8], [pb], [Bsc])
                fill()
                act(sc2[:n, 24:32], pt[:n, 0:8], AF.Exp, [pb], [Bsc])
                tt("dve", sc2[:n, 32:40], pt[:n, 8:16], sc2[:n, 16:24], ALU.subtract, [pb, Bsc], [Bsc])
                act(sc2[:n, 32:40], sc2[:n, 32:40], AF.Exp, [Bsc], [Bsc])
                tt("dve", sc2[:n, 40:48], sc[:n, 24:32], sc2[:n, 32:40], ALU.mult, [Bsc], [Bsc])
                P.ck("g1")
                tt("dve", gm[:n, :, 0:n], tU[:n, :n].unsqueeze(1).broadcast_to([n, 8, n]),
                   sc[:n, 56:64].unsqueeze(2).broadcast_to([n, 8, n]), ALU.mult, [Bc, Bsc], [Bgm])
                gbs = []
                for hb_ in range(2):
                    pt, pb, _ = psum("g")
                    mm(pt[:, 0:4 * n].rearrange("p (h n) -> p h n", h=4), onesf[:n, :], gm[:n, hb_ * 4:hb_ * 4 + 4, 0:n], True, True,
                       [Bc, Bgm], [pb])
                    gbs.append((pt, pb))
                    fill()
                for h in range(8):
                    pt, pb = gbs[h // 4]
                    stt(decT[:n, h, 0:n], pt[:n, (h % 4) * n:(h % 4 + 1) * n], sc2[:n, 16 + h:17 + h], tM[:n, :n], ALU.subtract, ALU.min,
                        [pb, Bsc, Bc], [BdecT])
                for hb_ in range(2):
                    pt, pb = gbs[hb_]
                    act(EB[:, hb_ * 4:hb_ * 4 + 4, 0:n], pt[:, 0:4 * n].rearrange("p (h n) -> p h n", h=4), AF.Exp, [pb], [BEB])
                act(decT[:n, :, 0:n], decT[:n, :, 0:n], AF.Exp, [BdecT], [BdecT])
                tt("dve", decTs[:n, :, 0:n], decT[:n, :, 0:n], tO[:n, :n].unsqueeze(1).broadcast_to([n, 8, n]), ALU.mult,
                   [BdecT, Bc], [BdecTs])
                P.ck("g2")
                W0, BW0 = Wp[0], BWp[0]
                for hb_ in range(2):
                    pk, pbk, _ = psum("g")
                    pq, pbq, _ = psum("g")
                    for hh in range(4):
                        h = hb_ * 4 + hh
                        mm(pk[:n, hh * 128:hh * 128 + n], kTt(h), kTt(h), True, True, [BqkvT], [pbk])
                        mm(pq[:n, hh * 128:hh * 128 + n], kTt(h), qTt(h), True, True, [BqkvT], [pbq])
                    for hh in range(4):
                        h = hb_ * 4 + hh
                        stt(W0[:n, h, 0:n], pk[:n, hh * 128:hh * 128 + n], sc2[:n, 8 + h:9 + h], decTs[:n, h, 0:n], ALU.mult, ALU.mult,
                            [pbk, Bsc, BdecTs], [BW0])
                        stt(attnT[:n, h, 0:n], pq[:n, hh * 128:hh * 128 + n], sc[:n, 24 + h:25 + h], decT[:n, h, 0:n], ALU.mult, ALU.mult,
                            [pbq, Bsc, BdecT], [BattnT])
                tt("dve", Zp[0][:n, :, 0:n], W0[:n, :, 0:n], ident[:n, :n].unsqueeze(1).broadcast_to([n, 8, n]), ALU.add, [BW0, Bc], [BZp[0]])
                pt, pb, _ = psum("g")
                pvw = bf(pt).rearrange("p (h c) -> p h c", h=8)
                for h in range(8):
                    tr(pvw[:n, h, 0:n], W0[:n, h, 0:n], ident[:n, :n], [BW0, Bc], [pb])
                act(Qp[0][:n, :, 0:n], pvw[:n, :, 0:n], AF.Copy, [pb], [BQp[0]])
                fill()
                cw = cq = cz = 0
                for lev in range(1, t.nlev + 1):
                    nq = 1 - cq
                    for hb_ in range(2):
                        pt, pb, _ = psum("g")
                        for hh in range(4):
                            h = hb_ * 4 + hh
                            mm(pt[:n, hh * 128:hh * 128 + n], Wp[cw][:n, h, 0:n], Qp[cq][:n, h, 0:n], True, True, [BWp[cw], BQp[cq]], [pb])
                        act(Qp[nq][:n, hb_ * 4:hb_ * 4 + 4, 0:n], pt[:n, :].rearrange("p (h c) -> p h c", h=4)[:, :, 0:n], AF.Copy,
                            [pb], [BQp[nq]])
                        fill()
                    if lev < t.nlev:
                        nw = 1 - cw
                        for hb_ in range(2):
                            pt, pb, _ = psum("g")
                            for hh in range(4):
                                h = hb_ * 4 + hh
                                mm(pt[:n, hh * 128:hh * 128 + n], Qp[cq][:n, h, 0:n], Wp[cw][:n, h, 0:n], True, True, [BWp[cw], BQp[cq]], [pb])
                            act(Wp[nw][:n, hb_ * 4:hb_ * 4 + 4, 0:n], pt[:n, :].rearrange("p (h c) -> p h c", h=4)[:, :, 0:n], AF.Copy,
                                [pb], [BWp[nw]])
                            fill()
                        cw = nw
                    cq = nq
                    nz = 1 - cz
                    for hb_ in range(2):
                        pt, pb, _ = psum("g")
                        for hh in range(4):
                            h = hb_ * 4 + hh
                            mm(pt[:n, hh * 128:hh * 128 + n], Qp[cq][:n, h, 0:n], Zp[cz][:n, h, 0:n], True, True, [BQp[cq], BZp[cz]], [pb])
                        tt("dve", Zp[nz][:n, hb_ * 4:hb_ * 4 + 4, 0:n], pt[:n, :].rearrange("p (h c) -> p h c", h=4)[:, :, 0:n],
                           Zp[cz][:n, hb_ * 4:hb_ * 4 + 4, 0:n], ALU.add, [pb, BZp[cz]], [BZp[nz]])
                        fill()
                    cz = nz
                Z = Zp[cz]; BZ = BZp[cz]
                P.ck("g4")
                pk, pbk, _ = psum("g"); pvk = bf(pk).rearrange("p (h c) -> p h c", h=8)
                pv2, pbv, _ = psum("g"); pvv = bf(pv2).rearrange("p (h c) -> p h c", h=8)
                for h in range(8):
                    tr(pvk[:n, h, :], kTt(h), ident[:, :], [BqkvT, Bc], [pbk])
                    tr(pvv[:n, h, :], vTt(h), ident[:, :], [BqkvT, Bc], [pbv])
                P.ck("g4a")
                bc = lambda colap: colap.unsqueeze(2).broadcast_to([n, 8, 128])
                tt("dve", rhsk[:n, :, :], pvk[:n, :, :], bc(sc2[:n, 24:32]), ALU.mult, [pbk, Bsc], [Brhsk])
                tt("dve", kdec[:n, :, :], pvk[:n, :, :], bc(sc2[:n, 40:48]), ALU.mult, [pbk, Bsc], [Bkdec])
                tt("dve", rhsv[:n, :, :], pvv[:n, :, :], bc(sc[:n, 16:24]), ALU.mult, [pbv, Bsc], [Brhsv])
                fill()
                P.ck("g4b")
                tt("pool", qd[:, :, 0:n], qkvT[:, 0:8, col:col + n], EB[:, :, 0:n], ALU.mult, [BqkvT, BEB], [Bqd])
                P.ck("g5")
                for hb_ in range(2):
                    pt, pb, _ = psum("g")
                    for hh in range(4):
                        h = hb_ * 4 + hh
                        mm(pt[:, hh * 128:hh * 128 + n], rhsk[:n, h, :], Z[:n, h, 0:n], True, True, [Brhsk, BZ], [pb])
                    P.op("act", lambda e, pt=pt, hb_=hb_, n=n: e.mul(nkcT[:, hb_ * 4:hb_ * 4 + 4, 0:n],
                                                                     pt[:, :].rearrange("p (h c) -> p h c", h=4)[:, :, 0:n], -1.0), [pb], [BnkcT])
                opsum = []
                if not isS:
                    for hb_ in range(2):
                        pu, pbu, _ = psum("g")
                        for hh in range(4):
                            h = hb_ * 4 + hh
                            mm(pu[:n, hh * 128:(hh + 1) * 128], Z[:n, h, 0:n], rhsv[:n, h, :], True, False, [BZ, Brhsv], [pbu])
                            mm(pu[:n, hh * 128:(hh + 1) * 128], nkcT[:, h, 0:n], Sbf[:, h, :], False, True, [BnkcT, BSbf], [pbu])
                        tt("dve", u[:n, hb_ * 4:hb_ * 4 + 4, :], pu[:n, :].rearrange("p (h e) -> p h e", h=4),
                           sc2[:n, hb_ * 4:hb_ * 4 + 4].unsqueeze(2).broadcast_to([n, 4, 128]), ALU.mult, [pbu, Bsc], [Bu])
                        fill()
                    for hb_ in range(2):
                        po, pbo, _ = psum("g")
                        for hh in range(4):
                            h = hb_ * 4 + hh
                            mm(po[:n, hh * 128:(hh + 1) * 128], qd[:, h, 0:n], Sbf[:, h, :], True, False, [Bqd, BSbf], [pbo])
                            mm(po[:n, hh * 128:(hh + 1) * 128], attnT[:n, h, 0:n], u[:n, h, :], False, True, [BattnT, Bu], [pbo])
                        opsum.append((po, pbo))
                        fill()
                    for hb_ in range(2):
                        pS, pbS, _ = psum("g")
                        for hh in range(4):
                            h = hb_ * 4 + hh
                            mm(pS[:, hh * 128:(hh + 1) * 128], kdec[:n, h, :], u[:n, h, :], True, True, [Bkdec, Bu], [pbS])
                        for hh in range(4):
                            h = hb_ * 4 + hh
                            stt(S[:, h, :], S[:, h, :], EB[:, h, n - 1:n], pS[:, hh * 128:(hh + 1) * 128], ALU.mult, ALU.add,
                                [BS, BEB, pbS], [BS])
                        act(Sbf[:, hb_ * 4:hb_ * 4 + 4, :], S[:, hb_ * 4:hb_ * 4 + 4, :], AF.Copy, [BS], [BSbf])
                        fill()
                else:
                    puT, pbuT, _ = psum("g"); puTv = puT[:, :].rearrange("p (h c) -> p h c", h=8)
                    poT, pboT, _ = psum("g"); poTv = poT[:, :].rearrange("p (h c) -> p h c", h=8)
                    for h in range(8):
                        mm(puTv[:, h, :], rhsv[:n, h, :], Z[:n, h, 0:n], h == 0, False, [Brhsv, BZ], [pbuT], skip=True)
                    for s in range(16):
                        s0 = S0[s % 3]; Bs0 = BS0[s % 3]; s0b = S0b[s % 2]; Bs0b = BS0b[s % 2]
                        P.dma("sp", lambda e, s=s, s0=s0: e.dma_start(out=s0[:], in_=st_delta[s].rearrange("h d e -> d h e")), S0key[s % 3], writes=[Bs0])
                        act(s0b[:], s0[:], AF.Copy, [Bs0], [Bs0b])
                        for h in range(8):
                            mm(puTv[:, h, 4 * s:4 * s + 4], s0b[:, h, :], nkcT[:, h, 4 * s:4 * s + 4], False, False, [Bs0b, BnkcT], [pbuT], skip=True)
                            mm(poTv[:, h, 4 * s:4 * s + 4], s0b[:, h, :], qd[:, h, 4 * s:4 * s + 4], (s == 0 and h == 0), False, [Bs0b, Bqd], [pboT], skip=True)
                    act(uTs[:], puTv, AF.Copy, [pbuT], [BuTs])
                    pt, pb, _ = psum("g"); ptv = bf(pt).rearrange("p (h c) -> p h c", h=8)
                    for h in range(8):
                        tr(ptv[:n, h, :], uTs[:, h, :], ident[:, :], [BuTs, Bc], [pb])
                    for h in range(8):
                        ts("dve", u[:n, h, :], ptv[:n, h, :], sc2[:n, h:h + 1], None, ALU.mult, None, [pb, Bsc], [Bu])
                    for h in range(8):
                        mm(poTv[:, h, :], u[:n, h, :], attnT[:n, h, 0:n], False, h == 7, [Bu, BattnT], [pboT], skip=True)
                    act(oTs[:], poTv, AF.Copy, [pboT], [BoTs])
                    for hb_ in range(2):
                        po, pbo, _ = psum("g")
                        for hh in range(4):
                            tr(po[:n, hh * 128:(hh + 1) * 128], oTs[:, hb_ * 4 + hh, :], identf[:, :], [BoTs, Bc], [pbo])
                        opsum.append((po, pbo))
                        fill()
                    def sample_state_update(n=n):
                        for s in range(16):
                            s0 = S0[(s + 1) % 3]; Bs0 = BS0[(s + 1) % 3]
                            P.dma("sp", lambda e, s=s, s0=s0: e.dma_start(out=s0[:], in_=st_delta[s].rearrange("h d e -> d h e")), S0key[(s + 1) % 3], writes=[Bs0])
                            ums = um[s % 2]; Bums = Bum[s % 2]
                            act(ums[:n, :], u[:n, :, :].rearrange("p h e -> p (h e)"), AF.Identity, [Bu, Bc], [Bums], scale=bmask[:n, s:s + 1])
                            so = So[s % 2]; Bso = BSo[s % 2]
                            for hb_ in range(2):
                                pS, pbS, _ = psum("g")
                                for hh in range(4):
                                    h = hb_ * 4 + hh
                                    mm(pS[:, hh * 128:(hh + 1) * 128], kdec[:n, h, :], ums[:n, h * 128:(h + 1) * 128], True, True, [Bkdec, Bums], [pbS])
                                for hh in range(4):
                                    h = hb_ * 4 + hh
                                    stt(so[:, h, :], s0[:, h, :], EB[:, h, 4 * s + 3:4 * s + 4], pS[:, hh * 128:(hh + 1) * 128], ALU.mult, ALU.add,
                                        [Bs0, BEB, pbS], [Bso])
                            P.dma("sp", lambda e, s=s, so=so: e.dma_start(out=o_sdelta[s].rearrange("h d e -> d h e"), in_=so[:]), ("x", 3 + s % 2),
                                  reads=[Bso])
                P.ck("g6")
                for h in range(8):
                    po, pbo = opsum[h // 4]
                    act(junk2[:n, :], po[:n, (h % 4) * 128:(h % 4 + 1) * 128], AF.Square, [pbo], [Bjunk2, Bms], accum_out=ms[:n, h:h + 1])
                ts("dve", ms[:n, 8:16], sc[:n, 0:8], 128.0 * EPS, None, ALU.mult, None, [Bsc], [Bms])
                stt(ms[:n, 8:16], ms[:n, 0:8], 1.0 / 128, ms[:n, 8:16], ALU.mult, ALU.add, [Bms], [Bms])
                act(ms[:n, 8:16], ms[:n, 8:16], AF.Ln, [Bms], [Bms])
                act(ms[:n, 0:8], ms[:n, 8:16], AF.Exp, [Bms], [Bms], scale=-0.5)
                for h in range(8):
                    po, pbo = opsum[h // 4]
                    stt(on[:n, h * 128:(h + 1) * 128], po[:n, (h % 4) * 128:(h % 4 + 1) * 128], ms[:n, h:h + 1], zs[t.xi][:n, h * 128:(h + 1) * 128],
                        ALU.mult, ALU.mult, [pbo, Bms, Bzs[t.xi]], [Bon])
                pt, pb, _ = psum("g"); ptv = bf(pt).rearrange("p (h c) -> p h c", h=8)
                for h in range(8):
                    tr(ptv[:, h, 0:n], on[:n, h * 128:(h + 1) * 128], ident[:n, :n], [Bon, Bc], [pb])
                ts("dve", onT[:, :, col:col + n], ptv[:, :, 0:n], ghead[:, 0:1], None, ALU.mult, None, [pb, Bp], [BonT])
                fill()
                P.ck("g7")
                if isS:
                    sample_state_update()
                P.ck("g8")

            for _ in cg:
                pass
            if isSG:
                emit_sample_tails()
            for q in range(2):
                wd, Bwd = wnext(("dout", q * 512, 512))
                wg, Bwg = wnext(("in", 7184 + q * 512, 512))
                for ml in range(4):
                    m = q * 4 + ml
                    for bk in blocks:
                        c0, c1 = bk; n = c1 - c0
                        pd_, pbd = fm(wd, Bwd, ml, onT, BonT, bk)
                        pg, pbg = fm(wg, Bwg, ml, hT, BhT, bk)
                        sg_, Bsg = ftmp(); t1, Bt1 = ftmp()
                        act(sg_[:, 0:n], pg[:, 0:n], AF.Sigmoid, [pbg], [Bsg])
                        tt("dve", t1[:, 0:n], pd_[:, 0:n], sg_[:, 0:n], ALU.mult, [pbd, Bsg], [Bt1])
                        tt("dve", mixT[:, m, c0:c1], mixT[:, m, c0:c1], t1[:, 0:n], ALU.add, [BmixT, Bt1], [BmixT])
            P.ck("p5")
            for nb in range(2):
                wo_, Bwo = wnext(("o", nb * 512, 512))
                for t in tiles:
                    pt, pb, _ = psum()
                    for kc in range(8):
                        mm(pt[:t.n, :], mixT[:, kc, t.col:t.col + t.n], wo_[:, kc, :], kc == 0, kc == 7, [BmixT, Bwo], [pb])
                    x = xt[t.xi]
                    tt("dve", x[:t.n, nb * 512:(nb + 1) * 512], pt[:t.n, :], x[:t.n, nb * 512:(nb + 1) * 512], ALU.add, [pb, Bxt[t.xi]], [Bxt[t.xi]])
            wprefetch()
            P.ck("p6")
            norm_to_T(G, gmlp)
            P.handover([BqkvT, BmixT, Bcpre], [BaT])
            for j in range(8):
                wu, Bwu = wnext(("up", j * 512, 512))
                for ml in range(4):
                    m = j * 4 + ml
                    for bk in blocks:
                        c0, c1 = bk; n = c1 - c0
                        pu, pbu = fm(wu, Bwu, ml, hT, BhT, bk)
                        r, Br = ftmp()
                        act(r[:, 0:n], pu[:, 0:n], AF.Relu, [pbu], [Br])
                        act(aT[:, m, c0:c1], r[:, 0:n], AF.Square, [Br], [BaT])
            P.ck("p8")
            for nb in range(2):
                banks = [psum() for _ in tiles]
                for kb in range(4):
                    wd, Bwd = wnext(("down", kb, nb))
                    for ti, t in enumerate(tiles):
                        pt, pb, _ = banks[ti]
                        for kc in range(8):
                            mm(pt[:t.n, :], aT[:, kb * 8 + kc, t.col:t.col + t.n], wd[:, kc, :], kb == 0 and kc == 0, kb == 3 and kc == 7,
                               [BaT, Bwd], [pb])
                for ti, t in enumerate(tiles):
                    pt, pb, _ = banks[ti]
                    x = xt[t.xi]
                    tt("dve", x[:t.n, nb * 512:(nb + 1) * 512], pt[:t.n, :], x[:t.n, nb * 512:(nb + 1) * 512], ALU.add, [pb, Bxt[t.xi]], [Bxt[t.xi]])
            P.ck("p9")
            for ti, t in enumerate(tiles):
                n = t.n; x = xt[t.xi]; Bx = Bxt[t.xi]
                y = yt[0]; By = Byt[0]
                act(junk[:n, :], x[:n, :], AF.Square, [Bx], [Bjunk, Bssx], accum_out=ssx[:n, 0:1])
                act(ssx[:n, 1:2], ssx[:n, 0:1], AF.Sqrt, [Bssx], [Bssx], scale=1.0 / D, bias=EPS)
                P.op("dve", lambda e, n=n: e.reciprocal(out=ssx[:n, 2:3], in_=ssx[:n, 1:2]), [Bssx], [Bssx])
                stt(y[:n, :], x[:n, :], ssx[:n, 2:3], gfin[:n, :], ALU.mult, ALU.mult, [Bx, Bssx, Bp], [By])
                if t.kind == "S":
                    P.dma("sp", lambda e, y=y: e.dma_start(out=y_s[:, :], in_=y[0:64, :]), ("yt", 0), reads=[By])
                elif t.t0 == 0:
                    P.dma("sp", lambda e, y=y: e.dma_start(out=y_p[0:112, :], in_=y[16:128, :]), ("yt", 0), reads=[By])
                else:
                    P.dma("sp", lambda e, y=y, t=t: e.dma_start(out=y_p[t.t0 - 16:t.t0 - 16 + t.n, :], in_=y[0:t.n, :]), ("yt", 0), reads=[By])
                if gi + 1 < len(groups):
                    for t2 in groups[gi + 1]["tiles"]:
                        if t2.xi == t.xi:
                            load_x(gi + 1, t2)

        if not max_groups:
            tail_out(glut, Bglut, 8, 30, 30,
                     lambda k: P.dma("sp", lambda e: e.dma_start(out=o_pconv[:, :], in_=stgT[0:30, :]), ("yt", 0), reads=[BstgT]), None)
            tail_out(qt, Bqt, 24, 3, 3,
                     lambda k: P.dma("sp", lambda e, k=k: e.dma_start(out=o_pqkv[:, k * 1024:(k + 1) * 1024], in_=stgT[0:3, :]), ("yt", 0), reads=[BstgT]), None)
            P.dma("sp", lambda e: e.dma_start(out=o_pdelta.rearrange("h d e -> d h e"), in_=S[:]), "o_p_delta", reads=[BS])
        P.ops["sp"].append({"fn": None, "deps": {("dma", k): v for k, v in P.dma_counts.items()}, "dma": None})
        P.emit()
    return nc


_CACHE = {}


def kernel(x_prompt, x_sample, state_conv, state_qkv_conv, state_delta, meta_tokens, g_mix, w_in,
           w_dw, b_dw, ln_g, ln_b, w_cout, w_short, a_log, dt_bias, g_head, w_dout, w_o, g_mlp,
           w_up, w_down, g_final):
    f = lambda a: np.ascontiguousarray(np.asarray(a, dtype=np.float32))
    if "nc" not in _CACHE:
        _CACHE["nc"] = build()
    nc = _CACHE["nc"]
    shared = dict(meta=f(meta_tokens), g_mix=f(g_mix[0]), w_in=f(w_in[0]), w_dw=f(w_dw[0]), b_dw=f(b_dw[0]), ln_g=f(ln_g[0]),
                  ln_b=f(ln_b[0]), w_cout=f(w_cout[0]), w_short=f(w_short[0]), a_log=f(a_log[0]), dt_bias=f(dt_bias[0]),
                  g_head=f(g_head[0]), w_dout=f(w_dout[0]), w_o=f(w_o[0]), g_mlp=f(g_mlp[0]), w_up=f(w_up[0]),
                  w_down=f(w_down[0]), g_final=f(g_final))
    xp = f(x_prompt); xs = f(x_sample); sc_ = f(state_conv); sq_ = f(state_qkv_conv); sd_ = f(state_delta)
    in_maps = []
    for c in range(8):
        d = dict(shared)
        d["x_p"] = xp[c]
        d["x_s"] = xs[16 * c:16 * c + 16].reshape(64, D)
        d["st_conv"] = sc_[0, 16 * c:16 * c + 16]
        d["st_qkv"] = sq_[0, 16 * c:16 * c + 16]
        d["st_delta"] = sd_[0, 16 * c:16 * c + 16]
        in_maps.append(d)
    res = run_bass_kernel_spmd(nc, in_maps, core_ids=list(range(8)))
    R = res.results
    y_prompt = np.stack([R[c]["y_p"] for c in range(8)]).astype(np.float32)
    y_sample = np.concatenate([R[c]["y_s"].reshape(16, 4, D) for c in range(8)]).astype(np.float32)
    p_conv = np.stack([R[c]["p_conv"] for c in range(8)])[None].astype(np.float32)
    p_qkv = np.stack([R[c]["p_qkv"] for c in range(8)])[None].astype(np.float32)
    p_delta = np.stack([R[c]["p_delta"] for c in range(8)])[None].astype(np.float32)
    s_conv = np.concatenate([R[c]["s_conv"] for c in range(8)])[None].astype(np.float32)
    s_qkv = np.concatenate([R[c]["s_qkv"] for c in range(8)])[None].astype(np.float32)
    s_delta = np.concatenate([R[c]["s_delta"] for c in range(8)])[None].astype(np.float32)
    return (y_prompt, y_sample, p_conv, p_qkv, p_delta, s_conv, s_qkv, s_delta)
```

```python
import numpy as np
from contextlib import ExitStack
import concourse.bass as bass
import concourse.mybir as mybir
from concourse.bass_utils import run_bass_kernel_spmd

F32 = mybir.dt.float32
BF16 = mybir.dt.bfloat16
AF = mybir.ActivationFunctionType
ALU = mybir.AluOpType

ENGS = ("pe", "act", "dve", "pool", "sp")
SAME_ENGINE_SYNC = {"pe": False, "act": True, "dve": True, "pool": True, "sp": False}

D = 1024
DIN = 8208
DFF = 4096
NPT = 2064
EPS = 1e-6
NEG = -30000.0


class Buf:
    __slots__ = ("name", "lw", "rd", "excl")

    def __init__(self, name, excl=False):
        self.name = name
        self.lw = None
        self.rd = {}
        self.excl = excl


class Prog:
    def __init__(self, nc):
        self.nc = nc
        self.ops = {e: [] for e in ENGS}
        self.dma_counts = {}
        self.dma_keys = []
        self.enabled = True
        self.stop_at = None

    def ck(self, name):
        if self.stop_at is not None and name == self.stop_at:
            self.enabled = False

    def _collect(self, eng, reads, writes):
        deps = {}

        def add(k, v):
            if k[0] == "eng" and k[1] == eng and not SAME_ENGINE_SYNC[eng]:
                return
            if k not in deps or deps[k] < v:
                deps[k] = v

        for b in reads:
            if b.lw is not None:
                add((b.lw[0], b.lw[1]), b.lw[2])
            if b.excl:
                for k, v in b.rd.items():
                    if k != ("eng", eng):
                        add(k, v)
        for b in writes:
            if b.lw is not None:
                add((b.lw[0], b.lw[1]), b.lw[2])
            for k, v in b.rd.items():
                add(k, v)
        return deps

    def _commit(self, tok, reads, writes):
        k = (tok[0], tok[1])
        for b in reads:
            if b.rd.get(k, -1) < tok[2]:
                b.rd[k] = tok[2]
        for b in writes:
            b.lw = tok
            b.rd = {}

    def op(self, eng, fn, reads=(), writes=()):
        if not self.enabled:
            return
        deps = self._collect(eng, reads, writes)
        idx = len(self.ops[eng])
        self.ops[eng].append({"fn": fn, "deps": deps, "dma": None})
        self._commit(("eng", eng, idx), reads, writes)

    def dma(self, eng, fn, semkey, reads=(), writes=()):
        if not self.enabled:
            return
        deps = self._collect(eng, reads, writes)
        if semkey not in self.dma_counts:
            self.dma_counts[semkey] = 0
            self.dma_keys.append(semkey)
        self.dma_counts[semkey] += 16
        cnt = self.dma_counts[semkey]
        self.ops[eng].append({"fn": fn, "deps": deps, "dma": semkey})
        self._commit(("dma", semkey, cnt), reads, writes)

    def wait_all(self, eng, bufs):
        deps = self._collect(eng, bufs, ())
        self.ops[eng].append({"fn": None, "deps": deps, "dma": None})

    def handover(self, old, new):
        merged = {}
        for b in old:
            if b.lw is not None:
                k = (b.lw[0], b.lw[1])
                merged[k] = max(merged.get(k, -1), b.lw[2])
            for k, v in b.rd.items():
                merged[k] = max(merged.get(k, -1), v)
        for b in new:
            b.lw = None
            b.rd = dict(merged)

    def emit(self):
        nc = self.nc
        sig = {e: set() for e in ENGS}
        for e in ENGS:
            for o in self.ops[e]:
                for (kind, key), v in o["deps"].items():
                    if kind == "eng":
                        sig[key].add(v)
        cum = {}
        for e in ENGS:
            c = 0
            m = {}
            for i in range(len(self.ops[e])):
                if i in sig[e]:
                    c += 1
                    m[i] = c
            cum[e] = m
        with ExitStack() as st:
            esem = {e: st.enter_context(nc.semaphore("s_" + e)) for e in ENGS}
            dsem = {k: st.enter_context(nc.semaphore("d_%d" % i)) for i, k in enumerate(self.dma_keys)}
            block = st.enter_context(nc.Block())
            engobj = {"pe": block.tensor, "act": block.scalar, "dve": block.vector,
                      "pool": block.gpsimd, "sp": block.sync}

            def make(ename):
                ops = self.ops[ename]

                def body(eng):
                    seen = {}
                    for i, o in enumerate(ops):
                        for (kind, key), v in o["deps"].items():
                            if kind == "eng":
                                val = cum[key][v]
                                s = esem[key]
                            else:
                                val = v
                                s = dsem[key]
                            if seen.get((kind, key), 0) >= val:
                                continue
                            seen[(kind, key)] = val
                            eng.wait_ge(s, val)
                        if o["fn"] is None:
                            continue
                        inst = o["fn"](eng)
                        if o["dma"] is not None:
                            inst.then_inc(dsem[o["dma"]], 16)
                        elif i in cum[ename]:
                            inst.then_inc(esem[ename], 1)
                return body

            for e in ENGS:
                if self.ops[e]:
                    engobj[e](make(e))


class TileD:
    def __init__(self, kind, n, t0, col, xi):
        self.kind, self.n, self.t0, self.col, self.xi = kind, n, t0, col, xi
        self.nlev = {128: 6, 16: 3, 64: 1}[n] if kind == "P" else 1


def make_groups(with_sample=True):
    groups = []
    ptiles = [(i * 128, min(128, NPT - i * 128)) for i in range(17)]
    split = [ptiles[0:4], ptiles[4:8], ptiles[8:12], ptiles[12:17]]
    for gi, pts in enumerate(split):
        tiles = []
        col = 0
        for xi, (t0, n) in enumerate(pts):
            tiles.append(TileD("P", n, t0, col, xi))
            col += n
        blocks = []
        c = 0
        while c < col:
            blocks.append((c, min(col, c + 512)))
            c += 512
        groups.append(dict(tiles=tiles, pc0=0, npc=col, ncol=col, blocks=blocks, last=(gi == 3), first=False))
    if with_sample:
        groups.append(dict(tiles=[TileD("S", 64, 0, 0, 0)], pc0=64, npc=0, ncol=64, blocks=[(0, 64)], last=False, first=True))
    return groups


def wsched():
    L = []
    for j in range(6):
        L.append(("in", 2048 + 512 * j, 512))
    L += [("in", 5120, 512), ("in", 5632, 512), ("in", 6144, 16)]
    for q in range(2):
        L.append(("in", q * 512, 512))
        L.append(("in", 1024 + q * 512, 512))
    for q in range(2):
        L += [("cout", q * 512, 512), ("in", 6160 + q * 512, 512)]
    for q in range(2):
        L += [("dout", q * 512, 512), ("in", 7184 + q * 512, 512)]
    L += [("o", 0, 512), ("o", 512, 512)]
    for j in range(8):
        L.append(("up", j * 512, 512))
    for nb in range(2):
        for kb in range(4):
            L.append(("down", kb, nb))
    return L


def build(with_sample=True, NW=3, debug=False, max_groups=None, stop_at=None):
    nc = bass.Bass("TRN2", target_bir_lowering=False)
    din = lambda name, shape: nc.dram_tensor(name, shape, F32, kind="ExternalInput").ap()
    dout = lambda name, shape: nc.dram_tensor(name, shape, F32, kind="ExternalOutput").ap()
    x_p = din("x_p", [2048, D]); x_s = din("x_s", [64, D])
    st_conv = din("st_conv", [16, 30, D]); st_qkv = din("st_qkv", [16, 3, 3072]); st_delta = din("st_delta", [16, 8, 128, 128])
    meta = din("meta", [16, D])
    g_mix = din("g_mix", [D]); w_in = din("w_in", [D, DIN]); w_dw = din("w_dw", [31, D]); b_dw = din("b_dw", [D])
    ln_g = din("ln_g", [D]); ln_b = din("ln_b", [D]); w_cout = din("w_cout", [D, D]); w_short = din("w_short", [4, 3072])
    a_log = din("a_log", [8]); dt_bias = din("dt_bias", [8]); g_head = din("g_head", [128]); w_dout = din("w_dout", [D, D])
    w_o = din("w_o", [D, D]); g_mlp = din("g_mlp", [D]); w_up = din("w_up", [D, DFF]); w_down = din("w_down", [DFF, D])
    g_final = din("g_final", [D])
    y_p = dout("y_p", [2048, D]); y_s = dout("y_s", [64, D])
    o_pconv = dout("p_conv", [30, D]); o_pqkv = dout("p_qkv", [3, 3072]); o_pdelta = dout("p_delta", [8, 128, 128])
    o_sconv = dout("s_conv", [16, 30, D]); o_sqkv = dout("s_qkv", [16, 3, 3072]); o_sdelta = dout("s_delta", [16, 8, 128, 128])
    wmap = {"in": w_in, "cout": w_cout, "dout": w_dout, "o": w_o, "up": w_up}

    P = Prog(nc)
    P.stop_at = stop_at
    groups = make_groups(with_sample)
    if max_groups:
        groups = groups[:max_groups]
    MAXC = 528
    MAXPC = max(g["npc"] for g in groups)

    with ExitStack() as st:
        st.enter_context(nc.allow_non_contiguous_dma(reason="small params"))
        cnt = [0]

        def sb(shape, dt, name=None):
            cnt[0] += 1
            return st.enter_context(nc.sbuf_tensor(name or ("t%d" % cnt[0]), shape, dt))

        psT = [st.enter_context(nc.psum_tensor("ps%d" % i, [128, 512], F32)) for i in range(8)]
        psB = [Buf("ps%d" % i, excl=True) for i in range(8)]
        ps_reserved = set()
        ps_ctr = [0]

        pool_ctr = {"d": 0, "g": 0}

        def psum(pool=None):
            if pool == "d":
                i = pool_ctr["d"] % 3
                pool_ctr["d"] += 1
                return psT[i], psB[i], i
            if pool == "g":
                i = 3 + pool_ctr["g"] % 5
                pool_ctr["g"] += 1
                return psT[i], psB[i], i
            while True:
                i = ps_ctr[0] % 8
                ps_ctr[0] += 1
                if i not in ps_reserved:
                    return psT[i], psB[i], i

        def bf(ap):
            return ap[:].bitcast(BF16)

        def mm(out, lhsT, rhs, start, stop, reads, writes, skip=False):
            if skip:
                P.op("pe", lambda e: e.matmul(out, lhsT=lhsT, rhs=rhs, start=start, stop=stop, skip_group_check=True), reads, writes)
            else:
                P.op("pe", lambda e: e.matmul(out, lhsT=lhsT, rhs=rhs, start=start, stop=stop), reads, writes)

        def tr(out, in_, idn, reads, writes):
            P.op("pe", lambda e: e.transpose(out=out, in_=in_, identity=idn), reads, writes)

        def act(out, in_, func, reads, writes, **kw):
            P.op("act", lambda e: e.activation(out=out, in_=in_, func=func, **kw), reads, writes)

        def ts(eng, out, in0, s1, s2, op0, op1, reads, writes):
            if op1 is None:
                P.op(eng, lambda e: e.tensor_scalar(out=out, in0=in0, scalar1=s1, scalar2=None, op0=op0), reads, writes)
            else:
                P.op(eng, lambda e: e.tensor_scalar(out=out, in0=in0, scalar1=s1, scalar2=s2, op0=op0, op1=op1), reads, writes)

        def tt(eng, out, in0, in1, op, reads, writes):
            P.op(eng, lambda e: e.tensor_tensor(out=out, in0=in0, in1=in1, op=op), reads, writes)

        def stt(out, in0, scalar, in1, op0, op1, reads, writes):
            P.op("dve", lambda e: e.scalar_tensor_tensor(out=out, in0=in0, scalar=scalar, in1=in1, op0=op0, op1=op1), reads, writes)

        def cp(eng, out, in_, reads, writes):
            P.op(eng, lambda e: e.tensor_copy(out=out, in_=in_), reads, writes)

        Bc = Buf("consts")
        identf = sb([128, 128], F32); ident = sb([128, 128], BF16)
        onesf = sb([128, 128], F32); onesb = sb([128, 128], BF16)
        triU = sb([128, 128], F32); mmin = sb([128, 128], F32); offd = sb([128, 128], F32)
        blk = sb([64, 64], F32); triUs = sb([64, 64], F32); mmins = sb([64, 64], F32); offds = sb([64, 64], F32)
        bmask = sb([64, 16], F32)
        tmpc = sb([64, 64], F32)

        def pool(fn):
            P.op("pool", fn, reads=[Bc], writes=[Bc])

        pool(lambda e: e.memset(identf[:], 0.0))
        pool(lambda e: e.affine_select(out=identf[:], in_=identf[:], pattern=[[-1, 128]], compare_op=ALU.not_equal,
                                       fill=1.0, base=0, channel_multiplier=1))
        pool(lambda e: e.tensor_copy(out=ident[:], in_=identf[:]))
        pool(lambda e: e.memset(onesf[:], 1.0))
        pool(lambda e: e.memset(onesb[:], 1.0))
        pool(lambda e: e.affine_select(out=triU[:], in_=onesf[:], pattern=[[1, 128]], compare_op=ALU.is_ge,
                                       fill=0.0, base=0, channel_multiplier=-1))
        pool(lambda e: e.memset(mmin[:], 0.0))
        pool(lambda e: e.affine_select(out=mmin[:], in_=mmin[:], pattern=[[1, 128]], compare_op=ALU.is_ge,
                                       fill=NEG, base=0, channel_multiplier=-1))
        pool(lambda e: e.affine_select(out=offd[:], in_=onesf[:], pattern=[[1, 128]], compare_op=ALU.not_equal,
                                       fill=0.0, base=0, channel_multiplier=-1))
        pool(lambda e: e.affine_select(out=blk[:], in_=onesf[:64, :64], pattern=[[-4, 16], [0, 4]], compare_op=ALU.is_ge,
                                       fill=0.0, base=0, channel_multiplier=1))
        pool(lambda e: e.affine_select(out=blk[:], in_=blk[:], pattern=[[4, 16], [0, 4]], compare_op=ALU.is_ge,
                                       fill=0.0, base=3, channel_multiplier=-1))
        pool(lambda e: e.tensor_tensor(out=triUs[:], in0=triU[:64, :64], in1=blk[:], op=ALU.mult))
        pool(lambda e: e.tensor_tensor(out=offds[:], in0=offd[:64, :64], in1=blk[:], op=ALU.mult))
        pool(lambda e: e.tensor_scalar(out=tmpc[:], in0=blk[:], scalar1=-NEG, scalar2=NEG, op0=ALU.mult, op1=ALU.add))
        pool(lambda e: e.tensor_tensor(out=mmins[:], in0=mmin[:64, :64], in1=blk[:], op=ALU.mult))
        pool(lambda e: e.tensor_tensor(out=mmins[:], in0=mmins[:], in1=tmpc[:], op=ALU.add))
        pool(lambda e: e.affine_select(out=bmask[:], in_=onesf[:64, :16], pattern=[[-4, 16]], compare_op=ALU.is_ge,
                                       fill=0.0, base=0, channel_multiplier=1))
        pool(lambda e: e.affine_select(out=bmask[:], in_=bmask[:], pattern=[[4, 16]], compare_op=ALU.is_ge,
                                       fill=0.0, base=3, channel_multiplier=-1))

        NXT = 5
        xt = [sb([128, D], F32, "xt%d" % i) for i in range(NXT)]; Bxt = [Buf("xt%d" % i) for i in range(NXT)]
        Bp = Buf("params")
        pstage = sb([40, 128], F32)
        pv = sb([128, 40], F32)
        wdw = sb([128, 8, 31], F32)
        wsh = sb([128, 24, 4], F32)
        ghead = sb([128, 1], F32)
        dtb = sb([128, 8], F32); nexpA = sb([128, 8], F32)
        gfin = sb([128, D], F32)
        for i, v in enumerate([g_mix, g_mlp, b_dw, ln_g, ln_b]):
            P.dma("sp", lambda e, i=i, v=v: e.dma_start(out=pstage[8 * i:8 * i + 8, :], in_=v.rearrange("(k p) -> k p", p=128)), "par", writes=[Bp])
        P.dma("sp", lambda e: e.dma_start(out=xt[0][0:31, :], in_=w_dw[:, :]), ("x", 0), writes=[Bxt[0]])
        for k in range(3):
            P.dma("sp", lambda e, k=k: e.dma_start(out=xt[1 + k][0:4, :], in_=w_short[:, k * 1024:(k + 1) * 1024]), ("x", 1 + k), writes=[Bxt[1 + k]])
        P.dma("sp", lambda e: e.dma_start(out=ghead[:], in_=g_head.rearrange("(p o) -> p o", o=1)), "par", writes=[Bp])
        P.dma("sp", lambda e: e.dma_start(out=dtb[:], in_=dt_bias.partition_broadcast(128)), "par", writes=[Bp])
        P.dma("sp", lambda e: e.dma_start(out=nexpA[:], in_=a_log.partition_broadcast(128)), "par", writes=[Bp])
        P.dma("sp", lambda e: e.dma_start(out=gfin[:], in_=g_final.partition_broadcast(128)), "par", writes=[Bp])
        act(nexpA[:], nexpA[:], AF.Exp, [Bp], [Bp])
        P.op("act", lambda e: e.mul(nexpA[:], nexpA[:], -1.0), [Bp], [Bp])
        pt, pb, _ = psum()
        tr(pt[:, 0:40], pstage[:, :], identf[:40, :40], [Bp, Bc], [pb])
        cp("dve", pv[:], pt[:, 0:40], [pb], [Bp])
        pt, pb, _ = psum()
        for m in range(8):
            tr(pt[:, m * 31:(m + 1) * 31], xt[0][0:31, m * 128:(m + 1) * 128], identf[:31, :31], [Bxt[0], Bc], [pb])
        cp("dve", wdw[:].rearrange("p m i -> p (m i)"), pt[:, 0:248], [pb], [Bp])
        pt, pb, _ = psum()
        for m in range(24):
            tr(pt[:, m * 4:(m + 1) * 4], xt[1 + m // 8][0:4, (m % 8) * 128:(m % 8 + 1) * 128], identf[:4, :4], [Bxt[1 + m // 8], Bc], [pb])
        cp("dve", wsh[:].rearrange("p m i -> p (m i)"), pt[:, 0:96], [pb], [Bp])
        gmix = pv[:, 0:8]; gmlp = pv[:, 8:16]; bdw = pv[:, 16:24]; lng = pv[:, 24:32]; lnb = pv[:, 32:40]

        P.ck("params")
        scr4 = sb([128, 2 * D], BF16, "scr4"); Bscr = Buf("scr4")
        hb = scr4[:, 0:D]; Bhb = Bscr
        junk = scr4[:, D:2 * D]; Bjunk = Bscr
        junk2 = sb([128, 128], BF16); Bjunk2 = Buf("junk2")
        ssx = sb([128, 4], F32); Bssx = Buf("ssx")
        hT = sb([128, 8, MAXC], BF16, "hT"); BhT = Buf("hT")
        NG = 2
        gext = [sb([128, 30 + MAXC], BF16, "gext%d" % i) for i in range(NG)]; Bgext = [Buf("gext%d" % i) for i in range(NG)]
        ghalo = sb([128, 8, 30], BF16); Bghalo = Buf("ghalo")
        qext = [sb([128, 3 + MAXC], BF16) for _ in range(NG)]; Bqext = [Buf("qext%d" % i) for i in range(NG)]
        qhalo = sb([128, 24, 3], BF16); Bqhalo = Buf("qhalo")
        glut = sb([128, 8, 30], F32, "glut"); Bglut = Buf("glut")
        qt = sb([128, 24, 3], F32); Bqt = Buf("qt")
        arena = sb([128, 32 * MAXC], BF16, "arena")

        def arena_views(mc):
            return (arena[:, 0:24 * mc].rearrange("p (m c) -> p m c", c=mc),
                    arena[:, 24 * mc:32 * mc].rearrange("p (m c) -> p m c", c=mc),
                    arena[:, 0:32 * mc].rearrange("p (m c) -> p m c", c=mc))
        BqkvT = Buf("qkvT"); Bcpre = Buf("cpre"); BmixT = Buf("mixT"); BaT = Buf("aT")
        if with_sample:
            o0 = 2048
            gexS = arena[:, o0:o0 + 4352].rearrange("p (m s t) -> p m s t", m=8, s=16); BgexS = Buf("gexS")
            o0 += 4352
            qexS = arena[:, o0:o0 + 2688].rearrange("p (m s t) -> p m s t", m=24, s=16); BqexS = Buf("qexS")
            o0 += 2688
            glutS = arena[:, o0:o0 + 1024].bitcast(F32).rearrange("p (m c) -> p m c", m=8); BglutS = Buf("glutS")
            o0 += 1024
            qtS = arena[:, o0:o0 + 3072].bitcast(F32).rearrange("p (m c) -> p m c", m=24); BqtS = Buf("qtS")
            o0 += 3072
            assert o0 <= 32 * MAXC
        f4 = [sb([128, 512], F32) for _ in range(2)]; Bf4 = [Buf("f4_%d" % i) for i in range(2)]
        f4i = [0]

        def ftmp():
            i = f4i[0] % 2
            f4i[0] += 1
            return f4[i], Bf4[i]

        cs2 = sb([128, 2, 512], BF16); Bcs2 = Buf("cs2")
        lnst = [sb([128, MAXC], F32) for _ in range(2)]; Blnst = Buf("lnst")
        decT = sb([128, 8, 128], F32); BdecT = Buf("decT")
        decTs = sb([128, 8, 128], F32); BdecTs = Buf("decTs")
        gm = decTs; Bgm = BdecTs
        EB = sb([128, 8, 128], F32); BEB = Buf("EB")
        cT = sb([128, 8, MAXC], BF16, "cT"); BcT = Buf("cT")
        zs = [sb([128, D], BF16) for _ in range(NXT)]; Bzs = [Buf("zs%d" % i) for i in range(NXT)]
        ba = sb([128, NXT, 16], F32); Bba = Buf("ba")
        onT = sb([128, 8, MAXC], BF16, "onT"); BonT = Buf("onT")
        wbuf = [sb([128, 8, 512], BF16) for _ in range(NW)]; Bw = [Buf("w%d" % i) for i in range(NW)]
        dgr = sb([128, 31 * 128], BF16, "dgr")
        dg31 = dgr[:, :].rearrange("p (i c) -> p i c", i=31); Bdg31q = [Buf("dg31q%d" % i) for i in range(4)]
        sqT = scr4[:, :].rearrange("p (j c) -> p j c", j=16); BsqT = Bscr
        dg4 = [sb([128, 4, 128], BF16) for _ in range(2)]; Bdg4 = [Buf("dg4a"), Buf("dg4b")]
        sc = sb([128, 64], F32, "sc"); sc2 = sb([128, 64], F32, "sc2"); Bsc = Buf("sc")
        Wp = [sb([128, 8, 128], BF16) for _ in range(2)]; BWp = [Buf("Wp0"), Buf("Wp1")]
        Qp = [sb([128, 8, 128], BF16) for _ in range(2)]; BQp = [Buf("Qp0"), Buf("Qp1")]
        Zp = [sb([128, 8, 128], BF16) for _ in range(2)]; BZp = [Buf("Zp0"), Buf("Zp1")]
        attnT = sb([128, 8, 128], BF16); BattnT = Buf("attnT")
        rhsk = sb([128, 8, 128], BF16); rhsv = sb([128, 8, 128], BF16); kdec = sb([128, 8, 128], BF16)
        Brhsk = Buf("rhsk"); Brhsv = Buf("rhsv"); Bkdec = Buf("kdec")
        nkcT = sb([128, 8, 128], BF16); BnkcT = Buf("nkcT")
        u = sb([128, 8, 128], BF16); Bu = Buf("u")
        qd = sb([128, 8, 128], BF16); Bqd = Buf("qd")
        on = sb([128, D], BF16, "on"); Bon = Buf("on")
        S = sb([128, 8, 128], F32); BS = Buf("S")
        Sbf = sb([128, 8, 128], BF16); BSbf = Buf("Sbf")
        ms = sb([128, 16], F32, "ms"); Bms = Buf("ms")
        yt = [sb([128, D], F32)]; Byt = [Buf("yt0")]
        stgT = yt[0]; BstgT = Byt[0]
        if with_sample:
            v3 = lambda a: a[:, :].rearrange("p (h c) -> p h c", h=8)
            S0 = [v3(xt[1]), v3(xt[2]), v3(yt[0])]; BS0 = [Bxt[1], Bxt[2], Byt[0]]
            S0key = [("x", 1), ("x", 2), ("yt", 0)]
            So = [v3(xt[3]), v3(xt[4])]; BSo = [Bxt[3], Bxt[4]]
            S0b = [v3(zs[1]), v3(zs[2])]; BS0b = [Bzs[1], Bzs[2]]
            um = [zs[3], zs[4]]; Bum = [Bzs[3], Bzs[4]]
            oTs = sb([128, 8, 64], F32, "oTs"); BoTs = Buf("oTs")
            uTs = sb([128, 8, 64], BF16, "uTs"); BuTs = Buf("uTs")
            stg = yt[0]; Bstg = Byt[0]
        Bout = {k: Buf("o_" + k) for k in ["y_p", "y_s", "p_conv", "p_qkv", "p_delta", "s_conv", "s_qkv", "s_delta"]}

        P.op("pool", lambda e: e.memset(ghalo[:], 0.0), [], [Bghalo])
        P.op("pool", lambda e: e.memset(qhalo[:], 0.0), [], [Bqhalo])
        P.op("pool", lambda e: e.memset(S[:], 0.0), [], [BS])
        P.op("pool", lambda e: e.memset(Sbf[:], 0.0), [], [BSbf])

        sched = wsched()
        allreq = sched * len(groups)
        wstate = {"next_load": 0, "next_use": 0}

        NT = len(sched)
        wsc = nc.dram_tensor("wsc", [NT, 128, 4096], BF16, kind="Internal").ap()
        Bwsc = [Buf("wsc%d" % j) for j in range(NT)]
        Bring = [Buf("ring%d" % j) for j in range(8)]
        pstate = {"next": 0}
        PLOOK = 6

        def wdims(j):
            kind, a, b = sched[j]
            if kind == "down":
                return w_down[a * 1024:(a + 1) * 1024, b * 512:(b + 1) * 512], 512
            return wmap[kind][:, a:a + b], b

        def pro_issue(upto):
            while pstate["next"] < min(NT, upto):
                j = pstate["next"]
                src, n = wdims(j)
                P.dma("pool", lambda e, j=j, src=src, n=n: e.dma_start(
                    out=wsc[j][:, 0:8 * n].rearrange("p (kc n) -> p kc n", n=n),
                    in_=src.rearrange("(kc p) n -> p kc n", p=128)), ("pro", j % 8), writes=[Bwsc[j], Bring[j % 8]])
                pstate["next"] += 1

        def w_issue(i):
            j = i % NT
            slot = i % NW
            pro_issue(j + PLOOK)
            _, n = wdims(j)
            P.dma("sp", lambda e, j=j, n=n, slot=slot: e.dma_start(
                out=wbuf[slot][:, :, 0:n], in_=wsc[j][:, 0:8 * n].rearrange("p (kc n) -> p kc n", n=n)),
                ("w", slot), reads=[Bwsc[j]], writes=[Bw[slot]])

        def wnext(tag):
            i = wstate["next_use"]
            assert allreq[i] == tag, (allreq[i], tag)
            while wstate["next_load"] < min(len(allreq), i + NW - 1):
                w_issue(wstate["next_load"])
                wstate["next_load"] += 1
            wstate["next_use"] += 1
            return wbuf[i % NW], Bw[i % NW]

        def wprefetch():
            i = wstate["next_use"]
            while wstate["next_load"] < min(len(allreq), i + NW - 1):
                w_issue(wstate["next_load"])
                wstate["next_load"] += 1

        def fm(wt, Bwt, mloc, src, Bsrc, blkc, KC=8, kc0=0, pool=None):
            c0, c1 = blkc
            pt, pb, _ = psum(pool)
            for kc in range(KC):
                mm(pt[:, 0:c1 - c0], wt[:, kc, mloc * 128:(mloc + 1) * 128], src[:, kc0 + kc, c0:c1], kc == 0, kc == KC - 1,
                   [Bwt, Bsrc], [pb])
            return pt, pb

        def norm_to_T(G, gvec):
            for t in G["tiles"]:
                n = t.n
                x = xt[t.xi]; Bx = Bxt[t.xi]
                act(junk[:n, :], x[:n, :], AF.Square, [Bx], [Bjunk, Bssx], accum_out=ssx[:n, 0:1])
                act(ssx[:n, 1:2], ssx[:n, 0:1], AF.Sqrt, [Bssx], [Bssx], scale=1.0 / D, bias=EPS)
                P.op("dve", lambda e, n=n: e.reciprocal(out=ssx[:n, 2:3], in_=ssx[:n, 1:2]), [Bssx], [Bssx])
                ts("dve", hb[:n, :], x[:n, :], ssx[:n, 2:3], None, ALU.mult, None, [Bx, Bssx], [Bhb])
                pt, pb, _ = psum()
                pv_ = bf(pt).rearrange("p (k c) -> p k c", k=8)
                for kc in range(8):
                    tr(pv_[:, kc, 0:n], hb[:n, kc * 128:(kc + 1) * 128], ident[:n, :n], [Bhb, Bc], [pb])
                tt("dve", hT[:, :, t.col:t.col + n], pv_[:, :, 0:n], gvec.unsqueeze(2).broadcast_to([128, 8, n]), ALU.mult,
                   [pb, Bp], [BhT])

        def tail_out(src, Bsrc, nm, ncols, nrows, dst_fn, key):
            for m0 in range(0, nm, 4):
                pt, pb, _ = psum()
                for mm_ in range(4):
                    tr(pt[:ncols, mm_ * 128:(mm_ + 1) * 128], src[:, m0 + mm_, :], identf[:, :], [Bsrc, Bc], [pb])
                cp("dve", stgT[:ncols, (m0 % 8) * 128:(m0 % 8 + 4) * 128], pt[:ncols, :], [pb], [BstgT])
                if (m0 + 4) % 8 == 0:
                    dst_fn(m0 // 8)

        def emit_sample_tails():
            def sconv_out(k):
                for s_ in range(16):
                    P.dma("sp", lambda e, s_=s_: e.dma_start(out=o_sconv[s_, 26:30, :], in_=stgT[4 * s_:4 * s_ + 4, :]), ("yt", 0),
                          reads=[BstgT])
            tail_out(glutS, BglutS, 8, 64, 64, sconv_out, None)

            def sqkv_out(k):
                for s_ in range(16):
                    P.dma("sp", lambda e, s_=s_, k=k: e.dma_start(out=o_sqkv[s_, :, k * 1024:(k + 1) * 1024], in_=stgT[4 * s_ + 1:4 * s_ + 4, :]), ("yt", 0),
                          reads=[BstgT])
            tail_out(qtS, BqtS, 24, 64, 64, sqkv_out, None)

        xloaded = set()

        def load_x(gi_, t):
            if (gi_, t.xi) in xloaded:
                return
            xloaded.add((gi_, t.xi))
            x = xt[t.xi]; Bx = Bxt[t.xi]
            if t.kind == "S":
                P.dma("sp", lambda e, x=x: e.dma_start(out=x[0:64, :], in_=x_s[:, :]), ("x", t.xi), writes=[Bx])
            elif t.t0 == 0:
                P.dma("sp", lambda e, x=x: e.dma_start(out=x[0:16, :], in_=meta[:, :]), ("x", t.xi), writes=[Bx])
                P.dma("sp", lambda e, x=x: e.dma_start(out=x[16:128, :], in_=x_p[0:112, :]), ("x", t.xi), writes=[Bx])
            else:
                P.dma("sp", lambda e, x=x, t=t: e.dma_start(out=x[0:t.n, :], in_=x_p[t.t0 - 16:t.t0 - 16 + t.n, :]), ("x", t.xi), writes=[Bx])

        for gi, G in enumerate(groups):
            tiles = G["tiles"]; blocks = G["blocks"]; pc0 = G["pc0"]; npc = G["npc"]
            isSG = (tiles[0].kind == "S")
            sblock = (0, 64) if isSG else None
            qkvT, cpre, aT = arena_views(64 if isSG else MAXC)
            mixT = cpre
            for t in tiles:
                load_x(gi, t)
            if isSG:
                P.handover([BaT, BqkvT, BmixT, Bcpre], [BgexS, BqexS, BglutS, BqtS, BaT])
                for sg in range(4):
                    P.dma("sp", lambda e, sg=sg: e.dma_start(out=stg[0:120, :], in_=st_conv[4 * sg:4 * sg + 4].rearrange("s t c -> (s t) c")),
                          ("yt", 0), writes=[Bstg])
                    pts = []
                    for half in range(2):
                        pt, pb, _ = psum()
                        for mm_ in range(4):
                            m = half * 4 + mm_
                            tr(pt[:, mm_ * 120:(mm_ + 1) * 120], stg[0:120, m * 128:(m + 1) * 128], identf[:120, :120], [Bstg, Bc], [pb])
                        act(gexS[:, half * 4:half * 4 + 4, 4 * sg:4 * sg + 4, 0:30],
                            pt[:, 0:480].rearrange("p (m s t) -> p m s t", m=4, s=4), AF.Copy, [pb], [BgexS])
                for hf in range(3):
                    P.dma("sp", lambda e, hf=hf: e.dma_start(out=stg[0:48, :], in_=st_qkv.rearrange("s t c -> (s t) c")[:, hf * 1024:(hf + 1) * 1024]),
                          ("yt", 0), writes=[Bstg])
                    for half in range(2):
                        pt, pb, _ = psum()
                        for mm_ in range(4):
                            m = half * 4 + mm_
                            tr(pt[:, mm_ * 48:(mm_ + 1) * 48], stg[0:48, m * 128:(m + 1) * 128], identf[:48, :48], [Bstg, Bc], [pb])
                        act(qexS[:, hf * 8 + half * 4:hf * 8 + half * 4 + 4, :, 0:3],
                            pt[:, 0:192].rearrange("p (m s t) -> p m s t", m=4, s=16), AF.Copy, [pb], [BqexS])
                P.dma("sp", lambda e: e.dma_start(out=o_sconv[:, 0:26, :], in_=st_conv[:, 4:30, :]), "o_s_conv")
            P.ck("st")
            wprefetch()
            norm_to_T(G, gmix)
            P.ck("p0")

            P.handover([BaT], [BqkvT, Bcpre])
            QT = [(0, 8), (8, 16), (16, 24), (24, 31)]
            wts = {}

            def build31(m):
                for qi, (i0, i1) in enumerate(QT):
                    tt("pool", dg31[:, i0:i1, :], ident[:].unsqueeze(1).broadcast_to([128, i1 - i0, 128]),
                       wdw[:, m, i0:i1].unsqueeze(2).broadcast_to([128, i1 - i0, 128]), ALU.mult, [Bc, Bp], [Bdg31q[qi]])

            def stageA(m):
                q, ml = m // 4, m % 4
                if ml == 0:
                    wts[q] = (wnext(("in", q * 512, 512)), wnext(("in", 1024 + q * 512, 512)))
                (wa, Bwa), (wb_, Bwb) = wts[q]
                gx = gext[m % NG]; Bgx = Bgext[m % NG]
                if npc:
                    cp("pool", gx[:, 0:30], ghalo[:, m, :], [Bghalo], [Bgx])
                for bk in blocks:
                    c0, c1 = bk; n = c1 - c0
                    pa, pba = fm(wa, Bwa, ml, hT, BhT, bk, pool="d")
                    yield
                    pb2, pbb = fm(wb_, Bwb, ml, hT, BhT, bk, pool="d")
                    sg_, Bsg = ftmp()
                    act(sg_[:, 0:n], pb2[:, 0:n], AF.Sigmoid, [pbb], [Bsg])
                    if bk == sblock:
                        tt("dve", gexS[:, m, :, 30:34], pa[:, 0:64].rearrange("p (s t) -> p s t", t=4),
                           sg_[:, 0:64].rearrange("p (s t) -> p s t", t=4), ALU.mult, [pba, Bsg], [BgexS])
                        tt("dve", glutS[:, m, :], pa[:, 0:64], sg_[:, 0:64], ALU.mult, [pba, Bsg], [BglutS])
                    else:
                        tt("dve", gx[:, 30 + c0 - pc0:30 + c1 - pc0], pa[:, 0:n], sg_[:, 0:n], ALU.mult, [pba, Bsg], [Bgx])
                        if G["last"] and c1 == G["ncol"]:
                            lo = max(c0, G["ncol"] - 30)
                            tt("dve", glut[:, m, 30 - (c1 - lo):30], pa[:, lo - c0:n], sg_[:, lo - c0:n], ALU.mult, [pba, Bsg], [Bglut])
                        elif G["last"] and c1 > G["ncol"] - 30:
                            lo = max(c0, G["ncol"] - 30)
                            off = lo - (G["ncol"] - 30)
                            tt("dve", glut[:, m, off:off + c1 - lo], pa[:, lo - c0:n], sg_[:, lo - c0:n], ALU.mult, [pba, Bsg], [Bglut])
                    yield
                if npc:
                    cp("pool", ghalo[:, m, :], gx[:, npc:npc + 30], [Bgx], [Bghalo])

            def stageB(m):
                gx = gext[m % NG]; Bgx = Bgext[m % NG]
                for bi, bk in enumerate(blocks):
                    c0, c1 = bk; n = c1 - c0
                    pt, pb, _ = psum("d")
                    for i in range(31):
                        Bq = Bdg31q[[qi for qi, (i0, i1) in enumerate(QT) if i0 <= i < i1][0]]
                        if bk == sblock:
                            mm(pt[:, 0:64].rearrange("p (s t) -> p s t", t=4), dg31[:, i, :], gexS[:, m, :, i:i + 4], i == 0, i == 30,
                               [Bq, BgexS], [pb])
                        else:
                            mm(pt[:, 0:n], dg31[:, i, :], gx[:, c0 - pc0 + i:c1 - pc0 + i], i == 0, i == 30, [Bq, Bgx], [pb])
                        if i in (7, 15, 23):
                            yield
                    act(cpre[:, m, c0:c1], pt[:, 0:n], AF.Identity, [pb], [Bcpre], bias=bdw[:, m:m + 1])
                    yield

            def conv_gen():
                yield from stageA(0)
                for m in range(8):
                    build31(m)
                    if m + 1 < 8:
                        yield from stageA(m + 1)
                    yield from stageB(m)
                for bk in blocks:
                    c0, c1 = bk; n = c1 - c0
                    if n > 64:
                        sA = psum("d"); sB = psum("d")
                    else:
                        sA = psum("d"); sB = None
                    for m in range(8):
                        act(cs2[:, 1, 0:n], cpre[:, m, c0:c1], AF.Square, [Bcpre], [Bcs2])
                        if sB is not None:
                            mm(sA[0][:, 0:n], onesb[:, :], cpre[:, m, c0:c1], m == 0, m == 7, [Bc, Bcpre], [sA[1]])
                            mm(sB[0][:, 0:n], onesb[:, :], cs2[:, 1, 0:n], m == 0, m == 7, [Bc, Bcs2], [sB[1]])
                        else:
                            cp("pool", cs2[:, 0, 0:n], cpre[:, m, c0:c1], [Bcpre], [Bcs2])
                            mm(sA[0][:, 0:2 * n].rearrange("p (a n) -> p a n", a=2), onesb[:, :], cs2[:, :, 0:n], m == 0, m == 7,
                               [Bc, Bcs2], [sA[1]])
                        yield
                    if sB is not None:
                        psum_s, Bs1 = sA[0][:, 0:n], sA[1]
                        psum_q, Bs2 = sB[0][:, 0:n], sB[1]
                    else:
                        psum_s, Bs1 = sA[0][:, 0:n], sA[1]
                        psum_q, Bs2 = sA[0][:, n:2 * n], sA[1]
                    mean, Bmean = ftmp(); var, Bvar = ftmp()
                    msq = lnst[1][:, c0:c1]
                    act(mean[:, 0:n], psum_s, AF.Copy, [Bs1], [Bmean], scale=1.0 / D)
                    tt("pool", msq, mean[:, 0:n], mean[:, 0:n], ALU.mult, [Bmean], [Blnst])
                    stt(var[:, 0:n], psum_q, 1.0 / D, msq, ALU.mult, ALU.subtract, [Bs2, Blnst], [Bvar])
                    act(var[:, 0:n], var[:, 0:n], AF.Sqrt, [Bvar], [Bvar], bias=EPS)
                    P.op("dve", lambda e, n=n, c0=c0, c1=c1, var=var: e.reciprocal(out=lnst[0][:, c0:c1], in_=var[:, 0:n]), [Bvar], [Blnst])
                    stt(lnst[1][:, c0:c1], mean[:, 0:n], -1.0, lnst[0][:, c0:c1], ALU.mult, ALU.mult, [Bmean, Blnst], [Blnst])
                    yield
                for m in range(8):
                    for bk in blocks:
                        c0, c1 = bk; n = c1 - c0
                        t1, Bt1 = ftmp(); t2, Bt2 = ftmp()
                        tt("dve", t1[:, 0:n], cpre[:, m, c0:c1], lnst[0][:, c0:c1], ALU.mult, [Bcpre, Blnst], [Bt1])
                        tt("pool", t2[:, 0:n], t1[:, 0:n], lnst[1][:, c0:c1], ALU.add, [Bt1, Blnst], [Bt2])
                        act(cT[:, m, c0:c1], t2[:, 0:n], AF.Silu, [Bt2, Bp], [BcT], scale=lng[:, m:m + 1], bias=lnb[:, m:m + 1])
                        yield
                P.handover([Bcpre], [BmixT])
                for q in range(2):
                    wc, Bwc = wnext(("cout", q * 512, 512))
                    wg, Bwg = wnext(("in", 6160 + q * 512, 512))
                    for ml in range(4):
                        m = q * 4 + ml
                        for bk in blocks:
                            c0, c1 = bk; n = c1 - c0
                            pc_, pbc = fm(wc, Bwc, ml, cT, BcT, bk, pool="d")
                            yield
                            pg, pbg = fm(wg, Bwg, ml, hT, BhT, bk, pool="d")
                            sg_, Bsg = ftmp()
                            act(sg_[:, 0:n], pg[:, 0:n], AF.Sigmoid, [pbg], [Bsg])
                            tt("dve", mixT[:, m, c0:c1], pc_[:, 0:n], sg_[:, 0:n], ALU.mult, [pbc, Bsg], [BmixT])
                            yield

            P.ck("p3")
            wq4 = {}

            def stage4A(m):
                j, ml = m // 4, m % 4
                if ml == 0:
                    wq4[j] = wnext(("in", 2048 + 512 * j, 512))
                wq, Bwq = wq4[j]
                qx = qext[m % NG]; Bqx = Bqext[m % NG]
                d4 = dg4[m % 2]; Bd4 = Bdg4[m % 2]
                tt("pool", d4[:], ident[:].unsqueeze(1).broadcast_to([128, 4, 128]),
                   wsh[:, m, :].unsqueeze(2).broadcast_to([128, 4, 128]), ALU.mult, [Bc, Bp], [Bd4])
                if npc:
                    cp("pool", qx[:, 0:3], qhalo[:, m, :], [Bqhalo], [Bqx])
                for bk in blocks:
                    c0, c1 = bk; n = c1 - c0
                    pq, pbq = fm(wq, Bwq, ml, hT, BhT, bk)
                    if bk == sblock:
                        act(qexS[:, m, :, 3:7], pq[:, 0:64].rearrange("p (s t) -> p s t", t=4), AF.Copy, [pbq], [BqexS])
                        act(qtS[:, m, :], pq[:, 0:64], AF.Copy, [pbq], [BqtS])
                    else:
                        act(qx[:, 3 + c0 - pc0:3 + c1 - pc0], pq[:, 0:n], AF.Copy, [pbq], [Bqx])
                        if G["last"] and c1 == G["ncol"]:
                            assert n >= 3
                            act(qt[:, m, :], pq[:, n - 3:n], AF.Copy, [pbq], [Bqt])
                if npc:
                    cp("pool", qhalo[:, m, :], qx[:, npc:npc + 3], [Bqx], [Bqhalo])

            def stage4B(m):
                qx = qext[m % NG]; Bqx = Bqext[m % NG]
                d4 = dg4[m % 2]; Bd4 = Bdg4[m % 2]
                for bk in blocks:
                    c0, c1 = bk; n = c1 - c0
                    pt, pb, _ = psum()
                    for i in range(4):
                        if bk == sblock:
                            mm(pt[:, 0:64].rearrange("p (s t) -> p s t", t=4), d4[:, i, :], qexS[:, m, :, i:i + 4], i == 0, i == 3,
                               [Bd4, BqexS], [pb])
                        else:
                            mm(pt[:, 0:n], d4[:, i, :], qx[:, c0 - pc0 + i:c1 - pc0 + i], i == 0, i == 3, [Bd4, Bqx], [pb])
                    act(qkvT[:, m, c0:c1], pt[:, 0:n], AF.Silu, [pb], [BqkvT])

            stage4A(0)
            for m in range(24):
                if m + 1 < 24:
                    stage4A(m + 1)
                stage4B(m)
            P.ck("p4")
            for nb in range(2):
                wz, Bwz = wnext(("in", 5120 + nb * 512, 512))
                for t in tiles:
                    pt, pb, _ = psum()
                    for kc in range(8):
                        mm(pt[:t.n, :], hT[:, kc, t.col:t.col + t.n], wz[:, kc, :], kc == 0, kc == 7, [BhT, Bwz], [pb])
                    act(zs[t.xi][:t.n, nb * 512:(nb + 1) * 512], pt[:t.n, :], AF.Silu, [pb], [Bzs[t.xi]])
            wba, Bwba = wnext(("in", 6144, 16))
            for t in tiles:
                pt, pb, _ = psum()
                for kc in range(8):
                    mm(pt[:t.n, 0:16], hT[:, kc, t.col:t.col + t.n], wba[:, kc, 0:16], kc == 0, kc == 7, [BhT, Bwba], [pb])
                act(ba[:t.n, t.xi, :], pt[:t.n, 0:16], AF.Copy, [pb], [Bba])
            wprefetch()

            P.ck("zba")
            cg = conv_gen()

            def fill(k=1):
                for _ in range(k):
                    next(cg, None)
            for t in tiles:
                n = t.n; col = t.col; isS = (t.kind == "S")
                tU = triUs if isS else triU
                tM = mmins if isS else mmin
                tO = offds if isS else offd
                tB = blk if isS else onesf
                qTt = lambda h: qkvT[:, h, col:col + n]
                kTt = lambda h: qkvT[:, 8 + h, col:col + n]
                vTt = lambda h: qkvT[:, 16 + h, col:col + n]
                tt("dve", sqT[:, :, 0:n], qkvT[:, 0:16, col:col + n], qkvT[:, 0:16, col:col + n], ALU.mult, [BqkvT], [BsqT])
                pt, pb, _ = psum("g")
                for j in range(16):
                    mm(pt[:n, j:j + 1], sqT[:, j, 0:n], onesb[:, 0:1], True, True, [BsqT, Bc], [pb])
                ts("dve", sc[:n, 0:16], pt[:n, 0:16], 1e-6, None, ALU.add, None, [pb], [Bsc])
                fill()
                act(sc[:n, 24:32], sc[:n, 8:16], AF.Ln, [Bsc], [Bsc])
                act(sc[:n, 16:24], sc[:n, 24:32], AF.Exp, [Bsc], [Bsc], scale=0.5)
                act(sc[:n, 24:32], sc[:n, 24:32], AF.Exp, [Bsc], [Bsc], scale=-0.5)
                act(sc[:n, 32:40], ba[:n, t.xi, 0:8], AF.Exp, [Bba], [Bsc], scale=-1.0)
                ts("dve", sc[:n, 32:40], sc[:n, 32:40], 1.0, None, ALU.add, None, [Bsc], [Bsc])
                P.op("dve", lambda e, n=n: e.reciprocal(out=sc[:n, 32:40], in_=sc[:n, 32:40]), [Bsc], [Bsc])
                tt("dve", sc[:n, 40:48], ba[:n, t.xi, 8:16], dtb[:n, :], ALU.add, [Bba, Bp], [Bsc])
                stt(sc[:n, 48:56], sc[:n, 40:48], -1.0, sc[:n, 40:48], ALU.mult, ALU.max, [Bsc], [Bsc])
                act(sc[:n, 48:56], sc[:n, 48:56], AF.Exp, [Bsc], [Bsc], scale=-1.0)
                act(sc[:n, 48:56], sc[:n, 48:56], AF.Ln, [Bsc], [Bsc], bias=1.0)
                stt(sc[:n, 40:48], sc[:n, 40:48], 0.0, sc[:n, 48:56], ALU.max, ALU.add, [Bsc], [Bsc])
                tt("dve", sc[:n, 56:64], sc[:n, 40:48], nexpA[:n, :], ALU.mult, [Bsc, Bp], [Bsc])
                tt("dve", sc2[:n, 0:8], sc[:n, 32:40], sc[:n, 24:32], ALU.mult, [Bsc], [Bsc])
                stt(sc2[:n, 8:16], sc2[:n, 0:8], -1.0, sc[:n, 24:32], ALU.mult, ALU.mult, [Bsc], [Bsc])
                pt, pb, _ = psum("g")
                mm(pt[:n, 0:8], tU[:n, :n], sc[:n, 56:64], True, True, [Bc, Bsc], [pb])
                mm(pt[:n, 8:16], tB[:n, :n], sc[:n, 56:64], True, True, [Bc, Bsc], [pb])
                cp("dve", sc2[:n, 16:24], pt[:n, 0:8], [pb], [Bsc])
                fill()
                act(sc2[:n, 24:32], pt[:n, 0:8], AF.Exp, [pb], [Bsc])
                tt("dve", sc2[:n, 32:40], pt[:n, 8:16], sc2[:n, 16:24], ALU.subtract, [pb, Bsc], [Bsc])
                act(sc2[:n, 32:40], sc2[:n, 32:40], AF.Exp, [Bsc], [Bsc])
                tt("dve", sc2[:n, 40:48], sc[:n, 24:32], sc2[:n, 32:40], ALU.mult, [Bsc], [Bsc])
                P.ck("g1")
                tt("dve", gm[:n, :, 0:n], tU[:n, :n].unsqueeze(1).broadcast_to([n, 8, n]),
                   sc[:n, 56:64].unsqueeze(2).broadcast_to([n, 8, n]), ALU.mult, [Bc, Bsc], [Bgm])
                gbs = []
                for hb_ in range(2):
                    pt, pb, _ = psum("g")
                    mm(pt[:, 0:4 * n].rearrange("p (h n) -> p h n", h=4), onesf[:n, :], gm[:n, hb_ * 4:hb_ * 4 + 4, 0:n], True, True,
                       [Bc, Bgm], [pb])
                    gbs.append((pt, pb))
                    fill()
                for h in range(8):
                    pt, pb = gbs[h // 4]
                    stt(decT[:n, h, 0:n], pt[:n, (h % 4) * n:(h % 4 + 1) * n], sc2[:n, 16 + h:17 + h], tM[:n, :n], ALU.subtract, ALU.min,
                        [pb, Bsc, Bc], [BdecT])
                for hb_ in range(2):
                    pt, pb = gbs[hb_]
                    act(EB[:, hb_ * 4:hb_ * 4 + 4, 0:n], pt[:, 0:4 * n].rearrange("p (h n) -> p h n", h=4), AF.Exp, [pb], [BEB])
                act(decT[:n, :, 0:n], decT[:n, :, 0:n], AF.Exp, [BdecT], [BdecT])
                tt("dve", decTs[:n, :, 0:n], decT[:n, :, 0:n], tO[:n, :n].unsqueeze(1).broadcast_to([n, 8, n]), ALU.mult,
                   [BdecT, Bc], [BdecTs])
                P.ck("g2")
                W0, BW0 = Wp[0], BWp[0]
                for hb_ in range(2):
                    pk, pbk, _ = psum("g")
                    pq, pbq, _ = psum("g")
                    for hh in range(4):
                        h = hb_ * 4 + hh
                        mm(pk[:n, hh * 128:hh * 128 + n], kTt(h), kTt(h), True, True, [BqkvT], [pbk])
                        mm(pq[:n, hh * 128:hh * 128 + n], kTt(h), qTt(h), True, True, [BqkvT], [pbq])
                    for hh in range(4):
                        h = hb_ * 4 + hh
                        stt(W0[:n, h, 0:n], pk[:n, hh * 128:hh * 128 + n], sc2[:n, 8 + h:9 + h], decTs[:n, h, 0:n], ALU.mult, ALU.mult,
                            [pbk, Bsc, BdecTs], [BW0])
                        stt(attnT[:n, h, 0:n], pq[:n, hh * 128:hh * 128 + n], sc[:n, 24 + h:25 + h], decT[:n, h, 0:n], ALU.mult, ALU.mult,
                            [pbq, Bsc, BdecT], [BattnT])
                tt("dve", Zp[0][:n, :, 0:n], W0[:n, :, 0:n], ident[:n, :n].unsqueeze(1).broadcast_to([n, 8, n]), ALU.add, [BW0, Bc], [BZp[0]])
                pt, pb, _ = psum("g")
                pvw = bf(pt).rearrange("p (h c) -> p h c", h=8)
                for h in range(8):
                    tr(pvw[:n, h, 0:n], W0[:n, h, 0:n], ident[:n, :n], [BW0, Bc], [pb])
                act(Qp[0][:n, :, 0:n], pvw[:n, :, 0:n], AF.Copy, [pb], [BQp[0]])
                fill()
                cw = cq = cz = 0
                for lev in range(1, t.nlev + 1):
                    nq = 1 - cq
                    for hb_ in range(2):
                        pt, pb, _ = psum("g")
                        for hh in range(4):
                            h = hb_ * 4 + hh
                            mm(pt[:n, hh * 128:hh * 128 + n], Wp[cw][:n, h, 0:n], Qp[cq][:n, h, 0:n], True, True, [BWp[cw], BQp[cq]], [pb])
                        act(Qp[nq][:n, hb_ * 4:hb_ * 4 + 4, 0:n], pt[:n, :].rearrange("p (h c) -> p h c", h=4)[:, :, 0:n], AF.Copy,
                            [pb], [BQp[nq]])
                        fill()
                    if lev < t.nlev:
                        nw = 1 - cw
                        for hb_ in range(2):
                            pt, pb, _ = psum("g")
                            for hh in range(4):
                                h = hb_ * 4 + hh
                                mm(pt[:n, hh * 128:hh * 128 + n], Qp[cq][:n, h, 0:n], Wp[cw][:n, h, 0:n], True, True, [BWp[cw], BQp[cq]], [pb])
                            act(Wp[nw][:n, hb_ * 4:hb_ * 4 + 4, 0:n], pt[:n, :].rearrange("p (h c) -> p h c", h=4)[:, :, 0:n], AF.Copy,
                                [pb], [BWp[nw]])
                            fill()
                        cw = nw
                    cq = nq
                    nz = 1 - cz
                    for hb_ in range(2):
                        pt, pb, _ = psum("g")
                        for hh in range(4):
                            h = hb_ * 4 + hh
                            mm(pt[:n, hh * 128:hh * 128 + n], Qp[cq][:n, h, 0:n], Zp[cz][:n, h, 0:n], True, True, [BQp[cq], BZp[cz]], [pb])
                        tt("dve", Zp[nz][:n, hb_ * 4:hb_ * 4 + 4, 0:n], pt[:n, :].rearrange("p (h c) -> p h c", h=4)[:, :, 0:n],
                           Zp[cz][:n, hb_ * 4:hb_ * 4 + 4, 0:n], ALU.add, [pb, BZp[cz]], [BZp[nz]])
                        fill()
                    cz = nz
                Z = Zp[cz]; BZ = BZp[cz]
                P.ck("g4")
                pk, pbk, _ = psum("g"); pvk = bf(pk).rearrange("p (h c) -> p h c", h=8)
                pv2, pbv, _ = psum("g"); pvv = bf(pv2).rearrange("p (h c) -> p h c", h=8)
                for h in range(8):
                    tr(pvk[:n, h, :], kTt(h), ident[:, :], [BqkvT, Bc], [pbk])
                    tr(pvv[:n, h, :], vTt(h), ident[:, :], [BqkvT, Bc], [pbv])
                P.ck("g4a")
                bc = lambda colap: colap.unsqueeze(2).broadcast_to([n, 8, 128])
                tt("dve", rhsk[:n, :, :], pvk[:n, :, :], bc(sc2[:n, 24:32]), ALU.mult, [pbk, Bsc], [Brhsk])
                tt("dve", kdec[:n, :, :], pvk[:n, :, :], bc(sc2[:n, 40:48]), ALU.mult, [pbk, Bsc], [Bkdec])
                tt("dve", rhsv[:n, :, :], pvv[:n, :, :], bc(sc[:n, 16:24]), ALU.mult, [pbv, Bsc], [Brhsv])
                fill()
                P.ck("g4b")
                tt("pool", qd[:, :, 0:n], qkvT[:, 0:8, col:col + n], EB[:, :, 0:n], ALU.mult, [BqkvT, BEB], [Bqd])
                P.ck("g5")
                for hb_ in range(2):
                    pt, pb, _ = psum("g")
                    for hh in range(4):
                        h = hb_ * 4 + hh
                        mm(pt[:, hh * 128:hh * 128 + n], rhsk[:n, h, :], Z[:n, h, 0:n], True, True, [Brhsk, BZ], [pb])
                    P.op("act", lambda e, pt=pt, hb_=hb_, n=n: e.mul(nkcT[:, hb_ * 4:hb_ * 4 + 4, 0:n],
                                                                     pt[:, :].rearrange("p (h c) -> p h c", h=4)[:, :, 0:n], -1.0), [pb], [BnkcT])
                opsum = []
                if not isS:
                    for hb_ in range(2):
                        pu, pbu, _ = psum("g")
                        for hh in range(4):
                            h = hb_ * 4 + hh
                            mm(pu[:n, hh * 128:(hh + 1) * 128], Z[:n, h, 0:n], rhsv[:n, h, :], True, False, [BZ, Brhsv], [pbu])
                            mm(pu[:n, hh * 128:(hh + 1) * 128], nkcT[:, h, 0:n], Sbf[:, h, :], False, True, [BnkcT, BSbf], [pbu])
                        tt("dve", u[:n, hb_ * 4:hb_ * 4 + 4, :], pu[:n, :].rearrange("p (h e) -> p h e", h=4),
                           sc2[:n, hb_ * 4:hb_ * 4 + 4].unsqueeze(2).broadcast_to([n, 4, 128]), ALU.mult, [pbu, Bsc], [Bu])
                        fill()
                    for hb_ in range(2):
                        po, pbo, _ = psum("g")
                        for hh in range(4):
                            h = hb_ * 4 + hh
                            mm(po[:n, hh * 128:(hh + 1) * 128], qd[:, h, 0:n], Sbf[:, h, :], True, False, [Bqd, BSbf], [pbo])
                            mm(po[:n, hh * 128:(hh + 1) * 128], attnT[:n, h, 0:n], u[:n, h, :], False, True, [BattnT, Bu], [pbo])
                        opsum.append((po, pbo))
                        fill()
                    for hb_ in range(2):
                        pS, pbS, _ = psum("g")
                        for hh in range(4):
                            h = hb_ * 4 + hh
                            mm(pS[:, hh * 128:(hh + 1) * 128], kdec[:n, h, :], u[:n, h, :], True, True, [Bkdec, Bu], [pbS])
                        for hh in range(4):
                            h = hb_ * 4 + hh
                            stt(S[:, h, :], S[:, h, :], EB[:, h, n - 1:n], pS[:, hh * 128:(hh + 1) * 128], ALU.mult, ALU.add,
                                [BS, BEB, pbS], [BS])
                        act(Sbf[:, hb_ * 4:hb_ * 4 + 4, :], S[:, hb_ * 4:hb_ * 4 + 4, :], AF.Copy, [BS], [BSbf])
                        fill()
                else:
                    puT, pbuT, _ = psum("g"); puTv = puT[:, :].rearrange("p (h c) -> p h c", h=8)
                    poT, pboT, _ = psum("g"); poTv = poT[:, :].rearrange("p (h c) -> p h c", h=8)
                    for h in range(8):
                        mm(puTv[:, h, :], rhsv[:n, h, :], Z[:n, h, 0:n], h == 0, False, [Brhsv, BZ], [pbuT], skip=True)
                    for s in range(16):
                        s0 = S0[s % 3]; Bs0 = BS0[s % 3]; s0b = S0b[s % 2]; Bs0b = BS0b[s % 2]
                        P.dma("sp", lambda e, s=s, s0=s0: e.dma_start(out=s0[:], in_=st_delta[s].rearrange("h d e -> d h e")), S0key[s % 3], writes=[Bs0])
                        act(s0b[:], s0[:], AF.Copy, [Bs0], [Bs0b])
                        for h in range(8):
                            mm(puTv[:, h, 4 * s:4 * s + 4], s0b[:, h, :], nkcT[:, h, 4 * s:4 * s + 4], False, False, [Bs0b, BnkcT], [pbuT], skip=True)
                            mm(poTv[:, h, 4 * s:4 * s + 4], s0b[:, h, :], qd[:, h, 4 * s:4 * s + 4], (s == 0 and h == 0), False, [Bs0b, Bqd], [pboT], skip=True)
                    act(uTs[:], puTv, AF.Copy, [pbuT], [BuTs])
                    pt, pb, _ = psum("g"); ptv = bf(pt).rearrange("p (h c) -> p h c", h=8)
                    for h in range(8):
                        tr(ptv[:n, h, :], uTs[:, h, :], ident[:, :], [BuTs, Bc], [pb])
                    for h in range(8):
                        ts("dve", u[:n, h, :], ptv[:n, h, :], sc2[:n, h:h + 1], None, ALU.mult, None, [pb, Bsc], [Bu])
                    for h in range(8):
                        mm(poTv[:, h, :], u[:n, h, :], attnT[:n, h, 0:n], False, h == 7, [Bu, BattnT], [pboT], skip=True)
                    act(oTs[:], poTv, AF.Copy, [pboT], [BoTs])
                    for hb_ in range(2):
                        po, pbo, _ = psum("g")
                        for hh in range(4):
                            tr(po[:n, hh * 128:(hh + 1) * 128], oTs[:, hb_ * 4 + hh, :], identf[:, :], [BoTs, Bc], [pbo])
                        opsum.append((po, pbo))
                        fill()
                    def sample_state_update(n=n):
                        for s in range(16):
                            s0 = S0[(s + 1) % 3]; Bs0 = BS0[(s + 1) % 3]
                            P.dma("sp", lambda e, s=s, s0=s0: e.dma_start(out=s0[:], in_=st_delta[s].rearrange("h d e -> d h e")), S0key[(s + 1) % 3], writes=[Bs0])
                            ums = um[s % 2]; Bums = Bum[s % 2]
                            act(ums[:n, :], u[:n, :, :].rearrange("p h e -> p (h e)"), AF.Identity, [Bu, Bc], [Bums], scale=bmask[:n, s:s + 1])
                            so = So[s % 2]; Bso = BSo[s % 2]
                            for hb_ in range(2):
                                pS, pbS, _ = psum("g")
                                for hh in range(4):
                                    h = hb_ * 4 + hh
                                    mm(pS[:, hh * 128:(hh + 1) * 128], kdec[:n, h, :], ums[:n, h * 128:(h + 1) * 128], True, True, [Bkdec, Bums], [pbS])
                                for hh in range(4):
                                    h = hb_ * 4 + hh
                                    stt(so[:, h, :], s0[:, h, :], EB[:, h, 4 * s + 3:4 * s + 4], pS[:, hh * 128:(hh + 1) * 128], ALU.mult, ALU.add,
                                        [Bs0, BEB, pbS], [Bso])
                            P.dma("pool", lambda e, s=s, so=so: e.dma_start(out=o_sdelta[s].rearrange("h d e -> d h e"), in_=so[:]), ("xp", 3 + s % 2),
                                  reads=[Bso])
                P.ck("g6")
                for h in range(8):
                    po, pbo = opsum[h // 4]
                    act(junk2[:n, :], po[:n, (h % 4) * 128:(h % 4 + 1) * 128], AF.Square, [pbo], [Bjunk2, Bms], accum_out=ms[:n, h:h + 1])
                ts("dve", ms[:n, 8:16], sc[:n, 0:8], 128.0 * EPS, None, ALU.mult, None, [Bsc], [Bms])
                stt(ms[:n, 8:16], ms[:n, 0:8], 1.0 / 128, ms[:n, 8:16], ALU.mult, ALU.add, [Bms], [Bms])
                act(ms[:n, 8:16], ms[:n, 8:16], AF.Ln, [Bms], [Bms])
                act(ms[:n, 0:8], ms[:n, 8:16], AF.Exp, [Bms], [Bms], scale=-0.5)
                for h in range(8):
                    po, pbo = opsum[h // 4]
                    stt(on[:n, h * 128:(h + 1) * 128], po[:n, (h % 4) * 128:(h % 4 + 1) * 128], ms[:n, h:h + 1], zs[t.xi][:n, h * 128:(h + 1) * 128],
                        ALU.mult, ALU.mult, [pbo, Bms, Bzs[t.xi]], [Bon])
                pt, pb, _ = psum("g"); ptv = bf(pt).rearrange("p (h c) -> p h c", h=8)
                for h in range(8):
                    tr(ptv[:, h, 0:n], on[:n, h * 128:(h + 1) * 128], ident[:n, :n], [Bon, Bc], [pb])
                ts("dve", onT[:, :, col:col + n], ptv[:, :, 0:n], ghead[:, 0:1], None, ALU.mult, None, [pb, Bp], [BonT])
                fill()
                P.ck("g7")
                if isS:
                    sample_state_update()
                P.ck("g8")

            for _ in cg:
                pass
            if isSG:
                emit_sample_tails()
            for q in range(2):
                wd, Bwd = wnext(("dout", q * 512, 512))
                wg, Bwg = wnext(("in", 7184 + q * 512, 512))
                for ml in range(4):
                    m = q * 4 + ml
                    for bk in blocks:
                        c0, c1 = bk; n = c1 - c0
                        pd_, pbd = fm(wd, Bwd, ml, onT, BonT, bk)
                        pg, pbg = fm(wg, Bwg, ml, hT, BhT, bk)
                        sg_, Bsg = ftmp(); t1, Bt1 = ftmp()
                        act(sg_[:, 0:n], pg[:, 0:n], AF.Sigmoid, [pbg], [Bsg])
                        tt("dve", t1[:, 0:n], pd_[:, 0:n], sg_[:, 0:n], ALU.mult, [pbd, Bsg], [Bt1])
                        tt("dve", mixT[:, m, c0:c1], mixT[:, m, c0:c1], t1[:, 0:n], ALU.add, [BmixT, Bt1], [BmixT])
            P.ck("p5")
            for nb in range(2):
                wo_, Bwo = wnext(("o", nb * 512, 512))
                for t in tiles:
                    pt, pb, _ = psum()
                    for kc in range(8):
                        mm(pt[:t.n, :], mixT[:, kc, t.col:t.col + t.n], wo_[:, kc, :], kc == 0, kc == 7, [BmixT, Bwo], [pb])
                    x = xt[t.xi]
                    tt("dve", x[:t.n, nb * 512:(nb + 1) * 512], pt[:t.n, :], x[:t.n, nb * 512:(nb + 1) * 512], ALU.add, [pb, Bxt[t.xi]], [Bxt[t.xi]])
            wprefetch()
            P.ck("p6")
            norm_to_T(G, gmlp)
            P.handover([BqkvT, BmixT, Bcpre], [BaT])
            for j in range(8):
                wu, Bwu = wnext(("up", j * 512, 512))
                for ml in range(4):
                    m = j * 4 + ml
                    for bk in blocks:
                        c0, c1 = bk; n = c1 - c0
                        pu, pbu = fm(wu, Bwu, ml, hT, BhT, bk)
                        r, Br = ftmp()
                        act(r[:, 0:n], pu[:, 0:n], AF.Relu, [pbu], [Br])
                        act(aT[:, m, c0:c1], r[:, 0:n], AF.Square, [Br], [BaT])
            P.ck("p8")
            for nb in range(2):
                banks = [psum() for _ in tiles]
                for kb in range(4):
                    wd, Bwd = wnext(("down", kb, nb))
                    for ti, t in enumerate(tiles):
                        pt, pb, _ = banks[ti]
                        for kc in range(8):
                            mm(pt[:t.n, :], aT[:, kb * 8 + kc, t.col:t.col + t.n], wd[:, kc, :], kb == 0 and kc == 0, kb == 3 and kc == 7,
                               [BaT, Bwd], [pb])
                for ti, t in enumerate(tiles):
                    pt, pb, _ = banks[ti]
                    x = xt[t.xi]
                    tt("dve", x[:t.n, nb * 512:(nb + 1) * 512], pt[:t.n, :], x[:t.n, nb * 512:(nb + 1) * 512], ALU.add, [pb, Bxt[t.xi]], [Bxt[t.xi]])
            P.ck("p9")
            for ti, t in enumerate(tiles):
                n = t.n; x = xt[t.xi]; Bx = Bxt[t.xi]
                y = yt[0]; By = Byt[0]
                act(junk[:n, :], x[:n, :], AF.Square, [Bx], [Bjunk, Bssx], accum_out=ssx[:n, 0:1])
                act(ssx[:n, 1:2], ssx[:n, 0:1], AF.Sqrt, [Bssx], [Bssx], scale=1.0 / D, bias=EPS)
                P.op("dve", lambda e, n=n: e.reciprocal(out=ssx[:n, 2:3], in_=ssx[:n, 1:2]), [Bssx], [Bssx])
                stt(y[:n, :], x[:n, :], ssx[:n, 2:3], gfin[:n, :], ALU.mult, ALU.mult, [Bx, Bssx, Bp], [By])
                if t.kind == "S":
                    P.dma("pool", lambda e, y=y: e.dma_start(out=y_s[:, :], in_=y[0:64, :]), ("ytp", 0), reads=[By])
                elif t.t0 == 0:
                    P.dma("pool", lambda e, y=y: e.dma_start(out=y_p[0:112, :], in_=y[16:128, :]), ("ytp", 0), reads=[By])
                else:
                    P.dma("pool", lambda e, y=y, t=t: e.dma_start(out=y_p[t.t0 - 16:t.t0 - 16 + t.n, :], in_=y[0:t.n, :]), ("ytp", 0), reads=[By])
                if gi + 1 < len(groups):
                    for t2 in groups[gi + 1]["tiles"]:
                        if t2.xi == t.xi:
                            load_x(gi + 1, t2)

        if not max_groups:
            tail_out(glut, Bglut, 8, 30, 30,
                     lambda k: P.dma("sp", lambda e: e.dma_start(out=o_pconv[:, :], in_=stgT[0:30, :]), ("yt", 0), reads=[BstgT]), None)
            tail_out(qt, Bqt, 24, 3, 3,
                     lambda k: P.dma("sp", lambda e, k=k: e.dma_start(out=o_pqkv[:, k * 1024:(k + 1) * 1024], in_=stgT[0:3, :]), ("yt", 0), reads=[BstgT]), None)
            P.dma("sp", lambda e: e.dma_start(out=o_pdelta.rearrange("h d e -> d h e"), in_=S[:]), "o_p_delta", reads=[BS])
        P.ops["sp"].append({"fn": None, "deps": {("dma", k): v for k, v in P.dma_counts.items()}, "dma": None})
        P.emit()
    return nc


_CACHE = {}


def kernel(x_prompt, x_sample, state_conv, state_qkv_conv, state_delta, meta_tokens, g_mix, w_in,
           w_dw, b_dw, ln_g, ln_b, w_cout, w_short, a_log, dt_bias, g_head, w_dout, w_o, g_mlp,
           w_up, w_down, g_final):
    f = lambda a: np.ascontiguousarray(np.asarray(a, dtype=np.float32))
    if "nc" not in _CACHE:
        _CACHE["nc"] = build()
    nc = _CACHE["nc"]
    shared = dict(meta=f(meta_tokens), g_mix=f(g_mix[0]), w_in=f(w_in[0]), w_dw=f(w_dw[0]), b_dw=f(b_dw[0]), ln_g=f(ln_g[0]),
                  ln_b=f(ln_b[0]), w_cout=f(w_cout[0]), w_short=f(w_short[0]), a_log=f(a_log[0]), dt_bias=f(dt_bias[0]),
                  g_head=f(g_head[0]), w_dout=f(w_dout[0]), w_o=f(w_o[0]), g_mlp=f(g_mlp[0]), w_up=f(w_up[0]),
                  w_down=f(w_down[0]), g_final=f(g_final))
    xp = f(x_prompt); xs = f(x_sample); sc_ = f(state_conv); sq_ = f(state_qkv_conv); sd_ = f(state_delta)
    in_maps = []
    for c in range(8):
        d = dict(shared)
        d["x_p"] = xp[c]
        d["x_s"] = xs[16 * c:16 * c + 16].reshape(64, D)
        d["st_conv"] = sc_[0, 16 * c:16 * c + 16]
        d["st_qkv"] = sq_[0, 16 * c:16 * c + 16]
        d["st_delta"] = sd_[0, 16 * c:16 * c + 16]
        in_maps.append(d)
    res = run_bass_kernel_spmd(nc, in_maps, core_ids=list(range(8)))
    R = res.results
    y_prompt = np.stack([R[c]["y_p"] for c in range(8)]).astype(np.float32)
    y_sample = np.concatenate([R[c]["y_s"].reshape(16, 4, D) for c in range(8)]).astype(np.float32)
    p_conv = np.stack([R[c]["p_conv"] for c in range(8)])[None].astype(np.float32)
    p_qkv = np.stack([R[c]["p_qkv"] for c in range(8)])[None].astype(np.float32)
    p_delta = np.stack([R[c]["p_delta"] for c in range(8)])[None].astype(np.float32)
    s_conv = np.concatenate([R[c]["s_conv"] for c in range(8)])[None].astype(np.float32)
    s_qkv = np.concatenate([R[c]["s_qkv"] for c in range(8)])[None].astype(np.float32)
    s_delta = np.concatenate([R[c]["s_delta"] for c in range(8)])[None].astype(np.float32)
    return (y_prompt, y_sample, p_conv, p_qkv, p_delta, s_conv, s_qkv, s_delta)
```

```python
import numpy as np
from contextlib import ExitStack
import concourse.bass as bass
import concourse.mybir as mybir
from concourse.bass_utils import run_bass_kernel_spmd

F32 = mybir.dt.float32
BF16 = mybir.dt.bfloat16
AF = mybir.ActivationFunctionType
ALU = mybir.AluOpType

ENGS = ("pe", "act", "dve", "pool", "sp")
SAME_ENGINE_SYNC = {"pe": False, "act": True, "dve": True, "pool": True, "sp": False}

D = 1024
DIN = 8208
DFF = 4096
NPT = 2064
EPS = 1e-6
NEG = -30000.0


class Buf:
    __slots__ = ("name", "lw", "rd", "excl")

    def __init__(self, name, excl=False):
        self.name = name
        self.lw = None
        self.rd = {}
        self.excl = excl


class Prog:
    def __init__(self, nc):
        self.nc = nc
        self.ops = {e: [] for e in ENGS}
        self.dma_counts = {}
        self.dma_keys = []
        self.enabled = True
        self.stop_at = None

    def ck(self, name):
        if self.stop_at is not None and name == self.stop_at:
            self.enabled = False

    def _collect(self, eng, reads, writes):
        deps = {}

        def add(k, v):
            if k[0] == "eng" and k[1] == eng and not SAME_ENGINE_SYNC[eng]:
                return
            if k not in deps or deps[k] < v:
                deps[k] = v

        for b in reads:
            if b.lw is not None:
                add((b.lw[0], b.lw[1]), b.lw[2])
            if b.excl:
                for k, v in b.rd.items():
                    if k != ("eng", eng):
                        add(k, v)
        for b in writes:
            if b.lw is not None:
                add((b.lw[0], b.lw[1]), b.lw[2])
            for k, v in b.rd.items():
                add(k, v)
        return deps

    def _commit(self, tok, reads, writes):
        k = (tok[0], tok[1])
        for b in reads:
            if b.rd.get(k, -1) < tok[2]:
                b.rd[k] = tok[2]
        for b in writes:
            b.lw = tok
            b.rd = {}

    def op(self, eng, fn, reads=(), writes=()):
        if not self.enabled:
            return
        deps = self._collect(eng, reads, writes)
        idx = len(self.ops[eng])
        self.ops[eng].append({"fn": fn, "deps": deps, "dma": None})
        self._commit(("eng", eng, idx), reads, writes)

    def dma(self, eng, fn, semkey, reads=(), writes=()):
        if not self.enabled:
            return
        deps = self._collect(eng, reads, writes)
        if semkey not in self.dma_counts:
            self.dma_counts[semkey] = 0
            self.dma_keys.append(semkey)
        self.dma_counts[semkey] += 16
        cnt = self.dma_counts[semkey]
        self.ops[eng].append({"fn": fn, "deps": deps, "dma": semkey})
        self._commit(("dma", semkey, cnt), reads, writes)

    def wait_all(self, eng, bufs):
        deps = self._collect(eng, bufs, ())
        self.ops[eng].append({"fn": None, "deps": deps, "dma": None})

    def handover(self, old, new):
        merged = {}
        for b in old:
            if b.lw is not None:
                k = (b.lw[0], b.lw[1])
                merged[k] = max(merged.get(k, -1), b.lw[2])
            for k, v in b.rd.items():
                merged[k] = max(merged.get(k, -1), v)
        for b in new:
            b.lw = None
            b.rd = dict(merged)

    def emit(self):
        nc = self.nc
        sig = {e: set() for e in ENGS}
        for e in ENGS:
            for o in self.ops[e]:
                for (kind, key), v in o["deps"].items():
                    if kind == "eng":
                        sig[key].add(v)
        cum = {}
        for e in ENGS:
            c = 0
            m = {}
            for i in range(len(self.ops[e])):
                if i in sig[e]:
                    c += 1
                    m[i] = c
            cum[e] = m
        with ExitStack() as st:
            esem = {e: st.enter_context(nc.semaphore("s_" + e)) for e in ENGS}
            dsem = {k: st.enter_context(nc.semaphore("d_%d" % i)) for i, k in enumerate(self.dma_keys)}
            block = st.enter_context(nc.Block())
            engobj = {"pe": block.tensor, "act": block.scalar, "dve": block.vector,
                      "pool": block.gpsimd, "sp": block.sync}

            def make(ename):
                ops = self.ops[ename]

                def body(eng):
                    seen = {}
                    for i, o in enumerate(ops):
                        for (kind, key), v in o["deps"].items():
                            if kind == "eng":
                                val = cum[key][v]
                                s = esem[key]
                            else:
                                val = v
                                s = dsem[key]
                            if seen.get((kind, key), 0) >= val:
                                continue
                            seen[(kind, key)] = val
                            eng.wait_ge(s, val)
                        if o["fn"] is None:
                            continue
                        inst = o["fn"](eng)
                        if o["dma"] is not None:
                            inst.then_inc(dsem[o["dma"]], 16)
                        elif i in cum[ename]:
                            inst.then_inc(esem[ename], 1)
                return body

            for e in ENGS:
                if self.ops[e]:
                    engobj[e](make(e))


class TileD:
    def __init__(self, kind, n, t0, col, xi):
        self.kind, self.n, self.t0, self.col, self.xi = kind, n, t0, col, xi
        self.nlev = {128: 6, 16: 3, 64: 1}[n] if kind == "P" else 1


def make_groups(with_sample=True):
    groups = []
    ptiles = [(i * 128, min(128, NPT - i * 128)) for i in range(17)]
    split = [ptiles[0:4], ptiles[4:8], ptiles[8:12], ptiles[12:17]]
    for gi, pts in enumerate(split):
        tiles = []
        col = 0
        for xi, (t0, n) in enumerate(pts):
            tiles.append(TileD("P", n, t0, col, xi))
            col += n
        blocks = []
        c = 0
        while c < col:
            blocks.append((c, min(col, c + 512)))
            c += 512
        groups.append(dict(tiles=tiles, pc0=0, npc=col, ncol=col, blocks=blocks, last=(gi == 3), first=False))
    if with_sample:
        groups.append(dict(tiles=[TileD("S", 64, 0, 0, 0)], pc0=64, npc=0, ncol=64, blocks=[(0, 64)], last=False, first=True))
    return groups


def wsched():
    L = []
    for j in range(6):
        L.append(("in", 2048 + 512 * j, 512))
    L += [("in", 5120, 512), ("in", 5632, 512), ("in", 6144, 16)]
    for q in range(2):
        L.append(("in", q * 512, 512))
        L.append(("in", 1024 + q * 512, 512))
    for q in range(2):
        L += [("cout", q * 512, 512), ("in", 6160 + q * 512, 512)]
    for q in range(2):
        L += [("dout", q * 512, 512), ("in", 7184 + q * 512, 512)]
    L += [("o", 0, 512), ("o", 512, 512)]
    for j in range(8):
        L.append(("up", j * 512, 512))
    for nb in range(2):
        for kb in range(4):
            L.append(("down", kb, nb))
    return L


def build(with_sample=True, NW=3, debug=False, max_groups=None, stop_at=None):
    nc = bass.Bass("TRN2", target_bir_lowering=False)
    din = lambda name, shape: nc.dram_tensor(name, shape, F32, kind="ExternalInput").ap()
    dout = lambda name, shape: nc.dram_tensor(name, shape, F32, kind="ExternalOutput").ap()
    x_p = din("x_p", [2048, D]); x_s = din("x_s", [64, D])
    st_conv = din("st_conv", [16, 30, D]); st_qkv = din("st_qkv", [16, 3, 3072]); st_delta = din("st_delta", [16, 8, 128, 128])
    meta = din("meta", [16, D])
    g_mix = din("g_mix", [D]); w_in = din("w_in", [D, DIN]); w_dw = din("w_dw", [31, D]); b_dw = din("b_dw", [D])
    ln_g = din("ln_g", [D]); ln_b = din("ln_b", [D]); w_cout = din("w_cout", [D, D]); w_short = din("w_short", [4, 3072])
    a_log = din("a_log", [8]); dt_bias = din("dt_bias", [8]); g_head = din("g_head", [128]); w_dout = din("w_dout", [D, D])
    w_o = din("w_o", [D, D]); g_mlp = din("g_mlp", [D]); w_up = din("w_up", [D, DFF]); w_down = din("w_down", [DFF, D])
    g_final = din("g_final", [D])
    y_p = dout("y_p", [2048, D]); y_s = dout("y_s", [64, D])
    o_pconv = dout("p_conv", [30, D]); o_pqkv = dout("p_qkv", [3, 3072]); o_pdelta = dout("p_delta", [8, 128, 128])
    o_sconv = dout("s_conv", [16, 30, D]); o_sqkv = dout("s_qkv", [16, 3, 3072]); o_sdelta = dout("s_delta", [16, 8, 128, 128])
    wmap = {"in": w_in, "cout": w_cout, "dout": w_dout, "o": w_o, "up": w_up}

    P = Prog(nc)
    P.stop_at = stop_at
    groups = make_groups(with_sample)
    if max_groups:
        groups = groups[:max_groups]
    MAXC = 528
    MAXPC = max(g["npc"] for g in groups)

    with ExitStack() as st:
        st.enter_context(nc.allow_non_contiguous_dma(reason="small params"))
        cnt = [0]

        def sb(shape, dt, name=None):
            cnt[0] += 1
            return st.enter_context(nc.sbuf_tensor(name or ("t%d" % cnt[0]), shape, dt))

        psT = [st.enter_context(nc.psum_tensor("ps%d" % i, [128, 512], F32)) for i in range(8)]
        psB = [Buf("ps%d" % i, excl=True) for i in range(8)]
        ps_reserved = set()
        ps_ctr = [0]

        pool_ctr = {"d": 0, "g": 0}

        def psum(pool=None):
            if pool == "d":
                i = pool_ctr["d"] % 3
                pool_ctr["d"] += 1
                return psT[i], psB[i], i
            if pool == "g":
                i = 3 + pool_ctr["g"] % 5
                pool_ctr["g"] += 1
                return psT[i], psB[i], i
            while True:
                i = ps_ctr[0] % 8
                ps_ctr[0] += 1
                if i not in ps_reserved:
                    return psT[i], psB[i], i

        def bf(ap):
            return ap[:].bitcast(BF16)

        def mm(out, lhsT, rhs, start, stop, reads, writes, skip=False):
            if skip:
                P.op("pe", lambda e: e.matmul(out, lhsT=lhsT, rhs=rhs, start=start, stop=stop, skip_group_check=True), reads, writes)
            else:
                P.op("pe", lambda e: e.matmul(out, lhsT=lhsT, rhs=rhs, start=start, stop=stop), reads, writes)

        def tr(out, in_, idn, reads, writes):
            P.op("pe", lambda e: e.transpose(out=out, in_=in_, identity=idn), reads, writes)

        def act(out, in_, func, reads, writes, **kw):
            P.op("act", lambda e: e.activation(out=out, in_=in_, func=func, **kw), reads, writes)

        def ts(eng, out, in0, s1, s2, op0, op1, reads, writes):
            if op1 is None:
                P.op(eng, lambda e: e.tensor_scalar(out=out, in0=in0, scalar1=s1, scalar2=None, op0=op0), reads, writes)
            else:
                P.op(eng, lambda e: e.tensor_scalar(out=out, in0=in0, scalar1=s1, scalar2=s2, op0=op0, op1=op1), reads, writes)

        def tt(eng, out, in0, in1, op, reads, writes):
            P.op(eng, lambda e: e.tensor_tensor(out=out, in0=in0, in1=in1, op=op), reads, writes)

        def stt(out, in0, scalar, in1, op0, op1, reads, writes):
            P.op("dve", lambda e: e.scalar_tensor_tensor(out=out, in0=in0, scalar=scalar, in1=in1, op0=op0, op1=op1), reads, writes)

        def cp(eng, out, in_, reads, writes):
            P.op(eng, lambda e: e.tensor_copy(out=out, in_=in_), reads, writes)

        Bc = Buf("consts")
        identf = sb([128, 128], F32); ident = sb([128, 128], BF16)
        onesf = sb([128, 128], F32); onesb = sb([128, 128], BF16)
        triU = sb([128, 128], F32); mmin = sb([128, 128], F32); offd = sb([128, 128], F32)
        blk = sb([64, 64], F32); triUs = sb([64, 64], F32); mmins = sb([64, 64], F32); offds = sb([64, 64], F32)
        bmask = sb([64, 16], F32)
        tmpc = sb([64, 64], F32)

        def pool(fn):
            P.op("pool", fn, reads=[Bc], writes=[Bc])

        pool(lambda e: e.memset(identf[:], 0.0))
        pool(lambda e: e.affine_select(out=identf[:], in_=identf[:], pattern=[[-1, 128]], compare_op=ALU.not_equal,
                                       fill=1.0, base=0, channel_multiplier=1))
        pool(lambda e: e.tensor_copy(out=ident[:], in_=identf[:]))
        pool(lambda e: e.memset(onesf[:], 1.0))
        pool(lambda e: e.memset(onesb[:], 1.0))
        pool(lambda e: e.affine_select(out=triU[:], in_=onesf[:], pattern=[[1, 128]], compare_op=ALU.is_ge,
                                       fill=0.0, base=0, channel_multiplier=-1))
        pool(lambda e: e.memset(mmin[:], 0.0))
        pool(lambda e: e.affine_select(out=mmin[:], in_=mmin[:], pattern=[[1, 128]], compare_op=ALU.is_ge,
                                       fill=NEG, base=0, channel_multiplier=-1))
        pool(lambda e: e.affine_select(out=offd[:], in_=onesf[:], pattern=[[1, 128]], compare_op=ALU.not_equal,
                                       fill=0.0, base=0, channel_multiplier=-1))
        pool(lambda e: e.affine_select(out=blk[:], in_=onesf[:64, :64], pattern=[[-4, 16], [0, 4]], compare_op=ALU.is_ge,
                                       fill=0.0, base=0, channel_multiplier=1))
        pool(lambda e: e.affine_select(out=blk[:], in_=blk[:], pattern=[[4, 16], [0, 4]], compare_op=ALU.is_ge,
                                       fill=0.0, base=3, channel_multiplier=-1))
        pool(lambda e: e.tensor_tensor(out=triUs[:], in0=triU[:64, :64], in1=blk[:], op=ALU.mult))
        pool(lambda e: e.tensor_tensor(out=offds[:], in0=offd[:64, :64], in1=blk[:], op=ALU.mult))
        pool(lambda e: e.tensor_scalar(out=tmpc[:], in0=blk[:], scalar1=-NEG, scalar2=NEG, op0=ALU.mult, op1=ALU.add))
        pool(lambda e: e.tensor_tensor(out=mmins[:], in0=mmin[:64, :64], in1=blk[:], op=ALU.mult))
        pool(lambda e: e.tensor_tensor(out=mmins[:], in0=mmins[:], in1=tmpc[:], op=ALU.add))
        pool(lambda e: e.affine_select(out=bmask[:], in_=onesf[:64, :16], pattern=[[-4, 16]], compare_op=ALU.is_ge,
                                       fill=0.0, base=0, channel_multiplier=1))
        pool(lambda e: e.affine_select(out=bmask[:], in_=bmask[:], pattern=[[4, 16]], compare_op=ALU.is_ge,
                                       fill=0.0, base=3, channel_multiplier=-1))

        NXT = 5
        xt = [sb([128, D], F32, "xt%d" % i) for i in range(NXT)]; Bxt = [Buf("xt%d" % i) for i in range(NXT)]
        Bp = Buf("params")
        pstage = sb([40, 128], F32)
        pv = sb([128, 40], F32)
        wdw = sb([128, 8, 31], F32)
        wsh = sb([128, 24, 4], F32)
        ghead = sb([128, 1], F32)
        dtb = sb([128, 8], F32); nexpA = sb([128, 8], F32)
        gfin = sb([128, D], F32)
        for i, v in enumerate([g_mix, g_mlp, b_dw, ln_g, ln_b]):
            P.dma("sp", lambda e, i=i, v=v: e.dma_start(out=pstage[8 * i:8 * i + 8, :], in_=v.rearrange("(k p) -> k p", p=128)), "par", writes=[Bp])
        P.dma("sp", lambda e: e.dma_start(out=xt[0][0:31, :], in_=w_dw[:, :]), ("x", 0), writes=[Bxt[0]])
        for k in range(3):
            P.dma("sp", lambda e, k=k: e.dma_start(out=xt[1 + k][0:4, :], in_=w_short[:, k * 1024:(k + 1) * 1024]), ("x", 1 + k), writes=[Bxt[1 + k]])
        P.dma("sp", lambda e: e.dma_start(out=ghead[:], in_=g_head.rearrange("(p o) -> p o", o=1)), "par", writes=[Bp])
        P.dma("sp", lambda e: e.dma_start(out=dtb[:], in_=dt_bias.partition_broadcast(128)), "par", writes=[Bp])
        P.dma("sp", lambda e: e.dma_start(out=nexpA[:], in_=a_log.partition_broadcast(128)), "par", writes=[Bp])
        P.dma("sp", lambda e: e.dma_start(out=gfin[:], in_=g_final.partition_broadcast(128)), "par", writes=[Bp])
        act(nexpA[:], nexpA[:], AF.Exp, [Bp], [Bp])
        P.op("act", lambda e: e.mul(nexpA[:], nexpA[:], -1.0), [Bp], [Bp])
        pt, pb, _ = psum()
        tr(pt[:, 0:40], pstage[:, :], identf[:40, :40], [Bp, Bc], [pb])
        cp("dve", pv[:], pt[:, 0:40], [pb], [Bp])
        pt, pb, _ = psum()
        for m in range(8):
            tr(pt[:, m * 31:(m + 1) * 31], xt[0][0:31, m * 128:(m + 1) * 128], identf[:31, :31], [Bxt[0], Bc], [pb])
        cp("dve", wdw[:].rearrange("p m i -> p (m i)"), pt[:, 0:248], [pb], [Bp])
        pt, pb, _ = psum()
        for m in range(24):
            tr(pt[:, m * 4:(m + 1) * 4], xt[1 + m // 8][0:4, (m % 8) * 128:(m % 8 + 1) * 128], identf[:4, :4], [Bxt[1 + m // 8], Bc], [pb])
        cp("dve", wsh[:].rearrange("p m i -> p (m i)"), pt[:, 0:96], [pb], [Bp])
        gmix = pv[:, 0:8]; gmlp = pv[:, 8:16]; bdw = pv[:, 16:24]; lng = pv[:, 24:32]; lnb = pv[:, 32:40]

        P.ck("params")
        scr4 = sb([128, 2 * D], BF16, "scr4"); Bscr = Buf("scr4")
        hb = scr4[:, 0:D]; Bhb = Bscr
        junk = scr4[:, D:2 * D]; Bjunk = Bscr
        junk2 = sb([128, 128], BF16); Bjunk2 = Buf("junk2")
        ssx = sb([128, 4], F32); Bssx = Buf("ssx")
        hT = sb([128, 8, MAXC], BF16, "hT"); BhT = Buf("hT")
        NG = 2
        gext = [sb([128, 30 + MAXC], BF16, "gext%d" % i) for i in range(NG)]; Bgext = [Buf("gext%d" % i) for i in range(NG)]
        ghalo = sb([128, 8, 30], BF16); Bghalo = Buf("ghalo")
        qext = [sb([128, 3 + MAXC], BF16) for _ in range(NG)]; Bqext = [Buf("qext%d" % i) for i in range(NG)]
        qhalo = sb([128, 24, 3], BF16); Bqhalo = Buf("qhalo")
        glut = sb([128, 8, 30], F32, "glut"); Bglut = Buf("glut")
        qt = sb([128, 24, 3], F32); Bqt = Buf("qt")
        arena = sb([128, 32 * MAXC], BF16, "arena")

        def arena_views(mc):
            return (arena[:, 0:24 * mc].rearrange("p (m c) -> p m c", c=mc),
                    arena[:, 24 * mc:32 * mc].rearrange("p (m c) -> p m c", c=mc),
                    arena[:, 0:32 * mc].rearrange("p (m c) -> p m c", c=mc))
        BqkvT = Buf("qkvT"); Bcpre = Buf("cpre"); BmixT = Buf("mixT"); BaT = Buf("aT")
        if with_sample:
            o0 = 2048
            gexS = arena[:, o0:o0 + 4352].rearrange("p (m s t) -> p m s t", m=8, s=16); BgexS = Buf("gexS")
            o0 += 4352
            qexS = arena[:, o0:o0 + 2688].rearrange("p (m s t) -> p m s t", m=24, s=16); BqexS = Buf("qexS")
            o0 += 2688
            glutS = arena[:, o0:o0 + 1024].bitcast(F32).rearrange("p (m c) -> p m c", m=8); BglutS = Buf("glutS")
            o0 += 1024
            qtS = arena[:, o0:o0 + 3072].bitcast(F32).rearrange("p (m c) -> p m c", m=24); BqtS = Buf("qtS")
            o0 += 3072
            assert o0 <= 32 * MAXC
        f4 = [sb([128, 512], F32) for _ in range(2)]; Bf4 = [Buf("f4_%d" % i) for i in range(2)]
        f4i = [0]

        def ftmp():
            i = f4i[0] % 2
            f4i[0] += 1
            return f4[i], Bf4[i]

        cs2 = sb([128, 2, 512], BF16); Bcs2 = Buf("cs2")
        lnst = [sb([128, MAXC], F32) for _ in range(2)]; Blnst = Buf("lnst")
        decT = sb([128, 8, 128], F32); BdecT = Buf("decT")
        decTs = sb([128, 8, 128], F32); BdecTs = Buf("decTs")
        gm = decTs; Bgm = BdecTs
        EB = sb([128, 8, 128], F32); BEB = Buf("EB")
        cT = sb([128, 8, MAXC], BF16, "cT"); BcT = Buf("cT")
        zs = [sb([128, D], BF16) for _ in range(NXT)]; Bzs = [Buf("zs%d" % i) for i in range(NXT)]
        ba = sb([128, NXT, 16], F32); Bba = Buf("ba")
        onT = sb([128, 8, MAXC], BF16, "onT"); BonT = Buf("onT")
        wbuf = [sb([128, 8, 512], BF16) for _ in range(NW)]; Bw = [Buf("w%d" % i) for i in range(NW)]
        dgr = sb([128, 31 * 128], BF16, "dgr")
        dg31 = dgr[:, :].rearrange("p (i c) -> p i c", i=31); Bdg31q = [Buf("dg31q%d" % i) for i in range(4)]
        sqT = scr4[:, :].rearrange("p (j c) -> p j c", j=16); BsqT = Bscr
        dg4 = [sb([128, 4, 128], BF16) for _ in range(2)]; Bdg4 = [Buf("dg4a"), Buf("dg4b")]
        sc = sb([128, 64], F32, "sc"); sc2 = sb([128, 64], F32, "sc2"); Bsc = Buf("sc")
        Wp = [sb([128, 8, 128], BF16) for _ in range(2)]; BWp = [Buf("Wp0"), Buf("Wp1")]
        Qp = [sb([128, 8, 128], BF16) for _ in range(2)]; BQp = [Buf("Qp0"), Buf("Qp1")]
        Zp = [sb([128, 8, 128], BF16) for _ in range(2)]; BZp = [Buf("Zp0"), Buf("Zp1")]
        attnT = sb([128, 8, 128], BF16); BattnT = Buf("attnT")
        rhsk = sb([128, 8, 128], BF16); rhsv = sb([128, 8, 128], BF16); kdec = sb([128, 8, 128], BF16)
        Brhsk = Buf("rhsk"); Brhsv = Buf("rhsv"); Bkdec = Buf("kdec")
        nkcT = sb([128, 8, 128], BF16); BnkcT = Buf("nkcT")
        u = sb([128, 8, 128], BF16); Bu = Buf("u")
        qd = sb([128, 8, 128], BF16); Bqd = Buf("qd")
        on = sb([128, D], BF16, "on"); Bon = Buf("on")
        S = sb([128, 8, 128], F32); BS = Buf("S")
        Sbf = sb([128, 8, 128], BF16); BSbf = Buf("Sbf")
        ms = sb([128, 16], F32, "ms"); Bms = Buf("ms")
        yt = [sb([128, D], F32)]; Byt = [Buf("yt0")]
        stgT = yt[0]; BstgT = Byt[0]
        if with_sample:
            v3 = lambda a: a[:, :].rearrange("p (h c) -> p h c", h=8)
            S0 = [v3(xt[1]), v3(xt[2]), v3(yt[0])]; BS0 = [Bxt[1], Bxt[2], Byt[0]]
            S0key = [("x", 1), ("x", 2), ("yt", 0)]
            So = [v3(xt[3]), v3(xt[4])]; BSo = [Bxt[3], Bxt[4]]
            S0b = [v3(zs[1]), v3(zs[2])]; BS0b = [Bzs[1], Bzs[2]]
            um = [zs[3], zs[4]]; Bum = [Bzs[3], Bzs[4]]
            oTs = sb([128, 8, 64], F32, "oTs"); BoTs = Buf("oTs")
            uTs = sb([128, 8, 64], BF16, "uTs"); BuTs = Buf("uTs")
            stg = yt[0]; Bstg = Byt[0]
        Bout = {k: Buf("o_" + k) for k in ["y_p", "y_s", "p_conv", "p_qkv", "p_delta", "s_conv", "s_qkv", "s_delta"]}

        P.op("pool", lambda e: e.memset(ghalo[:], 0.0), [], [Bghalo])
        P.op("pool", lambda e: e.memset(qhalo[:], 0.0), [], [Bqhalo])
        P.op("pool", lambda e: e.memset(S[:], 0.0), [], [BS])
        P.op("pool", lambda e: e.memset(Sbf[:], 0.0), [], [BSbf])

        sched = wsched()
        allreq = sched * len(groups)
        wstate = {"next_load": 0, "next_use": 0}

        NT = len(sched)
        wsc = nc.dram_tensor("wsc", [NT, 128, 4096], BF16, kind="Internal").ap()
        Bwsc = [Buf("wsc%d" % j) for j in range(NT)]
        Bring = [Buf("ring%d" % j) for j in range(8)]
        pstate = {"next": 0}
        PLOOK = 6

        def wdims(j):
            kind, a, b = sched[j]
            if kind == "down":
                return w_down[a * 1024:(a + 1) * 1024, b * 512:(b + 1) * 512], 512
            return wmap[kind][:, a:a + b], b

        def pro_issue(upto):
            while pstate["next"] < min(NT, upto):
                j = pstate["next"]
                src, n = wdims(j)
                P.dma("pool", lambda e, j=j, src=src, n=n: e.dma_start(
                    out=wsc[j][:, 0:8 * n].rearrange("p (kc n) -> p kc n", n=n),
                    in_=src.rearrange("(kc p) n -> p kc n", p=128)), ("pro", j % 8), writes=[Bwsc[j], Bring[j % 8]])
                pstate["next"] += 1

        def w_issue(i):
            j = i % NT
            slot = i % NW
            pro_issue(j + PLOOK)
            _, n = wdims(j)
            P.dma("sp", lambda e, j=j, n=n, slot=slot: e.dma_start(
                out=wbuf[slot][:, :, 0:n], in_=wsc[j][:, 0:8 * n].rearrange("p (kc n) -> p kc n", n=n)),
                ("w", slot), reads=[Bwsc[j]], writes=[Bw[slot]])

        def wnext(tag):
            i = wstate["next_use"]
            assert allreq[i] == tag, (allreq[i], tag)
            while wstate["next_load"] < min(len(allreq), i + NW - 1):
                w_issue(wstate["next_load"])
                wstate["next_load"] += 1
            wstate["next_use"] += 1
            return wbuf[i % NW], Bw[i % NW]

        def wprefetch():
            i = wstate["next_use"]
            while wstate["next_load"] < min(len(allreq), i + NW - 1):
                w_issue(wstate["next_load"])
                wstate["next_load"] += 1

        def fm(wt, Bwt, mloc, src, Bsrc, blkc, KC=8, kc0=0, pool=None):
            c0, c1 = blkc
            pt, pb, _ = psum(pool)
            for kc in range(KC):
                mm(pt[:, 0:c1 - c0], wt[:, kc, mloc * 128:(mloc + 1) * 128], src[:, kc0 + kc, c0:c1], kc == 0, kc == KC - 1,
                   [Bwt, Bsrc], [pb])
            return pt, pb

        def norm_to_T(G, gvec):
            for t in G["tiles"]:
                n = t.n
                x = xt[t.xi]; Bx = Bxt[t.xi]
                act(junk[:n, :], x[:n, :], AF.Square, [Bx], [Bjunk, Bssx], accum_out=ssx[:n, 0:1])
                act(ssx[:n, 1:2], ssx[:n, 0:1], AF.Sqrt, [Bssx], [Bssx], scale=1.0 / D, bias=EPS)
                P.op("dve", lambda e, n=n: e.reciprocal(out=ssx[:n, 2:3], in_=ssx[:n, 1:2]), [Bssx], [Bssx])
                ts("dve", hb[:n, :], x[:n, :], ssx[:n, 2:3], None, ALU.mult, None, [Bx, Bssx], [Bhb])
                pt, pb, _ = psum()
                pv_ = bf(pt).rearrange("p (k c) -> p k c", k=8)
                for kc in range(8):
                    tr(pv_[:, kc, 0:n], hb[:n, kc * 128:(kc + 1) * 128], ident[:n, :n], [Bhb, Bc], [pb])
                tt("dve", hT[:, :, t.col:t.col + n], pv_[:, :, 0:n], gvec.unsqueeze(2).broadcast_to([128, 8, n]), ALU.mult,
                   [pb, Bp], [BhT])

        def tail_out(src, Bsrc, nm, ncols, nrows, dst_fn, key):
            for m0 in range(0, nm, 4):
                pt, pb, _ = psum()
                for mm_ in range(4):
                    tr(pt[:ncols, mm_ * 128:(mm_ + 1) * 128], src[:, m0 + mm_, :], identf[:, :], [Bsrc, Bc], [pb])
                cp("dve", stgT[:ncols, (m0 % 8) * 128:(m0 % 8 + 4) * 128], pt[:ncols, :], [pb], [BstgT])
                if (m0 + 4) % 8 == 0:
                    dst_fn(m0 // 8)

        def emit_sample_tails():
            def sconv_out(k):
                for s_ in range(16):
                    P.dma("sp", lambda e, s_=s_: e.dma_start(out=o_sconv[s_, 26:30, :], in_=stgT[4 * s_:4 * s_ + 4, :]), ("yt", 0),
                          reads=[BstgT])
            tail_out(glutS, BglutS, 8, 64, 64, sconv_out, None)

            def sqkv_out(k):
                for s_ in range(16):
                    P.dma("sp", lambda e, s_=s_, k=k: e.dma_start(out=o_sqkv[s_, :, k * 1024:(k + 1) * 1024], in_=stgT[4 * s_ + 1:4 * s_ + 4, :]), ("yt", 0),
                          reads=[BstgT])
            tail_out(qtS, BqtS, 24, 64, 64, sqkv_out, None)

        xloaded = set()

        def load_x(gi_, t):
            if (gi_, t.xi) in xloaded:
                return
            xloaded.add((gi_, t.xi))
            x = xt[t.xi]; Bx = Bxt[t.xi]
            if t.kind == "S":
                P.dma("sp", lambda e, x=x: e.dma_start(out=x[0:64, :], in_=x_s[:, :]), ("x", t.xi), writes=[Bx])
            elif t.t0 == 0:
                P.dma("sp", lambda e, x=x: e.dma_start(out=x[0:16, :], in_=meta[:, :]), ("x", t.xi), writes=[Bx])
                P.dma("sp", lambda e, x=x: e.dma_start(out=x[16:128, :], in_=x_p[0:112, :]), ("x", t.xi), writes=[Bx])
            else:
                P.dma("sp", lambda e, x=x, t=t: e.dma_start(out=x[0:t.n, :], in_=x_p[t.t0 - 16:t.t0 - 16 + t.n, :]), ("x", t.xi), writes=[Bx])

        for gi, G in enumerate(groups):
            tiles = G["tiles"]; blocks = G["blocks"]; pc0 = G["pc0"]; npc = G["npc"]
            isSG = (tiles[0].kind == "S")
            sblock = (0, 64) if isSG else None
            qkvT, cpre, aT = arena_views(64 if isSG else MAXC)
            mixT = cpre
            for t in tiles:
                load_x(gi, t)
            if isSG:
                P.handover([BaT, BqkvT, BmixT, Bcpre], [BgexS, BqexS, BglutS, BqtS, BaT])
                for sg in range(4):
                    P.dma("sp", lambda e, sg=sg: e.dma_start(out=stg[0:120, :], in_=st_conv[4 * sg:4 * sg + 4].rearrange("s t c -> (s t) c")),
                          ("yt", 0), writes=[Bstg])
                    pts = []
                    for half in range(2):
                        pt, pb, _ = psum()
                        for mm_ in range(4):
                            m = half * 4 + mm_
                            tr(pt[:, mm_ * 120:(mm_ + 1) * 120], stg[0:120, m * 128:(m + 1) * 128], identf[:120, :120], [Bstg, Bc], [pb])
                        act(gexS[:, half * 4:half * 4 + 4, 4 * sg:4 * sg + 4, 0:30],
                            pt[:, 0:480].rearrange("p (m s t) -> p m s t", m=4, s=4), AF.Copy, [pb], [BgexS])
                for hf in range(3):
                    P.dma("sp", lambda e, hf=hf: e.dma_start(out=stg[0:48, :], in_=st_qkv.rearrange("s t c -> (s t) c")[:, hf * 1024:(hf + 1) * 1024]),
                          ("yt", 0), writes=[Bstg])
                    for half in range(2):
                        pt, pb, _ = psum()
                        for mm_ in range(4):
                            m = half * 4 + mm_
                            tr(pt[:, mm_ * 48:(mm_ + 1) * 48], stg[0:48, m * 128:(m + 1) * 128], identf[:48, :48], [Bstg, Bc], [pb])
                        act(qexS[:, hf * 8 + half * 4:hf * 8 + half * 4 + 4, :, 0:3],
                            pt[:, 0:192].rearrange("p (m s t) -> p m s t", m=4, s=16), AF.Copy, [pb], [BqexS])
                P.dma("sp", lambda e: e.dma_start(out=o_sconv[:, 0:26, :], in_=st_conv[:, 4:30, :]), "o_s_conv")
            P.ck("st")
            wprefetch()
            norm_to_T(G, gmix)
            P.ck("p0")

            P.handover([BaT], [BqkvT, Bcpre])
            QT = [(0, 8), (8, 16), (16, 24), (24, 31)]
            wts = {}

            def build31(m):
                for qi, (i0, i1) in enumerate(QT):
                    tt("pool", dg31[:, i0:i1, :], ident[:].unsqueeze(1).broadcast_to([128, i1 - i0, 128]),
                       wdw[:, m, i0:i1].unsqueeze(2).broadcast_to([128, i1 - i0, 128]), ALU.mult, [Bc, Bp], [Bdg31q[qi]])

            def stageA(m):
                q, ml = m // 4, m % 4
                if ml == 0:
                    wts[q] = (wnext(("in", q * 512, 512)), wnext(("in", 1024 + q * 512, 512)))
                (wa, Bwa), (wb_, Bwb) = wts[q]
                gx = gext[m % NG]; Bgx = Bgext[m % NG]
                if npc:
                    cp("pool", gx[:, 0:30], ghalo[:, m, :], [Bghalo], [Bgx])
                for bk in blocks:
                    c0, c1 = bk; n = c1 - c0
                    pa, pba = fm(wa, Bwa, ml, hT, BhT, bk, pool="d")
                    yield
                    pb2, pbb = fm(wb_, Bwb, ml, hT, BhT, bk, pool="d")
                    sg_, Bsg = ftmp()
                    act(sg_[:, 0:n], pb2[:, 0:n], AF.Sigmoid, [pbb], [Bsg])
                    if bk == sblock:
                        tt("dve", gexS[:, m, :, 30:34], pa[:, 0:64].rearrange("p (s t) -> p s t", t=4),
                           sg_[:, 0:64].rearrange("p (s t) -> p s t", t=4), ALU.mult, [pba, Bsg], [BgexS])
                        tt("dve", glutS[:, m, :], pa[:, 0:64], sg_[:, 0:64], ALU.mult, [pba, Bsg], [BglutS])
                    else:
                        tt("dve", gx[:, 30 + c0 - pc0:30 + c1 - pc0], pa[:, 0:n], sg_[:, 0:n], ALU.mult, [pba, Bsg], [Bgx])
                        if G["last"] and c1 == G["ncol"]:
                            lo = max(c0, G["ncol"] - 30)
                            tt("dve", glut[:, m, 30 - (c1 - lo):30], pa[:, lo - c0:n], sg_[:, lo - c0:n], ALU.mult, [pba, Bsg], [Bglut])
                        elif G["last"] and c1 > G["ncol"] - 30:
                            lo = max(c0, G["ncol"] - 30)
                            off = lo - (G["ncol"] - 30)
                            tt("dve", glut[:, m, off:off + c1 - lo], pa[:, lo - c0:n], sg_[:, lo - c0:n], ALU.mult, [pba, Bsg], [Bglut])
                    yield
                if npc:
                    cp("pool", ghalo[:, m, :], gx[:, npc:npc + 30], [Bgx], [Bghalo])

            def stageB(m):
                gx = gext[m % NG]; Bgx = Bgext[m % NG]
                for bi, bk in enumerate(blocks):
                    c0, c1 = bk; n = c1 - c0
                    pt, pb, _ = psum("d")
                    for i in range(31):
                        Bq = Bdg31q[[qi for qi, (i0, i1) in enumerate(QT) if i0 <= i < i1][0]]
                        if bk == sblock:
                            mm(pt[:, 0:64].rearrange("p (s t) -> p s t", t=4), dg31[:, i, :], gexS[:, m, :, i:i + 4], i == 0, i == 30,
                               [Bq, BgexS], [pb])
                        else:
                            mm(pt[:, 0:n], dg31[:, i, :], gx[:, c0 - pc0 + i:c1 - pc0 + i], i == 0, i == 30, [Bq, Bgx], [pb])
                        if i in (7, 15, 23):
                            yield
                    act(cpre[:, m, c0:c1], pt[:, 0:n], AF.Identity, [pb], [Bcpre], bias=bdw[:, m:m + 1])
                    yield

            def conv_gen():
                yield from stageA(0)
                for m in range(8):
                    build31(m)
                    if m + 1 < 8:
                        yield from stageA(m + 1)
                    yield from stageB(m)
                for bk in blocks:
                    c0, c1 = bk; n = c1 - c0
                    if n > 64:
                        sA = psum("d"); sB = psum("d")
                    else:
                        sA = psum("d"); sB = None
                    for m in range(8):
                        act(cs2[:, 1, 0:n], cpre[:, m, c0:c1], AF.Square, [Bcpre], [Bcs2])
                        if sB is not None:
                            mm(sA[0][:, 0:n], onesb[:, :], cpre[:, m, c0:c1], m == 0, m == 7, [Bc, Bcpre], [sA[1]])
                            mm(sB[0][:, 0:n], onesb[:, :], cs2[:, 1, 0:n], m == 0, m == 7, [Bc, Bcs2], [sB[1]])
                        else:
                            cp("pool", cs2[:, 0, 0:n], cpre[:, m, c0:c1], [Bcpre], [Bcs2])
                            mm(sA[0][:, 0:2 * n].rearrange("p (a n) -> p a n", a=2), onesb[:, :], cs2[:, :, 0:n], m == 0, m == 7,
                               [Bc, Bcs2], [sA[1]])
                        yield
                    if sB is not None:
                        psum_s, Bs1 = sA[0][:, 0:n], sA[1]
                        psum_q, Bs2 = sB[0][:, 0:n], sB[1]
                    else:
                        psum_s, Bs1 = sA[0][:, 0:n], sA[1]
                        psum_q, Bs2 = sA[0][:, n:2 * n], sA[1]
                    mean, Bmean = ftmp(); var, Bvar = ftmp()
                    msq = lnst[1][:, c0:c1]
                    act(mean[:, 0:n], psum_s, AF.Copy, [Bs1], [Bmean], scale=1.0 / D)
                    tt("pool", msq, mean[:, 0:n], mean[:, 0:n], ALU.mult, [Bmean], [Blnst])
                    stt(var[:, 0:n], psum_q, 1.0 / D, msq, ALU.mult, ALU.subtract, [Bs2, Blnst], [Bvar])
                    act(var[:, 0:n], var[:, 0:n], AF.Sqrt, [Bvar], [Bvar], bias=EPS)
                    P.op("dve", lambda e, n=n, c0=c0, c1=c1, var=var: e.reciprocal(out=lnst[0][:, c0:c1], in_=var[:, 0:n]), [Bvar], [Blnst])
                    stt(lnst[1][:, c0:c1], mean[:, 0:n], -1.0, lnst[0][:, c0:c1], ALU.mult, ALU.mult, [Bmean, Blnst], [Blnst])
                    yield
                for m in range(8):
                    for bk in blocks:
                        c0, c1 = bk; n = c1 - c0
                        t1, Bt1 = ftmp(); t2, Bt2 = ftmp()
                        tt("dve", t1[:, 0:n], cpre[:, m, c0:c1], lnst[0][:, c0:c1], ALU.mult, [Bcpre, Blnst], [Bt1])
                        tt("pool", t2[:, 0:n], t1[:, 0:n], lnst[1][:, c0:c1], ALU.add, [Bt1, Blnst], [Bt2])
                        act(cT[:, m, c0:c1], t2[:, 0:n], AF.Silu, [Bt2, Bp], [BcT], scale=lng[:, m:m + 1], bias=lnb[:, m:m + 1])
                        yield
                P.handover([Bcpre], [BmixT])
                for q in range(2):
                    wc, Bwc = wnext(("cout", q * 512, 512))
                    wg, Bwg = wnext(("in", 6160 + q * 512, 512))
                    for ml in range(4):
                        m = q * 4 + ml
                        for bk in blocks:
                            c0, c1 = bk; n = c1 - c0
                            pc_, pbc = fm(wc, Bwc, ml, cT, BcT, bk, pool="d")
                            yield
                            pg, pbg = fm(wg, Bwg, ml, hT, BhT, bk, pool="d")
                            sg_, Bsg = ftmp()
                            act(sg_[:, 0:n], pg[:, 0:n], AF.Sigmoid, [pbg], [Bsg])
                            tt("dve", mixT[:, m, c0:c1], pc_[:, 0:n], sg_[:, 0:n], ALU.mult, [pbc, Bsg], [BmixT])
                            yield

            P.ck("p3")
            wq4 = {}

            def stage4A(m):
                j, ml = m // 4, m % 4
                if ml == 0:
                    wq4[j] = wnext(("in", 2048 + 512 * j, 512))
                wq, Bwq = wq4[j]
                qx = qext[m % NG]; Bqx = Bqext[m % NG]
                d4 = dg4[m % 2]; Bd4 = Bdg4[m % 2]
                tt("pool", d4[:], ident[:].unsqueeze(1).broadcast_to([128, 4, 128]),
                   wsh[:, m, :].unsqueeze(2).broadcast_to([128, 4, 128]), ALU.mult, [Bc, Bp], [Bd4])
                if npc:
                    cp("pool", qx[:, 0:3], qhalo[:, m, :], [Bqhalo], [Bqx])
                for bk in blocks:
                    c0, c1 = bk; n = c1 - c0
                    pq, pbq = fm(wq, Bwq, ml, hT, BhT, bk)
                    if bk == sblock:
                        act(qexS[:, m, :, 3:7], pq[:, 0:64].rearrange("p (s t) -> p s t", t=4), AF.Copy, [pbq], [BqexS])
                        act(qtS[:, m, :], pq[:, 0:64], AF.Copy, [pbq], [BqtS])
                    else:
                        act(qx[:, 3 + c0 - pc0:3 + c1 - pc0], pq[:, 0:n], AF.Copy, [pbq], [Bqx])
                        if G["last"] and c1 == G["ncol"]:
                            assert n >= 3
                            act(qt[:, m, :], pq[:, n - 3:n], AF.Copy, [pbq], [Bqt])
                if npc:
                    cp("pool", qhalo[:, m, :], qx[:, npc:npc + 3], [Bqx], [Bqhalo])

            def stage4B(m):
                qx = qext[m % NG]; Bqx = Bqext[m % NG]
                d4 = dg4[m % 2]; Bd4 = Bdg4[m % 2]
                for bk in blocks:
                    c0, c1 = bk; n = c1 - c0
                    pt, pb, _ = psum()
                    for i in range(4):
                        if bk == sblock:
                            mm(pt[:, 0:64].rearrange("p (s t) -> p s t", t=4), d4[:, i, :], qexS[:, m, :, i:i + 4], i == 0, i == 3,
                               [Bd4, BqexS], [pb])
                        else:
                            mm(pt[:, 0:n], d4[:, i, :], qx[:, c0 - pc0 + i:c1 - pc0 + i], i == 0, i == 3, [Bd4, Bqx], [pb])
                    act(qkvT[:, m, c0:c1], pt[:, 0:n], AF.Silu, [pb], [BqkvT])

            stage4A(0)
            for m in range(24):
                if m + 1 < 24:
                    stage4A(m + 1)
                stage4B(m)
            P.ck("p4")
            for nb in range(2):
                wz, Bwz = wnext(("in", 5120 + nb * 512, 512))
                for t in tiles:
                    pt, pb, _ = psum()
                    for kc in range(8):
                        mm(pt[:t.n, :], hT[:, kc, t.col:t.col + t.n], wz[:, kc, :], kc == 0, kc == 7, [BhT, Bwz], [pb])
                    act(zs[t.xi][:t.n, nb * 512:(nb + 1) * 512], pt[:t.n, :], AF.Silu, [pb], [Bzs[t.xi]])
            wba, Bwba = wnext(("in", 6144, 16))
            for t in tiles:
                pt, pb, _ = psum()
                for kc in range(8):
                    mm(pt[:t.n, 0:16], hT[:, kc, t.col:t.col + t.n], wba[:, kc, 0:16], kc == 0, kc == 7, [BhT, Bwba], [pb])
                act(ba[:t.n, t.xi, :], pt[:t.n, 0:16], AF.Copy, [pb], [Bba])
            wprefetch()

            P.ck("zba")
            cg = conv_gen()

            _fk = 1

            _fm = 2
            _fc = [0]

            def fill(k=1):
                _fc[0] += 1
                if _fc[0] % _fm:
                    return
                for _ in range(k * _fk):
                    next(cg, None)
            for t in tiles:
                n = t.n; col = t.col; isS = (t.kind == "S")
                tU = triUs if isS else triU
                tM = mmins if isS else mmin
                tO = offds if isS else offd
                tB = blk if isS else onesf
                qTt = lambda h: qkvT[:, h, col:col + n]
                kTt = lambda h: qkvT[:, 8 + h, col:col + n]
                vTt = lambda h: qkvT[:, 16 + h, col:col + n]
                tt("dve", sqT[:, :, 0:n], qkvT[:, 0:16, col:col + n], qkvT[:, 0:16, col:col + n], ALU.mult, [BqkvT], [BsqT])
                pt, pb, _ = psum("g")
                for j in range(16):
                    mm(pt[:n, j:j + 1], sqT[:, j, 0:n], onesb[:, 0:1], True, True, [BsqT, Bc], [pb])
                ts("dve", sc[:n, 0:16], pt[:n, 0:16], 1e-6, None, ALU.add, None, [pb], [Bsc])
                fill()
                act(sc[:n, 24:32], sc[:n, 8:16], AF.Ln, [Bsc], [Bsc])
                act(sc[:n, 16:24], sc[:n, 24:32], AF.Exp, [Bsc], [Bsc], scale=0.5)
                act(sc[:n, 24:32], sc[:n, 24:32], AF.Exp, [Bsc], [Bsc], scale=-0.5)
                act(sc[:n, 32:40], ba[:n, t.xi, 0:8], AF.Exp, [Bba], [Bsc], scale=-1.0)
                ts("dve", sc[:n, 32:40], sc[:n, 32:40], 1.0, None, ALU.add, None, [Bsc], [Bsc])
                P.op("dve", lambda e, n=n: e.reciprocal(out=sc[:n, 32:40], in_=sc[:n, 32:40]), [Bsc], [Bsc])
                tt("dve", sc[:n, 40:48], ba[:n, t.xi, 8:16], dtb[:n, :], ALU.add, [Bba, Bp], [Bsc])
                stt(sc[:n, 48:56], sc[:n, 40:48], -1.0, sc[:n, 40:48], ALU.mult, ALU.max, [Bsc], [Bsc])
                act(sc[:n, 48:56], sc[:n, 48:56], AF.Exp, [Bsc], [Bsc], scale=-1.0)
                act(sc[:n, 48:56], sc[:n, 48:56], AF.Ln, [Bsc], [Bsc], bias=1.0)
                stt(sc[:n, 40:48], sc[:n, 40:48], 0.0, sc[:n, 48:56], ALU.max, ALU.add, [Bsc], [Bsc])
                tt("dve", sc[:n, 56:64], sc[:n, 40:48], nexpA[:n, :], ALU.mult, [Bsc, Bp], [Bsc])
                tt("dve", sc2[:n, 0:8], sc[:n, 32:40], sc[:n, 24:32], ALU.mult, [Bsc], [Bsc])
                stt(sc2[:n, 8:16], sc2[:n, 0:8], -1.0, sc[:n, 24:32], ALU.mult, ALU.mult, [Bsc], [Bsc])
                pt, pb, _ = psum("g")
                mm(pt[:n, 0:8], tU[:n, :n], sc[:n, 56:64], True, True, [Bc, Bsc], [pb])
                mm(pt[:n, 8:16], tB[:n, :n], sc[:n, 56:64], True, True, [Bc, Bsc], [pb])
                cp("dve", sc2[:n, 16:24], pt[:n, 0:8], [pb], [Bsc])
                fill()
                act(sc2[:n, 24:32], pt[:n, 0:8], AF.Exp, [pb], [Bsc])
                tt("dve", sc2[:n, 32:40], pt[:n, 8:16], sc2[:n, 16:24], ALU.subtract, [pb, Bsc], [Bsc])
                act(sc2[:n, 32:40], sc2[:n, 32:40], AF.Exp, [Bsc], [Bsc])
                tt("dve", sc2[:n, 40:48], sc[:n, 24:32], sc2[:n, 32:40], ALU.mult, [Bsc], [Bsc])
                P.ck("g1")
                tt("dve", gm[:n, :, 0:n], tU[:n, :n].unsqueeze(1).broadcast_to([n, 8, n]),
                   sc[:n, 56:64].unsqueeze(2).broadcast_to([n, 8, n]), ALU.mult, [Bc, Bsc], [Bgm])
                gbs = []
                for hb_ in range(2):
                    pt, pb, _ = psum("g")
                    mm(pt[:, 0:4 * n].rearrange("p (h n) -> p h n", h=4), onesf[:n, :], gm[:n, hb_ * 4:hb_ * 4 + 4, 0:n], True, True,
                       [Bc, Bgm], [pb])
                    gbs.append((pt, pb))
                    fill()
                for h in range(8):
                    pt, pb = gbs[h // 4]
                    stt(decT[:n, h, 0:n], pt[:n, (h % 4) * n:(h % 4 + 1) * n], sc2[:n, 16 + h:17 + h], tM[:n, :n], ALU.subtract, ALU.min,
                        [pb, Bsc, Bc], [BdecT])
                for hb_ in range(2):
                    pt, pb = gbs[hb_]
                    act(EB[:, hb_ * 4:hb_ * 4 + 4, 0:n], pt[:, 0:4 * n].rearrange("p (h n) -> p h n", h=4), AF.Exp, [pb], [BEB])
                act(decT[:n, :, 0:n], decT[:n, :, 0:n], AF.Exp, [BdecT], [BdecT])
                tt("dve", decTs[:n, :, 0:n], decT[:n, :, 0:n], tO[:n, :n].unsqueeze(1).broadcast_to([n, 8, n]), ALU.mult,
                   [BdecT, Bc], [BdecTs])
                P.ck("g2")
                W0, BW0 = Wp[0], BWp[0]
                for hb_ in range(2):
                    pk, pbk, _ = psum("g")
                    pq, pbq, _ = psum("g")
                    for hh in range(4):
                        h = hb_ * 4 + hh
                        mm(pk[:n, hh * 128:hh * 128 + n], kTt(h), kTt(h), True, True, [BqkvT], [pbk])
                        mm(pq[:n, hh * 128:hh * 128 + n], kTt(h), qTt(h), True, True, [BqkvT], [pbq])
                    for hh in range(4):
                        h = hb_ * 4 + hh
                        stt(W0[:n, h, 0:n], pk[:n, hh * 128:hh * 128 + n], sc2[:n, 8 + h:9 + h], decTs[:n, h, 0:n], ALU.mult, ALU.mult,
                            [pbk, Bsc, BdecTs], [BW0])
                        stt(attnT[:n, h, 0:n], pq[:n, hh * 128:hh * 128 + n], sc[:n, 24 + h:25 + h], decT[:n, h, 0:n], ALU.mult, ALU.mult,
                            [pbq, Bsc, BdecT], [BattnT])
                tt("dve", Zp[0][:n, :, 0:n], W0[:n, :, 0:n], ident[:n, :n].unsqueeze(1).broadcast_to([n, 8, n]), ALU.add, [BW0, Bc], [BZp[0]])
                pt, pb, _ = psum("g")
                pvw = bf(pt).rearrange("p (h c) -> p h c", h=8)
                for h in range(8):
                    tr(pvw[:n, h, 0:n], W0[:n, h, 0:n], ident[:n, :n], [BW0, Bc], [pb])
                act(Qp[0][:n, :, 0:n], pvw[:n, :, 0:n], AF.Copy, [pb], [BQp[0]])
                fill()
                cw = cq = cz = 0
                for lev in range(1, t.nlev + 1):
                    nq = 1 - cq
                    for hb_ in range(2):
                        pt, pb, _ = psum("g")
                        for hh in range(4):
                            h = hb_ * 4 + hh
                            mm(pt[:n, hh * 128:hh * 128 + n], Wp[cw][:n, h, 0:n], Qp[cq][:n, h, 0:n], True, True, [BWp[cw], BQp[cq]], [pb])
                        act(Qp[nq][:n, hb_ * 4:hb_ * 4 + 4, 0:n], pt[:n, :].rearrange("p (h c) -> p h c", h=4)[:, :, 0:n], AF.Copy,
                            [pb], [BQp[nq]])
                        fill()
                    if lev < t.nlev:
                        nw = 1 - cw
                        for hb_ in range(2):
                            pt, pb, _ = psum("g")
                            for hh in range(4):
                                h = hb_ * 4 + hh
                                mm(pt[:n, hh * 128:hh * 128 + n], Qp[cq][:n, h, 0:n], Wp[cw][:n, h, 0:n], True, True, [BWp[cw], BQp[cq]], [pb])
                            act(Wp[nw][:n, hb_ * 4:hb_ * 4 + 4, 0:n], pt[:n, :].rearrange("p (h c) -> p h c", h=4)[:, :, 0:n], AF.Copy,
                                [pb], [BWp[nw]])
                            fill()
                        cw = nw
                    cq = nq
                    nz = 1 - cz
                    for hb_ in range(2):
                        pt, pb, _ = psum("g")
                        for hh in range(4):
                            h = hb_ * 4 + hh
                            mm(pt[:n, hh * 128:hh * 128 + n], Qp[cq][:n, h, 0:n], Zp[cz][:n, h, 0:n], True, True, [BQp[cq], BZp[cz]], [pb])
                        tt("dve", Zp[nz][:n, hb_ * 4:hb_ * 4 + 4, 0:n], pt[:n, :].rearrange("p (h c) -> p h c", h=4)[:, :, 0:n],
                           Zp[cz][:n, hb_ * 4:hb_ * 4 + 4, 0:n], ALU.add, [pb, BZp[cz]], [BZp[nz]])
                        fill()
                    cz = nz
                Z = Zp[cz]; BZ = BZp[cz]
                P.ck("g4")
                pk, pbk, _ = psum("g"); pvk = bf(pk).rearrange("p (h c) -> p h c", h=8)
                pv2, pbv, _ = psum("g"); pvv = bf(pv2).rearrange("p (h c) -> p h c", h=8)
                for h in range(8):
                    tr(pvk[:n, h, :], kTt(h), ident[:, :], [BqkvT, Bc], [pbk])
                    tr(pvv[:n, h, :], vTt(h), ident[:, :], [BqkvT, Bc], [pbv])
                P.ck("g4a")
                bc = lambda colap: colap.unsqueeze(2).broadcast_to([n, 8, 128])
                tt("dve", rhsk[:n, :, :], pvk[:n, :, :], bc(sc2[:n, 24:32]), ALU.mult, [pbk, Bsc], [Brhsk])
                tt("dve", kdec[:n, :, :], pvk[:n, :, :], bc(sc2[:n, 40:48]), ALU.mult, [pbk, Bsc], [Bkdec])
                tt("dve", rhsv[:n, :, :], pvv[:n, :, :], bc(sc[:n, 16:24]), ALU.mult, [pbv, Bsc], [Brhsv])
                fill()
                P.ck("g4b")
                tt("pool", qd[:, :, 0:n], qkvT[:, 0:8, col:col + n], EB[:, :, 0:n], ALU.mult, [BqkvT, BEB], [Bqd])
                P.ck("g5")
                for hb_ in range(2):
                    pt, pb, _ = psum("g")
                    for hh in range(4):
                        h = hb_ * 4 + hh
                        mm(pt[:, hh * 128:hh * 128 + n], rhsk[:n, h, :], Z[:n, h, 0:n], True, True, [Brhsk, BZ], [pb])
                    P.op("act", lambda e, pt=pt, hb_=hb_, n=n: e.mul(nkcT[:, hb_ * 4:hb_ * 4 + 4, 0:n],
                                                                     pt[:, :].rearrange("p (h c) -> p h c", h=4)[:, :, 0:n], -1.0), [pb], [BnkcT])
                opsum = []
                if not isS:
                    for hb_ in range(2):
                        pu, pbu, _ = psum("g")
                        for hh in range(4):
                            h = hb_ * 4 + hh
                            mm(pu[:n, hh * 128:(hh + 1) * 128], Z[:n, h, 0:n], rhsv[:n, h, :], True, False, [BZ, Brhsv], [pbu])
                            mm(pu[:n, hh * 128:(hh + 1) * 128], nkcT[:, h, 0:n], Sbf[:, h, :], False, True, [BnkcT, BSbf], [pbu])
                        tt("dve", u[:n, hb_ * 4:hb_ * 4 + 4, :], pu[:n, :].rearrange("p (h e) -> p h e", h=4),
                           sc2[:n, hb_ * 4:hb_ * 4 + 4].unsqueeze(2).broadcast_to([n, 4, 128]), ALU.mult, [pbu, Bsc], [Bu])
                        fill()
                    for hb_ in range(2):
                        po, pbo, _ = psum("g")
                        for hh in range(4):
                            h = hb_ * 4 + hh
                            mm(po[:n, hh * 128:(hh + 1) * 128], qd[:, h, 0:n], Sbf[:, h, :], True, False, [Bqd, BSbf], [pbo])
                            mm(po[:n, hh * 128:(hh + 1) * 128], attnT[:n, h, 0:n], u[:n, h, :], False, True, [BattnT, Bu], [pbo])
                        opsum.append((po, pbo))
                        fill()
                    for hb_ in range(2):
                        pS, pbS, _ = psum("g")
                        for hh in range(4):
                            h = hb_ * 4 + hh
                            mm(pS[:, hh * 128:(hh + 1) * 128], kdec[:n, h, :], u[:n, h, :], True, True, [Bkdec, Bu], [pbS])
                        for hh in range(4):
                            h = hb_ * 4 + hh
                            stt(S[:, h, :], S[:, h, :], EB[:, h, n - 1:n], pS[:, hh * 128:(hh + 1) * 128], ALU.mult, ALU.add,
                                [BS, BEB, pbS], [BS])
                        act(Sbf[:, hb_ * 4:hb_ * 4 + 4, :], S[:, hb_ * 4:hb_ * 4 + 4, :], AF.Copy, [BS], [BSbf])
                        fill()
                else:
                    puT, pbuT, _ = psum("g"); puTv = puT[:, :].rearrange("p (h c) -> p h c", h=8)
                    poT, pboT, _ = psum("g"); poTv = poT[:, :].rearrange("p (h c) -> p h c", h=8)
                    for h in range(8):
                        mm(puTv[:, h, :], rhsv[:n, h, :], Z[:n, h, 0:n], h == 0, False, [Brhsv, BZ], [pbuT], skip=True)
                    for s in range(16):
                        s0 = S0[s % 3]; Bs0 = BS0[s % 3]; s0b = S0b[s % 2]; Bs0b = BS0b[s % 2]
                        P.dma("sp", lambda e, s=s, s0=s0: e.dma_start(out=s0[:], in_=st_delta[s].rearrange("h d e -> d h e")), S0key[s % 3], writes=[Bs0])
                        act(s0b[:], s0[:], AF.Copy, [Bs0], [Bs0b])
                        for h in range(8):
                            mm(puTv[:, h, 4 * s:4 * s + 4], s0b[:, h, :], nkcT[:, h, 4 * s:4 * s + 4], False, False, [Bs0b, BnkcT], [pbuT], skip=True)
                            mm(poTv[:, h, 4 * s:4 * s + 4], s0b[:, h, :], qd[:, h, 4 * s:4 * s + 4], (s == 0 and h == 0), False, [Bs0b, Bqd], [pboT], skip=True)
                    act(uTs[:], puTv, AF.Copy, [pbuT], [BuTs])
                    pt, pb, _ = psum("g"); ptv = bf(pt).rearrange("p (h c) -> p h c", h=8)
                    for h in range(8):
                        tr(ptv[:n, h, :], uTs[:, h, :], ident[:, :], [BuTs, Bc], [pb])
                    for h in range(8):
                        ts("dve", u[:n, h, :], ptv[:n, h, :], sc2[:n, h:h + 1], None, ALU.mult, None, [pb, Bsc], [Bu])
                    for h in range(8):
                        mm(poTv[:, h, :], u[:n, h, :], attnT[:n, h, 0:n], False, h == 7, [Bu, BattnT], [pboT], skip=True)
                    act(oTs[:], poTv, AF.Copy, [pboT], [BoTs])
                    for hb_ in range(2):
                        po, pbo, _ = psum("g")
                        for hh in range(4):
                            tr(po[:n, hh * 128:(hh + 1) * 128], oTs[:, hb_ * 4 + hh, :], identf[:, :], [BoTs, Bc], [pbo])
                        opsum.append((po, pbo))
                        fill()
                    def sample_state_update(n=n):
                        for s in range(16):
                            s0 = S0[(s + 1) % 3]; Bs0 = BS0[(s + 1) % 3]
                            P.dma("sp", lambda e, s=s, s0=s0: e.dma_start(out=s0[:], in_=st_delta[s].rearrange("h d e -> d h e")), S0key[(s + 1) % 3], writes=[Bs0])
                            ums = um[s % 2]; Bums = Bum[s % 2]
                            act(ums[:n, :], u[:n, :, :].rearrange("p h e -> p (h e)"), AF.Identity, [Bu, Bc], [Bums], scale=bmask[:n, s:s + 1])
                            so = So[s % 2]; Bso = BSo[s % 2]
                            for hb_ in range(2):
                                pS, pbS, _ = psum("g")
                                for hh in range(4):
                                    h = hb_ * 4 + hh
                                    mm(pS[:, hh * 128:(hh + 1) * 128], kdec[:n, h, :], ums[:n, h * 128:(h + 1) * 128], True, True, [Bkdec, Bums], [pbS])
                                for hh in range(4):
                                    h = hb_ * 4 + hh
                                    stt(so[:, h, :], s0[:, h, :], EB[:, h, 4 * s + 3:4 * s + 4], pS[:, hh * 128:(hh + 1) * 128], ALU.mult, ALU.add,
                                        [Bs0, BEB, pbS], [Bso])
                            P.dma("pool", lambda e, s=s, so=so: e.dma_start(out=o_sdelta[s].rearrange("h d e -> d h e"), in_=so[:]), ("xp", 3 + s % 2),
                                  reads=[Bso])
                P.ck("g6")
                for h in range(8):
                    po, pbo = opsum[h // 4]
                    act(junk2[:n, :], po[:n, (h % 4) * 128:(h % 4 + 1) * 128], AF.Square, [pbo], [Bjunk2, Bms], accum_out=ms[:n, h:h + 1])
                ts("dve", ms[:n, 8:16], sc[:n, 0:8], 128.0 * EPS, None, ALU.mult, None, [Bsc], [Bms])
                stt(ms[:n, 8:16], ms[:n, 0:8], 1.0 / 128, ms[:n, 8:16], ALU.mult, ALU.add, [Bms], [Bms])
                act(ms[:n, 8:16], ms[:n, 8:16], AF.Ln, [Bms], [Bms])
                act(ms[:n, 0:8], ms[:n, 8:16], AF.Exp, [Bms], [Bms], scale=-0.5)
                for h in range(8):
                    po, pbo = opsum[h // 4]
                    stt(on[:n, h * 128:(h + 1) * 128], po[:n, (h % 4) * 128:(h % 4 + 1) * 128], ms[:n, h:h + 1], zs[t.xi][:n, h * 128:(h + 1) * 128],
                        ALU.mult, ALU.mult, [pbo, Bms, Bzs[t.xi]], [Bon])
                pt, pb, _ = psum("g"); ptv = bf(pt).rearrange("p (h c) -> p h c", h=8)
                for h in range(8):
                    tr(ptv[:, h, 0:n], on[:n, h * 128:(h + 1) * 128], ident[:n, :n], [Bon, Bc], [pb])
                ts("dve", onT[:, :, col:col + n], ptv[:, :, 0:n], ghead[:, 0:1], None, ALU.mult, None, [pb, Bp], [BonT])
                fill()
                P.ck("g7")
                if isS:
                    sample_state_update()
                P.ck("g8")

            for _ in cg:
                pass
            if isSG:
                emit_sample_tails()
            for q in range(2):
                wd, Bwd = wnext(("dout", q * 512, 512))
                wg, Bwg = wnext(("in", 7184 + q * 512, 512))
                for ml in range(4):
                    m = q * 4 + ml
                    for bk in blocks:
                        c0, c1 = bk; n = c1 - c0
                        pd_, pbd = fm(wd, Bwd, ml, onT, BonT, bk)
                        pg, pbg = fm(wg, Bwg, ml, hT, BhT, bk)
                        sg_, Bsg = ftmp(); t1, Bt1 = ftmp()
                        act(sg_[:, 0:n], pg[:, 0:n], AF.Sigmoid, [pbg], [Bsg])
                        tt("dve", t1[:, 0:n], pd_[:, 0:n], sg_[:, 0:n], ALU.mult, [pbd, Bsg], [Bt1])
                        tt("dve", mixT[:, m, c0:c1], mixT[:, m, c0:c1], t1[:, 0:n], ALU.add, [BmixT, Bt1], [BmixT])
            P.ck("p5")
            for nb in range(2):
                wo_, Bwo = wnext(("o", nb * 512, 512))
                for t in tiles:
                    pt, pb, _ = psum()
                    for kc in range(8):
                        mm(pt[:t.n, :], mixT[:, kc, t.col:t.col + t.n], wo_[:, kc, :], kc == 0, kc == 7, [BmixT, Bwo], [pb])
                    x = xt[t.xi]
                    tt("dve", x[:t.n, nb * 512:(nb + 1) * 512], pt[:t.n, :], x[:t.n, nb * 512:(nb + 1) * 512], ALU.add, [pb, Bxt[t.xi]], [Bxt[t.xi]])
            wprefetch()
            P.ck("p6")
            norm_to_T(G, gmlp)
            P.handover([BqkvT, BmixT, Bcpre], [BaT])
            for j in range(8):
                wu, Bwu = wnext(("up", j * 512, 512))
                for ml in range(4):
                    m = j * 4 + ml
                    for bk in blocks:
                        c0, c1 = bk; n = c1 - c0
                        pu, pbu = fm(wu, Bwu, ml, hT, BhT, bk)
                        r, Br = ftmp()
                        act(r[:, 0:n], pu[:, 0:n], AF.Relu, [pbu], [Br])
                        act(aT[:, m, c0:c1], r[:, 0:n], AF.Square, [Br], [BaT])
            P.ck("p8")
            for nb in range(2):
                banks = [psum() for _ in tiles]
                for kb in range(4):
                    wd, Bwd = wnext(("down", kb, nb))
                    for ti, t in enumerate(tiles):
                        pt, pb, _ = banks[ti]
                        for kc in range(8):
                            mm(pt[:t.n, :], aT[:, kb * 8 + kc, t.col:t.col + t.n], wd[:, kc, :], kb == 0 and kc == 0, kb == 3 and kc == 7,
                               [BaT, Bwd], [pb])
                for ti, t in enumerate(tiles):
                    pt, pb, _ = banks[ti]
                    x = xt[t.xi]
                    tt("dve", x[:t.n, nb * 512:(nb + 1) * 512], pt[:t.n, :], x[:t.n, nb * 512:(nb + 1) * 512], ALU.add, [pb, Bxt[t.xi]], [Bxt[t.xi]])
            P.ck("p9")
            for ti, t in enumerate(tiles):
                n = t.n; x = xt[t.xi]; Bx = Bxt[t.xi]
                y = yt[0]; By = Byt[0]
                act(junk[:n, :], x[:n, :], AF.Square, [Bx], [Bjunk, Bssx], accum_out=ssx[:n, 0:1])
                act(ssx[:n, 1:2], ssx[:n, 0:1], AF.Sqrt, [Bssx], [Bssx], scale=1.0 / D, bias=EPS)
                P.op("dve", lambda e, n=n: e.reciprocal(out=ssx[:n, 2:3], in_=ssx[:n, 1:2]), [Bssx], [Bssx])
                stt(y[:n, :], x[:n, :], ssx[:n, 2:3], gfin[:n, :], ALU.mult, ALU.mult, [Bx, Bssx, Bp], [By])
                if t.kind == "S":
                    P.dma("pool", lambda e, y=y: e.dma_start(out=y_s[:, :], in_=y[0:64, :]), ("ytp", 0), reads=[By])
                elif t.t0 == 0:
                    P.dma("pool", lambda e, y=y: e.dma_start(out=y_p[0:112, :], in_=y[16:128, :]), ("ytp", 0), reads=[By])
                else:
                    P.dma("pool", lambda e, y=y, t=t: e.dma_start(out=y_p[t.t0 - 16:t.t0 - 16 + t.n, :], in_=y[0:t.n, :]), ("ytp", 0), reads=[By])
                if gi + 1 < len(groups):
                    for t2 in groups[gi + 1]["tiles"]:
                        if t2.xi == t.xi:
                            load_x(gi + 1, t2)

        if not max_groups:
            tail_out(glut, Bglut, 8, 30, 30,
                     lambda k: P.dma("sp", lambda e: e.dma_start(out=o_pconv[:, :], in_=stgT[0:30, :]), ("yt", 0), reads=[BstgT]), None)
            tail_out(qt, Bqt, 24, 3, 3,
                     lambda k: P.dma("sp", lambda e, k=k: e.dma_start(out=o_pqkv[:, k * 1024:(k + 1) * 1024], in_=stgT[0:3, :]), ("yt", 0), reads=[BstgT]), None)
            P.dma("sp", lambda e: e.dma_start(out=o_pdelta.rearrange("h d e -> d h e"), in_=S[:]), "o_p_delta", reads=[BS])
        P.ops["sp"].append({"fn": None, "deps": {("dma", k): v for k, v in P.dma_counts.items()}, "dma": None})
        P.emit()
    return nc


_CACHE = {}


def kernel(x_prompt, x_sample, state_conv, state_qkv_conv, state_delta, meta_tokens, g_mix, w_in,
           w_dw, b_dw, ln_g, ln_b, w_cout, w_short, a_log, dt_bias, g_head, w_dout, w_o, g_mlp,
           w_up, w_down, g_final):
    f = lambda a: np.ascontiguousarray(np.asarray(a, dtype=np.float32))
    if "nc" not in _CACHE:
        _CACHE["nc"] = build()
    nc = _CACHE["nc"]
    shared = dict(meta=f(meta_tokens), g_mix=f(g_mix[0]), w_in=f(w_in[0]), w_dw=f(w_dw[0]), b_dw=f(b_dw[0]), ln_g=f(ln_g[0]),
                  ln_b=f(ln_b[0]), w_cout=f(w_cout[0]), w_short=f(w_short[0]), a_log=f(a_log[0]), dt_bias=f(dt_bias[0]),
                  g_head=f(g_head[0]), w_dout=f(w_dout[0]), w_o=f(w_o[0]), g_mlp=f(g_mlp[0]), w_up=f(w_up[0]),
                  w_down=f(w_down[0]), g_final=f(g_final))
    xp = f(x_prompt); xs = f(x_sample); sc_ = f(state_conv); sq_ = f(state_qkv_conv); sd_ = f(state_delta)
    in_maps = []
    for c in range(8):
        d = dict(shared)
        d["x_p"] = xp[c]
        d["x_s"] = xs[16 * c:16 * c + 16].reshape(64, D)
        d["st_conv"] = sc_[0, 16 * c:16 * c + 16]
        d["st_qkv"] = sq_[0, 16 * c:16 * c + 16]
        d["st_delta"] = sd_[0, 16 * c:16 * c + 16]
        in_maps.append(d)
    res = run_bass_kernel_spmd(nc, in_maps, core_ids=list(range(8)))
    R = res.results
    y_prompt = np.stack([R[c]["y_p"] for c in range(8)]).astype(np.float32)
    y_sample = np.concatenate([R[c]["y_s"].reshape(16, 4, D) for c in range(8)]).astype(np.float32)
    p_conv = np.stack([R[c]["p_conv"] for c in range(8)])[None].astype(np.float32)
    p_qkv = np.stack([R[c]["p_qkv"] for c in range(8)])[None].astype(np.float32)
    p_delta = np.stack([R[c]["p_delta"] for c in range(8)])[None].astype(np.float32)
    s_conv = np.concatenate([R[c]["s_conv"] for c in range(8)])[None].astype(np.float32)
    s_qkv = np.concatenate([R[c]["s_qkv"] for c in range(8)])[None].astype(np.float32)
    s_delta = np.concatenate([R[c]["s_delta"] for c in range(8)])[None].astype(np.float32)
    return (y_prompt, y_sample, p_conv, p_qkv, p_delta, s_conv, s_qkv, s_delta)
```

```python
import numpy as np
from contextlib import ExitStack
import concourse.bass as bass
import concourse.mybir as mybir
from concourse.bass_utils import run_bass_kernel_spmd

F32 = mybir.dt.float32
BF16 = mybir.dt.bfloat16
AF = mybir.ActivationFunctionType
ALU = mybir.AluOpType

ENGS = ("pe", "act", "dve", "pool", "sp")
SAME_ENGINE_SYNC = {"pe": False, "act": True, "dve": True, "pool": True, "sp": False}

D = 1024
DIN = 8208
DFF = 4096
NPT = 2064
EPS = 1e-6
NEG = -30000.0


class Buf:
    __slots__ = ("name", "lw", "rd", "excl")

    def __init__(self, name, excl=False):
        self.name = name
        self.lw = None
        self.rd = {}
        self.excl = excl


class Prog:
    def __init__(self, nc):
        self.nc = nc
        self.ops = {e: [] for e in ENGS}
        self.dma_counts = {}
        self.dma_keys = []
        self.enabled = True
        self.stop_at = None

    def ck(self, name):
        if self.stop_at is not None and name == self.stop_at:
            self.enabled = False

    def _collect(self, eng, reads, writes):
        deps = {}

        def add(k, v):
            if k[0] == "eng" and k[1] == eng and not SAME_ENGINE_SYNC[eng]:
                return
            if k not in deps or deps[k] < v:
                deps[k] = v

        for b in reads:
            if b.lw is not None:
                add((b.lw[0], b.lw[1]), b.lw[2])
            if b.excl:
                for k, v in b.rd.items():
                    if k != ("eng", eng):
                        add(k, v)
        for b in writes:
            if b.lw is not None:
                add((b.lw[0], b.lw[1]), b.lw[2])
            for k, v in b.rd.items():
                add(k, v)
        return deps

    def _commit(self, tok, reads, writes):
        k = (tok[0], tok[1])
        for b in reads:
            if b.rd.get(k, -1) < tok[2]:
                b.rd[k] = tok[2]
        for b in writes:
            b.lw = tok
            b.rd = {}

    def op(self, eng, fn, reads=(), writes=()):
        if not self.enabled:
            return
        deps = self._collect(eng, reads, writes)
        idx = len(self.ops[eng])
        self.ops[eng].append({"fn": fn, "deps": deps, "dma": None})
        self._commit(("eng", eng, idx), reads, writes)

    def dma(self, eng, fn, semkey, reads=(), writes=()):
        if not self.enabled:
            return
        deps = self._collect(eng, reads, writes)
        if semkey not in self.dma_counts:
            self.dma_counts[semkey] = 0
            self.dma_keys.append(semkey)
        self.dma_counts[semkey] += 16
        cnt = self.dma_counts[semkey]
        self.ops[eng].append({"fn": fn, "deps": deps, "dma": semkey})
        self._commit(("dma", semkey, cnt), reads, writes)

    def wait_all(self, eng, bufs):
        deps = self._collect(eng, bufs, ())
        self.ops[eng].append({"fn": None, "deps": deps, "dma": None})

    def handover(self, old, new):
        merged = {}
        for b in old:
            if b.lw is not None:
                k = (b.lw[0], b.lw[1])
                merged[k] = max(merged.get(k, -1), b.lw[2])
            for k, v in b.rd.items():
                merged[k] = max(merged.get(k, -1), v)
        for b in new:
            b.lw = None
            b.rd = dict(merged)

    def emit(self):
        nc = self.nc
        sig = {e: set() for e in ENGS}
        for e in ENGS:
            for o in self.ops[e]:
                for (kind, key), v in o["deps"].items():
                    if kind == "eng":
                        sig[key].add(v)
        cum = {}
        for e in ENGS:
            c = 0
            m = {}
            for i in range(len(self.ops[e])):
                if i in sig[e]:
                    c += 1
                    m[i] = c
            cum[e] = m
        with ExitStack() as st:
            esem = {e: st.enter_context(nc.semaphore("s_" + e)) for e in ENGS}
            dsem = {k: st.enter_context(nc.semaphore("d_%d" % i)) for i, k in enumerate(self.dma_keys)}
            block = st.enter_context(nc.Block())
            engobj = {"pe": block.tensor, "act": block.scalar, "dve": block.vector,
                      "pool": block.gpsimd, "sp": block.sync}

            def make(ename):
                ops = self.ops[ename]

                def body(eng):
                    seen = {}
                    for i, o in enumerate(ops):
                        for (kind, key), v in o["deps"].items():
                            if kind == "eng":
                                val = cum[key][v]
                                s = esem[key]
                            else:
                                val = v
                                s = dsem[key]
                            if seen.get((kind, key), 0) >= val:
                                continue
                            seen[(kind, key)] = val
                            eng.wait_ge(s, val)
                        if o["fn"] is None:
                            continue
                        inst = o["fn"](eng)
                        if o["dma"] is not None:
                            inst.then_inc(dsem[o["dma"]], 16)
                        elif i in cum[ename]:
                            inst.then_inc(esem[ename], 1)
                return body

            for e in ENGS:
                if self.ops[e]:
                    engobj[e](make(e))


class TileD:
    def __init__(self, kind, n, t0, col, xi):
        self.kind, self.n, self.t0, self.col, self.xi = kind, n, t0, col, xi
        self.nlev = {128: 6, 16: 3, 64: 1}[n] if kind == "P" else 1


def make_groups(with_sample=True):
    groups = []
    ptiles = [(i * 128, min(128, NPT - i * 128)) for i in range(17)]
    split = [ptiles[0:4], ptiles[4:8], ptiles[8:12], ptiles[12:17]]
    for gi, pts in enumerate(split):
        tiles = []
        col = 0
        for xi, (t0, n) in enumerate(pts):
            tiles.append(TileD("P", n, t0, col, xi))
            col += n
        blocks = []
        c = 0
        while c < col:
            blocks.append((c, min(col, c + 512)))
            c += 512
        groups.append(dict(tiles=tiles, pc0=0, npc=col, ncol=col, blocks=blocks, last=(gi == 3), first=False))
    if with_sample:
        groups.append(dict(tiles=[TileD("S", 64, 0, 0, 0)], pc0=64, npc=0, ncol=64, blocks=[(0, 64)], last=False, first=True))
    return groups


def wsched():
    L = []
    for j in range(6):
        L.append(("in", 2048 + 512 * j, 512))
    L += [("in", 5120, 512), ("in", 5632, 512), ("in", 6144, 16)]
    for q in range(2):
        L.append(("in", q * 512, 512))
        L.append(("in", 1024 + q * 512, 512))
    for q in range(2):
        L += [("cout", q * 512, 512), ("in", 6160 + q * 512, 512)]
    for q in range(2):
        L += [("dout", q * 512, 512), ("in", 7184 + q * 512, 512)]
    L += [("o", 0, 512), ("o", 512, 512)]
    for j in range(8):
        L.append(("up", j * 512, 512))
    for nb in range(2):
        for kb in range(4):
            L.append(("down", kb, nb))
    return L


def build(with_sample=True, NW=3, debug=False, max_groups=None, stop_at=None):
    nc = bass.Bass("TRN2", target_bir_lowering=False)
    din = lambda name, shape: nc.dram_tensor(name, shape, F32, kind="ExternalInput").ap()
    dout = lambda name, shape: nc.dram_tensor(name, shape, F32, kind="ExternalOutput").ap()
    x_p = din("x_p", [2048, D]); x_s = din("x_s", [64, D])
    st_conv = din("st_conv", [16, 30, D]); st_qkv = din("st_qkv", [16, 3, 3072]); st_delta = din("st_delta", [16, 8, 128, 128])
    meta = din("meta", [16, D])
    g_mix = din("g_mix", [D]); w_in = din("w_in", [D, DIN]); w_dw = din("w_dw", [31, D]); b_dw = din("b_dw", [D])
    ln_g = din("ln_g", [D]); ln_b = din("ln_b", [D]); w_cout = din("w_cout", [D, D]); w_short = din("w_short", [4, 3072])
    a_log = din("a_log", [8]); dt_bias = din("dt_bias", [8]); g_head = din("g_head", [128]); w_dout = din("w_dout", [D, D])
    w_o = din("w_o", [D, D]); g_mlp = din("g_mlp", [D]); w_up = din("w_up", [D, DFF]); w_down = din("w_down", [DFF, D])
    g_final = din("g_final", [D])
    y_p = dout("y_p", [2048, D]); y_s = dout("y_s", [64, D])
    o_pconv = dout("p_conv", [30, D]); o_pqkv = dout("p_qkv", [3, 3072]); o_pdelta = dout("p_delta", [8, 128, 128])
    o_sconv = dout("s_conv", [16, 30, D]); o_sqkv = dout("s_qkv", [16, 3, 3072]); o_sdelta = dout("s_delta", [16, 8, 128, 128])
    wmap = {"in": w_in, "cout": w_cout, "dout": w_dout, "o": w_o, "up": w_up}

    P = Prog(nc)
    P.stop_at = stop_at
    groups = make_groups(with_sample)
    if max_groups:
        groups = groups[:max_groups]
    MAXC = 528
    MAXPC = max(g["npc"] for g in groups)

    with ExitStack() as st:
        st.enter_context(nc.allow_non_contiguous_dma(reason="small params"))
        cnt = [0]

        def sb(shape, dt, name=None):
            cnt[0] += 1
            return st.enter_context(nc.sbuf_tensor(name or ("t%d" % cnt[0]), shape, dt))

        psT = [st.enter_context(nc.psum_tensor("ps%d" % i, [128, 512], F32)) for i in range(8)]
        psB = [Buf("ps%d" % i, excl=True) for i in range(8)]
        ps_reserved = set()
        ps_ctr = [0]

        pool_ctr = {"d": 0, "g": 0}

        def psum(pool=None):
            if pool == "d":
                i = pool_ctr["d"] % 4
                pool_ctr["d"] += 1
                return psT[i], psB[i], i
            if pool == "g":
                i = 4 + pool_ctr["g"] % 4
                pool_ctr["g"] += 1
                return psT[i], psB[i], i
            while True:
                i = ps_ctr[0] % 8
                ps_ctr[0] += 1
                if i not in ps_reserved:
                    return psT[i], psB[i], i

        def bf(ap):
            return ap[:].bitcast(BF16)

        def mm(out, lhsT, rhs, start, stop, reads, writes, skip=False):
            if skip:
                P.op("pe", lambda e: e.matmul(out, lhsT=lhsT, rhs=rhs, start=start, stop=stop, skip_group_check=True), reads, writes)
            else:
                P.op("pe", lambda e: e.matmul(out, lhsT=lhsT, rhs=rhs, start=start, stop=stop), reads, writes)

        def tr(out, in_, idn, reads, writes):
            P.op("pe", lambda e: e.transpose(out=out, in_=in_, identity=idn), reads, writes)

        def act(out, in_, func, reads, writes, **kw):
            P.op("act", lambda e: e.activation(out=out, in_=in_, func=func, **kw), reads, writes)

        def ts(eng, out, in0, s1, s2, op0, op1, reads, writes):
            if op1 is None:
                P.op(eng, lambda e: e.tensor_scalar(out=out, in0=in0, scalar1=s1, scalar2=None, op0=op0), reads, writes)
            else:
                P.op(eng, lambda e: e.tensor_scalar(out=out, in0=in0, scalar1=s1, scalar2=s2, op0=op0, op1=op1), reads, writes)

        def tt(eng, out, in0, in1, op, reads, writes):
            P.op(eng, lambda e: e.tensor_tensor(out=out, in0=in0, in1=in1, op=op), reads, writes)

        def stt(out, in0, scalar, in1, op0, op1, reads, writes):
            P.op("dve", lambda e: e.scalar_tensor_tensor(out=out, in0=in0, scalar=scalar, in1=in1, op0=op0, op1=op1), reads, writes)

        def cp(eng, out, in_, reads, writes):
            P.op(eng, lambda e: e.tensor_copy(out=out, in_=in_), reads, writes)

        Bc = Buf("consts")
        identf = sb([128, 128], F32); ident = sb([128, 128], BF16)
        onesf = sb([128, 128], F32); onesb = sb([128, 128], BF16)
        triU = sb([128, 128], F32); mmin = sb([128, 128], F32); offd = sb([128, 128], F32)
        blk = sb([64, 64], F32); triUs = sb([64, 64], F32); mmins = sb([64, 64], F32); offds = sb([64, 64], F32)
        bmask = sb([64, 16], F32)
        tmpc = sb([64, 64], F32)

        def pool(fn):
            P.op("pool", fn, reads=[Bc], writes=[Bc])

        pool(lambda e: e.memset(identf[:], 0.0))
        pool(lambda e: e.affine_select(out=identf[:], in_=identf[:], pattern=[[-1, 128]], compare_op=ALU.not_equal,
                                       fill=1.0, base=0, channel_multiplier=1))
        pool(lambda e: e.tensor_copy(out=ident[:], in_=identf[:]))
        pool(lambda e: e.memset(onesf[:], 1.0))
        pool(lambda e: e.memset(onesb[:], 1.0))
        pool(lambda e: e.affine_select(out=triU[:], in_=onesf[:], pattern=[[1, 128]], compare_op=ALU.is_ge,
                                       fill=0.0, base=0, channel_multiplier=-1))
        pool(lambda e: e.memset(mmin[:], 0.0))
        pool(lambda e: e.affine_select(out=mmin[:], in_=mmin[:], pattern=[[1, 128]], compare_op=ALU.is_ge,
                                       fill=NEG, base=0, channel_multiplier=-1))
        pool(lambda e: e.affine_select(out=offd[:], in_=onesf[:], pattern=[[1, 128]], compare_op=ALU.not_equal,
                                       fill=0.0, base=0, channel_multiplier=-1))
        pool(lambda e: e.affine_select(out=blk[:], in_=onesf[:64, :64], pattern=[[-4, 16], [0, 4]], compare_op=ALU.is_ge,
                                       fill=0.0, base=0, channel_multiplier=1))
        pool(lambda e: e.affine_select(out=blk[:], in_=blk[:], pattern=[[4, 16], [0, 4]], compare_op=ALU.is_ge,
                                       fill=0.0, base=3, channel_multiplier=-1))
        pool(lambda e: e.tensor_tensor(out=triUs[:], in0=triU[:64, :64], in1=blk[:], op=ALU.mult))
        pool(lambda e: e.tensor_tensor(out=offds[:], in0=offd[:64, :64], in1=blk[:], op=ALU.mult))
        pool(lambda e: e.tensor_scalar(out=tmpc[:], in0=blk[:], scalar1=-NEG, scalar2=NEG, op0=ALU.mult, op1=ALU.add))
        pool(lambda e: e.tensor_tensor(out=mmins[:], in0=mmin[:64, :64], in1=blk[:], op=ALU.mult))
        pool(lambda e: e.tensor_tensor(out=mmins[:], in0=mmins[:], in1=tmpc[:], op=ALU.add))
        pool(lambda e: e.affine_select(out=bmask[:], in_=onesf[:64, :16], pattern=[[-4, 16]], compare_op=ALU.is_ge,
                                       fill=0.0, base=0, channel_multiplier=1))
        pool(lambda e: e.affine_select(out=bmask[:], in_=bmask[:], pattern=[[4, 16]], compare_op=ALU.is_ge,
                                       fill=0.0, base=3, channel_multiplier=-1))

        NXT = 5
        xt = [sb([128, D], F32, "xt%d" % i) for i in range(NXT)]; Bxt = [Buf("xt%d" % i) for i in range(NXT)]
        Bp = Buf("params")
        pstage = sb([40, 128], F32)
        pv = sb([128, 40], F32)
        wdw = sb([128, 8, 31], F32)
        wsh = sb([128, 24, 4], F32)
        ghead = sb([128, 1], F32)
        dtb = sb([128, 8], F32); nexpA = sb([128, 8], F32)
        gfin = sb([128, D], F32)
        for i, v in enumerate([g_mix, g_mlp, b_dw, ln_g, ln_b]):
            P.dma("sp", lambda e, i=i, v=v: e.dma_start(out=pstage[8 * i:8 * i + 8, :], in_=v.rearrange("(k p) -> k p", p=128)), "par", writes=[Bp])
        P.dma("sp", lambda e: e.dma_start(out=xt[0][0:31, :], in_=w_dw[:, :]), ("x", 0), writes=[Bxt[0]])
        for k in range(3):
            P.dma("sp", lambda e, k=k: e.dma_start(out=xt[1 + k][0:4, :], in_=w_short[:, k * 1024:(k + 1) * 1024]), ("x", 1 + k), writes=[Bxt[1 + k]])
        P.dma("sp", lambda e: e.dma_start(out=ghead[:], in_=g_head.rearrange("(p o) -> p o", o=1)), "par", writes=[Bp])
        P.dma("sp", lambda e: e.dma_start(out=dtb[:], in_=dt_bias.partition_broadcast(128)), "par", writes=[Bp])
        P.dma("sp", lambda e: e.dma_start(out=nexpA[:], in_=a_log.partition_broadcast(128)), "par", writes=[Bp])
        P.dma("sp", lambda e: e.dma_start(out=gfin[:], in_=g_final.partition_broadcast(128)), "par", writes=[Bp])
        act(nexpA[:], nexpA[:], AF.Exp, [Bp], [Bp])
        P.op("act", lambda e: e.mul(nexpA[:], nexpA[:], -1.0), [Bp], [Bp])
        pt, pb, _ = psum()
        tr(pt[:, 0:40], pstage[:, :], identf[:40, :40], [Bp, Bc], [pb])
        cp("dve", pv[:], pt[:, 0:40], [pb], [Bp])
        pt, pb, _ = psum()
        for m in range(8):
            tr(pt[:, m * 31:(m + 1) * 31], xt[0][0:31, m * 128:(m + 1) * 128], identf[:31, :31], [Bxt[0], Bc], [pb])
        cp("dve", wdw[:].rearrange("p m i -> p (m i)"), pt[:, 0:248], [pb], [Bp])
        pt, pb, _ = psum()
        for m in range(24):
            tr(pt[:, m * 4:(m + 1) * 4], xt[1 + m // 8][0:4, (m % 8) * 128:(m % 8 + 1) * 128], identf[:4, :4], [Bxt[1 + m // 8], Bc], [pb])
        cp("dve", wsh[:].rearrange("p m i -> p (m i)"), pt[:, 0:96], [pb], [Bp])
        gmix = pv[:, 0:8]; gmlp = pv[:, 8:16]; bdw = pv[:, 16:24]; lng = pv[:, 24:32]; lnb = pv[:, 32:40]

        P.ck("params")
        scr4 = sb([128, 2 * D], BF16, "scr4"); Bscr = Buf("scr4")
        hb = scr4[:, 0:D]; Bhb = Bscr
        junk = scr4[:, D:2 * D]; Bjunk = Bscr
        junk2 = sb([128, 128], BF16); Bjunk2 = Buf("junk2")
        ssx = sb([128, 4], F32); Bssx = Buf("ssx")
        hT = sb([128, 8, MAXC], BF16, "hT"); BhT = Buf("hT")
        NG = 2
        gext = [sb([128, 30 + MAXC], BF16, "gext%d" % i) for i in range(NG)]; Bgext = [Buf("gext%d" % i) for i in range(NG)]
        ghalo = sb([128, 8, 30], BF16); Bghalo = Buf("ghalo")
        qext = [sb([128, 3 + MAXC], BF16) for _ in range(NG)]; Bqext = [Buf("qext%d" % i) for i in range(NG)]
        qhalo = sb([128, 24, 3], BF16); Bqhalo = Buf("qhalo")
        glut = sb([128, 8, 30], F32, "glut"); Bglut = Buf("glut")
        qt = sb([128, 24, 3], F32); Bqt = Buf("qt")
        arena = sb([128, 32 * MAXC], BF16, "arena")

        def arena_views(mc):
            return (arena[:, 0:24 * mc].rearrange("p (m c) -> p m c", c=mc),
                    arena[:, 24 * mc:32 * mc].rearrange("p (m c) -> p m c", c=mc),
                    arena[:, 0:32 * mc].rearrange("p (m c) -> p m c", c=mc))
        BqkvT = Buf("qkvT"); Bcpre = Buf("cpre"); BmixT = Buf("mixT"); BaT = Buf("aT")
        if with_sample:
            o0 = 2048
            gexS = arena[:, o0:o0 + 4352].rearrange("p (m s t) -> p m s t", m=8, s=16); BgexS = Buf("gexS")
            o0 += 4352
            qexS = arena[:, o0:o0 + 2688].rearrange("p (m s t) -> p m s t", m=24, s=16); BqexS = Buf("qexS")
            o0 += 2688
            glutS = arena[:, o0:o0 + 1024].bitcast(F32).rearrange("p (m c) -> p m c", m=8); BglutS = Buf("glutS")
            o0 += 1024
            qtS = arena[:, o0:o0 + 3072].bitcast(F32).rearrange("p (m c) -> p m c", m=24); BqtS = Buf("qtS")
            o0 += 3072
            assert o0 <= 32 * MAXC
        f4 = [sb([128, 512], F32) for _ in range(2)]; Bf4 = [Buf("f4_%d" % i) for i in range(2)]
        f4i = [0]

        def ftmp():
            i = f4i[0] % 2
            f4i[0] += 1
            return f4[i], Bf4[i]

        cs2 = sb([128, 2, 512], BF16); Bcs2 = Buf("cs2")
        lnst = [sb([128, MAXC], F32) for _ in range(2)]; Blnst = Buf("lnst")
        decT = sb([128, 8, 128], F32); BdecT = Buf("decT")
        decTs = sb([128, 8, 128], F32); BdecTs = Buf("decTs")
        gm = decTs; Bgm = BdecTs
        EB = sb([128, 8, 128], F32); BEB = Buf("EB")
        cT = sb([128, 8, MAXC], BF16, "cT"); BcT = Buf("cT")
        zs = [sb([128, D], BF16) for _ in range(NXT)]; Bzs = [Buf("zs%d" % i) for i in range(NXT)]
        ba = sb([128, NXT, 16], F32); Bba = Buf("ba")
        onT = sb([128, 8, MAXC], BF16, "onT"); BonT = Buf("onT")
        wbuf = [sb([128, 8, 512], BF16) for _ in range(NW)]; Bw = [Buf("w%d" % i) for i in range(NW)]
        dgr = sb([128, 31 * 128], BF16, "dgr")
        dg31 = dgr[:, :].rearrange("p (i c) -> p i c", i=31); Bdg31q = [Buf("dg31q%d" % i) for i in range(4)]
        sqT = scr4[:, :].rearrange("p (j c) -> p j c", j=16); BsqT = Bscr
        dg4 = [sb([128, 4, 128], BF16) for _ in range(2)]; Bdg4 = [Buf("dg4a"), Buf("dg4b")]
        sc = sb([128, 64], F32, "sc"); sc2 = sb([128, 64], F32, "sc2"); Bsc = Buf("sc")
        Wp = [sb([128, 8, 128], BF16) for _ in range(2)]; BWp = [Buf("Wp0"), Buf("Wp1")]
        Qp = [sb([128, 8, 128], BF16) for _ in range(2)]; BQp = [Buf("Qp0"), Buf("Qp1")]
        Zp = [sb([128, 8, 128], BF16) for _ in range(2)]; BZp = [Buf("Zp0"), Buf("Zp1")]
        attnT = sb([128, 8, 128], BF16); BattnT = Buf("attnT")
        rhsk = sb([128, 8, 128], BF16); rhsv = sb([128, 8, 128], BF16); kdec = sb([128, 8, 128], BF16)
        Brhsk = Buf("rhsk"); Brhsv = Buf("rhsv"); Bkdec = Buf("kdec")
        nkcT = sb([128, 8, 128], BF16); BnkcT = Buf("nkcT")
        u = sb([128, 8, 128], BF16); Bu = Buf("u")
        qd = sb([128, 8, 128], BF16); Bqd = Buf("qd")
        on = sb([128, D], BF16, "on"); Bon = Buf("on")
        S = sb([128, 8, 128], F32); BS = Buf("S")
        Sbf = sb([128, 8, 128], BF16); BSbf = Buf("Sbf")
        ms = sb([128, 16], F32, "ms"); Bms = Buf("ms")
        yt = [sb([128, D], F32)]; Byt = [Buf("yt0")]
        stgT = yt[0]; BstgT = Byt[0]
        if with_sample:
            v3 = lambda a: a[:, :].rearrange("p (h c) -> p h c", h=8)
            S0 = [v3(xt[1]), v3(xt[2]), v3(yt[0])]; BS0 = [Bxt[1], Bxt[2], Byt[0]]
            S0key = [("x", 1), ("x", 2), ("yt", 0)]
            So = [v3(xt[3]), v3(xt[4])]; BSo = [Bxt[3], Bxt[4]]
            S0b = [v3(zs[1]), v3(zs[2])]; BS0b = [Bzs[1], Bzs[2]]
            um = [zs[3], zs[4]]; Bum = [Bzs[3], Bzs[4]]
            oTs = sb([128, 8, 64], F32, "oTs"); BoTs = Buf("oTs")
            uTs = sb([128, 8, 64], BF16, "uTs"); BuTs = Buf("uTs")
            stg = yt[0]; Bstg = Byt[0]
        Bout = {k: Buf("o_" + k) for k in ["y_p", "y_s", "p_conv", "p_qkv", "p_delta", "s_conv", "s_qkv", "s_delta"]}

        P.op("pool", lambda e: e.memset(ghalo[:], 0.0), [], [Bghalo])
        P.op("pool", lambda e: e.memset(qhalo[:], 0.0), [], [Bqhalo])
        P.op("pool", lambda e: e.memset(S[:], 0.0), [], [BS])
        P.op("pool", lambda e: e.memset(Sbf[:], 0.0), [], [BSbf])

        sched = wsched()
        allreq = sched * len(groups)
        wstate = {"next_load": 0, "next_use": 0}

        NT = len(sched)
        wsc = nc.dram_tensor("wsc", [NT, 128, 4096], BF16, kind="Internal").ap()
        Bwsc = [Buf("wsc%d" % j) for j in range(NT)]
        Bring = [Buf("ring%d" % j) for j in range(8)]
        pstate = {"next": 0}
        PLOOK = 6

        def wdims(j):
            kind, a, b = sched[j]
            if kind == "down":
                return w_down[a * 1024:(a + 1) * 1024, b * 512:(b + 1) * 512], 512
            return wmap[kind][:, a:a + b], b

        def pro_issue(upto):
            while pstate["next"] < min(NT, upto):
                j = pstate["next"]
                src, n = wdims(j)
                P.dma("pool", lambda e, j=j, src=src, n=n: e.dma_start(
                    out=wsc[j][:, 0:8 * n].rearrange("p (kc n) -> p kc n", n=n),
                    in_=src.rearrange("(kc p) n -> p kc n", p=128)), ("pro", j % 8), writes=[Bwsc[j], Bring[j % 8]])
                pstate["next"] += 1

        def w_issue(i):
            j = i % NT
            slot = i % NW
            pro_issue(j + PLOOK)
            _, n = wdims(j)
            P.dma("sp", lambda e, j=j, n=n, slot=slot: e.dma_start(
                out=wbuf[slot][:, :, 0:n], in_=wsc[j][:, 0:8 * n].rearrange("p (kc n) -> p kc n", n=n)),
                ("w", slot), reads=[Bwsc[j]], writes=[Bw[slot]])

        def wnext(tag):
            i = wstate["next_use"]
            assert allreq[i] == tag, (allreq[i], tag)
            while wstate["next_load"] < min(len(allreq), i + NW - 1):
                w_issue(wstate["next_load"])
                wstate["next_load"] += 1
            wstate["next_use"] += 1
            return wbuf[i % NW], Bw[i % NW]

        def wprefetch():
            i = wstate["next_use"]
            while wstate["next_load"] < min(len(allreq), i + NW - 1):
                w_issue(wstate["next_load"])
                wstate["next_load"] += 1

        def fm(wt, Bwt, mloc, src, Bsrc, blkc, KC=8, kc0=0, pool=None):
            c0, c1 = blkc
            pt, pb, _ = psum(pool)
            for kc in range(KC):
                mm(pt[:, 0:c1 - c0], wt[:, kc, mloc * 128:(mloc + 1) * 128], src[:, kc0 + kc, c0:c1], kc == 0, kc == KC - 1,
                   [Bwt, Bsrc], [pb])
            return pt, pb

        def norm_to_T(G, gvec):
            for t in G["tiles"]:
                n = t.n
                x = xt[t.xi]; Bx = Bxt[t.xi]
                act(junk[:n, :], x[:n, :], AF.Square, [Bx], [Bjunk, Bssx], accum_out=ssx[:n, 0:1])
                act(ssx[:n, 1:2], ssx[:n, 0:1], AF.Sqrt, [Bssx], [Bssx], scale=1.0 / D, bias=EPS)
                P.op("dve", lambda e, n=n: e.reciprocal(out=ssx[:n, 2:3], in_=ssx[:n, 1:2]), [Bssx], [Bssx])
                ts("dve", hb[:n, :], x[:n, :], ssx[:n, 2:3], None, ALU.mult, None, [Bx, Bssx], [Bhb])
                pt, pb, _ = psum()
                pv_ = bf(pt).rearrange("p (k c) -> p k c", k=8)
                for kc in range(8):
                    tr(pv_[:, kc, 0:n], hb[:n, kc * 128:(kc + 1) * 128], ident[:n, :n], [Bhb, Bc], [pb])
                tt("dve", hT[:, :, t.col:t.col + n], pv_[:, :, 0:n], gvec.unsqueeze(2).broadcast_to([128, 8, n]), ALU.mult,
                   [pb, Bp], [BhT])

        def tail_out(src, Bsrc, nm, ncols, nrows, dst_fn, key):
            for m0 in range(0, nm, 4):
                pt, pb, _ = psum()
                for mm_ in range(4):
                    tr(pt[:ncols, mm_ * 128:(mm_ + 1) * 128], src[:, m0 + mm_, :], identf[:, :], [Bsrc, Bc], [pb])
                cp("dve", stgT[:ncols, (m0 % 8) * 128:(m0 % 8 + 4) * 128], pt[:ncols, :], [pb], [BstgT])
                if (m0 + 4) % 8 == 0:
                    dst_fn(m0 // 8)

        def emit_sample_tails():
            def sconv_out(k):
                for s_ in range(16):
                    P.dma("sp", lambda e, s_=s_: e.dma_start(out=o_sconv[s_, 26:30, :], in_=stgT[4 * s_:4 * s_ + 4, :]), ("yt", 0),
                          reads=[BstgT])
            tail_out(glutS, BglutS, 8, 64, 64, sconv_out, None)

            def sqkv_out(k):
                for s_ in range(16):
                    P.dma("sp", lambda e, s_=s_, k=k: e.dma_start(out=o_sqkv[s_, :, k * 1024:(k + 1) * 1024], in_=stgT[4 * s_ + 1:4 * s_ + 4, :]), ("yt", 0),
                          reads=[BstgT])
            tail_out(qtS, BqtS, 24, 64, 64, sqkv_out, None)

        xloaded = set()

        def load_x(gi_, t):
            if (gi_, t.xi) in xloaded:
                return
            xloaded.add((gi_, t.xi))
            x = xt[t.xi]; Bx = Bxt[t.xi]
            if t.kind == "S":
                P.dma("sp", lambda e, x=x: e.dma_start(out=x[0:64, :], in_=x_s[:, :]), ("x", t.xi), writes=[Bx])
            elif t.t0 == 0:
                P.dma("sp", lambda e, x=x: e.dma_start(out=x[0:16, :], in_=meta[:, :]), ("x", t.xi), writes=[Bx])
                P.dma("sp", lambda e, x=x: e.dma_start(out=x[16:128, :], in_=x_p[0:112, :]), ("x", t.xi), writes=[Bx])
            else:
                P.dma("sp", lambda e, x=x, t=t: e.dma_start(out=x[0:t.n, :], in_=x_p[t.t0 - 16:t.t0 - 16 + t.n, :]), ("x", t.xi), writes=[Bx])

        for gi, G in enumerate(groups):
            tiles = G["tiles"]; blocks = G["blocks"]; pc0 = G["pc0"]; npc = G["npc"]
            isSG = (tiles[0].kind == "S")
            sblock = (0, 64) if isSG else None
            qkvT, cpre, aT = arena_views(64 if isSG else MAXC)
            mixT = cpre
            for t in tiles:
                load_x(gi, t)
            if isSG:
                P.handover([BaT, BqkvT, BmixT, Bcpre], [BgexS, BqexS, BglutS, BqtS, BaT])
                for sg in range(4):
                    P.dma("sp", lambda e, sg=sg: e.dma_start(out=stg[0:120, :], in_=st_conv[4 * sg:4 * sg + 4].rearrange("s t c -> (s t) c")),
                          ("yt", 0), writes=[Bstg])
                    pts = []
                    for half in range(2):
                        pt, pb, _ = psum()
                        for mm_ in range(4):
                            m = half * 4 + mm_
                            tr(pt[:, mm_ * 120:(mm_ + 1) * 120], stg[0:120, m * 128:(m + 1) * 128], identf[:120, :120], [Bstg, Bc], [pb])
                        act(gexS[:, half * 4:half * 4 + 4, 4 * sg:4 * sg + 4, 0:30],
                            pt[:, 0:480].rearrange("p (m s t) -> p m s t", m=4, s=4), AF.Copy, [pb], [BgexS])
                for hf in range(3):
                    P.dma("sp", lambda e, hf=hf: e.dma_start(out=stg[0:48, :], in_=st_qkv.rearrange("s t c -> (s t) c")[:, hf * 1024:(hf + 1) * 1024]),
                          ("yt", 0), writes=[Bstg])
                    for half in range(2):
                        pt, pb, _ = psum()
                        for mm_ in range(4):
                            m = half * 4 + mm_
                            tr(pt[:, mm_ * 48:(mm_ + 1) * 48], stg[0:48, m * 128:(m + 1) * 128], identf[:48, :48], [Bstg, Bc], [pb])
                        act(qexS[:, hf * 8 + half * 4:hf * 8 + half * 4 + 4, :, 0:3],
                            pt[:, 0:192].rearrange("p (m s t) -> p m s t", m=4, s=16), AF.Copy, [pb], [BqexS])
                P.dma("sp", lambda e: e.dma_start(out=o_sconv[:, 0:26, :], in_=st_conv[:, 4:30, :]), "o_s_conv")
            P.ck("st")
            wprefetch()
            norm_to_T(G, gmix)
            P.ck("p0")

            P.handover([BaT], [BqkvT, Bcpre])
            QT = [(0, 8), (8, 16), (16, 24), (24, 31)]
            wts = {}

            def build31(m):
                for qi, (i0, i1) in enumerate(QT):
                    tt("pool", dg31[:, i0:i1, :], ident[:].unsqueeze(1).broadcast_to([128, i1 - i0, 128]),
                       wdw[:, m, i0:i1].unsqueeze(2).broadcast_to([128, i1 - i0, 128]), ALU.mult, [Bc, Bp], [Bdg31q[qi]])

            def stageA(m):
                q, ml = m // 4, m % 4
                if ml == 0:
                    wts[q] = (wnext(("in", q * 512, 512)), wnext(("in", 1024 + q * 512, 512)))
                (wa, Bwa), (wb_, Bwb) = wts[q]
                gx = gext[m % NG]; Bgx = Bgext[m % NG]
                if npc:
                    cp("pool", gx[:, 0:30], ghalo[:, m, :], [Bghalo], [Bgx])
                for bk in blocks:
                    c0, c1 = bk; n = c1 - c0
                    pa, pba = fm(wa, Bwa, ml, hT, BhT, bk, pool="d")
                    yield
                    pb2, pbb = fm(wb_, Bwb, ml, hT, BhT, bk, pool="d")
                    sg_, Bsg = ftmp()
                    act(sg_[:, 0:n], pb2[:, 0:n], AF.Sigmoid, [pbb], [Bsg])
                    if bk == sblock:
                        tt("dve", gexS[:, m, :, 30:34], pa[:, 0:64].rearrange("p (s t) -> p s t", t=4),
                           sg_[:, 0:64].rearrange("p (s t) -> p s t", t=4), ALU.mult, [pba, Bsg], [BgexS])
                        tt("dve", glutS[:, m, :], pa[:, 0:64], sg_[:, 0:64], ALU.mult, [pba, Bsg], [BglutS])
                    else:
                        tt("dve", gx[:, 30 + c0 - pc0:30 + c1 - pc0], pa[:, 0:n], sg_[:, 0:n], ALU.mult, [pba, Bsg], [Bgx])
                        if G["last"] and c1 == G["ncol"]:
                            lo = max(c0, G["ncol"] - 30)
                            tt("dve", glut[:, m, 30 - (c1 - lo):30], pa[:, lo - c0:n], sg_[:, lo - c0:n], ALU.mult, [pba, Bsg], [Bglut])
                        elif G["last"] and c1 > G["ncol"] - 30:
                            lo = max(c0, G["ncol"] - 30)
                            off = lo - (G["ncol"] - 30)
                            tt("dve", glut[:, m, off:off + c1 - lo], pa[:, lo - c0:n], sg_[:, lo - c0:n], ALU.mult, [pba, Bsg], [Bglut])
                    yield
                if npc:
                    cp("pool", ghalo[:, m, :], gx[:, npc:npc + 30], [Bgx], [Bghalo])

            def stageB(m):
                gx = gext[m % NG]; Bgx = Bgext[m % NG]
                for bi, bk in enumerate(blocks):
                    c0, c1 = bk; n = c1 - c0
                    pt, pb, _ = psum("d")
                    for i in range(31):
                        Bq = Bdg31q[[qi for qi, (i0, i1) in enumerate(QT) if i0 <= i < i1][0]]
                        if bk == sblock:
                            mm(pt[:, 0:64].rearrange("p (s t) -> p s t", t=4), dg31[:, i, :], gexS[:, m, :, i:i + 4], i == 0, i == 30,
                               [Bq, BgexS], [pb])
                        else:
                            mm(pt[:, 0:n], dg31[:, i, :], gx[:, c0 - pc0 + i:c1 - pc0 + i], i == 0, i == 30, [Bq, Bgx], [pb])
                        if i in (7, 15, 23):
                            yield
                    act(cpre[:, m, c0:c1], pt[:, 0:n], AF.Identity, [pb], [Bcpre], bias=bdw[:, m:m + 1])
                    yield

            def conv_gen():
                yield from stageA(0)
                for m in range(8):
                    build31(m)
                    if m + 1 < 8:
                        yield from stageA(m + 1)
                    yield from stageB(m)
                for bk in blocks:
                    c0, c1 = bk; n = c1 - c0
                    if n > 64:
                        sA = psum("d"); sB = psum("d")
                    else:
                        sA = psum("d"); sB = None
                    for m in range(8):
                        act(cs2[:, 1, 0:n], cpre[:, m, c0:c1], AF.Square, [Bcpre], [Bcs2])
                        if sB is not None:
                            mm(sA[0][:, 0:n], onesb[:, :], cpre[:, m, c0:c1], m == 0, m == 7, [Bc, Bcpre], [sA[1]])
                            mm(sB[0][:, 0:n], onesb[:, :], cs2[:, 1, 0:n], m == 0, m == 7, [Bc, Bcs2], [sB[1]])
                        else:
                            cp("pool", cs2[:, 0, 0:n], cpre[:, m, c0:c1], [Bcpre], [Bcs2])
                            mm(sA[0][:, 0:2 * n].rearrange("p (a n) -> p a n", a=2), onesb[:, :], cs2[:, :, 0:n], m == 0, m == 7,
                               [Bc, Bcs2], [sA[1]])
                        yield
                    if sB is not None:
                        psum_s, Bs1 = sA[0][:, 0:n], sA[1]
                        psum_q, Bs2 = sB[0][:, 0:n], sB[1]
                    else:
                        psum_s, Bs1 = sA[0][:, 0:n], sA[1]
                        psum_q, Bs2 = sA[0][:, n:2 * n], sA[1]
                    mean, Bmean = ftmp(); var, Bvar = ftmp()
                    msq = lnst[1][:, c0:c1]
                    act(mean[:, 0:n], psum_s, AF.Copy, [Bs1], [Bmean], scale=1.0 / D)
                    tt("pool", msq, mean[:, 0:n], mean[:, 0:n], ALU.mult, [Bmean], [Blnst])
                    stt(var[:, 0:n], psum_q, 1.0 / D, msq, ALU.mult, ALU.subtract, [Bs2, Blnst], [Bvar])
                    act(var[:, 0:n], var[:, 0:n], AF.Ln, [Bvar], [Bvar], bias=EPS)
                    act(lnst[0][:, c0:c1], var[:, 0:n], AF.Exp, [Bvar], [Blnst], scale=-0.5)
                    stt(lnst[1][:, c0:c1], mean[:, 0:n], -1.0, lnst[0][:, c0:c1], ALU.mult, ALU.mult, [Bmean, Blnst], [Blnst])
                    yield
                for m in range(8):
                    for bk in blocks:
                        c0, c1 = bk; n = c1 - c0
                        t1, Bt1 = ftmp(); t2, Bt2 = ftmp()
                        tt("dve", t1[:, 0:n], cpre[:, m, c0:c1], lnst[0][:, c0:c1], ALU.mult, [Bcpre, Blnst], [Bt1])
                        tt("pool", t2[:, 0:n], t1[:, 0:n], lnst[1][:, c0:c1], ALU.add, [Bt1, Blnst], [Bt2])
                        act(cT[:, m, c0:c1], t2[:, 0:n], AF.Silu, [Bt2, Bp], [BcT], scale=lng[:, m:m + 1], bias=lnb[:, m:m + 1])
                        yield
                P.handover([Bcpre], [BmixT])
                for q in range(2):
                    wc, Bwc = wnext(("cout", q * 512, 512))
                    wg, Bwg = wnext(("in", 6160 + q * 512, 512))
                    for ml in range(4):
                        m = q * 4 + ml
                        for bk in blocks:
                            c0, c1 = bk; n = c1 - c0
                            pc_, pbc = fm(wc, Bwc, ml, cT, BcT, bk, pool="d")
                            yield
                            pg, pbg = fm(wg, Bwg, ml, hT, BhT, bk, pool="d")
                            sg_, Bsg = ftmp()
                            act(sg_[:, 0:n], pg[:, 0:n], AF.Sigmoid, [pbg], [Bsg])
                            tt("dve", mixT[:, m, c0:c1], pc_[:, 0:n], sg_[:, 0:n], ALU.mult, [pbc, Bsg], [BmixT])
                            yield

            P.ck("p3")
            wq4 = {}

            def stage4A(m):
                j, ml = m // 4, m % 4
                if ml == 0:
                    wq4[j] = wnext(("in", 2048 + 512 * j, 512))
                wq, Bwq = wq4[j]
                qx = qext[m % NG]; Bqx = Bqext[m % NG]
                d4 = dg4[m % 2]; Bd4 = Bdg4[m % 2]
                tt("pool", d4[:], ident[:].unsqueeze(1).broadcast_to([128, 4, 128]),
                   wsh[:, m, :].unsqueeze(2).broadcast_to([128, 4, 128]), ALU.mult, [Bc, Bp], [Bd4])
                if npc:
                    cp("pool", qx[:, 0:3], qhalo[:, m, :], [Bqhalo], [Bqx])
                for bk in blocks:
                    c0, c1 = bk; n = c1 - c0
                    pq, pbq = fm(wq, Bwq, ml, hT, BhT, bk)
                    if bk == sblock:
                        act(qexS[:, m, :, 3:7], pq[:, 0:64].rearrange("p (s t) -> p s t", t=4), AF.Copy, [pbq], [BqexS])
                        act(qtS[:, m, :], pq[:, 0:64], AF.Copy, [pbq], [BqtS])
                    else:
                        act(qx[:, 3 + c0 - pc0:3 + c1 - pc0], pq[:, 0:n], AF.Copy, [pbq], [Bqx])
                        if G["last"] and c1 == G["ncol"]:
                            assert n >= 3
                            act(qt[:, m, :], pq[:, n - 3:n], AF.Copy, [pbq], [Bqt])
                if npc:
                    cp("pool", qhalo[:, m, :], qx[:, npc:npc + 3], [Bqx], [Bqhalo])

            def stage4B(m):
                qx = qext[m % NG]; Bqx = Bqext[m % NG]
                d4 = dg4[m % 2]; Bd4 = Bdg4[m % 2]
                for bk in blocks:
                    c0, c1 = bk; n = c1 - c0
                    pt, pb, _ = psum()
                    for i in range(4):
                        if bk == sblock:
                            mm(pt[:, 0:64].rearrange("p (s t) -> p s t", t=4), d4[:, i, :], qexS[:, m, :, i:i + 4], i == 0, i == 3,
                               [Bd4, BqexS], [pb])
                        else:
                            mm(pt[:, 0:n], d4[:, i, :], qx[:, c0 - pc0 + i:c1 - pc0 + i], i == 0, i == 3, [Bd4, Bqx], [pb])
                    act(qkvT[:, m, c0:c1], pt[:, 0:n], AF.Silu, [pb], [BqkvT])

            stage4A(0)
            for m in range(24):
                if m + 1 < 24:
                    stage4A(m + 1)
                stage4B(m)
            P.ck("p4")
            for nb in range(2):
                wz, Bwz = wnext(("in", 5120 + nb * 512, 512))
                for t in tiles:
                    pt, pb, _ = psum()
                    for kc in range(8):
                        mm(pt[:t.n, :], hT[:, kc, t.col:t.col + t.n], wz[:, kc, :], kc == 0, kc == 7, [BhT, Bwz], [pb])
                    act(zs[t.xi][:t.n, nb * 512:(nb + 1) * 512], pt[:t.n, :], AF.Silu, [pb], [Bzs[t.xi]])
            wba, Bwba = wnext(("in", 6144, 16))
            for t in tiles:
                pt, pb, _ = psum()
                for kc in range(8):
                    mm(pt[:t.n, 0:16], hT[:, kc, t.col:t.col + t.n], wba[:, kc, 0:16], kc == 0, kc == 7, [BhT, Bwba], [pb])
                act(ba[:t.n, t.xi, :], pt[:t.n, 0:16], AF.Copy, [pb], [Bba])
            wprefetch()

            P.ck("zba")
            cg = conv_gen()

            _fk = 1

            _fm = 2
            _fc = [0]

            def fill(k=1):
                _fc[0] += 1
                if _fc[0] % _fm:
                    return
                for _ in range(k * _fk):
                    next(cg, None)
            for t in tiles:
                n = t.n; col = t.col; isS = (t.kind == "S")
                tU = triUs if isS else triU
                tM = mmins if isS else mmin
                tO = offds if isS else offd
                tB = blk if isS else onesf
                qTt = lambda h: qkvT[:, h, col:col + n]
                kTt = lambda h: qkvT[:, 8 + h, col:col + n]
                vTt = lambda h: qkvT[:, 16 + h, col:col + n]
                tt("dve", sqT[:, :, 0:n], qkvT[:, 0:16, col:col + n], qkvT[:, 0:16, col:col + n], ALU.mult, [BqkvT], [BsqT])
                pt, pb, _ = psum("g")
                for j in range(16):
                    mm(pt[:n, j:j + 1], sqT[:, j, 0:n], onesb[:, 0:1], True, True, [BsqT, Bc], [pb])
                ts("dve", sc[:n, 0:16], pt[:n, 0:16], 1e-6, None, ALU.add, None, [pb], [Bsc])
                fill()
                act(sc[:n, 24:32], sc[:n, 8:16], AF.Ln, [Bsc], [Bsc])
                act(sc[:n, 16:24], sc[:n, 24:32], AF.Exp, [Bsc], [Bsc], scale=0.5)
                act(sc[:n, 24:32], sc[:n, 24:32], AF.Exp, [Bsc], [Bsc], scale=-0.5)
                act(sc[:n, 32:40], ba[:n, t.xi, 0:8], AF.Exp, [Bba], [Bsc], scale=-1.0)
                ts("dve", sc[:n, 32:40], sc[:n, 32:40], 1.0, None, ALU.add, None, [Bsc], [Bsc])
                P.op("dve", lambda e, n=n: e.reciprocal(out=sc[:n, 32:40], in_=sc[:n, 32:40]), [Bsc], [Bsc])
                tt("dve", sc[:n, 40:48], ba[:n, t.xi, 8:16], dtb[:n, :], ALU.add, [Bba, Bp], [Bsc])
                stt(sc[:n, 48:56], sc[:n, 40:48], -1.0, sc[:n, 40:48], ALU.mult, ALU.max, [Bsc], [Bsc])
                act(sc[:n, 48:56], sc[:n, 48:56], AF.Exp, [Bsc], [Bsc], scale=-1.0)
                act(sc[:n, 48:56], sc[:n, 48:56], AF.Ln, [Bsc], [Bsc], bias=1.0)
                stt(sc[:n, 40:48], sc[:n, 40:48], 0.0, sc[:n, 48:56], ALU.max, ALU.add, [Bsc], [Bsc])
                tt("dve", sc[:n, 56:64], sc[:n, 40:48], nexpA[:n, :], ALU.mult, [Bsc, Bp], [Bsc])
                tt("dve", sc2[:n, 0:8], sc[:n, 32:40], sc[:n, 24:32], ALU.mult, [Bsc], [Bsc])
                stt(sc2[:n, 8:16], sc2[:n, 0:8], -1.0, sc[:n, 24:32], ALU.mult, ALU.mult, [Bsc], [Bsc])
                pt, pb, _ = psum("g")
                mm(pt[:n, 0:8], tU[:n, :n], sc[:n, 56:64], True, True, [Bc, Bsc], [pb])
                mm(pt[:n, 8:16], tB[:n, :n], sc[:n, 56:64], True, True, [Bc, Bsc], [pb])
                cp("dve", sc2[:n, 16:24], pt[:n, 0:8], [pb], [Bsc])
                fill()
                act(sc2[:n, 24:32], pt[:n, 0:8], AF.Exp, [pb], [Bsc])
                tt("dve", sc2[:n, 32:40], pt[:n, 8:16], sc2[:n, 16:24], ALU.subtract, [pb, Bsc], [Bsc])
                act(sc2[:n, 32:40], sc2[:n, 32:40], AF.Exp, [Bsc], [Bsc])
                tt("dve", sc2[:n, 40:48], sc[:n, 24:32], sc2[:n, 32:40], ALU.mult, [Bsc], [Bsc])
                P.ck("g1")
                tt("dve", gm[:n, :, 0:n], tU[:n, :n].unsqueeze(1).broadcast_to([n, 8, n]),
                   sc[:n, 56:64].unsqueeze(2).broadcast_to([n, 8, n]), ALU.mult, [Bc, Bsc], [Bgm])
                gbs = []
                for hb_ in range(2):
                    pt, pb, _ = psum("g")
                    mm(pt[:, 0:4 * n].rearrange("p (h n) -> p h n", h=4), onesf[:n, :], gm[:n, hb_ * 4:hb_ * 4 + 4, 0:n], True, True,
                       [Bc, Bgm], [pb])
                    gbs.append((pt, pb))
                    fill()
                for h in range(8):
                    pt, pb = gbs[h // 4]
                    stt(decT[:n, h, 0:n], pt[:n, (h % 4) * n:(h % 4 + 1) * n], sc2[:n, 16 + h:17 + h], tM[:n, :n], ALU.subtract, ALU.min,
                        [pb, Bsc, Bc], [BdecT])
                for hb_ in range(2):
                    pt, pb = gbs[hb_]
                    act(EB[:, hb_ * 4:hb_ * 4 + 4, 0:n], pt[:, 0:4 * n].rearrange("p (h n) -> p h n", h=4), AF.Exp, [pb], [BEB])
                act(decT[:n, :, 0:n], decT[:n, :, 0:n], AF.Exp, [BdecT], [BdecT])
                tt("dve", decTs[:n, :, 0:n], decT[:n, :, 0:n], tO[:n, :n].unsqueeze(1).broadcast_to([n, 8, n]), ALU.mult,
                   [BdecT, Bc], [BdecTs])
                P.ck("g2")
                W0, BW0 = Wp[0], BWp[0]
                for hb_ in range(2):
                    pk, pbk, _ = psum("g")
                    pq, pbq, _ = psum("g")
                    for hh in range(4):
                        h = hb_ * 4 + hh
                        mm(pk[:n, hh * 128:hh * 128 + n], kTt(h), kTt(h), True, True, [BqkvT], [pbk])
                        mm(pq[:n, hh * 128:hh * 128 + n], kTt(h), qTt(h), True, True, [BqkvT], [pbq])
                    for hh in range(4):
                        h = hb_ * 4 + hh
                        stt(W0[:n, h, 0:n], pk[:n, hh * 128:hh * 128 + n], sc2[:n, 8 + h:9 + h], decTs[:n, h, 0:n], ALU.mult, ALU.mult,
                            [pbk, Bsc, BdecTs], [BW0])
                        stt(attnT[:n, h, 0:n], pq[:n, hh * 128:hh * 128 + n], sc[:n, 24 + h:25 + h], decT[:n, h, 0:n], ALU.mult, ALU.mult,
                            [pbq, Bsc, BdecT], [BattnT])
                tt("dve", Zp[0][:n, :, 0:n], W0[:n, :, 0:n], ident[:n, :n].unsqueeze(1).broadcast_to([n, 8, n]), ALU.add, [BW0, Bc], [BZp[0]])
                pt, pb, _ = psum("g")
                pvw = bf(pt).rearrange("p (h c) -> p h c", h=8)
                for h in range(8):
                    tr(pvw[:n, h, 0:n], W0[:n, h, 0:n], ident[:n, :n], [BW0, Bc], [pb])
                act(Qp[0][:n, :, 0:n], pvw[:n, :, 0:n], AF.Copy, [pb], [BQp[0]])
                fill()
                cw = cq = cz = 0
                for lev in range(1, t.nlev + 1):
                    nq = 1 - cq
                    for hb_ in range(2):
                        pt, pb, _ = psum("g")
                        for hh in range(4):
                            h = hb_ * 4 + hh
                            mm(pt[:n, hh * 128:hh * 128 + n], Wp[cw][:n, h, 0:n], Qp[cq][:n, h, 0:n], True, True, [BWp[cw], BQp[cq]], [pb])
                        act(Qp[nq][:n, hb_ * 4:hb_ * 4 + 4, 0:n], pt[:n, :].rearrange("p (h c) -> p h c", h=4)[:, :, 0:n], AF.Copy,
                            [pb], [BQp[nq]])
                        fill()
                    if lev < t.nlev:
                        nw = 1 - cw
                        for hb_ in range(2):
                            pt, pb, _ = psum("g")
                            for hh in range(4):
                                h = hb_ * 4 + hh
                                mm(pt[:n, hh * 128:hh * 128 + n], Qp[cq][:n, h, 0:n], Wp[cw][:n, h, 0:n], True, True, [BWp[cw], BQp[cq]], [pb])
                            act(Wp[nw][:n, hb_ * 4:hb_ * 4 + 4, 0:n], pt[:n, :].rearrange("p (h c) -> p h c", h=4)[:, :, 0:n], AF.Copy,
                                [pb], [BWp[nw]])
                            fill()
                        cw = nw
                    cq = nq
                    nz = 1 - cz
                    for hb_ in range(2):
                        pt, pb, _ = psum("g")
                        for hh in range(4):
                            h = hb_ * 4 + hh
                            mm(pt[:n, hh * 128:hh * 128 + n], Qp[cq][:n, h, 0:n], Zp[cz][:n, h, 0:n], True, True, [BQp[cq], BZp[cz]], [pb])
                        tt("dve", Zp[nz][:n, hb_ * 4:hb_ * 4 + 4, 0:n], pt[:n, :].rearrange("p (h c) -> p h c", h=4)[:, :, 0:n],
                           Zp[cz][:n, hb_ * 4:hb_ * 4 + 4, 0:n], ALU.add, [pb, BZp[cz]], [BZp[nz]])
                        fill()
                    cz = nz
                Z = Zp[cz]; BZ = BZp[cz]
                P.ck("g4")
                pk, pbk, _ = psum("g"); pvk = bf(pk).rearrange("p (h c) -> p h c", h=8)
                pv2, pbv, _ = psum("g"); pvv = bf(pv2).rearrange("p (h c) -> p h c", h=8)
                for h in range(8):
                    tr(pvk[:n, h, :], kTt(h), ident[:, :], [BqkvT, Bc], [pbk])
                    tr(pvv[:n, h, :], vTt(h), ident[:, :], [BqkvT, Bc], [pbv])
                P.ck("g4a")
                bc = lambda colap: colap.unsqueeze(2).broadcast_to([n, 8, 128])
                tt("dve", rhsk[:n, :, :], pvk[:n, :, :], bc(sc2[:n, 24:32]), ALU.mult, [pbk, Bsc], [Brhsk])
                tt("dve", kdec[:n, :, :], pvk[:n, :, :], bc(sc2[:n, 40:48]), ALU.mult, [pbk, Bsc], [Bkdec])
                tt("dve", rhsv[:n, :, :], pvv[:n, :, :], bc(sc[:n, 16:24]), ALU.mult, [pbv, Bsc], [Brhsv])
                fill()
                P.ck("g4b")
                tt("pool", qd[:, :, 0:n], qkvT[:, 0:8, col:col + n], EB[:, :, 0:n], ALU.mult, [BqkvT, BEB], [Bqd])
                P.ck("g5")
                for hb_ in range(2):
                    pt, pb, _ = psum("g")
                    for hh in range(4):
                        h = hb_ * 4 + hh
                        mm(pt[:, hh * 128:hh * 128 + n], rhsk[:n, h, :], Z[:n, h, 0:n], True, True, [Brhsk, BZ], [pb])
                    P.op("act", lambda e, pt=pt, hb_=hb_, n=n: e.mul(nkcT[:, hb_ * 4:hb_ * 4 + 4, 0:n],
                                                                     pt[:, :].rearrange("p (h c) -> p h c", h=4)[:, :, 0:n], -1.0), [pb], [BnkcT])
                opsum = []
                if not isS:
                    for hb_ in range(2):
                        pu, pbu, _ = psum("g")
                        for hh in range(4):
                            h = hb_ * 4 + hh
                            mm(pu[:n, hh * 128:(hh + 1) * 128], Z[:n, h, 0:n], rhsv[:n, h, :], True, False, [BZ, Brhsv], [pbu])
                            mm(pu[:n, hh * 128:(hh + 1) * 128], nkcT[:, h, 0:n], Sbf[:, h, :], False, True, [BnkcT, BSbf], [pbu])
                        tt("dve", u[:n, hb_ * 4:hb_ * 4 + 4, :], pu[:n, :].rearrange("p (h e) -> p h e", h=4),
                           sc2[:n, hb_ * 4:hb_ * 4 + 4].unsqueeze(2).broadcast_to([n, 4, 128]), ALU.mult, [pbu, Bsc], [Bu])
                        fill()
                    for hb_ in range(2):
                        po, pbo, _ = psum("g")
                        for hh in range(4):
                            h = hb_ * 4 + hh
                            mm(po[:n, hh * 128:(hh + 1) * 128], qd[:, h, 0:n], Sbf[:, h, :], True, False, [Bqd, BSbf], [pbo])
                            mm(po[:n, hh * 128:(hh + 1) * 128], attnT[:n, h, 0:n], u[:n, h, :], False, True, [BattnT, Bu], [pbo])
                        opsum.append((po, pbo))
                        fill()
                    for hb_ in range(2):
                        pS, pbS, _ = psum("g")
                        for hh in range(4):
                            h = hb_ * 4 + hh
                            mm(pS[:, hh * 128:(hh + 1) * 128], kdec[:n, h, :], u[:n, h, :], True, True, [Bkdec, Bu], [pbS])
                        for hh in range(4):
                            h = hb_ * 4 + hh
                            stt(S[:, h, :], S[:, h, :], EB[:, h, n - 1:n], pS[:, hh * 128:(hh + 1) * 128], ALU.mult, ALU.add,
                                [BS, BEB, pbS], [BS])
                        act(Sbf[:, hb_ * 4:hb_ * 4 + 4, :], S[:, hb_ * 4:hb_ * 4 + 4, :], AF.Copy, [BS], [BSbf])
                        fill()
                else:
                    puT, pbuT, _ = psum("g"); puTv = puT[:, :].rearrange("p (h c) -> p h c", h=8)
                    poT, pboT, _ = psum("g"); poTv = poT[:, :].rearrange("p (h c) -> p h c", h=8)
                    for h in range(8):
                        mm(puTv[:, h, :], rhsv[:n, h, :], Z[:n, h, 0:n], h == 0, False, [Brhsv, BZ], [pbuT], skip=True)
                    for s in range(16):
                        s0 = S0[s % 3]; Bs0 = BS0[s % 3]; s0b = S0b[s % 2]; Bs0b = BS0b[s % 2]
                        P.dma("sp", lambda e, s=s, s0=s0: e.dma_start(out=s0[:], in_=st_delta[s].rearrange("h d e -> d h e")), S0key[s % 3], writes=[Bs0])
                        act(s0b[:], s0[:], AF.Copy, [Bs0], [Bs0b])
                        for h in range(8):
                            mm(puTv[:, h, 4 * s:4 * s + 4], s0b[:, h, :], nkcT[:, h, 4 * s:4 * s + 4], False, False, [Bs0b, BnkcT], [pbuT], skip=True)
                            mm(poTv[:, h, 4 * s:4 * s + 4], s0b[:, h, :], qd[:, h, 4 * s:4 * s + 4], (s == 0 and h == 0), False, [Bs0b, Bqd], [pboT], skip=True)
                    act(uTs[:], puTv, AF.Copy, [pbuT], [BuTs])
                    pt, pb, _ = psum("g"); ptv = bf(pt).rearrange("p (h c) -> p h c", h=8)
                    for h in range(8):
                        tr(ptv[:n, h, :], uTs[:, h, :], ident[:, :], [BuTs, Bc], [pb])
                    for h in range(8):
                        ts("dve", u[:n, h, :], ptv[:n, h, :], sc2[:n, h:h + 1], None, ALU.mult, None, [pb, Bsc], [Bu])
                    for h in range(8):
                        mm(poTv[:, h, :], u[:n, h, :], attnT[:n, h, 0:n], False, h == 7, [Bu, BattnT], [pboT], skip=True)
                    act(oTs[:], poTv, AF.Copy, [pboT], [BoTs])
                    for hb_ in range(2):
                        po, pbo, _ = psum("g")
                        for hh in range(4):
                            tr(po[:n, hh * 128:(hh + 1) * 128], oTs[:, hb_ * 4 + hh, :], identf[:, :], [BoTs, Bc], [pbo])
                        opsum.append((po, pbo))
                        fill()
                    def sample_state_update(n=n):
                        for s in range(16):
                            s0 = S0[(s + 1) % 3]; Bs0 = BS0[(s + 1) % 3]
                            P.dma("sp", lambda e, s=s, s0=s0: e.dma_start(out=s0[:], in_=st_delta[s].rearrange("h d e -> d h e")), S0key[(s + 1) % 3], writes=[Bs0])
                            ums = um[s % 2]; Bums = Bum[s % 2]
                            act(ums[:n, :], u[:n, :, :].rearrange("p h e -> p (h e)"), AF.Identity, [Bu, Bc], [Bums], scale=bmask[:n, s:s + 1])
                            so = So[s % 2]; Bso = BSo[s % 2]
                            for hb_ in range(2):
                                pS, pbS, _ = psum("g")
                                for hh in range(4):
                                    h = hb_ * 4 + hh
                                    mm(pS[:, hh * 128:(hh + 1) * 128], kdec[:n, h, :], ums[:n, h * 128:(h + 1) * 128], True, True, [Bkdec, Bums], [pbS])
                                for hh in range(4):
                                    h = hb_ * 4 + hh
                                    stt(so[:, h, :], s0[:, h, :], EB[:, h, 4 * s + 3:4 * s + 4], pS[:, hh * 128:(hh + 1) * 128], ALU.mult, ALU.add,
                                        [Bs0, BEB, pbS], [Bso])
                            P.dma("pool", lambda e, s=s, so=so: e.dma_start(out=o_sdelta[s].rearrange("h d e -> d h e"), in_=so[:]), ("xp", 3 + s % 2),
                                  reads=[Bso])
                P.ck("g6")
                for h in range(8):
                    po, pbo = opsum[h // 4]
                    act(junk2[:n, :], po[:n, (h % 4) * 128:(h % 4 + 1) * 128], AF.Square, [pbo], [Bjunk2, Bms], accum_out=ms[:n, h:h + 1])
                ts("dve", ms[:n, 8:16], sc[:n, 0:8], 128.0 * EPS, None, ALU.mult, None, [Bsc], [Bms])
                stt(ms[:n, 8:16], ms[:n, 0:8], 1.0 / 128, ms[:n, 8:16], ALU.mult, ALU.add, [Bms], [Bms])
                act(ms[:n, 8:16], ms[:n, 8:16], AF.Ln, [Bms], [Bms])
                act(ms[:n, 0:8], ms[:n, 8:16], AF.Exp, [Bms], [Bms], scale=-0.5)
                for h in range(8):
                    po, pbo = opsum[h // 4]
                    stt(on[:n, h * 128:(h + 1) * 128], po[:n, (h % 4) * 128:(h % 4 + 1) * 128], ms[:n, h:h + 1], zs[t.xi][:n, h * 128:(h + 1) * 128],
                        ALU.mult, ALU.mult, [pbo, Bms, Bzs[t.xi]], [Bon])
                pt, pb, _ = psum("g"); ptv = bf(pt).rearrange("p (h c) -> p h c", h=8)
                for h in range(8):
                    tr(ptv[:, h, 0:n], on[:n, h * 128:(h + 1) * 128], ident[:n, :n], [Bon, Bc], [pb])
                ts("dve", onT[:, :, col:col + n], ptv[:, :, 0:n], ghead[:, 0:1], None, ALU.mult, None, [pb, Bp], [BonT])
                fill()
                P.ck("g7")
                if isS:
                    sample_state_update()
                P.ck("g8")

            for _ in cg:
                pass
            if isSG:
                emit_sample_tails()
            for q in range(2):
                wd, Bwd = wnext(("dout", q * 512, 512))
                wg, Bwg = wnext(("in", 7184 + q * 512, 512))
                for ml in range(4):
                    m = q * 4 + ml
                    for bk in blocks:
                        c0, c1 = bk; n = c1 - c0
                        pd_, pbd = fm(wd, Bwd, ml, onT, BonT, bk)
                        pg, pbg = fm(wg, Bwg, ml, hT, BhT, bk)
                        sg_, Bsg = ftmp(); t1, Bt1 = ftmp()
                        act(sg_[:, 0:n], pg[:, 0:n], AF.Sigmoid, [pbg], [Bsg])
                        tt("dve", t1[:, 0:n], pd_[:, 0:n], sg_[:, 0:n], ALU.mult, [pbd, Bsg], [Bt1])
                        tt("dve", mixT[:, m, c0:c1], mixT[:, m, c0:c1], t1[:, 0:n], ALU.add, [BmixT, Bt1], [BmixT])
            P.ck("p5")
            for nb in range(2):
                wo_, Bwo = wnext(("o", nb * 512, 512))
                for t in tiles:
                    pt, pb, _ = psum()
                    for kc in range(8):
                        mm(pt[:t.n, :], mixT[:, kc, t.col:t.col + t.n], wo_[:, kc, :], kc == 0, kc == 7, [BmixT, Bwo], [pb])
                    x = xt[t.xi]
                    tt("dve", x[:t.n, nb * 512:(nb + 1) * 512], pt[:t.n, :], x[:t.n, nb * 512:(nb + 1) * 512], ALU.add, [pb, Bxt[t.xi]], [Bxt[t.xi]])
            wprefetch()
            P.ck("p6")
            norm_to_T(G, gmlp)
            P.handover([BqkvT, BmixT, Bcpre], [BaT])
            for j in range(8):
                wu, Bwu = wnext(("up", j * 512, 512))
                for ml in range(4):
                    m = j * 4 + ml
                    for bk in blocks:
                        c0, c1 = bk; n = c1 - c0
                        pu, pbu = fm(wu, Bwu, ml, hT, BhT, bk)
                        r, Br = ftmp()
                        act(r[:, 0:n], pu[:, 0:n], AF.Relu, [pbu], [Br])
                        act(aT[:, m, c0:c1], r[:, 0:n], AF.Square, [Br], [BaT])
            P.ck("p8")
            for nb in range(2):
                banks = [psum() for _ in tiles]
                for kb in range(4):
                    wd, Bwd = wnext(("down", kb, nb))
                    for ti, t in enumerate(tiles):
                        pt, pb, _ = banks[ti]
                        for kc in range(8):
                            mm(pt[:t.n, :], aT[:, kb * 8 + kc, t.col:t.col + t.n], wd[:, kc, :], kb == 0 and kc == 0, kb == 3 and kc == 7,
                               [BaT, Bwd], [pb])
                for ti, t in enumerate(tiles):
                    pt, pb, _ = banks[ti]
                    x = xt[t.xi]
                    tt("dve", x[:t.n, nb * 512:(nb + 1) * 512], pt[:t.n, :], x[:t.n, nb * 512:(nb + 1) * 512], ALU.add, [pb, Bxt[t.xi]], [Bxt[t.xi]])
            P.ck("p9")
            for ti, t in enumerate(tiles):
                n = t.n; x = xt[t.xi]; Bx = Bxt[t.xi]
                y = yt[0]; By = Byt[0]
                act(junk[:n, :], x[:n, :], AF.Square, [Bx], [Bjunk, Bssx], accum_out=ssx[:n, 0:1])
                act(ssx[:n, 1:2], ssx[:n, 0:1], AF.Sqrt, [Bssx], [Bssx], scale=1.0 / D, bias=EPS)
                P.op("dve", lambda e, n=n: e.reciprocal(out=ssx[:n, 2:3], in_=ssx[:n, 1:2]), [Bssx], [Bssx])
                stt(y[:n, :], x[:n, :], ssx[:n, 2:3], gfin[:n, :], ALU.mult, ALU.mult, [Bx, Bssx, Bp], [By])
                if t.kind == "S":
                    P.dma("pool", lambda e, y=y: e.dma_start(out=y_s[:, :], in_=y[0:64, :]), ("ytp", 0), reads=[By])
                elif t.t0 == 0:
                    P.dma("pool", lambda e, y=y: e.dma_start(out=y_p[0:112, :], in_=y[16:128, :]), ("ytp", 0), reads=[By])
                else:
                    P.dma("pool", lambda e, y=y, t=t: e.dma_start(out=y_p[t.t0 - 16:t.t0 - 16 + t.n, :], in_=y[0:t.n, :]), ("ytp", 0), reads=[By])
                if gi + 1 < len(groups):
                    for t2 in groups[gi + 1]["tiles"]:
                        if t2.xi == t.xi:
                            load_x(gi + 1, t2)

        if not max_groups:
            tail_out(glut, Bglut, 8, 30, 30,
                     lambda k: P.dma("sp", lambda e: e.dma_start(out=o_pconv[:, :], in_=stgT[0:30, :]), ("yt", 0), reads=[BstgT]), None)
            tail_out(qt, Bqt, 24, 3, 3,
                     lambda k: P.dma("sp", lambda e, k=k: e.dma_start(out=o_pqkv[:, k * 1024:(k + 1) * 1024], in_=stgT[0:3, :]), ("yt", 0), reads=[BstgT]), None)
            P.dma("sp", lambda e: e.dma_start(out=o_pdelta.rearrange("h d e -> d h e"), in_=S[:]), "o_p_delta", reads=[BS])
        P.ops["sp"].append({"fn": None, "deps": {("dma", k): v for k, v in P.dma_counts.items()}, "dma": None})
        P.emit()
    return nc


_CACHE = {}


def kernel(x_prompt, x_sample, state_conv, state_qkv_conv, state_delta, meta_tokens, g_mix, w_in,
           w_dw, b_dw, ln_g, ln_b, w_cout, w_short, a_log, dt_bias, g_head, w_dout, w_o, g_mlp,
           w_up, w_down, g_final):
    f = lambda a: np.ascontiguousarray(np.asarray(a, dtype=np.float32))
    if "nc" not in _CACHE:
        _CACHE["nc"] = build()
    nc = _CACHE["nc"]
    shared = dict(meta=f(meta_tokens), g_mix=f(g_mix[0]), w_in=f(w_in[0]), w_dw=f(w_dw[0]), b_dw=f(b_dw[0]), ln_g=f(ln_g[0]),
                  ln_b=f(ln_b[0]), w_cout=f(w_cout[0]), w_short=f(w_short[0]), a_log=f(a_log[0]), dt_bias=f(dt_bias[0]),
                  g_head=f(g_head[0]), w_dout=f(w_dout[0]), w_o=f(w_o[0]), g_mlp=f(g_mlp[0]), w_up=f(w_up[0]),
                  w_down=f(w_down[0]), g_final=f(g_final))
    xp = f(x_prompt); xs = f(x_sample); sc_ = f(state_conv); sq_ = f(state_qkv_conv); sd_ = f(state_delta)
    in_maps = []
    for c in range(8):
        d = dict(shared)
        d["x_p"] = xp[c]
        d["x_s"] = xs[16 * c:16 * c + 16].reshape(64, D)
        d["st_conv"] = sc_[0, 16 * c:16 * c + 16]
        d["st_qkv"] = sq_[0, 16 * c:16 * c + 16]
        d["st_delta"] = sd_[0, 16 * c:16 * c + 16]
        in_maps.append(d)
    res = run_bass_kernel_spmd(nc, in_maps, core_ids=list(range(8)))
    R = res.results
    y_prompt = np.stack([R[c]["y_p"] for c in range(8)]).astype(np.float32)
    y_sample = np.concatenate([R[c]["y_s"].reshape(16, 4, D) for c in range(8)]).astype(np.float32)
    p_conv = np.stack([R[c]["p_conv"] for c in range(8)])[None].astype(np.float32)
    p_qkv = np.stack([R[c]["p_qkv"] for c in range(8)])[None].astype(np.float32)
    p_delta = np.stack([R[c]["p_delta"] for c in range(8)])[None].astype(np.float32)
    s_conv = np.concatenate([R[c]["s_conv"] for c in range(8)])[None].astype(np.float32)
    s_qkv = np.concatenate([R[c]["s_qkv"] for c in range(8)])[None].astype(np.float32)
    s_delta = np.concatenate([R[c]["s_delta"] for c in range(8)])[None].astype(np.float32)
    return (y_prompt, y_sample, p_conv, p_qkv, p_delta, s_conv, s_qkv, s_delta)
```

```python
import numpy as np
from contextlib import ExitStack
import concourse.bass as bass
import concourse.mybir as mybir
from concourse.bass_utils import run_bass_kernel_spmd

F32 = mybir.dt.float32
BF16 = mybir.dt.bfloat16
AF = mybir.ActivationFunctionType
ALU = mybir.AluOpType

ENGS = ("pe", "act", "dve", "pool", "sp")
SAME_ENGINE_SYNC = {"pe": False, "act": True, "dve": True, "pool": True, "sp": False}

D = 1024
DIN = 8208
DFF = 4096
NPT = 2064
EPS = 1e-6
NEG = -30000.0


class Buf:
    __slots__ = ("name", "lw", "rd", "excl")

    def __init__(self, name, excl=False):
        self.name = name
        self.lw = None
        self.rd = {}
        self.excl = excl


class Prog:
    def __init__(self, nc):
        self.nc = nc
        self.ops = {e: [] for e in ENGS}
        self.dma_counts = {}
        self.dma_keys = []
        self.enabled = True
        self.stop_at = None

    def ck(self, name):
        if self.stop_at is not None and name == self.stop_at:
            self.enabled = False

    def _collect(self, eng, reads, writes):
        deps = {}

        def add(k, v):
            if k[0] == "eng" and k[1] == eng and not SAME_ENGINE_SYNC[eng]:
                return
            if k not in deps or deps[k] < v:
                deps[k] = v

        for b in reads:
            if b.lw is not None:
                add((b.lw[0], b.lw[1]), b.lw[2])
            if b.excl:
                for k, v in b.rd.items():
                    if k != ("eng", eng):
                        add(k, v)
        for b in writes:
            if b.lw is not None:
                add((b.lw[0], b.lw[1]), b.lw[2])
            for k, v in b.rd.items():
                add(k, v)
        return deps

    def _commit(self, tok, reads, writes):
        k = (tok[0], tok[1])
        for b in reads:
            if b.rd.get(k, -1) < tok[2]:
                b.rd[k] = tok[2]
        for b in writes:
            b.lw = tok
            b.rd = {}

    def op(self, eng, fn, reads=(), writes=()):
        if not self.enabled:
            return
        deps = self._collect(eng, reads, writes)
        idx = len(self.ops[eng])
        self.ops[eng].append({"fn": fn, "deps": deps, "dma": None})
        self._commit(("eng", eng, idx), reads, writes)

    def dma(self, eng, fn, semkey, reads=(), writes=()):
        if not self.enabled:
            return
        deps = self._collect(eng, reads, writes)
        if semkey not in self.dma_counts:
            self.dma_counts[semkey] = 0
            self.dma_keys.append(semkey)
        self.dma_counts[semkey] += 16
        cnt = self.dma_counts[semkey]
        self.ops[eng].append({"fn": fn, "deps": deps, "dma": semkey})
        self._commit(("dma", semkey, cnt), reads, writes)

    def wait_all(self, eng, bufs):
        deps = self._collect(eng, bufs, ())
        self.ops[eng].append({"fn": None, "deps": deps, "dma": None})

    def handover(self, old, new):
        merged = {}
        for b in old:
            if b.lw is not None:
                k = (b.lw[0], b.lw[1])
                merged[k] = max(merged.get(k, -1), b.lw[2])
            for k, v in b.rd.items():
                merged[k] = max(merged.get(k, -1), v)
        for b in new:
            b.lw = None
            b.rd = dict(merged)

    def emit(self):
        nc = self.nc
        sig = {e: set() for e in ENGS}
        for e in ENGS:
            for o in self.ops[e]:
                for (kind, key), v in o["deps"].items():
                    if kind == "eng":
                        sig[key].add(v)
        cum = {}
        for e in ENGS:
            c = 0
            m = {}
            for i in range(len(self.ops[e])):
                if i in sig[e]:
                    c += 1
                    m[i] = c
            cum[e] = m
        with ExitStack() as st:
            esem = {e: st.enter_context(nc.semaphore("s_" + e)) for e in ENGS}
            dsem = {k: st.enter_context(nc.semaphore("d_%d" % i)) for i, k in enumerate(self.dma_keys)}
            block = st.enter_context(nc.Block())
            engobj = {"pe": block.tensor, "act": block.scalar, "dve": block.vector,
                      "pool": block.gpsimd, "sp": block.sync}

            def make(ename):
                ops = self.ops[ename]

                def body(eng):
                    seen = {}
                    for i, o in enumerate(ops):
                        for (kind, key), v in o["deps"].items():
                            if kind == "eng":
                                val = cum[key][v]
                                s = esem[key]
                            else:
                                val = v
                                s = dsem[key]
                            if seen.get((kind, key), 0) >= val:
                                continue
                            seen[(kind, key)] = val
                            eng.wait_ge(s, val)
                        if o["fn"] is None:
                            continue
                        inst = o["fn"](eng)
                        if o["dma"] is not None:
                            inst.then_inc(dsem[o["dma"]], 16)
                        elif i in cum[ename]:
                            inst.then_inc(esem[ename], 1)
                return body

            for e in ENGS:
                if self.ops[e]:
                    engobj[e](make(e))


class TileD:
    def __init__(self, kind, n, t0, col, xi):
        self.kind, self.n, self.t0, self.col, self.xi = kind, n, t0, col, xi
        self.nlev = {128: 6, 16: 3, 64: 1}[n] if kind == "P" else 1


def make_groups(with_sample=True):
    groups = []
    ptiles = [(i * 128, min(128, NPT - i * 128)) for i in range(17)]
    split = [ptiles[0:4], ptiles[4:8], ptiles[8:12], ptiles[12:17]]
    for gi, pts in enumerate(split):
        tiles = []
        col = 0
        for xi, (t0, n) in enumerate(pts):
            tiles.append(TileD("P", n, t0, col, xi))
            col += n
        blocks = []
        c = 0
        while c < col:
            blocks.append((c, min(col, c + 512)))
            c += 512
        groups.append(dict(tiles=tiles, pc0=0, npc=col, ncol=col, blocks=blocks, last=(gi == 3), first=False))
    if with_sample:
        groups.append(dict(tiles=[TileD("S", 64, 0, 0, 0)], pc0=64, npc=0, ncol=64, blocks=[(0, 64)], last=False, first=True))
    return groups


def wsched():
    L = []
    for j in range(6):
        L.append(("in", 2048 + 512 * j, 512))
    L += [("in", 5120, 512), ("in", 5632, 512), ("in", 6144, 16)]
    for q in range(2):
        L.append(("in", q * 512, 512))
        L.append(("in", 1024 + q * 512, 512))
    for q in range(2):
        L += [("cout", q * 512, 512), ("in", 6160 + q * 512, 512)]
    for q in range(2):
        L += [("dout", q * 512, 512), ("in", 7184 + q * 512, 512)]
    L += [("o", 0, 512), ("o", 512, 512)]
    for j in range(8):
        L.append(("up", j * 512, 512))
    for nb in range(2):
        for kb in range(4):
            L.append(("down", kb, nb))
    return L


def build(with_sample=True, NW=3, debug=False, max_groups=None, stop_at=None):
    nc = bass.Bass("TRN2", target_bir_lowering=False)
    din = lambda name, shape: nc.dram_tensor(name, shape, F32, kind="ExternalInput").ap()
    dout = lambda name, shape: nc.dram_tensor(name, shape, F32, kind="ExternalOutput").ap()
    x_p = din("x_p", [2048, D]); x_s = din("x_s", [64, D])
    st_conv = din("st_conv", [16, 30, D]); st_qkv = din("st_qkv", [16, 3, 3072]); st_delta = din("st_delta", [16, 8, 128, 128])
    meta = din("meta", [16, D])
    g_mix = din("g_mix", [D]); w_in = din("w_in", [D, DIN]); w_dw = din("w_dw", [31, D]); b_dw = din("b_dw", [D])
    ln_g = din("ln_g", [D]); ln_b = din("ln_b", [D]); w_cout = din("w_cout", [D, D]); w_short = din("w_short", [4, 3072])
    a_log = din("a_log", [8]); dt_bias = din("dt_bias", [8]); g_head = din("g_head", [128]); w_dout = din("w_dout", [D, D])
    w_o = din("w_o", [D, D]); g_mlp = din("g_mlp", [D]); w_up = din("w_up", [D, DFF]); w_down = din("w_down", [DFF, D])
    g_final = din("g_final", [D])
    y_p = dout("y_p", [2048, D]); y_s = dout("y_s", [64, D])
    o_pconv = dout("p_conv", [30, D]); o_pqkv = dout("p_qkv", [3, 3072]); o_pdelta = dout("p_delta", [8, 128, 128])
    o_sconv = dout("s_conv", [16, 30, D]); o_sqkv = dout("s_qkv", [16, 3, 3072]); o_sdelta = dout("s_delta", [16, 8, 128, 128])
    wmap = {"in": w_in, "cout": w_cout, "dout": w_dout, "o": w_o, "up": w_up}

    P = Prog(nc)
    P.stop_at = stop_at
    groups = make_groups(with_sample)
    if max_groups:
        groups = groups[:max_groups]
    MAXC = 528
    MAXPC = max(g["npc"] for g in groups)

    with ExitStack() as st:
        st.enter_context(nc.allow_non_contiguous_dma(reason="small params"))
        cnt = [0]

        def sb(shape, dt, name=None):
            cnt[0] += 1
            return st.enter_context(nc.sbuf_tensor(name or ("t%d" % cnt[0]), shape, dt))

        psT = [st.enter_context(nc.psum_tensor("ps%d" % i, [128, 512], F32)) for i in range(8)]
        psB = [Buf("ps%d" % i, excl=True) for i in range(8)]
        ps_reserved = set()
        ps_ctr = [0]

        pool_ctr = {"d": 0, "g": 0}

        def psum(pool=None):
            if pool == "d":
                i = pool_ctr["d"] % 4
                pool_ctr["d"] += 1
                return psT[i], psB[i], i
            if pool == "g":
                i = 4 + pool_ctr["g"] % 4
                pool_ctr["g"] += 1
                return psT[i], psB[i], i
            while True:
                i = ps_ctr[0] % 8
                ps_ctr[0] += 1
                if i not in ps_reserved:
                    return psT[i], psB[i], i

        def bf(ap):
            return ap[:].bitcast(BF16)

        def mm(out, lhsT, rhs, start, stop, reads, writes, skip=False):
            if skip:
                P.op("pe", lambda e: e.matmul(out, lhsT=lhsT, rhs=rhs, start=start, stop=stop, skip_group_check=True), reads, writes)
            else:
                P.op("pe", lambda e: e.matmul(out, lhsT=lhsT, rhs=rhs, start=start, stop=stop), reads, writes)

        def tr(out, in_, idn, reads, writes):
            P.op("pe", lambda e: e.transpose(out=out, in_=in_, identity=idn), reads, writes)

        def act(out, in_, func, reads, writes, **kw):
            P.op("act", lambda e: e.activation(out=out, in_=in_, func=func, **kw), reads, writes)

        def ts(eng, out, in0, s1, s2, op0, op1, reads, writes):
            if op1 is None:
                P.op(eng, lambda e: e.tensor_scalar(out=out, in0=in0, scalar1=s1, scalar2=None, op0=op0), reads, writes)
            else:
                P.op(eng, lambda e: e.tensor_scalar(out=out, in0=in0, scalar1=s1, scalar2=s2, op0=op0, op1=op1), reads, writes)

        def tt(eng, out, in0, in1, op, reads, writes):
            P.op(eng, lambda e: e.tensor_tensor(out=out, in0=in0, in1=in1, op=op), reads, writes)

        def stt(out, in0, scalar, in1, op0, op1, reads, writes):
            P.op("dve", lambda e: e.scalar_tensor_tensor(out=out, in0=in0, scalar=scalar, in1=in1, op0=op0, op1=op1), reads, writes)

        def cp(eng, out, in_, reads, writes):
            P.op(eng, lambda e: e.tensor_copy(out=out, in_=in_), reads, writes)

        Bc = Buf("consts")
        identf = sb([128, 128], F32); ident = sb([128, 128], BF16)
        onesf = sb([128, 128], F32); onesb = sb([128, 128], BF16)
        triU = sb([128, 128], F32); mmin = sb([128, 128], F32); offd = sb([128, 128], F32)
        blk = sb([64, 64], F32); triUs = sb([64, 64], F32); mmins = sb([64, 64], F32); offds = sb([64, 64], F32)
        bmask = sb([64, 16], F32)
        tmpc = sb([64, 64], F32)

        def pool(fn):
            P.op("pool", fn, reads=[Bc], writes=[Bc])

        pool(lambda e: e.memset(identf[:], 0.0))
        pool(lambda e: e.affine_select(out=identf[:], in_=identf[:], pattern=[[-1, 128]], compare_op=ALU.not_equal,
                                       fill=1.0, base=0, channel_multiplier=1))
        pool(lambda e: e.tensor_copy(out=ident[:], in_=identf[:]))
        pool(lambda e: e.memset(onesf[:], 1.0))
        pool(lambda e: e.memset(onesb[:], 1.0))
        pool(lambda e: e.affine_select(out=triU[:], in_=onesf[:], pattern=[[1, 128]], compare_op=ALU.is_ge,
                                       fill=0.0, base=0, channel_multiplier=-1))
        pool(lambda e: e.memset(mmin[:], 0.0))
        pool(lambda e: e.affine_select(out=mmin[:], in_=mmin[:], pattern=[[1, 128]], compare_op=ALU.is_ge,
                                       fill=NEG, base=0, channel_multiplier=-1))
        pool(lambda e: e.affine_select(out=offd[:], in_=onesf[:], pattern=[[1, 128]], compare_op=ALU.not_equal,
                                       fill=0.0, base=0, channel_multiplier=-1))
        pool(lambda e: e.affine_select(out=blk[:], in_=onesf[:64, :64], pattern=[[-4, 16], [0, 4]], compare_op=ALU.is_ge,
                                       fill=0.0, base=0, channel_multiplier=1))
        pool(lambda e: e.affine_select(out=blk[:], in_=blk[:], pattern=[[4, 16], [0, 4]], compare_op=ALU.is_ge,
                                       fill=0.0, base=3, channel_multiplier=-1))
        pool(lambda e: e.tensor_tensor(out=triUs[:], in0=triU[:64, :64], in1=blk[:], op=ALU.mult))
        pool(lambda e: e.tensor_tensor(out=offds[:], in0=offd[:64, :64], in1=blk[:], op=ALU.mult))
        pool(lambda e: e.tensor_scalar(out=tmpc[:], in0=blk[:], scalar1=-NEG, scalar2=NEG, op0=ALU.mult, op1=ALU.add))
        pool(lambda e: e.tensor_tensor(out=mmins[:], in0=mmin[:64, :64], in1=blk[:], op=ALU.mult))
        pool(lambda e: e.tensor_tensor(out=mmins[:], in0=mmins[:], in1=tmpc[:], op=ALU.add))
        pool(lambda e: e.affine_select(out=bmask[:], in_=onesf[:64, :16], pattern=[[-4, 16]], compare_op=ALU.is_ge,
                                       fill=0.0, base=0, channel_multiplier=1))
        pool(lambda e: e.affine_select(out=bmask[:], in_=bmask[:], pattern=[[4, 16]], compare_op=ALU.is_ge,
                                       fill=0.0, base=3, channel_multiplier=-1))

        NXT = 5
        xt = [sb([128, D], F32, "xt%d" % i) for i in range(NXT)]; Bxt = [Buf("xt%d" % i) for i in range(NXT)]
        Bp = Buf("params")
        pstage = sb([40, 128], F32)
        pv = sb([128, 40], F32)
        wdw = sb([128, 8, 31], F32)
        wsh = sb([128, 24, 4], F32)
        ghead = sb([128, 1], F32)
        dtb = sb([128, 8], F32); nexpA = sb([128, 8], F32)
        gfin = sb([128, D], F32)
        for i, v in enumerate([g_mix, g_mlp, b_dw, ln_g, ln_b]):
            P.dma("sp", lambda e, i=i, v=v: e.dma_start(out=pstage[8 * i:8 * i + 8, :], in_=v.rearrange("(k p) -> k p", p=128)), "par", writes=[Bp])
        P.dma("sp", lambda e: e.dma_start(out=xt[0][0:31, :], in_=w_dw[:, :]), ("x", 0), writes=[Bxt[0]])
        for k in range(3):
            P.dma("sp", lambda e, k=k: e.dma_start(out=xt[1 + k][0:4, :], in_=w_short[:, k * 1024:(k + 1) * 1024]), ("x", 1 + k), writes=[Bxt[1 + k]])
        P.dma("sp", lambda e: e.dma_start(out=ghead[:], in_=g_head.rearrange("(p o) -> p o", o=1)), "par", writes=[Bp])
        P.dma("sp", lambda e: e.dma_start(out=dtb[:], in_=dt_bias.partition_broadcast(128)), "par", writes=[Bp])
        P.dma("sp", lambda e: e.dma_start(out=nexpA[:], in_=a_log.partition_broadcast(128)), "par", writes=[Bp])
        P.dma("sp", lambda e: e.dma_start(out=gfin[:], in_=g_final.partition_broadcast(128)), "par", writes=[Bp])
        act(nexpA[:], nexpA[:], AF.Exp, [Bp], [Bp])
        P.op("act", lambda e: e.mul(nexpA[:], nexpA[:], -1.0), [Bp], [Bp])
        pt, pb, _ = psum()
        tr(pt[:, 0:40], pstage[:, :], identf[:40, :40], [Bp, Bc], [pb])
        cp("dve", pv[:], pt[:, 0:40], [pb], [Bp])
        pt, pb, _ = psum()
        for m in range(8):
            tr(pt[:, m * 31:(m + 1) * 31], xt[0][0:31, m * 128:(m + 1) * 128], identf[:31, :31], [Bxt[0], Bc], [pb])
        cp("dve", wdw[:].rearrange("p m i -> p (m i)"), pt[:, 0:248], [pb], [Bp])
        pt, pb, _ = psum()
        for m in range(24):
            tr(pt[:, m * 4:(m + 1) * 4], xt[1 + m // 8][0:4, (m % 8) * 128:(m % 8 + 1) * 128], identf[:4, :4], [Bxt[1 + m // 8], Bc], [pb])
        cp("dve", wsh[:].rearrange("p m i -> p (m i)"), pt[:, 0:96], [pb], [Bp])
        gmix = pv[:, 0:8]; gmlp = pv[:, 8:16]; bdw = pv[:, 16:24]; lng = pv[:, 24:32]; lnb = pv[:, 32:40]

        P.ck("params")
        scr4 = sb([128, 2 * D], BF16, "scr4"); Bscr = Buf("scr4")
        hb = scr4[:, 0:D]; Bhb = Bscr
        junk = scr4[:, D:2 * D]; Bjunk = Bscr
        junk2 = sb([128, 128], BF16); Bjunk2 = Buf("junk2")
        ssx = sb([128, 4], F32); Bssx = Buf("ssx")
        hT = sb([128, 8, MAXC], BF16, "hT"); BhT = Buf("hT")
        NG = 2
        gext = [sb([128, 30 + MAXC], BF16, "gext%d" % i) for i in range(NG)]; Bgext = [Buf("gext%d" % i) for i in range(NG)]
        ghalo = sb([128, 8, 30], BF16); Bghalo = Buf("ghalo")
        qext = [sb([128, 3 + MAXC], BF16) for _ in range(NG)]; Bqext = [Buf("qext%d" % i) for i in range(NG)]
        qhalo = sb([128, 24, 3], BF16); Bqhalo = Buf("qhalo")
        glut = sb([128, 8, 30], F32, "glut"); Bglut = Buf("glut")
        qt = sb([128, 24, 3], F32); Bqt = Buf("qt")
        arena = sb([128, 32 * MAXC], BF16, "arena")

        def arena_views(mc):
            return (arena[:, 0:24 * mc].rearrange("p (m c) -> p m c", c=mc),
                    arena[:, 24 * mc:32 * mc].rearrange("p (m c) -> p m c", c=mc),
                    arena[:, 0:32 * mc].rearrange("p (m c) -> p m c", c=mc))
        BqkvT = Buf("qkvT"); Bcpre = Buf("cpre"); BmixT = Buf("mixT"); BaT = Buf("aT")
        if with_sample:
            o0 = 2048
            gexS = arena[:, o0:o0 + 4352].rearrange("p (m s t) -> p m s t", m=8, s=16); BgexS = Buf("gexS")
            o0 += 4352
            qexS = arena[:, o0:o0 + 2688].rearrange("p (m s t) -> p m s t", m=24, s=16); BqexS = Buf("qexS")
            o0 += 2688
            glutS = arena[:, o0:o0 + 1024].bitcast(F32).rearrange("p (m c) -> p m c", m=8); BglutS = Buf("glutS")
            o0 += 1024
            qtS = arena[:, o0:o0 + 3072].bitcast(F32).rearrange("p (m c) -> p m c", m=24); BqtS = Buf("qtS")
            o0 += 3072
            assert o0 <= 32 * MAXC
        f4 = [sb([128, 512], F32) for _ in range(2)]; Bf4 = [Buf("f4_%d" % i) for i in range(2)]
        f4i = [0]

        def ftmp():
            i = f4i[0] % 2
            f4i[0] += 1
            return f4[i], Bf4[i]

        cs2 = sb([128, 2, 512], BF16); Bcs2 = Buf("cs2")
        lnst = [sb([128, MAXC], F32) for _ in range(2)]; Blnst = Buf("lnst")
        decT = sb([128, 8, 128], F32); BdecT = Buf("decT")
        decTs = sb([128, 8, 128], F32); BdecTs = Buf("decTs")
        gm = decTs; Bgm = BdecTs
        EB = sb([128, 8, 128], F32); BEB = Buf("EB")
        cT = sb([128, 8, MAXC], BF16, "cT"); BcT = Buf("cT")
        zs = [sb([128, D], BF16) for _ in range(NXT)]; Bzs = [Buf("zs%d" % i) for i in range(NXT)]
        ba = sb([128, NXT, 16], F32); Bba = Buf("ba")
        onT = sb([128, 8, MAXC], BF16, "onT"); BonT = Buf("onT")
        wbuf = [sb([128, 8, 512], BF16) for _ in range(NW)]; Bw = [Buf("w%d" % i) for i in range(NW)]
        dgr = sb([128, 31 * 128], BF16, "dgr")
        dg31 = dgr[:, :].rearrange("p (i c) -> p i c", i=31); Bdg31q = [Buf("dg31q%d" % i) for i in range(4)]
        sqT = scr4[:, :].rearrange("p (j c) -> p j c", j=16); BsqT = Bscr
        dg4 = [sb([128, 4, 128], BF16) for _ in range(2)]; Bdg4 = [Buf("dg4a"), Buf("dg4b")]
        sc = sb([128, 64], F32, "sc"); sc2 = sb([128, 64], F32, "sc2"); Bsc = Buf("sc")
        Wp = [sb([128, 8, 128], BF16) for _ in range(2)]; BWp = [Buf("Wp0"), Buf("Wp1")]
        Qp = [sb([128, 8, 128], BF16) for _ in range(2)]; BQp = [Buf("Qp0"), Buf("Qp1")]
        Zp = [sb([128, 8, 128], BF16) for _ in range(2)]; BZp = [Buf("Zp0"), Buf("Zp1")]
        attnT = sb([128, 8, 128], BF16); BattnT = Buf("attnT")
        rhsk = sb([128, 8, 128], BF16); rhsv = sb([128, 8, 128], BF16); kdec = sb([128, 8, 128], BF16)
        Brhsk = Buf("rhsk"); Brhsv = Buf("rhsv"); Bkdec = Buf("kdec")
        nkcT = sb([128, 8, 128], BF16); BnkcT = Buf("nkcT")
        u = sb([128, 8, 128], BF16); Bu = Buf("u")
        qd = sb([128, 8, 128], BF16); Bqd = Buf("qd")
        on = sb([128, D], BF16, "on"); Bon = Buf("on")
        S = sb([128, 8, 128], F32); BS = Buf("S")
        Sbf = sb([128, 8, 128], BF16); BSbf = Buf("Sbf")
        ms = sb([128, 16], F32, "ms"); Bms = Buf("ms")
        yt = [sb([128, D], F32)]; Byt = [Buf("yt0")]
        stgT = yt[0]; BstgT = Byt[0]
        if with_sample:
            v3 = lambda a: a[:, :].rearrange("p (h c) -> p h c", h=8)
            S0 = [v3(xt[1]), v3(xt[2]), v3(yt[0])]; BS0 = [Bxt[1], Bxt[2], Byt[0]]
            S0key = [("x", 1), ("x", 2), ("yt", 0)]
            So = [v3(xt[3]), v3(xt[4])]; BSo = [Bxt[3], Bxt[4]]
            S0b = [v3(zs[1]), v3(zs[2])]; BS0b = [Bzs[1], Bzs[2]]
            um = [zs[3], zs[4]]; Bum = [Bzs[3], Bzs[4]]
            oTs = sb([128, 8, 64], F32, "oTs"); BoTs = Buf("oTs")
            uTs = sb([128, 8, 64], BF16, "uTs"); BuTs = Buf("uTs")
            stg = yt[0]; Bstg = Byt[0]
        Bout = {k: Buf("o_" + k) for k in ["y_p", "y_s", "p_conv", "p_qkv", "p_delta", "s_conv", "s_qkv", "s_delta"]}

        P.op("pool", lambda e: e.memset(ghalo[:], 0.0), [], [Bghalo])
        P.op("pool", lambda e: e.memset(qhalo[:], 0.0), [], [Bqhalo])
        P.op("pool", lambda e: e.memset(S[:], 0.0), [], [BS])
        P.op("pool", lambda e: e.memset(Sbf[:], 0.0), [], [BSbf])

        sched = wsched()
        allreq = sched * len(groups)
        wstate = {"next_load": 0, "next_use": 0}

        NT = len(sched)
        wsc = nc.dram_tensor("wsc", [NT, 128, 4096], BF16, kind="Internal").ap()
        Bwsc = [Buf("wsc%d" % j) for j in range(NT)]
        Bring = [Buf("ring%d" % j) for j in range(8)]
        pstate = {"next": 0}
        PLOOK = 6

        def wdims(j):
            kind, a, b = sched[j]
            if kind == "down":
                return w_down[a * 1024:(a + 1) * 1024, b * 512:(b + 1) * 512], 512
            return wmap[kind][:, a:a + b], b

        def pro_issue(upto):
            while pstate["next"] < min(NT, upto):
                j = pstate["next"]
                src, n = wdims(j)
                P.dma("pool", lambda e, j=j, src=src, n=n: e.dma_start(
                    out=wsc[j][:, 0:8 * n].rearrange("p (kc n) -> p kc n", n=n),
                    in_=src.rearrange("(kc p) n -> p kc n", p=128)), ("pro", j % 8), writes=[Bwsc[j], Bring[j % 8]])
                pstate["next"] += 1

        def w_issue(i):
            j = i % NT
            slot = i % NW
            pro_issue(j + PLOOK)
            _, n = wdims(j)
            P.dma("sp", lambda e, j=j, n=n, slot=slot: e.dma_start(
                out=wbuf[slot][:, :, 0:n], in_=wsc[j][:, 0:8 * n].rearrange("p (kc n) -> p kc n", n=n)),
                ("w", slot), reads=[Bwsc[j]], writes=[Bw[slot]])

        def wnext(tag):
            i = wstate["next_use"]
            assert allreq[i] == tag, (allreq[i], tag)
            while wstate["next_load"] < min(len(allreq), i + NW - 1):
                w_issue(wstate["next_load"])
                wstate["next_load"] += 1
            wstate["next_use"] += 1
            return wbuf[i % NW], Bw[i % NW]

        def wprefetch():
            i = wstate["next_use"]
            while wstate["next_load"] < min(len(allreq), i + NW - 1):
                w_issue(wstate["next_load"])
                wstate["next_load"] += 1

        def fm(wt, Bwt, mloc, src, Bsrc, blkc, KC=8, kc0=0, pool=None):
            c0, c1 = blkc
            pt, pb, _ = psum(pool)
            for kc in range(KC):
                mm(pt[:, 0:c1 - c0], wt[:, kc, mloc * 128:(mloc + 1) * 128], src[:, kc0 + kc, c0:c1], kc == 0, kc == KC - 1,
                   [Bwt, Bsrc], [pb])
            return pt, pb

        def norm_to_T(G, gvec):
            for t in G["tiles"]:
                n = t.n
                x = xt[t.xi]; Bx = Bxt[t.xi]
                act(junk[:n, :], x[:n, :], AF.Square, [Bx], [Bjunk, Bssx], accum_out=ssx[:n, 0:1])
                act(ssx[:n, 1:2], ssx[:n, 0:1], AF.Sqrt, [Bssx], [Bssx], scale=1.0 / D, bias=EPS)
                P.op("dve", lambda e, n=n: e.reciprocal(out=ssx[:n, 2:3], in_=ssx[:n, 1:2]), [Bssx], [Bssx])
                ts("dve", hb[:n, :], x[:n, :], ssx[:n, 2:3], None, ALU.mult, None, [Bx, Bssx], [Bhb])
                pt, pb, _ = psum()
                pv_ = bf(pt).rearrange("p (k c) -> p k c", k=8)
                for kc in range(8):
                    tr(pv_[:, kc, 0:n], hb[:n, kc * 128:(kc + 1) * 128], ident[:n, :n], [Bhb, Bc], [pb])
                tt("dve", hT[:, :, t.col:t.col + n], pv_[:, :, 0:n], gvec.unsqueeze(2).broadcast_to([128, 8, n]), ALU.mult,
                   [pb, Bp], [BhT])

        def tail_out(src, Bsrc, nm, ncols, nrows, dst_fn, key):
            for m0 in range(0, nm, 4):
                pt, pb, _ = psum()
                for mm_ in range(4):
                    tr(pt[:ncols, mm_ * 128:(mm_ + 1) * 128], src[:, m0 + mm_, :], identf[:, :], [Bsrc, Bc], [pb])
                cp("dve", stgT[:ncols, (m0 % 8) * 128:(m0 % 8 + 4) * 128], pt[:ncols, :], [pb], [BstgT])
                if (m0 + 4) % 8 == 0:
                    dst_fn(m0 // 8)

        def emit_sample_tails():
            def sconv_out(k):
                for s_ in range(16):
                    P.dma("sp", lambda e, s_=s_: e.dma_start(out=o_sconv[s_, 26:30, :], in_=stgT[4 * s_:4 * s_ + 4, :]), ("yt", 0),
                          reads=[BstgT])
            tail_out(glutS, BglutS, 8, 64, 64, sconv_out, None)

            def sqkv_out(k):
                for s_ in range(16):
                    P.dma("sp", lambda e, s_=s_, k=k: e.dma_start(out=o_sqkv[s_, :, k * 1024:(k + 1) * 1024], in_=stgT[4 * s_ + 1:4 * s_ + 4, :]), ("yt", 0),
                          reads=[BstgT])
            tail_out(qtS, BqtS, 24, 64, 64, sqkv_out, None)

        xloaded = set()

        def load_x(gi_, t):
            if (gi_, t.xi) in xloaded:
                return
            xloaded.add((gi_, t.xi))
            x = xt[t.xi]; Bx = Bxt[t.xi]
            if t.kind == "S":
                P.dma("sp", lambda e, x=x: e.dma_start(out=x[0:64, :], in_=x_s[:, :]), ("x", t.xi), writes=[Bx])
            elif t.t0 == 0:
                P.dma("sp", lambda e, x=x: e.dma_start(out=x[0:16, :], in_=meta[:, :]), ("x", t.xi), writes=[Bx])
                P.dma("sp", lambda e, x=x: e.dma_start(out=x[16:128, :], in_=x_p[0:112, :]), ("x", t.xi), writes=[Bx])
            else:
                P.dma("sp", lambda e, x=x, t=t: e.dma_start(out=x[0:t.n, :], in_=x_p[t.t0 - 16:t.t0 - 16 + t.n, :]), ("x", t.xi), writes=[Bx])

        for gi, G in enumerate(groups):
            tiles = G["tiles"]; blocks = G["blocks"]; pc0 = G["pc0"]; npc = G["npc"]
            isSG = (tiles[0].kind == "S")
            sblock = (0, 64) if isSG else None
            qkvT, cpre, aT = arena_views(64 if isSG else MAXC)
            mixT = cpre
            for t in tiles:
                load_x(gi, t)
            if isSG:
                P.handover([BaT, BqkvT, BmixT, Bcpre], [BgexS, BqexS, BglutS, BqtS, BaT])
                for sg in range(4):
                    P.dma("sp", lambda e, sg=sg: e.dma_start(out=stg[0:120, :], in_=st_conv[4 * sg:4 * sg + 4].rearrange("s t c -> (s t) c")),
                          ("yt", 0), writes=[Bstg])
                    pts = []
                    for half in range(2):
                        pt, pb, _ = psum()
                        for mm_ in range(4):
                            m = half * 4 + mm_
                            tr(pt[:, mm_ * 120:(mm_ + 1) * 120], stg[0:120, m * 128:(m + 1) * 128], identf[:120, :120], [Bstg, Bc], [pb])
                        act(gexS[:, half * 4:half * 4 + 4, 4 * sg:4 * sg + 4, 0:30],
                            pt[:, 0:480].rearrange("p (m s t) -> p m s t", m=4, s=4), AF.Copy, [pb], [BgexS])
                for hf in range(3):
                    P.dma("sp", lambda e, hf=hf: e.dma_start(out=stg[0:48, :], in_=st_qkv.rearrange("s t c -> (s t) c")[:, hf * 1024:(hf + 1) * 1024]),
                          ("yt", 0), writes=[Bstg])
                    for half in range(2):
                        pt, pb, _ = psum()
                        for mm_ in range(4):
                            m = half * 4 + mm_
                            tr(pt[:, mm_ * 48:(mm_ + 1) * 48], stg[0:48, m * 128:(m + 1) * 128], identf[:48, :48], [Bstg, Bc], [pb])
                        act(qexS[:, hf * 8 + half * 4:hf * 8 + half * 4 + 4, :, 0:3],
                            pt[:, 0:192].rearrange("p (m s t) -> p m s t", m=4, s=16), AF.Copy, [pb], [BqexS])
                P.dma("sp", lambda e: e.dma_start(out=o_sconv[:, 0:26, :], in_=st_conv[:, 4:30, :]), "o_s_conv")
            P.ck("st")
            wprefetch()
            norm_to_T(G, gmix)
            P.ck("p0")

            P.handover([BaT], [BqkvT, Bcpre])
            QT = [(0, 8), (8, 16), (16, 24), (24, 31)]
            wts = {}

            def build31(m):
                for qi, (i0, i1) in enumerate(QT):
                    tt("pool", dg31[:, i0:i1, :], ident[:].unsqueeze(1).broadcast_to([128, i1 - i0, 128]),
                       wdw[:, m, i0:i1].unsqueeze(2).broadcast_to([128, i1 - i0, 128]), ALU.mult, [Bc, Bp], [Bdg31q[qi]])

            def stageA(m):
                q, ml = m // 4, m % 4
                if ml == 0:
                    wts[q] = (wnext(("in", q * 512, 512)), wnext(("in", 1024 + q * 512, 512)))
                (wa, Bwa), (wb_, Bwb) = wts[q]
                gx = gext[m % NG]; Bgx = Bgext[m % NG]
                if npc:
                    cp("pool", gx[:, 0:30], ghalo[:, m, :], [Bghalo], [Bgx])
                for bk in blocks:
                    c0, c1 = bk; n = c1 - c0
                    pa, pba = fm(wa, Bwa, ml, hT, BhT, bk, pool="d")
                    yield
                    pb2, pbb = fm(wb_, Bwb, ml, hT, BhT, bk, pool="d")
                    sg_, Bsg = ftmp()
                    act(sg_[:, 0:n], pb2[:, 0:n], AF.Sigmoid, [pbb], [Bsg])
                    if bk == sblock:
                        tt("dve", gexS[:, m, :, 30:34], pa[:, 0:64].rearrange("p (s t) -> p s t", t=4),
                           sg_[:, 0:64].rearrange("p (s t) -> p s t", t=4), ALU.mult, [pba, Bsg], [BgexS])
                        tt("dve", glutS[:, m, :], pa[:, 0:64], sg_[:, 0:64], ALU.mult, [pba, Bsg], [BglutS])
                    else:
                        tt("dve", gx[:, 30 + c0 - pc0:30 + c1 - pc0], pa[:, 0:n], sg_[:, 0:n], ALU.mult, [pba, Bsg], [Bgx])
                        if G["last"] and c1 == G["ncol"]:
                            lo = max(c0, G["ncol"] - 30)
                            tt("dve", glut[:, m, 30 - (c1 - lo):30], pa[:, lo - c0:n], sg_[:, lo - c0:n], ALU.mult, [pba, Bsg], [Bglut])
                        elif G["last"] and c1 > G["ncol"] - 30:
                            lo = max(c0, G["ncol"] - 30)
                            off = lo - (G["ncol"] - 30)
                            tt("dve", glut[:, m, off:off + c1 - lo], pa[:, lo - c0:n], sg_[:, lo - c0:n], ALU.mult, [pba, Bsg], [Bglut])
                    yield
                if npc:
                    cp("pool", ghalo[:, m, :], gx[:, npc:npc + 30], [Bgx], [Bghalo])

            def stageB(m):
                gx = gext[m % NG]; Bgx = Bgext[m % NG]
                for bi, bk in enumerate(blocks):
                    c0, c1 = bk; n = c1 - c0
                    pt, pb, _ = psum("d")
                    for i in range(31):
                        Bq = Bdg31q[[qi for qi, (i0, i1) in enumerate(QT) if i0 <= i < i1][0]]
                        if bk == sblock:
                            mm(pt[:, 0:64].rearrange("p (s t) -> p s t", t=4), dg31[:, i, :], gexS[:, m, :, i:i + 4], i == 0, i == 30,
                               [Bq, BgexS], [pb])
                        else:
                            mm(pt[:, 0:n], dg31[:, i, :], gx[:, c0 - pc0 + i:c1 - pc0 + i], i == 0, i == 30, [Bq, Bgx], [pb])
                        if i in (7, 15, 23):
                            yield
                    act(cpre[:, m, c0:c1], pt[:, 0:n], AF.Identity, [pb], [Bcpre], bias=bdw[:, m:m + 1])
                    yield

            def conv_gen():
                yield from stageA(0)
                for m in range(8):
                    build31(m)
                    if m + 1 < 8:
                        yield from stageA(m + 1)
                    yield from stageB(m)
                for bk in blocks:
                    c0, c1 = bk; n = c1 - c0
                    if n > 64:
                        sA = psum("d"); sB = psum("d")
                    else:
                        sA = psum("d"); sB = None
                    for m in range(8):
                        act(cs2[:, 1, 0:n], cpre[:, m, c0:c1], AF.Square, [Bcpre], [Bcs2])
                        if sB is not None:
                            mm(sA[0][:, 0:n], onesb[:, :], cpre[:, m, c0:c1], m == 0, m == 7, [Bc, Bcpre], [sA[1]])
                            mm(sB[0][:, 0:n], onesb[:, :], cs2[:, 1, 0:n], m == 0, m == 7, [Bc, Bcs2], [sB[1]])
                        else:
                            cp("pool", cs2[:, 0, 0:n], cpre[:, m, c0:c1], [Bcpre], [Bcs2])
                            mm(sA[0][:, 0:2 * n].rearrange("p (a n) -> p a n", a=2), onesb[:, :], cs2[:, :, 0:n], m == 0, m == 7,
                               [Bc, Bcs2], [sA[1]])
                        yield
                    if sB is not None:
                        psum_s, Bs1 = sA[0][:, 0:n], sA[1]
                        psum_q, Bs2 = sB[0][:, 0:n], sB[1]
                    else:
                        psum_s, Bs1 = sA[0][:, 0:n], sA[1]
                        psum_q, Bs2 = sA[0][:, n:2 * n], sA[1]
                    mean, Bmean = ftmp(); var, Bvar = ftmp()
                    msq = lnst[1][:, c0:c1]
                    act(mean[:, 0:n], psum_s, AF.Copy, [Bs1], [Bmean], scale=1.0 / D)
                    tt("pool", msq, mean[:, 0:n], mean[:, 0:n], ALU.mult, [Bmean], [Blnst])
                    stt(var[:, 0:n], psum_q, 1.0 / D, msq, ALU.mult, ALU.subtract, [Bs2, Blnst], [Bvar])
                    act(var[:, 0:n], var[:, 0:n], AF.Ln, [Bvar], [Bvar], bias=EPS)
                    act(lnst[0][:, c0:c1], var[:, 0:n], AF.Exp, [Bvar], [Blnst], scale=-0.5)
                    stt(lnst[1][:, c0:c1], mean[:, 0:n], -1.0, lnst[0][:, c0:c1], ALU.mult, ALU.mult, [Bmean, Blnst], [Blnst])
                    yield
                for m in range(8):
                    for bk in blocks:
                        c0, c1 = bk; n = c1 - c0
                        t1, Bt1 = ftmp(); t2, Bt2 = ftmp()
                        tt("dve", t1[:, 0:n], cpre[:, m, c0:c1], lnst[0][:, c0:c1], ALU.mult, [Bcpre, Blnst], [Bt1])
                        tt("pool", t2[:, 0:n], t1[:, 0:n], lnst[1][:, c0:c1], ALU.add, [Bt1, Blnst], [Bt2])
                        act(cT[:, m, c0:c1], t2[:, 0:n], AF.Silu, [Bt2, Bp], [BcT], scale=lng[:, m:m + 1], bias=lnb[:, m:m + 1])
                        yield
                P.handover([Bcpre], [BmixT])
                for q in range(2):
                    wc, Bwc = wnext(("cout", q * 512, 512))
                    wg, Bwg = wnext(("in", 6160 + q * 512, 512))
                    for ml in range(4):
                        m = q * 4 + ml
                        for bk in blocks:
                            c0, c1 = bk; n = c1 - c0
                            pc_, pbc = fm(wc, Bwc, ml, cT, BcT, bk, pool="d")
                            yield
                            pg, pbg = fm(wg, Bwg, ml, hT, BhT, bk, pool="d")
                            sg_, Bsg = ftmp()
                            act(sg_[:, 0:n], pg[:, 0:n], AF.Sigmoid, [pbg], [Bsg])
                            tt("dve", mixT[:, m, c0:c1], pc_[:, 0:n], sg_[:, 0:n], ALU.mult, [pbc, Bsg], [BmixT])
                            yield

            P.ck("p3")
            wq4 = {}

            def stage4A(m):
                j, ml = m // 4, m % 4
                if ml == 0:
                    wq4[j] = wnext(("in", 2048 + 512 * j, 512))
                wq, Bwq = wq4[j]
                qx = qext[m % NG]; Bqx = Bqext[m % NG]
                d4 = dg4[m % 2]; Bd4 = Bdg4[m % 2]
                tt("pool", d4[:], ident[:].unsqueeze(1).broadcast_to([128, 4, 128]),
                   wsh[:, m, :].unsqueeze(2).broadcast_to([128, 4, 128]), ALU.mult, [Bc, Bp], [Bd4])
                if npc:
                    cp("pool", qx[:, 0:3], qhalo[:, m, :], [Bqhalo], [Bqx])
                for bk in blocks:
                    c0, c1 = bk; n = c1 - c0
                    pq, pbq = fm(wq, Bwq, ml, hT, BhT, bk)
                    if bk == sblock:
                        act(qexS[:, m, :, 3:7], pq[:, 0:64].rearrange("p (s t) -> p s t", t=4), AF.Copy, [pbq], [BqexS])
                        act(qtS[:, m, :], pq[:, 0:64], AF.Copy, [pbq], [BqtS])
                    else:
                        act(qx[:, 3 + c0 - pc0:3 + c1 - pc0], pq[:, 0:n], AF.Copy, [pbq], [Bqx])
                        if G["last"] and c1 == G["ncol"]:
                            assert n >= 3
                            act(qt[:, m, :], pq[:, n - 3:n], AF.Copy, [pbq], [Bqt])
                if npc:
                    cp("pool", qhalo[:, m, :], qx[:, npc:npc + 3], [Bqx], [Bqhalo])

            def stage4B(m):
                qx = qext[m % NG]; Bqx = Bqext[m % NG]
                d4 = dg4[m % 2]; Bd4 = Bdg4[m % 2]
                for bk in blocks:
                    c0, c1 = bk; n = c1 - c0
                    pt, pb, _ = psum()
                    for i in range(4):
                        if bk == sblock:
                            mm(pt[:, 0:64].rearrange("p (s t) -> p s t", t=4), d4[:, i, :], qexS[:, m, :, i:i + 4], i == 0, i == 3,
                               [Bd4, BqexS], [pb])
                        else:
                            mm(pt[:, 0:n], d4[:, i, :], qx[:, c0 - pc0 + i:c1 - pc0 + i], i == 0, i == 3, [Bd4, Bqx], [pb])
                    act(qkvT[:, m, c0:c1], pt[:, 0:n], AF.Silu, [pb], [BqkvT])

            stage4A(0)
            for m in range(24):
                if m + 1 < 24:
                    stage4A(m + 1)
                stage4B(m)
            P.ck("p4")
            for nb in range(2):
                wz, Bwz = wnext(("in", 5120 + nb * 512, 512))
                for t in tiles:
                    pt, pb, _ = psum()
                    for kc in range(8):
                        mm(pt[:t.n, :], hT[:, kc, t.col:t.col + t.n], wz[:, kc, :], kc == 0, kc == 7, [BhT, Bwz], [pb])
                    act(zs[t.xi][:t.n, nb * 512:(nb + 1) * 512], pt[:t.n, :], AF.Silu, [pb], [Bzs[t.xi]])
            wba, Bwba = wnext(("in", 6144, 16))
            for t in tiles:
                pt, pb, _ = psum()
                for kc in range(8):
                    mm(pt[:t.n, 0:16], hT[:, kc, t.col:t.col + t.n], wba[:, kc, 0:16], kc == 0, kc == 7, [BhT, Bwba], [pb])
                act(ba[:t.n, t.xi, :], pt[:t.n, 0:16], AF.Copy, [pb], [Bba])
            wprefetch()

            P.ck("zba")
            cg = conv_gen()

            _fk = 1

            _fm = 2
            _fc = [0]

            def fill(k=1):
                _fc[0] += 1
                if _fc[0] % _fm:
                    return
                for _ in range(k * _fk):
                    next(cg, None)
            for t in tiles:
                n = t.n; col = t.col; isS = (t.kind == "S")
                tU = triUs if isS else triU
                tM = mmins if isS else mmin
                tO = offds if isS else offd
                tB = blk if isS else onesf
                qTt = lambda h: qkvT[:, h, col:col + n]
                kTt = lambda h: qkvT[:, 8 + h, col:col + n]
                vTt = lambda h: qkvT[:, 16 + h, col:col + n]
                tt("dve", sqT[:, :, 0:n], qkvT[:, 0:16, col:col + n], qkvT[:, 0:16, col:col + n], ALU.mult, [BqkvT], [BsqT])
                pt, pb, _ = psum("g")
                for j in range(16):
                    mm(pt[:n, j:j + 1], sqT[:, j, 0:n], onesb[:, 0:1], True, True, [BsqT, Bc], [pb])
                ts("dve", sc[:n, 0:16], pt[:n, 0:16], 1e-6, None, ALU.add, None, [pb], [Bsc])
                pass
                act(sc[:n, 24:32], sc[:n, 8:16], AF.Ln, [Bsc], [Bsc])
                act(sc[:n, 16:24], sc[:n, 24:32], AF.Exp, [Bsc], [Bsc], scale=0.5)
                act(sc[:n, 24:32], sc[:n, 24:32], AF.Exp, [Bsc], [Bsc], scale=-0.5)
                act(sc[:n, 32:40], ba[:n, t.xi, 0:8], AF.Exp, [Bba], [Bsc], scale=-1.0)
                ts("dve", sc[:n, 32:40], sc[:n, 32:40], 1.0, None, ALU.add, None, [Bsc], [Bsc])
                P.op("dve", lambda e, n=n: e.reciprocal(out=sc[:n, 32:40], in_=sc[:n, 32:40]), [Bsc], [Bsc])
                tt("dve", sc[:n, 40:48], ba[:n, t.xi, 8:16], dtb[:n, :], ALU.add, [Bba, Bp], [Bsc])
                stt(sc[:n, 48:56], sc[:n, 40:48], -1.0, sc[:n, 40:48], ALU.mult, ALU.max, [Bsc], [Bsc])
                act(sc[:n, 48:56], sc[:n, 48:56], AF.Exp, [Bsc], [Bsc], scale=-1.0)
                act(sc[:n, 48:56], sc[:n, 48:56], AF.Ln, [Bsc], [Bsc], bias=1.0)
                stt(sc[:n, 40:48], sc[:n, 40:48], 0.0, sc[:n, 48:56], ALU.max, ALU.add, [Bsc], [Bsc])
                tt("dve", sc[:n, 56:64], sc[:n, 40:48], nexpA[:n, :], ALU.mult, [Bsc, Bp], [Bsc])
                tt("dve", sc2[:n, 0:8], sc[:n, 32:40], sc[:n, 24:32], ALU.mult, [Bsc], [Bsc])
                stt(sc2[:n, 8:16], sc2[:n, 0:8], -1.0, sc[:n, 24:32], ALU.mult, ALU.mult, [Bsc], [Bsc])
                pt, pb, _ = psum("g")
                mm(pt[:n, 0:8], tU[:n, :n], sc[:n, 56:64], True, True, [Bc, Bsc], [pb])
                mm(pt[:n, 8:16], tB[:n, :n], sc[:n, 56:64], True, True, [Bc, Bsc], [pb])
                cp("dve", sc2[:n, 16:24], pt[:n, 0:8], [pb], [Bsc])
                pass
                act(sc2[:n, 24:32], pt[:n, 0:8], AF.Exp, [pb], [Bsc])
                tt("dve", sc2[:n, 32:40], pt[:n, 8:16], sc2[:n, 16:24], ALU.subtract, [pb, Bsc], [Bsc])
                act(sc2[:n, 32:40], sc2[:n, 32:40], AF.Exp, [Bsc], [Bsc])
                tt("dve", sc2[:n, 40:48], sc[:n, 24:32], sc2[:n, 32:40], ALU.mult, [Bsc], [Bsc])
                P.ck("g1")
                tt("dve", gm[:n, :, 0:n], tU[:n, :n].unsqueeze(1).broadcast_to([n, 8, n]),
                   sc[:n, 56:64].unsqueeze(2).broadcast_to([n, 8, n]), ALU.mult, [Bc, Bsc], [Bgm])
                gbs = []
                for hb_ in range(2):
                    pt, pb, _ = psum("g")
                    mm(pt[:, 0:4 * n].rearrange("p (h n) -> p h n", h=4), onesf[:n, :], gm[:n, hb_ * 4:hb_ * 4 + 4, 0:n], True, True,
                       [Bc, Bgm], [pb])
                    gbs.append((pt, pb))
                    pass
                for h in range(8):
                    pt, pb = gbs[h // 4]
                    stt(decT[:n, h, 0:n], pt[:n, (h % 4) * n:(h % 4 + 1) * n], sc2[:n, 16 + h:17 + h], tM[:n, :n], ALU.subtract, ALU.min,
                        [pb, Bsc, Bc], [BdecT])
                for hb_ in range(2):
                    pt, pb = gbs[hb_]
                    act(EB[:, hb_ * 4:hb_ * 4 + 4, 0:n], pt[:, 0:4 * n].rearrange("p (h n) -> p h n", h=4), AF.Exp, [pb], [BEB])
                act(decT[:n, :, 0:n], decT[:n, :, 0:n], AF.Exp, [BdecT], [BdecT])
                tt("dve", decTs[:n, :, 0:n], decT[:n, :, 0:n], tO[:n, :n].unsqueeze(1).broadcast_to([n, 8, n]), ALU.mult,
                   [BdecT, Bc], [BdecTs])
                P.ck("g2")
                W0, BW0 = Wp[0], BWp[0]
                for hb_ in range(2):
                    pk, pbk, _ = psum("g")
                    pq, pbq, _ = psum("g")
                    for hh in range(4):
                        h = hb_ * 4 + hh
                        mm(pk[:n, hh * 128:hh * 128 + n], kTt(h), kTt(h), True, True, [BqkvT], [pbk])
                        mm(pq[:n, hh * 128:hh * 128 + n], kTt(h), qTt(h), True, True, [BqkvT], [pbq])
                    for hh in range(4):
                        h = hb_ * 4 + hh
                        stt(W0[:n, h, 0:n], pk[:n, hh * 128:hh * 128 + n], sc2[:n, 8 + h:9 + h], decTs[:n, h, 0:n], ALU.mult, ALU.mult,
                            [pbk, Bsc, BdecTs], [BW0])
                        stt(attnT[:n, h, 0:n], pq[:n, hh * 128:hh * 128 + n], sc[:n, 24 + h:25 + h], decT[:n, h, 0:n], ALU.mult, ALU.mult,
                            [pbq, Bsc, BdecT], [BattnT])
                tt("dve", Zp[0][:n, :, 0:n], W0[:n, :, 0:n], ident[:n, :n].unsqueeze(1).broadcast_to([n, 8, n]), ALU.add, [BW0, Bc], [BZp[0]])
                pt, pb, _ = psum("g")
                pvw = bf(pt).rearrange("p (h c) -> p h c", h=8)
                for h in range(8):
                    tr(pvw[:n, h, 0:n], W0[:n, h, 0:n], ident[:n, :n], [BW0, Bc], [pb])
                act(Qp[0][:n, :, 0:n], pvw[:n, :, 0:n], AF.Copy, [pb], [BQp[0]])
                fill()
                cw = cq = cz = 0
                for lev in range(1, t.nlev + 1):
                    nq = 1 - cq
                    for hb_ in range(2):
                        pt, pb, _ = psum("g")
                        for hh in range(4):
                            h = hb_ * 4 + hh
                            mm(pt[:n, hh * 128:hh * 128 + n], Wp[cw][:n, h, 0:n], Qp[cq][:n, h, 0:n], True, True, [BWp[cw], BQp[cq]], [pb])
                        act(Qp[nq][:n, hb_ * 4:hb_ * 4 + 4, 0:n], pt[:n, :].rearrange("p (h c) -> p h c", h=4)[:, :, 0:n], AF.Copy,
                            [pb], [BQp[nq]])
                        fill()
                    if lev < t.nlev:
                        nw = 1 - cw
                        for hb_ in range(2):
                            pt, pb, _ = psum("g")
                            for hh in range(4):
                                h = hb_ * 4 + hh
                                mm(pt[:n, hh * 128:hh * 128 + n], Qp[cq][:n, h, 0:n], Wp[cw][:n, h, 0:n], True, True, [BWp[cw], BQp[cq]], [pb])
                            act(Wp[nw][:n, hb_ * 4:hb_ * 4 + 4, 0:n], pt[:n, :].rearrange("p (h c) -> p h c", h=4)[:, :, 0:n], AF.Copy,
                                [pb], [BWp[nw]])
                            fill()
                        cw = nw
                    cq = nq
                    nz = 1 - cz
                    for hb_ in range(2):
                        pt, pb, _ = psum("g")
                        for hh in range(4):
                            h = hb_ * 4 + hh
                            mm(pt[:n, hh * 128:hh * 128 + n], Qp[cq][:n, h, 0:n], Zp[cz][:n, h, 0:n], True, True, [BQp[cq], BZp[cz]], [pb])
                        tt("dve", Zp[nz][:n, hb_ * 4:hb_ * 4 + 4, 0:n], pt[:n, :].rearrange("p (h c) -> p h c", h=4)[:, :, 0:n],
                           Zp[cz][:n, hb_ * 4:hb_ * 4 + 4, 0:n], ALU.add, [pb, BZp[cz]], [BZp[nz]])
                        fill()
                    cz = nz
                Z = Zp[cz]; BZ = BZp[cz]
                P.ck("g4")
                pk, pbk, _ = psum("g"); pvk = bf(pk).rearrange("p (h c) -> p h c", h=8)
                pv2, pbv, _ = psum("g"); pvv = bf(pv2).rearrange("p (h c) -> p h c", h=8)
                for h in range(8):
                    tr(pvk[:n, h, :], kTt(h), ident[:, :], [BqkvT, Bc], [pbk])
                    tr(pvv[:n, h, :], vTt(h), ident[:, :], [BqkvT, Bc], [pbv])
                P.ck("g4a")
                bc = lambda colap: colap.unsqueeze(2).broadcast_to([n, 8, 128])
                tt("dve", rhsk[:n, :, :], pvk[:n, :, :], bc(sc2[:n, 24:32]), ALU.mult, [pbk, Bsc], [Brhsk])
                tt("dve", kdec[:n, :, :], pvk[:n, :, :], bc(sc2[:n, 40:48]), ALU.mult, [pbk, Bsc], [Bkdec])
                tt("dve", rhsv[:n, :, :], pvv[:n, :, :], bc(sc[:n, 16:24]), ALU.mult, [pbv, Bsc], [Brhsv])
                fill()
                P.ck("g4b")
                tt("pool", qd[:, :, 0:n], qkvT[:, 0:8, col:col + n], EB[:, :, 0:n], ALU.mult, [BqkvT, BEB], [Bqd])
                P.ck("g5")
                for hb_ in range(2):
                    pt, pb, _ = psum("g")
                    for hh in range(4):
                        h = hb_ * 4 + hh
                        mm(pt[:, hh * 128:hh * 128 + n], rhsk[:n, h, :], Z[:n, h, 0:n], True, True, [Brhsk, BZ], [pb])
                    P.op("act", lambda e, pt=pt, hb_=hb_, n=n: e.mul(nkcT[:, hb_ * 4:hb_ * 4 + 4, 0:n],
                                                                     pt[:, :].rearrange("p (h c) -> p h c", h=4)[:, :, 0:n], -1.0), [pb], [BnkcT])
                opsum = []
                if not isS:
                    for hb_ in range(2):
                        pu, pbu, _ = psum("g")
                        for hh in range(4):
                            h = hb_ * 4 + hh
                            mm(pu[:n, hh * 128:(hh + 1) * 128], Z[:n, h, 0:n], rhsv[:n, h, :], True, False, [BZ, Brhsv], [pbu])
                            mm(pu[:n, hh * 128:(hh + 1) * 128], nkcT[:, h, 0:n], Sbf[:, h, :], False, True, [BnkcT, BSbf], [pbu])
                        tt("dve", u[:n, hb_ * 4:hb_ * 4 + 4, :], pu[:n, :].rearrange("p (h e) -> p h e", h=4),
                           sc2[:n, hb_ * 4:hb_ * 4 + 4].unsqueeze(2).broadcast_to([n, 4, 128]), ALU.mult, [pbu, Bsc], [Bu])
                        fill()
                    for hb_ in range(2):
                        po, pbo, _ = psum("g")
                        for hh in range(4):
                            h = hb_ * 4 + hh
                            mm(po[:n, hh * 128:(hh + 1) * 128], qd[:, h, 0:n], Sbf[:, h, :], True, False, [Bqd, BSbf], [pbo])
                            mm(po[:n, hh * 128:(hh + 1) * 128], attnT[:n, h, 0:n], u[:n, h, :], False, True, [BattnT, Bu], [pbo])
                        opsum.append((po, pbo))
                        fill()
                    for hb_ in range(2):
                        pS, pbS, _ = psum("g")
                        for hh in range(4):
                            h = hb_ * 4 + hh
                            mm(pS[:, hh * 128:(hh + 1) * 128], kdec[:n, h, :], u[:n, h, :], True, True, [Bkdec, Bu], [pbS])
                        for hh in range(4):
                            h = hb_ * 4 + hh
                            stt(S[:, h, :], S[:, h, :], EB[:, h, n - 1:n], pS[:, hh * 128:(hh + 1) * 128], ALU.mult, ALU.add,
                                [BS, BEB, pbS], [BS])
                        act(Sbf[:, hb_ * 4:hb_ * 4 + 4, :], S[:, hb_ * 4:hb_ * 4 + 4, :], AF.Copy, [BS], [BSbf])
                        fill()
                else:
                    puT, pbuT, _ = psum("g"); puTv = puT[:, :].rearrange("p (h c) -> p h c", h=8)
                    poT, pboT, _ = psum("g"); poTv = poT[:, :].rearrange("p (h c) -> p h c", h=8)
                    for h in range(8):
                        mm(puTv[:, h, :], rhsv[:n, h, :], Z[:n, h, 0:n], h == 0, False, [Brhsv, BZ], [pbuT], skip=True)
                    for s in range(16):
                        s0 = S0[s % 3]; Bs0 = BS0[s % 3]; s0b = S0b[s % 2]; Bs0b = BS0b[s % 2]
                        P.dma("sp", lambda e, s=s, s0=s0: e.dma_start(out=s0[:], in_=st_delta[s].rearrange("h d e -> d h e")), S0key[s % 3], writes=[Bs0])
                        act(s0b[:], s0[:], AF.Copy, [Bs0], [Bs0b])
                        for h in range(8):
                            mm(puTv[:, h, 4 * s:4 * s + 4], s0b[:, h, :], nkcT[:, h, 4 * s:4 * s + 4], False, False, [Bs0b, BnkcT], [pbuT], skip=True)
                            mm(poTv[:, h, 4 * s:4 * s + 4], s0b[:, h, :], qd[:, h, 4 * s:4 * s + 4], (s == 0 and h == 0), False, [Bs0b, Bqd], [pboT], skip=True)
                    act(uTs[:], puTv, AF.Copy, [pbuT], [BuTs])
                    pt, pb, _ = psum("g"); ptv = bf(pt).rearrange("p (h c) -> p h c", h=8)
                    for h in range(8):
                        tr(ptv[:n, h, :], uTs[:, h, :], ident[:, :], [BuTs, Bc], [pb])
                    for h in range(8):
                        ts("dve", u[:n, h, :], ptv[:n, h, :], sc2[:n, h:h + 1], None, ALU.mult, None, [pb, Bsc], [Bu])
                    for h in range(8):
                        mm(poTv[:, h, :], u[:n, h, :], attnT[:n, h, 0:n], False, h == 7, [Bu, BattnT], [pboT], skip=True)
                    act(oTs[:], poTv, AF.Copy, [pboT], [BoTs])
                    for hb_ in range(2):
                        po, pbo, _ = psum("g")
                        for hh in range(4):
                            tr(po[:n, hh * 128:(hh + 1) * 128], oTs[:, hb_ * 4 + hh, :], identf[:, :], [BoTs, Bc], [pbo])
                        opsum.append((po, pbo))
                        pass
                    def sample_state_update(n=n):
                        for s in range(16):
                            s0 = S0[(s + 1) % 3]; Bs0 = BS0[(s + 1) % 3]
                            P.dma("sp", lambda e, s=s, s0=s0: e.dma_start(out=s0[:], in_=st_delta[s].rearrange("h d e -> d h e")), S0key[(s + 1) % 3], writes=[Bs0])
                            ums = um[s % 2]; Bums = Bum[s % 2]
                            act(ums[:n, :], u[:n, :, :].rearrange("p h e -> p (h e)"), AF.Identity, [Bu, Bc], [Bums], scale=bmask[:n, s:s + 1])
                            so = So[s % 2]; Bso = BSo[s % 2]
                            for hb_ in range(2):
                                pS, pbS, _ = psum("g")
                                for hh in range(4):
                                    h = hb_ * 4 + hh
                                    mm(pS[:, hh * 128:(hh + 1) * 128], kdec[:n, h, :], ums[:n, h * 128:(h + 1) * 128], True, True, [Bkdec, Bums], [pbS])
                                for hh in range(4):
                                    h = hb_ * 4 + hh
                                    stt(so[:, h, :], s0[:, h, :], EB[:, h, 4 * s + 3:4 * s + 4], pS[:, hh * 128:(hh + 1) * 128], ALU.mult, ALU.add,
                                        [Bs0, BEB, pbS], [Bso])
                            P.dma("pool", lambda e, s=s, so=so: e.dma_start(out=o_sdelta[s].rearrange("h d e -> d h e"), in_=so[:]), ("xp", 3 + s % 2),
                                  reads=[Bso])
                P.ck("g6")
                for h in range(8):
                    po, pbo = opsum[h // 4]
                    act(junk2[:n, :], po[:n, (h % 4) * 128:(h % 4 + 1) * 128], AF.Square, [pbo], [Bjunk2, Bms], accum_out=ms[:n, h:h + 1])
                ts("dve", ms[:n, 8:16], sc[:n, 0:8], 128.0 * EPS, None, ALU.mult, None, [Bsc], [Bms])
                stt(ms[:n, 8:16], ms[:n, 0:8], 1.0 / 128, ms[:n, 8:16], ALU.mult, ALU.add, [Bms], [Bms])
                act(ms[:n, 8:16], ms[:n, 8:16], AF.Ln, [Bms], [Bms])
                act(ms[:n, 0:8], ms[:n, 8:16], AF.Exp, [Bms], [Bms], scale=-0.5)
                for h in range(8):
                    po, pbo = opsum[h // 4]
                    stt(on[:n, h * 128:(h + 1) * 128], po[:n, (h % 4) * 128:(h % 4 + 1) * 128], ms[:n, h:h + 1], zs[t.xi][:n, h * 128:(h + 1) * 128],
                        ALU.mult, ALU.mult, [pbo, Bms, Bzs[t.xi]], [Bon])
                pt, pb, _ = psum("g"); ptv = bf(pt).rearrange("p (h c) -> p h c", h=8)
                for h in range(8):
                    tr(ptv[:, h, 0:n], on[:n, h * 128:(h + 1) * 128], ident[:n, :n], [Bon, Bc], [pb])
                ts("dve", onT[:, :, col:col + n], ptv[:, :, 0:n], ghead[:, 0:1], None, ALU.mult, None, [pb, Bp], [BonT])
                fill()
                P.ck("g7")
                if isS:
                    sample_state_update()
                P.ck("g8")

            for _ in cg:
                pass
            if isSG:
                emit_sample_tails()
            for q in range(2):
                wd, Bwd = wnext(("dout", q * 512, 512))
                wg, Bwg = wnext(("in", 7184 + q * 512, 512))
                for ml in range(4):
                    m = q * 4 + ml
                    for bk in blocks:
                        c0, c1 = bk; n = c1 - c0
                        pd_, pbd = fm(wd, Bwd, ml, onT, BonT, bk)
                        pg, pbg = fm(wg, Bwg, ml, hT, BhT, bk)
                        sg_, Bsg = ftmp(); t1, Bt1 = ftmp()
                        act(sg_[:, 0:n], pg[:, 0:n], AF.Sigmoid, [pbg], [Bsg])
                        tt("dve", t1[:, 0:n], pd_[:, 0:n], sg_[:, 0:n], ALU.mult, [pbd, Bsg], [Bt1])
                        tt("dve", mixT[:, m, c0:c1], mixT[:, m, c0:c1], t1[:, 0:n], ALU.add, [BmixT, Bt1], [BmixT])
            P.ck("p5")
            for nb in range(2):
                wo_, Bwo = wnext(("o", nb * 512, 512))
                for t in tiles:
                    pt, pb, _ = psum()
                    for kc in range(8):
                        mm(pt[:t.n, :], mixT[:, kc, t.col:t.col + t.n], wo_[:, kc, :], kc == 0, kc == 7, [BmixT, Bwo], [pb])
                    x = xt[t.xi]
                    tt("dve", x[:t.n, nb * 512:(nb + 1) * 512], pt[:t.n, :], x[:t.n, nb * 512:(nb + 1) * 512], ALU.add, [pb, Bxt[t.xi]], [Bxt[t.xi]])
            wprefetch()
            P.ck("p6")
            norm_to_T(G, gmlp)
            P.handover([BqkvT, BmixT, Bcpre], [BaT])
            for j in range(8):
                wu, Bwu = wnext(("up", j * 512, 512))
                for ml in range(4):
                    m = j * 4 + ml
                    for bk in blocks:
                        c0, c1 = bk; n = c1 - c0
                        pu, pbu = fm(wu, Bwu, ml, hT, BhT, bk)
                        r, Br = ftmp()
                        act(r[:, 0:n], pu[:, 0:n], AF.Relu, [pbu], [Br])
                        act(aT[:, m, c0:c1], r[:, 0:n], AF.Square, [Br], [BaT])
            P.ck("p8")
            for nb in range(2):
                banks = [psum() for _ in tiles]
                for kb in range(4):
                    wd, Bwd = wnext(("down", kb, nb))
                    for ti, t in enumerate(tiles):
                        pt, pb, _ = banks[ti]
                        for kc in range(8):
                            mm(pt[:t.n, :], aT[:, kb * 8 + kc, t.col:t.col + t.n], wd[:, kc, :], kb == 0 and kc == 0, kb == 3 and kc == 7,
                               [BaT, Bwd], [pb])
                for ti, t in enumerate(tiles):
                    pt, pb, _ = banks[ti]
                    x = xt[t.xi]
                    tt("dve", x[:t.n, nb * 512:(nb + 1) * 512], pt[:t.n, :], x[:t.n, nb * 512:(nb + 1) * 512], ALU.add, [pb, Bxt[t.xi]], [Bxt[t.xi]])
            P.ck("p9")
            for ti, t in enumerate(tiles):
                n = t.n; x = xt[t.xi]; Bx = Bxt[t.xi]
                y = yt[0]; By = Byt[0]
                act(junk[:n, :], x[:n, :], AF.Square, [Bx], [Bjunk, Bssx], accum_out=ssx[:n, 0:1])
                act(ssx[:n, 1:2], ssx[:n, 0:1], AF.Sqrt, [Bssx], [Bssx], scale=1.0 / D, bias=EPS)
                P.op("dve", lambda e, n=n: e.reciprocal(out=ssx[:n, 2:3], in_=ssx[:n, 1:2]), [Bssx], [Bssx])
                stt(y[:n, :], x[:n, :], ssx[:n, 2:3], gfin[:n, :], ALU.mult, ALU.mult, [Bx, Bssx, Bp], [By])
                if t.kind == "S":
                    P.dma("pool", lambda e, y=y: e.dma_start(out=y_s[:, :], in_=y[0:64, :]), ("ytp", 0), reads=[By])
                elif t.t0 == 0:
                    P.dma("pool", lambda e, y=y: e.dma_start(out=y_p[0:112, :], in_=y[16:128, :]), ("ytp", 0), reads=[By])
                else:
                    P.dma("pool", lambda e, y=y, t=t: e.dma_start(out=y_p[t.t0 - 16:t.t0 - 16 + t.n, :], in_=y[0:t.n, :]), ("ytp", 0), reads=[By])
                if gi + 1 < len(groups):
                    for t2 in groups[gi + 1]["tiles"]:
                        if t2.xi == t.xi:
                            load_x(gi + 1, t2)

        if not max_groups:
            tail_out(glut, Bglut, 8, 30, 30,
                     lambda k: P.dma("sp", lambda e: e.dma_start(out=o_pconv[:, :], in_=stgT[0:30, :]), ("yt", 0), reads=[BstgT]), None)
            tail_out(qt, Bqt, 24, 3, 3,
                     lambda k: P.dma("sp", lambda e, k=k: e.dma_start(out=o_pqkv[:, k * 1024:(k + 1) * 1024], in_=stgT[0:3, :]), ("yt", 0), reads=[BstgT]), None)
            P.dma("sp", lambda e: e.dma_start(out=o_pdelta.rearrange("h d e -> d h e"), in_=S[:]), "o_p_delta", reads=[BS])
        P.ops["sp"].append({"fn": None, "deps": {("dma", k): v for k, v in P.dma_counts.items()}, "dma": None})
        P.emit()
    return nc


_CACHE = {}


def kernel(x_prompt, x_sample, state_conv, state_qkv_conv, state_delta, meta_tokens, g_mix, w_in,
           w_dw, b_dw, ln_g, ln_b, w_cout, w_short, a_log, dt_bias, g_head, w_dout, w_o, g_mlp,
           w_up, w_down, g_final):
    f = lambda a: np.ascontiguousarray(np.asarray(a, dtype=np.float32))
    if "nc" not in _CACHE:
        _CACHE["nc"] = build()
    nc = _CACHE["nc"]
    shared = dict(meta=f(meta_tokens), g_mix=f(g_mix[0]), w_in=f(w_in[0]), w_dw=f(w_dw[0]), b_dw=f(b_dw[0]), ln_g=f(ln_g[0]),
                  ln_b=f(ln_b[0]), w_cout=f(w_cout[0]), w_short=f(w_short[0]), a_log=f(a_log[0]), dt_bias=f(dt_bias[0]),
                  g_head=f(g_head[0]), w_dout=f(w_dout[0]), w_o=f(w_o[0]), g_mlp=f(g_mlp[0]), w_up=f(w_up[0]),
                  w_down=f(w_down[0]), g_final=f(g_final))
    xp = f(x_prompt); xs = f(x_sample); sc_ = f(state_conv); sq_ = f(state_qkv_conv); sd_ = f(state_delta)
    in_maps = []
    for c in range(8):
        d = dict(shared)
        d["x_p"] = xp[c]
        d["x_s"] = xs[16 * c:16 * c + 16].reshape(64, D)
        d["st_conv"] = sc_[0, 16 * c:16 * c + 16]
        d["st_qkv"] = sq_[0, 16 * c:16 * c + 16]
        d["st_delta"] = sd_[0, 16 * c:16 * c + 16]
        in_maps.append(d)
    res = run_bass_kernel_spmd(nc, in_maps, core_ids=list(range(8)))
    R = res.results
    y_prompt = np.stack([R[c]["y_p"] for c in range(8)]).astype(np.float32)
    y_sample = np.concatenate([R[c]["y_s"].reshape(16, 4, D) for c in range(8)]).astype(np.float32)
    p_conv = np.stack([R[c]["p_conv"] for c in range(8)])[None].astype(np.float32)
    p_qkv = np.stack([R[c]["p_qkv"] for c in range(8)])[None].astype(np.float32)
    p_delta = np.stack([R[c]["p_delta"] for c in range(8)])[None].astype(np.float32)
    s_conv = np.concatenate([R[c]["s_conv"] for c in range(8)])[None].astype(np.float32)
    s_qkv = np.concatenate([R[c]["s_qkv"] for c in range(8)])[None].astype(np.float32)
    s_delta = np.concatenate([R[c]["s_delta"] for c in range(8)])[None].astype(np.float32)
    return (y_prompt, y_sample, p_conv, p_qkv, p_delta, s_conv, s_qkv, s_delta)
```

```python
import numpy as np
from contextlib import ExitStack
import concourse.bass as bass
import concourse.mybir as mybir
from concourse.bass_utils import run_bass_kernel_spmd

F32 = mybir.dt.float32
BF16 = mybir.dt.bfloat16
AF = mybir.ActivationFunctionType
ALU = mybir.AluOpType

ENGS = ("pe", "act", "dve", "pool", "sp")
SAME_ENGINE_SYNC = {"pe": False, "act": True, "dve": True, "pool": True, "sp": False}

D = 1024
DIN = 8208
DFF = 4096
NPT = 2064
EPS = 1e-6
NEG = -30000.0


class Buf:
    __slots__ = ("name", "lw", "rd", "excl")

    def __init__(self, name, excl=False):
        self.name = name
        self.lw = None
        self.rd = {}
        self.excl = excl


class Prog:
    def __init__(self, nc):
        self.nc = nc
        self.ops = {e: [] for e in ENGS}
        self.dma_counts = {}
        self.dma_keys = []
        self.enabled = True
        self.stop_at = None

    def ck(self, name):
        if self.stop_at is not None and name == self.stop_at:
            self.enabled = False

    def _collect(self, eng, reads, writes):
        deps = {}

        def add(k, v):
            if k[0] == "eng" and k[1] == eng and not SAME_ENGINE_SYNC[eng]:
                return
            if k not in deps or deps[k] < v:
                deps[k] = v

        for b in reads:
            if b.lw is not None:
                add((b.lw[0], b.lw[1]), b.lw[2])
            if b.excl:
                for k, v in b.rd.items():
                    if k != ("eng", eng):
                        add(k, v)
        for b in writes:
            if b.lw is not None:
                add((b.lw[0], b.lw[1]), b.lw[2])
            for k, v in b.rd.items():
                add(k, v)
        return deps

    def _commit(self, tok, reads, writes):
        k = (tok[0], tok[1])
        for b in reads:
            if b.rd.get(k, -1) < tok[2]:
                b.rd[k] = tok[2]
        for b in writes:
            b.lw = tok
            b.rd = {}

    def op(self, eng, fn, reads=(), writes=()):
        if not self.enabled:
            return
        deps = self._collect(eng, reads, writes)
        idx = len(self.ops[eng])
        self.ops[eng].append({"fn": fn, "deps": deps, "dma": None})
        self._commit(("eng", eng, idx), reads, writes)

    def dma(self, eng, fn, semkey, reads=(), writes=()):
        if not self.enabled:
            return
        deps = self._collect(eng, reads, writes)
        if semkey not in self.dma_counts:
            self.dma_counts[semkey] = 0
            self.dma_keys.append(semkey)
        self.dma_counts[semkey] += 16
        cnt = self.dma_counts[semkey]
        self.ops[eng].append({"fn": fn, "deps": deps, "dma": semkey})
        self._commit(("dma", semkey, cnt), reads, writes)

    def wait_all(self, eng, bufs):
        deps = self._collect(eng, bufs, ())
        self.ops[eng].append({"fn": None, "deps": deps, "dma": None})

    def handover(self, old, new):
        merged = {}
        for b in old:
            if b.lw is not None:
                k = (b.lw[0], b.lw[1])
                merged[k] = max(merged.get(k, -1), b.lw[2])
            for k, v in b.rd.items():
                merged[k] = max(merged.get(k, -1), v)
        for b in new:
            b.lw = None
            b.rd = dict(merged)

    def emit(self):
        nc = self.nc
        sig = {e: set() for e in ENGS}
        for e in ENGS:
            for o in self.ops[e]:
                for (kind, key), v in o["deps"].items():
                    if kind == "eng":
                        sig[key].add(v)
        cum = {}
        for e in ENGS:
            c = 0
            m = {}
            for i in range(len(self.ops[e])):
                if i in sig[e]:
                    c += 1
                    m[i] = c
            cum[e] = m
        with ExitStack() as st:
            esem = {e: st.enter_context(nc.semaphore("s_" + e)) for e in ENGS}
            dsem = {k: st.enter_context(nc.semaphore("d_%d" % i)) for i, k in enumerate(self.dma_keys)}
            block = st.enter_context(nc.Block())
            engobj = {"pe": block.tensor, "act": block.scalar, "dve": block.vector,
                      "pool": block.gpsimd, "sp": block.sync}

            def make(ename):
                ops = self.ops[ename]

                def body(eng):
                    seen = {}
                    for i, o in enumerate(ops):
                        for (kind, key), v in o["deps"].items():
                            if kind == "eng":
                                val = cum[key][v]
                                s = esem[key]
                            else:
                                val = v
                                s = dsem[key]
                            if seen.get((kind, key), 0) >= val:
                                continue
                            seen[(kind, key)] = val
                            eng.wait_ge(s, val)
                        if o["fn"] is None:
                            continue
                        inst = o["fn"](eng)
                        if o["dma"] is not None:
                            inst.then_inc(dsem[o["dma"]], 16)
                        elif i in cum[ename]:
                            inst.then_inc(esem[ename], 1)
                return body

            for e in ENGS:
                if self.ops[e]:
                    engobj[e](make(e))


class TileD:
    def __init__(self, kind, n, t0, col, xi):
        self.kind, self.n, self.t0, self.col, self.xi = kind, n, t0, col, xi
        self.nlev = {128: 6, 16: 3, 64: 1}[n] if kind == "P" else 1


def make_groups(with_sample=True):
    groups = []
    ptiles = [(i * 128, min(128, NPT - i * 128)) for i in range(17)]
    split = [ptiles[0:4], ptiles[4:8], ptiles[8:12], ptiles[12:17]]
    for gi, pts in enumerate(split):
        tiles = []
        col = 0
        for xi, (t0, n) in enumerate(pts):
            tiles.append(TileD("P", n, t0, col, xi))
            col += n
        blocks = []
        c = 0
        while c < col:
            blocks.append((c, min(col, c + 512)))
            c += 512
        groups.append(dict(tiles=tiles, pc0=0, npc=col, ncol=col, blocks=blocks, last=(gi == 3), first=False))
    if with_sample:
        groups.append(dict(tiles=[TileD("S", 64, 0, 0, 0)], pc0=64, npc=0, ncol=64, blocks=[(0, 64)], last=False, first=True))
    return groups


def wsched():
    L = []
    for j in range(6):
        L.append(("in", 2048 + 512 * j, 512))
    L += [("in", 5120, 512), ("in", 5632, 512), ("in", 6144, 16)]
    for q in range(2):
        L.append(("in", q * 512, 512))
        L.append(("in", 1024 + q * 512, 512))
    for q in range(2):
        L += [("cout", q * 512, 512), ("in", 6160 + q * 512, 512)]
    for q in range(2):
        L += [("dout", q * 512, 512), ("in", 7184 + q * 512, 512)]
    L += [("o", 0, 512), ("o", 512, 512)]
    for j in range(8):
        L.append(("up", j * 512, 512))
    for nb in range(2):
        for kb in range(4):
            L.append(("down", kb, nb))
    return L


def build(with_sample=True, NW=3, debug=False, max_groups=None, stop_at=None):
    nc = bass.Bass("TRN2", target_bir_lowering=False)
    din = lambda name, shape: nc.dram_tensor(name, shape, F32, kind="ExternalInput").ap()
    dout = lambda name, shape: nc.dram_tensor(name, shape, F32, kind="ExternalOutput").ap()
    x_p = din("x_p", [2048, D]); x_s = din("x_s", [64, D])
    st_conv = din("st_conv", [16, 30, D]); st_qkv = din("st_qkv", [16, 3, 3072]); st_delta = din("st_delta", [16, 8, 128, 128])
    meta = din("meta", [16, D])
    g_mix = din("g_mix", [D]); w_in = din("w_in", [D, DIN]); w_dw = din("w_dw", [31, D]); b_dw = din("b_dw", [D])
    ln_g = din("ln_g", [D]); ln_b = din("ln_b", [D]); w_cout = din("w_cout", [D, D]); w_short = din("w_short", [4, 3072])
    a_log = din("a_log", [8]); dt_bias = din("dt_bias", [8]); g_head = din("g_head", [128]); w_dout = din("w_dout", [D, D])
    w_o = din("w_o", [D, D]); g_mlp = din("g_mlp", [D]); w_up = din("w_up", [D, DFF]); w_down = din("w_down", [DFF, D])
    g_final = din("g_final", [D])
    y_p = dout("y_p", [2048, D]); y_s = dout("y_s", [64, D])
    o_pconv = dout("p_conv", [30, D]); o_pqkv = dout("p_qkv", [3, 3072]); o_pdelta = dout("p_delta", [8, 128, 128])
    o_sconv = dout("s_conv", [16, 30, D]); o_sqkv = dout("s_qkv", [16, 3, 3072]); o_sdelta = dout("s_delta", [16, 8, 128, 128])
    wmap = {"in": w_in, "cout": w_cout, "dout": w_dout, "o": w_o, "up": w_up}

    P = Prog(nc)
    P.stop_at = stop_at
    groups = make_groups(with_sample)
    if max_groups:
        groups = groups[:max_groups]
    MAXC = 528
    MAXPC = max(g["npc"] for g in groups)

    with ExitStack() as st:
        st.enter_context(nc.allow_non_contiguous_dma(reason="small params"))
        cnt = [0]

        def sb(shape, dt, name=None):
            cnt[0] += 1
            return st.enter_context(nc.sbuf_tensor(name or ("t%d" % cnt[0]), shape, dt))

        psT = [st.enter_context(nc.psum_tensor("ps%d" % i, [128, 512], F32)) for i in range(8)]
        psB = [Buf("ps%d" % i, excl=True) for i in range(8)]
        ps_reserved = set()
        ps_ctr = [0]

        pool_ctr = {"d": 0, "g": 0}

        def psum(pool=None):
            if pool == "d":
                i = pool_ctr["d"] % 4
                pool_ctr["d"] += 1
                return psT[i], psB[i], i
            if pool == "g":
                i = 4 + pool_ctr["g"] % 4
                pool_ctr["g"] += 1
                return psT[i], psB[i], i
            while True:
                i = ps_ctr[0] % 8
                ps_ctr[0] += 1
                if i not in ps_reserved:
                    return psT[i], psB[i], i

        def bf(ap):
            return ap[:].bitcast(BF16)

        def mm(out, lhsT, rhs, start, stop, reads, writes, skip=False):
            if skip:
                P.op("pe", lambda e: e.matmul(out, lhsT=lhsT, rhs=rhs, start=start, stop=stop, skip_group_check=True), reads, writes)
            else:
                P.op("pe", lambda e: e.matmul(out, lhsT=lhsT, rhs=rhs, start=start, stop=stop), reads, writes)

        def tr(out, in_, idn, reads, writes):
            P.op("pe", lambda e: e.transpose(out=out, in_=in_, identity=idn), reads, writes)

        def act(out, in_, func, reads, writes, **kw):
            P.op("act", lambda e: e.activation(out=out, in_=in_, func=func, **kw), reads, writes)

        def ts(eng, out, in0, s1, s2, op0, op1, reads, writes):
            if op1 is None:
                P.op(eng, lambda e: e.tensor_scalar(out=out, in0=in0, scalar1=s1, scalar2=None, op0=op0), reads, writes)
            else:
                P.op(eng, lambda e: e.tensor_scalar(out=out, in0=in0, scalar1=s1, scalar2=s2, op0=op0, op1=op1), reads, writes)

        def tt(eng, out, in0, in1, op, reads, writes):
            P.op(eng, lambda e: e.tensor_tensor(out=out, in0=in0, in1=in1, op=op), reads, writes)

        def stt(out, in0, scalar, in1, op0, op1, reads, writes):
            P.op("dve", lambda e: e.scalar_tensor_tensor(out=out, in0=in0, scalar=scalar, in1=in1, op0=op0, op1=op1), reads, writes)

        def cp(eng, out, in_, reads, writes):
            P.op(eng, lambda e: e.tensor_copy(out=out, in_=in_), reads, writes)

        Bc = Buf("consts")
        identf = sb([128, 128], F32); ident = sb([128, 128], BF16)
        onesf = sb([128, 128], F32); onesb = sb([128, 128], BF16)
        triU = sb([128, 128], F32); mmin = sb([128, 128], F32); offd = sb([128, 128], F32)
        blk = sb([64, 64], F32); triUs = sb([64, 64], F32); mmins = sb([64, 64], F32); offds = sb([64, 64], F32)
        bmask = sb([64, 16], F32)
        tmpc = sb([64, 64], F32)

        def pool(fn):
            P.op("pool", fn, reads=[Bc], writes=[Bc])

        pool(lambda e: e.memset(identf[:], 0.0))
        pool(lambda e: e.affine_select(out=identf[:], in_=identf[:], pattern=[[-1, 128]], compare_op=ALU.not_equal,
                                       fill=1.0, base=0, channel_multiplier=1))
        pool(lambda e: e.tensor_copy(out=ident[:], in_=identf[:]))
        pool(lambda e: e.memset(onesf[:], 1.0))
        pool(lambda e: e.memset(onesb[:], 1.0))
        pool(lambda e: e.affine_select(out=triU[:], in_=onesf[:], pattern=[[1, 128]], compare_op=ALU.is_ge,
                                       fill=0.0, base=0, channel_multiplier=-1))
        pool(lambda e: e.memset(mmin[:], 0.0))
        pool(lambda e: e.affine_select(out=mmin[:], in_=mmin[:], pattern=[[1, 128]], compare_op=ALU.is_ge,
                                       fill=NEG, base=0, channel_multiplier=-1))
        pool(lambda e: e.affine_select(out=offd[:], in_=onesf[:], pattern=[[1, 128]], compare_op=ALU.not_equal,
                                       fill=0.0, base=0, channel_multiplier=-1))
        pool(lambda e: e.affine_select(out=blk[:], in_=onesf[:64, :64], pattern=[[-4, 16], [0, 4]], compare_op=ALU.is_ge,
                                       fill=0.0, base=0, channel_multiplier=1))
        pool(lambda e: e.affine_select(out=blk[:], in_=blk[:], pattern=[[4, 16], [0, 4]], compare_op=ALU.is_ge,
                                       fill=0.0, base=3, channel_multiplier=-1))
        pool(lambda e: e.tensor_tensor(out=triUs[:], in0=triU[:64, :64], in1=blk[:], op=ALU.mult))
        pool(lambda e: e.tensor_tensor(out=offds[:], in0=offd[:64, :64], in1=blk[:], op=ALU.mult))
        pool(lambda e: e.tensor_scalar(out=tmpc[:], in0=blk[:], scalar1=-NEG, scalar2=NEG, op0=ALU.mult, op1=ALU.add))
        pool(lambda e: e.tensor_tensor(out=mmins[:], in0=mmin[:64, :64], in1=blk[:], op=ALU.mult))
        pool(lambda e: e.tensor_tensor(out=mmins[:], in0=mmins[:], in1=tmpc[:], op=ALU.add))
        pool(lambda e: e.affine_select(out=bmask[:], in_=onesf[:64, :16], pattern=[[-4, 16]], compare_op=ALU.is_ge,
                                       fill=0.0, base=0, channel_multiplier=1))
        pool(lambda e: e.affine_select(out=bmask[:], in_=bmask[:], pattern=[[4, 16]], compare_op=ALU.is_ge,
                                       fill=0.0, base=3, channel_multiplier=-1))

        NXT = 5
        xt = [sb([128, D], F32, "xt%d" % i) for i in range(NXT)]; Bxt = [Buf("xt%d" % i) for i in range(NXT)]
        Bp = Buf("params")
        pstage = sb([40, 128], F32)
        pv = sb([128, 40], F32)
        wdw = sb([128, 8, 31], F32)
        wsh = sb([128, 24, 4], F32)
        ghead = sb([128, 1], F32)
        dtb = sb([128, 8], F32); nexpA = sb([128, 8], F32)
        gfin = sb([128, D], F32)
        for i, v in enumerate([g_mix, g_mlp, b_dw, ln_g, ln_b]):
            P.dma("sp", lambda e, i=i, v=v: e.dma_start(out=pstage[8 * i:8 * i + 8, :], in_=v.rearrange("(k p) -> k p", p=128)), "par", writes=[Bp])
        P.dma("sp", lambda e: e.dma_start(out=xt[0][0:31, :], in_=w_dw[:, :]), ("x", 0), writes=[Bxt[0]])
        for k in range(3):
            P.dma("sp", lambda e, k=k: e.dma_start(out=xt[1 + k][0:4, :], in_=w_short[:, k * 1024:(k + 1) * 1024]), ("x", 1 + k), writes=[Bxt[1 + k]])
        P.dma("sp", lambda e: e.dma_start(out=ghead[:], in_=g_head.rearrange("(p o) -> p o", o=1)), "par", writes=[Bp])
        P.dma("sp", lambda e: e.dma_start(out=dtb[:], in_=dt_bias.partition_broadcast(128)), "par", writes=[Bp])
        P.dma("sp", lambda e: e.dma_start(out=nexpA[:], in_=a_log.partition_broadcast(128)), "par", writes=[Bp])
        P.dma("sp", lambda e: e.dma_start(out=gfin[:], in_=g_final.partition_broadcast(128)), "par", writes=[Bp])
        act(nexpA[:], nexpA[:], AF.Exp, [Bp], [Bp])
        P.op("act", lambda e: e.mul(nexpA[:], nexpA[:], -1.0), [Bp], [Bp])
        pt, pb, _ = psum()
        tr(pt[:, 0:40], pstage[:, :], identf[:40, :40], [Bp, Bc], [pb])
        cp("dve", pv[:], pt[:, 0:40], [pb], [Bp])
        pt, pb, _ = psum()
        for m in range(8):
            tr(pt[:, m * 31:(m + 1) * 31], xt[0][0:31, m * 128:(m + 1) * 128], identf[:31, :31], [Bxt[0], Bc], [pb])
        cp("dve", wdw[:].rearrange("p m i -> p (m i)"), pt[:, 0:248], [pb], [Bp])
        pt, pb, _ = psum()
        for m in range(24):
            tr(pt[:, m * 4:(m + 1) * 4], xt[1 + m // 8][0:4, (m % 8) * 128:(m % 8 + 1) * 128], identf[:4, :4], [Bxt[1 + m // 8], Bc], [pb])
        cp("dve", wsh[:].rearrange("p m i -> p (m i)"), pt[:, 0:96], [pb], [Bp])
        gmix = pv[:, 0:8]; gmlp = pv[:, 8:16]; bdw = pv[:, 16:24]; lng = pv[:, 24:32]; lnb = pv[:, 32:40]

        P.ck("params")
        scr4 = sb([128, 2 * D], BF16, "scr4"); Bscr = Buf("scr4")
        hb = scr4[:, 0:D]; Bhb = Bscr
        junk = scr4[:, D:2 * D]; Bjunk = Bscr
        junk2 = sb([128, 128], BF16); Bjunk2 = Buf("junk2")
        ssx = sb([128, 4], F32); Bssx = Buf("ssx")
        hT = sb([128, 8, MAXC], BF16, "hT"); BhT = Buf("hT")
        NG = 2
        gext = [sb([128, 30 + MAXC], BF16, "gext%d" % i) for i in range(NG)]; Bgext = [Buf("gext%d" % i) for i in range(NG)]
        ghalo = sb([128, 8, 30], BF16); Bghalo = Buf("ghalo")
        qext = [sb([128, 3 + MAXC], BF16) for _ in range(NG)]; Bqext = [Buf("qext%d" % i) for i in range(NG)]
        qhalo = sb([128, 24, 3], BF16); Bqhalo = Buf("qhalo")
        glut = sb([128, 8, 30], F32, "glut"); Bglut = Buf("glut")
        qt = sb([128, 24, 3], F32); Bqt = Buf("qt")
        arena = sb([128, 32 * MAXC], BF16, "arena")

        def arena_views(mc):
            return (arena[:, 0:24 * mc].rearrange("p (m c) -> p m c", c=mc),
                    arena[:, 24 * mc:32 * mc].rearrange("p (m c) -> p m c", c=mc),
                    arena[:, 0:32 * mc].rearrange("p (m c) -> p m c", c=mc))
        BqkvT = Buf("qkvT"); Bcpre = Buf("cpre"); BmixT = Buf("mixT"); BaT = Buf("aT")
        if with_sample:
            o0 = 2048
            gexS = arena[:, o0:o0 + 4352].rearrange("p (m s t) -> p m s t", m=8, s=16); BgexS = Buf("gexS")
            o0 += 4352
            qexS = arena[:, o0:o0 + 2688].rearrange("p (m s t) -> p m s t", m=24, s=16); BqexS = Buf("qexS")
            o0 += 2688
            glutS = arena[:, o0:o0 + 1024].bitcast(F32).rearrange("p (m c) -> p m c", m=8); BglutS = Buf("glutS")
            o0 += 1024
            qtS = arena[:, o0:o0 + 3072].bitcast(F32).rearrange("p (m c) -> p m c", m=24); BqtS = Buf("qtS")
            o0 += 3072
            assert o0 <= 32 * MAXC
        f4 = [sb([128, 512], F32) for _ in range(2)]; Bf4 = [Buf("f4_%d" % i) for i in range(2)]
        f4i = [0]

        def ftmp():
            i = f4i[0] % 2
            f4i[0] += 1
            return f4[i], Bf4[i]

        cs2 = sb([128, 2, 512], BF16); Bcs2 = Buf("cs2")
        lnst = [sb([128, MAXC], F32) for _ in range(2)]; Blnst = Buf("lnst")
        decT = sb([128, 8, 128], F32); BdecT = Buf("decT")
        decTs = sb([128, 8, 128], F32); BdecTs = Buf("decTs")
        gm = decTs; Bgm = BdecTs
        EB = sb([128, 8, 128], F32); BEB = Buf("EB")
        cT = sb([128, 8, MAXC], BF16, "cT"); BcT = Buf("cT")
        zs = [sb([128, D], BF16) for _ in range(NXT)]; Bzs = [Buf("zs%d" % i) for i in range(NXT)]
        ba = sb([128, NXT, 16], F32); Bba = Buf("ba")
        onT = sb([128, 8, MAXC], BF16, "onT"); BonT = Buf("onT")
        wbuf = [sb([128, 8, 512], BF16) for _ in range(NW)]; Bw = [Buf("w%d" % i) for i in range(NW)]
        dgr = sb([128, 31 * 128], BF16, "dgr")
        dg31 = dgr[:, :].rearrange("p (i c) -> p i c", i=31); Bdg31q = [Buf("dg31q%d" % i) for i in range(4)]
        sqT = scr4[:, :].rearrange("p (j c) -> p j c", j=16); BsqT = Bscr
        dg4 = [sb([128, 4, 128], BF16) for _ in range(2)]; Bdg4 = [Buf("dg4a"), Buf("dg4b")]
        sc = sb([128, 64], F32, "sc"); sc2 = sb([128, 64], F32, "sc2"); Bsc = Buf("sc")
        Wp = [sb([128, 8, 128], BF16) for _ in range(2)]; BWp = [Buf("Wp0"), Buf("Wp1")]
        Qp = [sb([128, 8, 128], BF16) for _ in range(2)]; BQp = [Buf("Qp0"), Buf("Qp1")]
        Zp = [sb([128, 8, 128], BF16) for _ in range(2)]; BZp = [Buf("Zp0"), Buf("Zp1")]
        attnT = sb([128, 8, 128], BF16); BattnT = Buf("attnT")
        rhsk = sb([128, 8, 128], BF16); rhsv = sb([128, 8, 128], BF16); kdec = sb([128, 8, 128], BF16)
        Brhsk = Buf("rhsk"); Brhsv = Buf("rhsv"); Bkdec = Buf("kdec")
        nkcT = sb([128, 8, 128], BF16); BnkcT = Buf("nkcT")
        u = sb([128, 8, 128], BF16); Bu = Buf("u")
        qd = sb([128, 8, 128], BF16); Bqd = Buf("qd")
        on = sb([128, D], BF16, "on"); Bon = Buf("on")
        S = sb([128, 8, 128], F32); BS = Buf("S")
        Sbf = sb([128, 8, 128], BF16); BSbf = Buf("Sbf")
        ms = sb([128, 16], F32, "ms"); Bms = Buf("ms")
        yt = [sb([128, D], F32)]; Byt = [Buf("yt0")]
        stgT = yt[0]; BstgT = Byt[0]
        if with_sample:
            v3 = lambda a: a[:, :].rearrange("p (h c) -> p h c", h=8)
            S0 = [v3(xt[1]), v3(xt[2]), v3(yt[0])]; BS0 = [Bxt[1], Bxt[2], Byt[0]]
            S0key = [("x", 1), ("x", 2), ("yt", 0)]
            So = [v3(xt[3]), v3(xt[4])]; BSo = [Bxt[3], Bxt[4]]
            S0b = [v3(zs[1]), v3(zs[2])]; BS0b = [Bzs[1], Bzs[2]]
            um = [zs[3], zs[4]]; Bum = [Bzs[3], Bzs[4]]
            oTs = sb([128, 8, 64], F32, "oTs"); BoTs = Buf("oTs")
            uTs = sb([128, 8, 64], BF16, "uTs"); BuTs = Buf("uTs")
            stg = yt[0]; Bstg = Byt[0]
        Bout = {k: Buf("o_" + k) for k in ["y_p", "y_s", "p_conv", "p_qkv", "p_delta", "s_conv", "s_qkv", "s_delta"]}

        P.op("pool", lambda e: e.memset(ghalo[:], 0.0), [], [Bghalo])
        P.op("pool", lambda e: e.memset(qhalo[:], 0.0), [], [Bqhalo])
        P.op("pool", lambda e: e.memset(S[:], 0.0), [], [BS])
        P.op("pool", lambda e: e.memset(Sbf[:], 0.0), [], [BSbf])

        sched = wsched()
        allreq = sched * len(groups)
        wstate = {"next_load": 0, "next_use": 0}

        NT = len(sched)
        wsc = nc.dram_tensor("wsc", [NT, 128, 4096], BF16, kind="Internal").ap()
        Bwsc = [Buf("wsc%d" % j) for j in range(NT)]
        Bring = [Buf("ring%d" % j) for j in range(8)]
        pstate = {"next": 0}
        PLOOK = 6

        def wdims(j):
            kind, a, b = sched[j]
            if kind == "down":
                return w_down[a * 1024:(a + 1) * 1024, b * 512:(b + 1) * 512], 512
            return wmap[kind][:, a:a + b], b

        def pro_issue(upto):
            while pstate["next"] < min(NT, upto):
                j = pstate["next"]
                src, n = wdims(j)
                P.dma("pool", lambda e, j=j, src=src, n=n: e.dma_start(
                    out=wsc[j][:, 0:8 * n].rearrange("p (kc n) -> p kc n", n=n),
                    in_=src.rearrange("(kc p) n -> p kc n", p=128)), ("pro", j % 8), writes=[Bwsc[j], Bring[j % 8]])
                pstate["next"] += 1

        def w_issue(i):
            j = i % NT
            slot = i % NW
            pro_issue(j + PLOOK)
            _, n = wdims(j)
            P.dma("sp", lambda e, j=j, n=n, slot=slot: e.dma_start(
                out=wbuf[slot][:, :, 0:n], in_=wsc[j][:, 0:8 * n].rearrange("p (kc n) -> p kc n", n=n)),
                ("w", slot), reads=[Bwsc[j]], writes=[Bw[slot]])

        def wnext(tag):
            i = wstate["next_use"]
            assert allreq[i] == tag, (allreq[i], tag)
            while wstate["next_load"] < min(len(allreq), i + NW - 1):
                w_issue(wstate["next_load"])
                wstate["next_load"] += 1
            wstate["next_use"] += 1
            return wbuf[i % NW], Bw[i % NW]

        def wprefetch():
            i = wstate["next_use"]
            while wstate["next_load"] < min(len(allreq), i + NW - 1):
                w_issue(wstate["next_load"])
                wstate["next_load"] += 1

        def fm(wt, Bwt, mloc, src, Bsrc, blkc, KC=8, kc0=0, pool=None):
            c0, c1 = blkc
            pt, pb, _ = psum(pool)
            for kc in range(KC):
                mm(pt[:, 0:c1 - c0], wt[:, kc, mloc * 128:(mloc + 1) * 128], src[:, kc0 + kc, c0:c1], kc == 0, kc == KC - 1,
                   [Bwt, Bsrc], [pb])
            return pt, pb

        def norm_to_T(G, gvec):
            for t in G["tiles"]:
                n = t.n
                x = xt[t.xi]; Bx = Bxt[t.xi]
                act(junk[:n, :], x[:n, :], AF.Square, [Bx], [Bjunk, Bssx], accum_out=ssx[:n, 0:1])
                act(ssx[:n, 1:2], ssx[:n, 0:1], AF.Ln, [Bssx], [Bssx], scale=1.0 / D, bias=EPS)
                act(ssx[:n, 2:3], ssx[:n, 1:2], AF.Exp, [Bssx], [Bssx], scale=-0.5)
                ts("dve", hb[:n, :], x[:n, :], ssx[:n, 2:3], None, ALU.mult, None, [Bx, Bssx], [Bhb])
                pt, pb, _ = psum()
                pv_ = bf(pt).rearrange("p (k c) -> p k c", k=8)
                for kc in range(8):
                    tr(pv_[:, kc, 0:n], hb[:n, kc * 128:(kc + 1) * 128], ident[:n, :n], [Bhb, Bc], [pb])
                tt("dve", hT[:, :, t.col:t.col + n], pv_[:, :, 0:n], gvec.unsqueeze(2).broadcast_to([128, 8, n]), ALU.mult,
                   [pb, Bp], [BhT])

        def tail_out(src, Bsrc, nm, ncols, nrows, dst_fn, key):
            for m0 in range(0, nm, 4):
                pt, pb, _ = psum()
                for mm_ in range(4):
                    tr(pt[:ncols, mm_ * 128:(mm_ + 1) * 128], src[:, m0 + mm_, :], identf[:, :], [Bsrc, Bc], [pb])
                cp("dve", stgT[:ncols, (m0 % 8) * 128:(m0 % 8 + 4) * 128], pt[:ncols, :], [pb], [BstgT])
                if (m0 + 4) % 8 == 0:
                    dst_fn(m0 // 8)

        def emit_sample_tails():
            def sconv_out(k):
                for s_ in range(16):
                    P.dma("sp", lambda e, s_=s_: e.dma_start(out=o_sconv[s_, 26:30, :], in_=stgT[4 * s_:4 * s_ + 4, :]), ("yt", 0),
                          reads=[BstgT])
            tail_out(glutS, BglutS, 8, 64, 64, sconv_out, None)

            def sqkv_out(k):
                for s_ in range(16):
                    P.dma("sp", lambda e, s_=s_, k=k: e.dma_start(out=o_sqkv[s_, :, k * 1024:(k + 1) * 1024], in_=stgT[4 * s_ + 1:4 * s_ + 4, :]), ("yt", 0),
                          reads=[BstgT])
            tail_out(qtS, BqtS, 24, 64, 64, sqkv_out, None)

        xloaded = set()

        def load_x(gi_, t):
            if (gi_, t.xi) in xloaded:
                return
            xloaded.add((gi_, t.xi))
            x = xt[t.xi]; Bx = Bxt[t.xi]
            if t.kind == "S":
                P.dma("sp", lambda e, x=x: e.dma_start(out=x[0:64, :], in_=x_s[:, :]), ("x", t.xi), writes=[Bx])
            elif t.t0 == 0:
                P.dma("sp", lambda e, x=x: e.dma_start(out=x[0:16, :], in_=meta[:, :]), ("x", t.xi), writes=[Bx])
                P.dma("sp", lambda e, x=x: e.dma_start(out=x[16:128, :], in_=x_p[0:112, :]), ("x", t.xi), writes=[Bx])
            else:
                P.dma("sp", lambda e, x=x, t=t: e.dma_start(out=x[0:t.n, :], in_=x_p[t.t0 - 16:t.t0 - 16 + t.n, :]), ("x", t.xi), writes=[Bx])

        for gi, G in enumerate(groups):
            tiles = G["tiles"]; blocks = G["blocks"]; pc0 = G["pc0"]; npc = G["npc"]
            isSG = (tiles[0].kind == "S")
            sblock = (0, 64) if isSG else None
            qkvT, cpre, aT = arena_views(64 if isSG else MAXC)
            mixT = cpre
            for t in tiles:
                load_x(gi, t)
            if isSG:
                P.handover([BaT, BqkvT, BmixT, Bcpre], [BgexS, BqexS, BglutS, BqtS, BaT])
                for sg in range(4):
                    P.dma("sp", lambda e, sg=sg: e.dma_start(out=stg[0:120, :], in_=st_conv[4 * sg:4 * sg + 4].rearrange("s t c -> (s t) c")),
                          ("yt", 0), writes=[Bstg])
                    pts = []
                    for half in range(2):
                        pt, pb, _ = psum()
                        for mm_ in range(4):
                            m = half * 4 + mm_
                            tr(pt[:, mm_ * 120:(mm_ + 1) * 120], stg[0:120, m * 128:(m + 1) * 128], identf[:120, :120], [Bstg, Bc], [pb])
                        act(gexS[:, half * 4:half * 4 + 4, 4 * sg:4 * sg + 4, 0:30],
                            pt[:, 0:480].rearrange("p (m s t) -> p m s t", m=4, s=4), AF.Copy, [pb], [BgexS])
                for hf in range(3):
                    P.dma("sp", lambda e, hf=hf: e.dma_start(out=stg[0:48, :], in_=st_qkv.rearrange("s t c -> (s t) c")[:, hf * 1024:(hf + 1) * 1024]),
                          ("yt", 0), writes=[Bstg])
                    for half in range(2):
                        pt, pb, _ = psum()
                        for mm_ in range(4):
                            m = half * 4 + mm_
                            tr(pt[:, mm_ * 48:(mm_ + 1) * 48], stg[0:48, m * 128:(m + 1) * 128], identf[:48, :48], [Bstg, Bc], [pb])
                        act(qexS[:, hf * 8 + half * 4:hf * 8 + half * 4 + 4, :, 0:3],
                            pt[:, 0:192].rearrange("p (m s t) -> p m s t", m=4, s=16), AF.Copy, [pb], [BqexS])
                P.dma("sp", lambda e: e.dma_start(out=o_sconv[:, 0:26, :], in_=st_conv[:, 4:30, :]), "o_s_conv")
            P.ck("st")
            wprefetch()
            norm_to_T(G, gmix)
            P.ck("p0")

            P.handover([BaT], [BqkvT, Bcpre])
            QT = [(0, 8), (8, 16), (16, 24), (24, 31)]
            wts = {}

            def build31(m):
                for qi, (i0, i1) in enumerate(QT):
                    tt("pool", dg31[:, i0:i1, :], ident[:].unsqueeze(1).broadcast_to([128, i1 - i0, 128]),
                       wdw[:, m, i0:i1].unsqueeze(2).broadcast_to([128, i1 - i0, 128]), ALU.mult, [Bc, Bp], [Bdg31q[qi]])

            def stageA(m):
                q, ml = m // 4, m % 4
                if ml == 0:
                    wts[q] = (wnext(("in", q * 512, 512)), wnext(("in", 1024 + q * 512, 512)))
                (wa, Bwa), (wb_, Bwb) = wts[q]
                gx = gext[m % NG]; Bgx = Bgext[m % NG]
                if npc:
                    cp("pool", gx[:, 0:30], ghalo[:, m, :], [Bghalo], [Bgx])
                for bk in blocks:
                    c0, c1 = bk; n = c1 - c0
                    pa, pba = fm(wa, Bwa, ml, hT, BhT, bk, pool="d")
                    yield
                    pb2, pbb = fm(wb_, Bwb, ml, hT, BhT, bk, pool="d")
                    sg_, Bsg = ftmp()
                    act(sg_[:, 0:n], pb2[:, 0:n], AF.Sigmoid, [pbb], [Bsg])
                    if bk == sblock:
                        tt("dve", gexS[:, m, :, 30:34], pa[:, 0:64].rearrange("p (s t) -> p s t", t=4),
                           sg_[:, 0:64].rearrange("p (s t) -> p s t", t=4), ALU.mult, [pba, Bsg], [BgexS])
                        tt("dve", glutS[:, m, :], pa[:, 0:64], sg_[:, 0:64], ALU.mult, [pba, Bsg], [BglutS])
                    else:
                        tt("dve", gx[:, 30 + c0 - pc0:30 + c1 - pc0], pa[:, 0:n], sg_[:, 0:n], ALU.mult, [pba, Bsg], [Bgx])
                        if G["last"] and c1 == G["ncol"]:
                            lo = max(c0, G["ncol"] - 30)
                            tt("dve", glut[:, m, 30 - (c1 - lo):30], pa[:, lo - c0:n], sg_[:, lo - c0:n], ALU.mult, [pba, Bsg], [Bglut])
                        elif G["last"] and c1 > G["ncol"] - 30:
                            lo = max(c0, G["ncol"] - 30)
                            off = lo - (G["ncol"] - 30)
                            tt("dve", glut[:, m, off:off + c1 - lo], pa[:, lo - c0:n], sg_[:, lo - c0:n], ALU.mult, [pba, Bsg], [Bglut])
                    yield
                if npc:
                    cp("pool", ghalo[:, m, :], gx[:, npc:npc + 30], [Bgx], [Bghalo])

            def stageB(m):
                gx = gext[m % NG]; Bgx = Bgext[m % NG]
                for bi, bk in enumerate(blocks):
                    c0, c1 = bk; n = c1 - c0
                    pt, pb, _ = psum("d")
                    for i in range(31):
                        Bq = Bdg31q[[qi for qi, (i0, i1) in enumerate(QT) if i0 <= i < i1][0]]
                        if bk == sblock:
                            mm(pt[:, 0:64].rearrange("p (s t) -> p s t", t=4), dg31[:, i, :], gexS[:, m, :, i:i + 4], i == 0, i == 30,
                               [Bq, BgexS], [pb])
                        else:
                            mm(pt[:, 0:n], dg31[:, i, :], gx[:, c0 - pc0 + i:c1 - pc0 + i], i == 0, i == 30, [Bq, Bgx], [pb])
                        if i in (7, 15, 23):
                            yield
                    act(cpre[:, m, c0:c1], pt[:, 0:n], AF.Identity, [pb], [Bcpre], bias=bdw[:, m:m + 1])
                    yield

            def conv_gen():
                yield from stageA(0)
                for m in range(8):
                    build31(m)
                    if m + 1 < 8:
                        yield from stageA(m + 1)
                    yield from stageB(m)
                for bk in blocks:
                    c0, c1 = bk; n = c1 - c0
                    if n > 64:
                        sA = psum("d"); sB = psum("d")
                    else:
                        sA = psum("d"); sB = None
                    for m in range(8):
                        act(cs2[:, 1, 0:n], cpre[:, m, c0:c1], AF.Square, [Bcpre], [Bcs2])
                        if sB is not None:
                            mm(sA[0][:, 0:n], onesb[:, :], cpre[:, m, c0:c1], m == 0, m == 7, [Bc, Bcpre], [sA[1]])
                            mm(sB[0][:, 0:n], onesb[:, :], cs2[:, 1, 0:n], m == 0, m == 7, [Bc, Bcs2], [sB[1]])
                        else:
                            cp("pool", cs2[:, 0, 0:n], cpre[:, m, c0:c1], [Bcpre], [Bcs2])
                            mm(sA[0][:, 0:2 * n].rearrange("p (a n) -> p a n", a=2), onesb[:, :], cs2[:, :, 0:n], m == 0, m == 7,
                               [Bc, Bcs2], [sA[1]])
                        yield
                    if sB is not None:
                        psum_s, Bs1 = sA[0][:, 0:n], sA[1]
                        psum_q, Bs2 = sB[0][:, 0:n], sB[1]
                    else:
                        psum_s, Bs1 = sA[0][:, 0:n], sA[1]
                        psum_q, Bs2 = sA[0][:, n:2 * n], sA[1]
                    mean, Bmean = ftmp(); var, Bvar = ftmp()
                    msq = lnst[1][:, c0:c1]
                    act(mean[:, 0:n], psum_s, AF.Copy, [Bs1], [Bmean], scale=1.0 / D)
                    tt("pool", msq, mean[:, 0:n], mean[:, 0:n], ALU.mult, [Bmean], [Blnst])
                    stt(var[:, 0:n], psum_q, 1.0 / D, msq, ALU.mult, ALU.subtract, [Bs2, Blnst], [Bvar])
                    act(var[:, 0:n], var[:, 0:n], AF.Ln, [Bvar], [Bvar], bias=EPS)
                    act(lnst[0][:, c0:c1], var[:, 0:n], AF.Exp, [Bvar], [Blnst], scale=-0.5)
                    stt(lnst[1][:, c0:c1], mean[:, 0:n], -1.0, lnst[0][:, c0:c1], ALU.mult, ALU.mult, [Bmean, Blnst], [Blnst])
                    yield
                for m in range(8):
                    for bk in blocks:
                        c0, c1 = bk; n = c1 - c0
                        t1, Bt1 = ftmp(); t2, Bt2 = ftmp()
                        tt("dve", t1[:, 0:n], cpre[:, m, c0:c1], lnst[0][:, c0:c1], ALU.mult, [Bcpre, Blnst], [Bt1])
                        tt("pool", t2[:, 0:n], t1[:, 0:n], lnst[1][:, c0:c1], ALU.add, [Bt1, Blnst], [Bt2])
                        act(cT[:, m, c0:c1], t2[:, 0:n], AF.Silu, [Bt2, Bp], [BcT], scale=lng[:, m:m + 1], bias=lnb[:, m:m + 1])
                        yield
                P.handover([Bcpre], [BmixT])
                for q in range(2):
                    wc, Bwc = wnext(("cout", q * 512, 512))
                    wg, Bwg = wnext(("in", 6160 + q * 512, 512))
                    for ml in range(4):
                        m = q * 4 + ml
                        for bk in blocks:
                            c0, c1 = bk; n = c1 - c0
                            pc_, pbc = fm(wc, Bwc, ml, cT, BcT, bk, pool="d")
                            yield
                            pg, pbg = fm(wg, Bwg, ml, hT, BhT, bk, pool="d")
                            sg_, Bsg = ftmp()
                            act(sg_[:, 0:n], pg[:, 0:n], AF.Sigmoid, [pbg], [Bsg])
                            tt("dve", mixT[:, m, c0:c1], pc_[:, 0:n], sg_[:, 0:n], ALU.mult, [pbc, Bsg], [BmixT])
                            yield

            P.ck("p3")
            wq4 = {}

            def stage4A(m):
                j, ml = m // 4, m % 4
                if ml == 0:
                    wq4[j] = wnext(("in", 2048 + 512 * j, 512))
                wq, Bwq = wq4[j]
                qx = qext[m % NG]; Bqx = Bqext[m % NG]
                d4 = dg4[m % 2]; Bd4 = Bdg4[m % 2]
                tt("pool", d4[:], ident[:].unsqueeze(1).broadcast_to([128, 4, 128]),
                   wsh[:, m, :].unsqueeze(2).broadcast_to([128, 4, 128]), ALU.mult, [Bc, Bp], [Bd4])
                if npc:
                    cp("pool", qx[:, 0:3], qhalo[:, m, :], [Bqhalo], [Bqx])
                for bk in blocks:
                    c0, c1 = bk; n = c1 - c0
                    pq, pbq = fm(wq, Bwq, ml, hT, BhT, bk)
                    if bk == sblock:
                        act(qexS[:, m, :, 3:7], pq[:, 0:64].rearrange("p (s t) -> p s t", t=4), AF.Copy, [pbq], [BqexS])
                        act(qtS[:, m, :], pq[:, 0:64], AF.Copy, [pbq], [BqtS])
                    else:
                        act(qx[:, 3 + c0 - pc0:3 + c1 - pc0], pq[:, 0:n], AF.Copy, [pbq], [Bqx])
                        if G["last"] and c1 == G["ncol"]:
                            assert n >= 3
                            act(qt[:, m, :], pq[:, n - 3:n], AF.Copy, [pbq], [Bqt])
                if npc:
                    cp("pool", qhalo[:, m, :], qx[:, npc:npc + 3], [Bqx], [Bqhalo])

            def stage4B(m):
                qx = qext[m % NG]; Bqx = Bqext[m % NG]
                d4 = dg4[m % 2]; Bd4 = Bdg4[m % 2]
                for bk in blocks:
                    c0, c1 = bk; n = c1 - c0
                    pt, pb, _ = psum()
                    for i in range(4):
                        if bk == sblock:
                            mm(pt[:, 0:64].rearrange("p (s t) -> p s t", t=4), d4[:, i, :], qexS[:, m, :, i:i + 4], i == 0, i == 3,
                               [Bd4, BqexS], [pb])
                        else:
                            mm(pt[:, 0:n], d4[:, i, :], qx[:, c0 - pc0 + i:c1 - pc0 + i], i == 0, i == 3, [Bd4, Bqx], [pb])
                    act(qkvT[:, m, c0:c1], pt[:, 0:n], AF.Silu, [pb], [BqkvT])

            stage4A(0)
            for m in range(24):
                if m + 1 < 24:
                    stage4A(m + 1)
                stage4B(m)
            P.ck("p4")
            for nb in range(2):
                wz, Bwz = wnext(("in", 5120 + nb * 512, 512))
                for t in tiles:
                    pt, pb, _ = psum()
                    for kc in range(8):
                        mm(pt[:t.n, :], hT[:, kc, t.col:t.col + t.n], wz[:, kc, :], kc == 0, kc == 7, [BhT, Bwz], [pb])
                    act(zs[t.xi][:t.n, nb * 512:(nb + 1) * 512], pt[:t.n, :], AF.Silu, [pb], [Bzs[t.xi]])
            wba, Bwba = wnext(("in", 6144, 16))
            for t in tiles:
                pt, pb, _ = psum()
                for kc in range(8):
                    mm(pt[:t.n, 0:16], hT[:, kc, t.col:t.col + t.n], wba[:, kc, 0:16], kc == 0, kc == 7, [BhT, Bwba], [pb])
                act(ba[:t.n, t.xi, :], pt[:t.n, 0:16], AF.Copy, [pb], [Bba])
            wprefetch()

            P.ck("zba")
            cg = conv_gen()

            _fk = 1

            _fm = 2
            _fc = [0]

            def fill(k=1):
                _fc[0] += 1
                if _fc[0] % _fm:
                    return
                for _ in range(k * _fk):
                    next(cg, None)
            for t in tiles:
                n = t.n; col = t.col; isS = (t.kind == "S")
                tU = triUs if isS else triU
                tM = mmins if isS else mmin
                tO = offds if isS else offd
                tB = blk if isS else onesf
                qTt = lambda h: qkvT[:, h, col:col + n]
                kTt = lambda h: qkvT[:, 8 + h, col:col + n]
                vTt = lambda h: qkvT[:, 16 + h, col:col + n]
                tt("dve", sqT[:, :, 0:n], qkvT[:, 0:16, col:col + n], qkvT[:, 0:16, col:col + n], ALU.mult, [BqkvT], [BsqT])
                pt, pb, _ = psum("g")
                for j in range(16):
                    mm(pt[:n, j:j + 1], sqT[:, j, 0:n], onesb[:, 0:1], True, True, [BsqT, Bc], [pb])
                ts("dve", sc[:n, 0:16], pt[:n, 0:16], 1e-6, None, ALU.add, None, [pb], [Bsc])
                pass
                act(sc[:n, 24:32], sc[:n, 8:16], AF.Ln, [Bsc], [Bsc])
                act(sc[:n, 16:24], sc[:n, 24:32], AF.Exp, [Bsc], [Bsc], scale=0.5)
                act(sc[:n, 24:32], sc[:n, 24:32], AF.Exp, [Bsc], [Bsc], scale=-0.5)
                act(sc[:n, 32:40], ba[:n, t.xi, 0:8], AF.Exp, [Bba], [Bsc], scale=-1.0)
                ts("dve", sc[:n, 32:40], sc[:n, 32:40], 1.0, None, ALU.add, None, [Bsc], [Bsc])
                P.op("dve", lambda e, n=n: e.reciprocal(out=sc[:n, 32:40], in_=sc[:n, 32:40]), [Bsc], [Bsc])
                tt("dve", sc[:n, 40:48], ba[:n, t.xi, 8:16], dtb[:n, :], ALU.add, [Bba, Bp], [Bsc])
                stt(sc[:n, 48:56], sc[:n, 40:48], -1.0, sc[:n, 40:48], ALU.mult, ALU.max, [Bsc], [Bsc])
                act(sc[:n, 48:56], sc[:n, 48:56], AF.Exp, [Bsc], [Bsc], scale=-1.0)
                act(sc[:n, 48:56], sc[:n, 48:56], AF.Ln, [Bsc], [Bsc], bias=1.0)
                stt(sc[:n, 40:48], sc[:n, 40:48], 0.0, sc[:n, 48:56], ALU.max, ALU.add, [Bsc], [Bsc])
                tt("dve", sc[:n, 56:64], sc[:n, 40:48], nexpA[:n, :], ALU.mult, [Bsc, Bp], [Bsc])
                tt("dve", sc2[:n, 0:8], sc[:n, 32:40], sc[:n, 24:32], ALU.mult, [Bsc], [Bsc])
                stt(sc2[:n, 8:16], sc2[:n, 0:8], -1.0, sc[:n, 24:32], ALU.mult, ALU.mult, [Bsc], [Bsc])
                pt, pb, _ = psum("g")
                mm(pt[:n, 0:8], tU[:n, :n], sc[:n, 56:64], True, True, [Bc, Bsc], [pb])
                mm(pt[:n, 8:16], tB[:n, :n], sc[:n, 56:64], True, True, [Bc, Bsc], [pb])
                cp("dve", sc2[:n, 16:24], pt[:n, 0:8], [pb], [Bsc])
                pass
                act(sc2[:n, 24:32], pt[:n, 0:8], AF.Exp, [pb], [Bsc])
                tt("dve", sc2[:n, 32:40], pt[:n, 8:16], sc2[:n, 16:24], ALU.subtract, [pb, Bsc], [Bsc])
                act(sc2[:n, 32:40], sc2[:n, 32:40], AF.Exp, [Bsc], [Bsc])
                tt("dve", sc2[:n, 40:48], sc[:n, 24:32], sc2[:n, 32:40], ALU.mult, [Bsc], [Bsc])
                P.ck("g1")
                tt("dve", gm[:n, :, 0:n], tU[:n, :n].unsqueeze(1).broadcast_to([n, 8, n]),
                   sc[:n, 56:64].unsqueeze(2).broadcast_to([n, 8, n]), ALU.mult, [Bc, Bsc], [Bgm])
                gbs = []
                for hb_ in range(2):
                    pt, pb, _ = psum("g")
                    mm(pt[:, 0:4 * n].rearrange("p (h n) -> p h n", h=4), onesf[:n, :], gm[:n, hb_ * 4:hb_ * 4 + 4, 0:n], True, True,
                       [Bc, Bgm], [pb])
                    gbs.append((pt, pb))
                    pass
                for h in range(8):
                    pt, pb = gbs[h // 4]
                    stt(decT[:n, h, 0:n], pt[:n, (h % 4) * n:(h % 4 + 1) * n], sc2[:n, 16 + h:17 + h], tM[:n, :n], ALU.subtract, ALU.min,
                        [pb, Bsc, Bc], [BdecT])
                for hb_ in range(2):
                    pt, pb = gbs[hb_]
                    act(EB[:, hb_ * 4:hb_ * 4 + 4, 0:n], pt[:, 0:4 * n].rearrange("p (h n) -> p h n", h=4), AF.Exp, [pb], [BEB])
                act(decT[:n, :, 0:n], decT[:n, :, 0:n], AF.Exp, [BdecT], [BdecT])
                tt("dve", decTs[:n, :, 0:n], decT[:n, :, 0:n], tO[:n, :n].unsqueeze(1).broadcast_to([n, 8, n]), ALU.mult,
                   [BdecT, Bc], [BdecTs])
                P.ck("g2")
                W0, BW0 = Wp[0], BWp[0]
                for hb_ in range(2):
                    pk, pbk, _ = psum("g")
                    pq, pbq, _ = psum("g")
                    for hh in range(4):
                        h = hb_ * 4 + hh
                        mm(pk[:n, hh * 128:hh * 128 + n], kTt(h), kTt(h), True, True, [BqkvT], [pbk])
                        mm(pq[:n, hh * 128:hh * 128 + n], kTt(h), qTt(h), True, True, [BqkvT], [pbq])
                    for hh in range(4):
                        h = hb_ * 4 + hh
                        stt(W0[:n, h, 0:n], pk[:n, hh * 128:hh * 128 + n], sc2[:n, 8 + h:9 + h], decTs[:n, h, 0:n], ALU.mult, ALU.mult,
                            [pbk, Bsc, BdecTs], [BW0])
                        stt(attnT[:n, h, 0:n], pq[:n, hh * 128:hh * 128 + n], sc[:n, 24 + h:25 + h], decT[:n, h, 0:n], ALU.mult, ALU.mult,
                            [pbq, Bsc, BdecT], [BattnT])
                tt("dve", Zp[0][:n, :, 0:n], W0[:n, :, 0:n], ident[:n, :n].unsqueeze(1).broadcast_to([n, 8, n]), ALU.add, [BW0, Bc], [BZp[0]])
                pt, pb, _ = psum("g")
                pvw = bf(pt).rearrange("p (h c) -> p h c", h=8)
                for h in range(8):
                    tr(pvw[:n, h, 0:n], W0[:n, h, 0:n], ident[:n, :n], [BW0, Bc], [pb])
                act(Qp[0][:n, :, 0:n], pvw[:n, :, 0:n], AF.Copy, [pb], [BQp[0]])
                fill()
                cw = cq = cz = 0
                for lev in range(1, t.nlev + 1):
                    nq = 1 - cq
                    for hb_ in range(2):
                        pt, pb, _ = psum("g")
                        for hh in range(4):
                            h = hb_ * 4 + hh
                            mm(pt[:n, hh * 128:hh * 128 + n], Wp[cw][:n, h, 0:n], Qp[cq][:n, h, 0:n], True, True, [BWp[cw], BQp[cq]], [pb])
                        act(Qp[nq][:n, hb_ * 4:hb_ * 4 + 4, 0:n], pt[:n, :].rearrange("p (h c) -> p h c", h=4)[:, :, 0:n], AF.Copy,
                            [pb], [BQp[nq]])
                        fill()
                    if lev < t.nlev:
                        nw = 1 - cw
                        for hb_ in range(2):
                            pt, pb, _ = psum("g")
                            for hh in range(4):
                                h = hb_ * 4 + hh
                                mm(pt[:n, hh * 128:hh * 128 + n], Qp[cq][:n, h, 0:n], Wp[cw][:n, h, 0:n], True, True, [BWp[cw], BQp[cq]], [pb])
                            act(Wp[nw][:n, hb_ * 4:hb_ * 4 + 4, 0:n], pt[:n, :].rearrange("p (h c) -> p h c", h=4)[:, :, 0:n], AF.Copy,
                                [pb], [BWp[nw]])
                            fill()
                        cw = nw
                    cq = nq
                    nz = 1 - cz
                    for hb_ in range(2):
                        pt, pb, _ = psum("g")
                        for hh in range(4):
                            h = hb_ * 4 + hh
                            mm(pt[:n, hh * 128:hh * 128 + n], Qp[cq][:n, h, 0:n], Zp[cz][:n, h, 0:n], True, True, [BQp[cq], BZp[cz]], [pb])
                        tt("dve", Zp[nz][:n, hb_ * 4:hb_ * 4 + 4, 0:n], pt[:n, :].rearrange("p (h c) -> p h c", h=4)[:, :, 0:n],
                           Zp[cz][:n, hb_ * 4:hb_ * 4 + 4, 0:n], ALU.add, [pb, BZp[cz]], [BZp[nz]])
                        fill()
                    cz = nz
                Z = Zp[cz]; BZ = BZp[cz]
                P.ck("g4")
                pk, pbk, _ = psum("g"); pvk = bf(pk).rearrange("p (h c) -> p h c", h=8)
                pv2, pbv, _ = psum("g"); pvv = bf(pv2).rearrange("p (h c) -> p h c", h=8)
                for h in range(8):
                    tr(pvk[:n, h, :], kTt(h), ident[:, :], [BqkvT, Bc], [pbk])
                    tr(pvv[:n, h, :], vTt(h), ident[:, :], [BqkvT, Bc], [pbv])
                P.ck("g4a")
                bc = lambda colap: colap.unsqueeze(2).broadcast_to([n, 8, 128])
                tt("dve", rhsk[:n, :, :], pvk[:n, :, :], bc(sc2[:n, 24:32]), ALU.mult, [pbk, Bsc], [Brhsk])
                tt("dve", kdec[:n, :, :], pvk[:n, :, :], bc(sc2[:n, 40:48]), ALU.mult, [pbk, Bsc], [Bkdec])
                tt("dve", rhsv[:n, :, :], pvv[:n, :, :], bc(sc[:n, 16:24]), ALU.mult, [pbv, Bsc], [Brhsv])
                fill()
                P.ck("g4b")
                tt("pool", qd[:, :, 0:n], qkvT[:, 0:8, col:col + n], EB[:, :, 0:n], ALU.mult, [BqkvT, BEB], [Bqd])
                P.ck("g5")
                for hb_ in range(2):
                    pt, pb, _ = psum("g")
                    for hh in range(4):
                        h = hb_ * 4 + hh
                        mm(pt[:, hh * 128:hh * 128 + n], rhsk[:n, h, :], Z[:n, h, 0:n], True, True, [Brhsk, BZ], [pb])
                    P.op("act", lambda e, pt=pt, hb_=hb_, n=n: e.mul(nkcT[:, hb_ * 4:hb_ * 4 + 4, 0:n],
                                                                     pt[:, :].rearrange("p (h c) -> p h c", h=4)[:, :, 0:n], -1.0), [pb], [BnkcT])
                opsum = []
                if not isS:
                    for hb_ in range(2):
                        pu, pbu, _ = psum("g")
                        for hh in range(4):
                            h = hb_ * 4 + hh
                            mm(pu[:n, hh * 128:(hh + 1) * 128], Z[:n, h, 0:n], rhsv[:n, h, :], True, False, [BZ, Brhsv], [pbu])
                            mm(pu[:n, hh * 128:(hh + 1) * 128], nkcT[:, h, 0:n], Sbf[:, h, :], False, True, [BnkcT, BSbf], [pbu])
                        tt("dve", u[:n, hb_ * 4:hb_ * 4 + 4, :], pu[:n, :].rearrange("p (h e) -> p h e", h=4),
                           sc2[:n, hb_ * 4:hb_ * 4 + 4].unsqueeze(2).broadcast_to([n, 4, 128]), ALU.mult, [pbu, Bsc], [Bu])
                        fill()
                    for hb_ in range(2):
                        po, pbo, _ = psum("g")
                        for hh in range(4):
                            h = hb_ * 4 + hh
                            mm(po[:n, hh * 128:(hh + 1) * 128], qd[:, h, 0:n], Sbf[:, h, :], True, False, [Bqd, BSbf], [pbo])
                            mm(po[:n, hh * 128:(hh + 1) * 128], attnT[:n, h, 0:n], u[:n, h, :], False, True, [BattnT, Bu], [pbo])
                        opsum.append((po, pbo))
                        fill()
                    for hb_ in range(2):
                        pS, pbS, _ = psum("g")
                        for hh in range(4):
                            h = hb_ * 4 + hh
                            mm(pS[:, hh * 128:(hh + 1) * 128], kdec[:n, h, :], u[:n, h, :], True, True, [Bkdec, Bu], [pbS])
                        for hh in range(4):
                            h = hb_ * 4 + hh
                            stt(S[:, h, :], S[:, h, :], EB[:, h, n - 1:n], pS[:, hh * 128:(hh + 1) * 128], ALU.mult, ALU.add,
                                [BS, BEB, pbS], [BS])
                        act(Sbf[:, hb_ * 4:hb_ * 4 + 4, :], S[:, hb_ * 4:hb_ * 4 + 4, :], AF.Copy, [BS], [BSbf])
                        fill()
                else:
                    puT, pbuT, _ = psum("g"); puTv = puT[:, :].rearrange("p (h c) -> p h c", h=8)
                    poT, pboT, _ = psum("g"); poTv = poT[:, :].rearrange("p (h c) -> p h c", h=8)
                    for h in range(8):
                        mm(puTv[:, h, :], rhsv[:n, h, :], Z[:n, h, 0:n], h == 0, False, [Brhsv, BZ], [pbuT], skip=True)
                    for s in range(16):
                        s0 = S0[s % 3]; Bs0 = BS0[s % 3]; s0b = S0b[s % 2]; Bs0b = BS0b[s % 2]
                        P.dma("sp", lambda e, s=s, s0=s0: e.dma_start(out=s0[:], in_=st_delta[s].rearrange("h d e -> d h e")), S0key[s % 3], writes=[Bs0])
                        act(s0b[:], s0[:], AF.Copy, [Bs0], [Bs0b])
                        for h in range(8):
                            mm(puTv[:, h, 4 * s:4 * s + 4], s0b[:, h, :], nkcT[:, h, 4 * s:4 * s + 4], False, False, [Bs0b, BnkcT], [pbuT], skip=True)
                            mm(poTv[:, h, 4 * s:4 * s + 4], s0b[:, h, :], qd[:, h, 4 * s:4 * s + 4], (s == 0 and h == 0), False, [Bs0b, Bqd], [pboT], skip=True)
                    act(uTs[:], puTv, AF.Copy, [pbuT], [BuTs])
                    pt, pb, _ = psum("g"); ptv = bf(pt).rearrange("p (h c) -> p h c", h=8)
                    for h in range(8):
                        tr(ptv[:n, h, :], uTs[:, h, :], ident[:, :], [BuTs, Bc], [pb])
                    for h in range(8):
                        ts("dve", u[:n, h, :], ptv[:n, h, :], sc2[:n, h:h + 1], None, ALU.mult, None, [pb, Bsc], [Bu])
                    for h in range(8):
                        mm(poTv[:, h, :], u[:n, h, :], attnT[:n, h, 0:n], False, h == 7, [Bu, BattnT], [pboT], skip=True)
                    act(oTs[:], poTv, AF.Copy, [pboT], [BoTs])
                    for hb_ in range(2):
                        po, pbo, _ = psum("g")
                        for hh in range(4):
                            tr(po[:n, hh * 128:(hh + 1) * 128], oTs[:, hb_ * 4 + hh, :], identf[:, :], [BoTs, Bc], [pbo])
                        opsum.append((po, pbo))
                        pass
                    def sample_state_update(n=n):
                        for s in range(16):
                            s0 = S0[(s + 1) % 3]; Bs0 = BS0[(s + 1) % 3]
                            P.dma("sp", lambda e, s=s, s0=s0: e.dma_start(out=s0[:], in_=st_delta[s].rearrange("h d e -> d h e")), S0key[(s + 1) % 3], writes=[Bs0])
                            ums = um[s % 2]; Bums = Bum[s % 2]
                            act(ums[:n, :], u[:n, :, :].rearrange("p h e -> p (h e)"), AF.Identity, [Bu, Bc], [Bums], scale=bmask[:n, s:s + 1])
                            so = So[s % 2]; Bso = BSo[s % 2]
                            for hb_ in range(2):
                                pS, pbS, _ = psum("g")
                                for hh in range(4):
                                    h = hb_ * 4 + hh
                                    mm(pS[:, hh * 128:(hh + 1) * 128], kdec[:n, h, :], ums[:n, h * 128:(h + 1) * 128], True, True, [Bkdec, Bums], [pbS])
                                for hh in range(4):
                                    h = hb_ * 4 + hh
                                    stt(so[:, h, :], s0[:, h, :], EB[:, h, 4 * s + 3:4 * s + 4], pS[:, hh * 128:(hh + 1) * 128], ALU.mult, ALU.add,
                                        [Bs0, BEB, pbS], [Bso])
                            P.dma("pool", lambda e, s=s, so=so: e.dma_start(out=o_sdelta[s].rearrange("h d e -> d h e"), in_=so[:]), ("xp", 3 + s % 2),
                                  reads=[Bso])
                P.ck("g6")
                for h in range(8):
                    po, pbo = opsum[h // 4]
                    act(junk2[:n, :], po[:n, (h % 4) * 128:(h % 4 + 1) * 128], AF.Square, [pbo], [Bjunk2, Bms], accum_out=ms[:n, h:h + 1])
                ts("dve", ms[:n, 8:16], sc[:n, 0:8], 128.0 * EPS, None, ALU.mult, None, [Bsc], [Bms])
                stt(ms[:n, 8:16], ms[:n, 0:8], 1.0 / 128, ms[:n, 8:16], ALU.mult, ALU.add, [Bms], [Bms])
                act(ms[:n, 8:16], ms[:n, 8:16], AF.Ln, [Bms], [Bms])
                act(ms[:n, 0:8], ms[:n, 8:16], AF.Exp, [Bms], [Bms], scale=-0.5)
                for h in range(8):
                    po, pbo = opsum[h // 4]
                    stt(on[:n, h * 128:(h + 1) * 128], po[:n, (h % 4) * 128:(h % 4 + 1) * 128], ms[:n, h:h + 1], zs[t.xi][:n, h * 128:(h + 1) * 128],
                        ALU.mult, ALU.mult, [pbo, Bms, Bzs[t.xi]], [Bon])
                pt, pb, _ = psum("g"); ptv = bf(pt).rearrange("p (h c) -> p h c", h=8)
                for h in range(8):
                    tr(ptv[:, h, 0:n], on[:n, h * 128:(h + 1) * 128], ident[:n, :n], [Bon, Bc], [pb])
                ts("dve", onT[:, :, col:col + n], ptv[:, :, 0:n], ghead[:, 0:1], None, ALU.mult, None, [pb, Bp], [BonT])
                fill()
                P.ck("g7")
                if isS:
                    sample_state_update()
                P.ck("g8")

            for _ in cg:
                pass
            if isSG:
                emit_sample_tails()
            for q in range(2):
                wd, Bwd = wnext(("dout", q * 512, 512))
                wg, Bwg = wnext(("in", 7184 + q * 512, 512))
                for ml in range(4):
                    m = q * 4 + ml
                    for bk in blocks:
                        c0, c1 = bk; n = c1 - c0
                        pd_, pbd = fm(wd, Bwd, ml, onT, BonT, bk)
                        pg, pbg = fm(wg, Bwg, ml, hT, BhT, bk)
                        sg_, Bsg = ftmp(); t1, Bt1 = ftmp()
                        act(sg_[:, 0:n], pg[:, 0:n], AF.Sigmoid, [pbg], [Bsg])
                        tt("dve", t1[:, 0:n], pd_[:, 0:n], sg_[:, 0:n], ALU.mult, [pbd, Bsg], [Bt1])
                        tt("dve", mixT[:, m, c0:c1], mixT[:, m, c0:c1], t1[:, 0:n], ALU.add, [BmixT, Bt1], [BmixT])
            P.ck("p5")
            for nb in range(2):
                wo_, Bwo = wnext(("o", nb * 512, 512))
                for t in tiles:
                    pt, pb, _ = psum()
                    for kc in range(8):
                        mm(pt[:t.n, :], mixT[:, kc, t.col:t.col + t.n], wo_[:, kc, :], kc == 0, kc == 7, [BmixT, Bwo], [pb])
                    x = xt[t.xi]
                    tt("dve", x[:t.n, nb * 512:(nb + 1) * 512], pt[:t.n, :], x[:t.n, nb * 512:(nb + 1) * 512], ALU.add, [pb, Bxt[t.xi]], [Bxt[t.xi]])
            wprefetch()
            P.ck("p6")
            norm_to_T(G, gmlp)
            P.handover([BqkvT, BmixT, Bcpre], [BaT])
            for j in range(8):
                wu, Bwu = wnext(("up", j * 512, 512))
                for ml in range(4):
                    m = j * 4 + ml
                    for bk in blocks:
                        c0, c1 = bk; n = c1 - c0
                        pu, pbu = fm(wu, Bwu, ml, hT, BhT, bk)
                        r, Br = ftmp()
                        act(r[:, 0:n], pu[:, 0:n], AF.Relu, [pbu], [Br])
                        act(aT[:, m, c0:c1], r[:, 0:n], AF.Square, [Br], [BaT])
            P.ck("p8")
            for nb in range(2):
                banks = [psum() for _ in tiles]
                for kb in range(4):
                    wd, Bwd = wnext(("down", kb, nb))
                    for ti, t in enumerate(tiles):
                        pt, pb, _ = banks[ti]
                        for kc in range(8):
                            mm(pt[:t.n, :], aT[:, kb * 8 + kc, t.col:t.col + t.n], wd[:, kc, :], kb == 0 and kc == 0, kb == 3 and kc == 7,
                               [BaT, Bwd], [pb])
                for ti, t in enumerate(tiles):
                    pt, pb, _ = banks[ti]
                    x = xt[t.xi]
                    tt("dve", x[:t.n, nb * 512:(nb + 1) * 512], pt[:t.n, :], x[:t.n, nb * 512:(nb + 1) * 512], ALU.add, [pb, Bxt[t.xi]], [Bxt[t.xi]])
            P.ck("p9")
            for ti, t in enumerate(tiles):
                n = t.n; x = xt[t.xi]; Bx = Bxt[t.xi]
                y = yt[0]; By = Byt[0]
                act(junk[:n, :], x[:n, :], AF.Square, [Bx], [Bjunk, Bssx], accum_out=ssx[:n, 0:1])
                act(ssx[:n, 1:2], ssx[:n, 0:1], AF.Ln, [Bssx], [Bssx], scale=1.0 / D, bias=EPS)
                act(ssx[:n, 2:3], ssx[:n, 1:2], AF.Exp, [Bssx], [Bssx], scale=-0.5)
                stt(y[:n, :], x[:n, :], ssx[:n, 2:3], gfin[:n, :], ALU.mult, ALU.mult, [Bx, Bssx, Bp], [By])
                if t.kind == "S":
                    P.dma("pool", lambda e, y=y: e.dma_start(out=y_s[:, :], in_=y[0:64, :]), ("ytp", 0), reads=[By])
                elif t.t0 == 0:
                    P.dma("pool", lambda e, y=y: e.dma_start(out=y_p[0:112, :], in_=y[16:128, :]), ("ytp", 0), reads=[By])
                else:
                    P.dma("pool", lambda e, y=y, t=t: e.dma_start(out=y_p[t.t0 - 16:t.t0 - 16 + t.n, :], in_=y[0:t.n, :]), ("ytp", 0), reads=[By])
                if gi + 1 < len(groups):
                    for t2 in groups[gi + 1]["tiles"]:
                        if t2.xi == t.xi:
                            load_x(gi + 1, t2)

        if not max_groups:
            tail_out(glut, Bglut, 8, 30, 30,
                     lambda k: P.dma("sp", lambda e: e.dma_start(out=o_pconv[:, :], in_=stgT[0:30, :]), ("yt", 0), reads=[BstgT]), None)
            tail_out(qt, Bqt, 24, 3, 3,
                     lambda k: P.dma("sp", lambda e, k=k: e.dma_start(out=o_pqkv[:, k * 1024:(k + 1) * 1024], in_=stgT[0:3, :]), ("yt", 0), reads=[BstgT]), None)
            P.dma("sp", lambda e: e.dma_start(out=o_pdelta.rearrange("h d e -> d h e"), in_=S[:]), "o_p_delta", reads=[BS])
        P.ops["sp"].append({"fn": None, "deps": {("dma", k): v for k, v in P.dma_counts.items()}, "dma": None})
        P.emit()
    return nc


_CACHE = {}


def kernel(x_prompt, x_sample, state_conv, state_qkv_conv, state_delta, meta_tokens, g_mix, w_in,
           w_dw, b_dw, ln_g, ln_b, w_cout, w_short, a_log, dt_bias, g_head, w_dout, w_o, g_mlp,
           w_up, w_down, g_final):
    f = lambda a: np.ascontiguousarray(np.asarray(a, dtype=np.float32))
    if "nc" not in _CACHE:
        _CACHE["nc"] = build()
    nc = _CACHE["nc"]
    shared = dict(meta=f(meta_tokens), g_mix=f(g_mix[0]), w_in=f(w_in[0]), w_dw=f(w_dw[0]), b_dw=f(b_dw[0]), ln_g=f(ln_g[0]),
                  ln_b=f(ln_b[0]), w_cout=f(w_cout[0]), w_short=f(w_short[0]), a_log=f(a_log[0]), dt_bias=f(dt_bias[0]),
                  g_head=f(g_head[0]), w_dout=f(w_dout[0]), w_o=f(w_o[0]), g_mlp=f(g_mlp[0]), w_up=f(w_up[0]),
                  w_down=f(w_down[0]), g_final=f(g_final))
    xp = f(x_prompt); xs = f(x_sample); sc_ = f(state_conv); sq_ = f(state_qkv_conv); sd_ = f(state_delta)
    in_maps = []
    for c in range(8):
        d = dict(shared)
        d["x_p"] = xp[c]
        d["x_s"] = xs[16 * c:16 * c + 16].reshape(64, D)
        d["st_conv"] = sc_[0, 16 * c:16 * c + 16]
        d["st_qkv"] = sq_[0, 16 * c:16 * c + 16]
        d["st_delta"] = sd_[0, 16 * c:16 * c + 16]
        in_maps.append(d)
    res = run_bass_kernel_spmd(nc, in_maps, core_ids=list(range(8)))
    R = res.results
    y_prompt = np.stack([R[c]["y_p"] for c in range(8)]).astype(np.float32)
    y_sample = np.concatenate([R[c]["y_s"].reshape(16, 4, D) for c in range(8)]).astype(np.float32)
    p_conv = np.stack([R[c]["p_conv"] for c in range(8)])[None].astype(np.float32)
    p_qkv = np.stack([R[c]["p_qkv"] for c in range(8)])[None].astype(np.float32)
    p_delta = np.stack([R[c]["p_delta"] for c in range(8)])[None].astype(np.float32)
    s_conv = np.concatenate([R[c]["s_conv"] for c in range(8)])[None].astype(np.float32)
    s_qkv = np.concatenate([R[c]["s_qkv"] for c in range(8)])[None].astype(np.float32)
    s_delta = np.concatenate([R[c]["s_delta"] for c in range(8)])[None].astype(np.float32)
    return (y_prompt, y_sample, p_conv, p_qkv, p_delta, s_conv, s_qkv, s_delta)
```
